# Optimizing a Trainium2 kernel written in Bass

```python
import math
import jax, jax.numpy as jnp
from jax import lax
import numpy as np

D_MODEL = 2048
BATCH = 4
SEQ = 2048
DEPTH = 2
DEC_BATCH = 128
DEC_SEQ = 8
PAST_LEN = 16384
PAGE_SIZE = 128

RWKV_HEADS = 12
RWKV_HEAD_DIM = 64
RWKV_WIDTH = RWKV_HEADS * RWKV_HEAD_DIM
RWKV_DECAY_RANK = 64
RWKV_AAA_RANK = 64
RWKV_GATE_RANK = 128
RWKV_COLS = 3 * RWKV_WIDTH + RWKV_DECAY_RANK + RWKV_AAA_RANK + RWKV_GATE_RANK
RWKV_SPLITS = (RWKV_WIDTH, 2 * RWKV_WIDTH, 3 * RWKV_WIDTH,
               3 * RWKV_WIDTH + RWKV_DECAY_RANK,
               3 * RWKV_WIDTH + RWKV_DECAY_RANK + RWKV_AAA_RANK)
RWKV_LN_EPS = 64e-5
HGRN_HEADS = 6
HGRN_EXPAND = 128
HGRN_HEAD_DIM = 128
HGRN_WIDTH = HGRN_HEADS * HGRN_HEAD_DIM
HGRN_COLS = 4 * HGRN_WIDTH
HGRN_CHUNK = 16
LRU_BLOCKS = 8
LRU_BLOCK_DIM = 64
LRU_WIDTH = LRU_BLOCKS * LRU_BLOCK_DIM
LRU_CONV = 4
LRU_C = 8.0
LRU_COLS = 2 * LRU_WIDTH
MIX_WIDTH = RWKV_WIDTH + HGRN_WIDTH + LRU_WIDTH
IN_COLS = RWKV_COLS + HGRN_COLS + LRU_COLS
D_FF = 5632
FFN_CONV = 3
NORM_EPS = 1e-6

kernel_name = 'hybrid_rwkv7_hgrn2_rglru_convffn_step'


def rms_norm(x, w):
    xf = x.astype(jnp.float32)
    y = xf * lax.rsqrt(jnp.mean(xf * xf, axis=-1, keepdims=True) + NORM_EPS)
    return (y * w.astype(jnp.float32)).astype(x.dtype)


def causal_dwconv(x, buf, w, b):
    K = w.shape[0]
    T = x.shape[1]
    xp = jnp.concatenate([buf.astype(x.dtype), x], axis=1)
    y = b
    for j in range(K):
        y = y + w[j] * xp[:, j:j + T]
    return y, xp[:, T:]


def rwkv7_scan(S0, r, w, k, v, a, b):
    def step(S, inp):
        r_t, w_t, k_t, v_t, a_t, b_t = inp
        sa = jnp.einsum('bhvk,bhk->bhv', S, a_t)
        S = (S * w_t[:, :, None, :] + sa[..., None] * b_t[:, :, None, :]
             + v_t[..., None] * k_t[:, :, None, :])
        return S, jnp.einsum('bhvk,bhk->bhv', S, r_t)
    xs = tuple(jnp.moveaxis(t, 1, 0) for t in (r, w, k, v, a, b))
    S, ys = lax.scan(step, S0, xs)
    return jnp.moveaxis(ys, 0, 1), S


def rwkv7_mix(p, shift_prev, S0, lp):
    B, T, _ = p.shape
    H, N = RWKV_HEADS, RWKV_HEAD_DIM
    f32 = jnp.float32
    prev = jnp.concatenate([shift_prev[:, None].astype(p.dtype), p[:, :-1]], axis=1)
    xs = p + (prev - p) * lp['rwkv_mu']
    r, k, v, xw, xa, xg = jnp.split(xs, RWKV_SPLITS, axis=-1)
    w_log = -jax.nn.softplus(-(lp['rwkv_w0'] + jnp.tanh(xw) @ lp['rwkv_w2']).astype(f32)) - 0.5
    decay = jnp.exp(-jnp.exp(w_log))
    a = jax.nn.sigmoid((lp['rwkv_a0'] + xa @ lp['rwkv_a2']).astype(f32))
    g = jax.nn.sigmoid(xg) @ lp['rwkv_g2']
    kf = k.astype(f32)
    kk = (kf * lp['rwkv_k_k']).reshape(B, T, H, N)
    kk = kk / jnp.maximum(jnp.sqrt(jnp.sum(kk * kk, axis=-1, keepdims=True)), 1e-12)
    kf = kf * (1.0 + (a - 1.0) * lp['rwkv_k_a'])
    hd = lambda t: t.astype(f32).reshape(B, T, H, N)
    r_h, k_h, v_h, a_h, w_h = hd(r), hd(kf), hd(v), hd(a), hd(decay)
    y, S = rwkv7_scan(S0.astype(f32), r_h, w_h, k_h, v_h, -kk, kk * a_h)
    mean = jnp.mean(y, axis=-1, keepdims=True)
    var = jnp.mean(jnp.square(y - mean), axis=-1, keepdims=True)
    y = (y - mean) * lax.rsqrt(var + RWKV_LN_EPS)
    y = y.reshape(B, T, RWKV_WIDTH) * lp['rwkv_ln_w'] + lp['rwkv_ln_b']
    bonus = jnp.sum(r_h * k_h * lp['rwkv_r_k'], axis=-1, keepdims=True) * v_h
    y = (y + bonus.reshape(B, T, RWKV_WIDTH)) * g
    return y.astype(p.dtype), S.astype(p.dtype), p[:, -1]


def hgrn2_chunked(S0, q, k, v, log_f):
    B, T, H, DK = q.shape
    DV = v.shape[-1]
    C = math.gcd(T, HGRN_CHUNK)
    N = T // C
    to_chunks = lambda t: jnp.moveaxis(t.reshape(B, N, C, H, t.shape[-1]), 1, 0)
    causal = jnp.tril(jnp.ones((C, C), dtype=bool))[None, :, :, None, None]

    def step(S, inp):
        q_c, k_c, v_c, lf_c = inp
        b = jnp.cumsum(lf_c, axis=1)
        o_inter = jnp.einsum('bthk,bhkv->bthv', q_c * jnp.exp(b), S)
        dec = jnp.exp(jnp.where(causal, b[:, :, None] - b[:, None, :], -jnp.inf))
        A = jnp.einsum('bthk,btshk,bshk->btsh', q_c, dec, k_c)
        o = o_inter + jnp.einsum('btsh,bshv->bthv', A, v_c)
        b_last = b[:, -1]
        S = (jnp.exp(b_last)[..., None] * S
             + jnp.einsum('bshk,bshv->bhkv', k_c * jnp.exp(b_last[:, None] - b), v_c))
        return S, o

    S, o = lax.scan(step, S0, (to_chunks(q), to_chunks(k), to_chunks(v), to_chunks(log_f)))
    return jnp.moveaxis(o, 0, 1).reshape(B, T, H, DV), S


def hgrn2_mix(p, S0, lp):
    B, T, _ = p.shape
    f32 = jnp.float32
    q, f, i, g = jnp.split(p, 4, axis=-1)
    q = jax.nn.silu(q.astype(f32))
    lb = lp['hgrn_lb']
    fg = lb + (1.0 - lb) * jax.nn.sigmoid(f.astype(f32))
    log_f = jnp.log(fg)
    k = 1.0 - fg
    hd = lambda t, d: t.astype(f32).reshape(B, T, HGRN_HEADS, d)
    o, S = hgrn2_chunked(S0.astype(f32), hd(q, HGRN_EXPAND), hd(k, HGRN_EXPAND),
                         hd(i, HGRN_HEAD_DIM), hd(log_f, HGRN_EXPAND))
    o = rms_norm(o, lp['hgrn_norm_w'].reshape(HGRN_HEADS, HGRN_HEAD_DIM)).reshape(B, T, HGRN_WIDTH)
    o = o * jax.nn.silu(g.astype(f32))
    return o.astype(p.dtype), S.astype(p.dtype)


def rglru_mix(p, conv_buf, h0, lp):
    B, T, _ = p.shape
    f32 = jnp.float32
    xb, gate = jnp.split(p, 2, axis=-1)
    xc, new_buf = causal_dwconv(xb, conv_buf, lp['rglru_conv_w'], lp['rglru_conv_b'])
    xh = xc.reshape(B, T, LRU_BLOCKS, LRU_BLOCK_DIM)
    gr = jnp.einsum('bthi,hij->bthj', xh, lp['rglru_wa']).reshape(B, T, LRU_WIDTH) + lp['rglru_ba']
    gi = jnp.einsum('bthi,hij->bthj', xh, lp['rglru_wx']).reshape(B, T, LRU_WIDTH) + lp['rglru_bx']
    r = jax.nn.sigmoid(gr.astype(f32))
    i = jax.nn.sigmoid(gi.astype(f32))
    log_a = -LRU_C * r * jax.nn.softplus(-lp['rglru_lambda'].astype(f32))
    a = jnp.exp(log_a)
    u = jnp.sqrt(-jnp.expm1(2.0 * log_a)) * (i * xc.astype(f32))

    def combine(lhs, rhs):
        a1, b1 = lhs
        a2, b2 = rhs
        return a1 * a2, a2 * b1 + b2

    a_cum, h = lax.associative_scan(combine, (a, u), axis=1)
    h = h + a_cum * h0.astype(f32)[:, None]
    y = h * jax.nn.gelu(gate.astype(f32))
    return y.astype(p.dtype), new_buf, h[:, -1].astype(p.dtype)


def trunk_layer(x, st, lp):
    S_rw, shift_rw, S_hg, h_lru, buf_lru, buf_ffn = st
    hn = rms_norm(x, lp['norm_mix'])
    proj = hn @ lp['w_in']
    p_rw, p_hg, p_lru = jnp.split(proj, (RWKV_COLS, RWKV_COLS + HGRN_COLS), axis=-1)
    y_rw, S_rw, shift_rw = rwkv7_mix(p_rw, shift_rw, S_rw, lp)
    y_hg, S_hg = hgrn2_mix(p_hg, S_hg, lp)
    y_lru, buf_lru, h_lru = rglru_mix(p_lru, buf_lru, h_lru, lp)
    x = x + jnp.concatenate([y_rw, y_hg, y_lru], axis=-1) @ lp['w_out']
    hn = rms_norm(x, lp['norm_ffn'])
    gte, val = jnp.split(hn @ lp['ffn_w_up'], 2, axis=-1)
    gc, buf_ffn = causal_dwconv(gte, buf_ffn, lp['ffn_conv_w'], lp['ffn_conv_b'])
    x = x + (jax.nn.silu(gc) * val) @ lp['ffn_w_down']
    return x, (S_rw, shift_rw, S_hg, h_lru, buf_lru, buf_ffn)


def zero_states(batch, dtype):
    return (jnp.zeros((batch, RWKV_HEADS, RWKV_HEAD_DIM, RWKV_HEAD_DIM), dtype),
            jnp.zeros((batch, RWKV_COLS), dtype),
            jnp.zeros((batch, HGRN_HEADS, HGRN_EXPAND, HGRN_HEAD_DIM), dtype),
            jnp.zeros((batch, LRU_WIDTH), dtype),
            jnp.zeros((batch, LRU_CONV - 1, LRU_WIDTH), dtype),
            jnp.zeros((batch, FFN_CONV - 1, D_FF), dtype))


def setup_inputs(seed: int = 0) -> dict:
    key = jax.random.key(seed)
    ks = iter(jax.random.split(key, 48))
    f32 = jnp.float32
    nrm = lambda shape, scale: scale * jax.random.normal(next(ks), shape, f32)
    uni = lambda shape, lo, hi: jax.random.uniform(next(ks), shape, f32, minval=lo, maxval=hi)
    lam_a = uni((DEPTH, LRU_WIDTH), 0.9, 0.999)
    lam_s = lam_a ** (1.0 / LRU_C)
    return {
        'x_prompt': nrm((BATCH, SEQ, D_MODEL), 1.0),
        'x_sample': nrm((DEC_BATCH, DEC_SEQ, D_MODEL), 1.0),
        'state_rwkv': nrm((DEPTH, DEC_BATCH, RWKV_HEADS, RWKV_HEAD_DIM, RWKV_HEAD_DIM), 0.5),
        'state_rwkv_shift': nrm((DEPTH, DEC_BATCH, RWKV_COLS), 1.0),
        'state_hgrn': nrm((DEPTH, DEC_BATCH, HGRN_HEADS, HGRN_EXPAND, HGRN_HEAD_DIM), 0.5),
        'state_rglru': nrm((DEPTH, DEC_BATCH, LRU_WIDTH), 0.5),
        'cache_rglru_conv': nrm((DEPTH, DEC_BATCH, LRU_CONV - 1, LRU_WIDTH), 1.0),
        'cache_ffn_conv': nrm((DEPTH, DEC_BATCH, FFN_CONV - 1, D_FF), 1.0),
        'norm_mix': 1.0 + nrm((DEPTH, D_MODEL), 0.02),
        'w_in': nrm((DEPTH, D_MODEL, IN_COLS), D_MODEL ** -0.5),
        'rwkv_mu': uni((DEPTH, RWKV_COLS), 0.0, 1.0),
        'rwkv_w0': uni((DEPTH, RWKV_WIDTH), -6.0, 0.0),
        'rwkv_w2': nrm((DEPTH, RWKV_DECAY_RANK, RWKV_WIDTH), 0.5 * RWKV_DECAY_RANK ** -0.5),
        'rwkv_a0': nrm((DEPTH, RWKV_WIDTH), 0.1),
        'rwkv_a2': nrm((DEPTH, RWKV_AAA_RANK, RWKV_WIDTH), RWKV_AAA_RANK ** -0.5),
        'rwkv_g2': nrm((DEPTH, RWKV_GATE_RANK, RWKV_WIDTH), RWKV_GATE_RANK ** -0.5),
        'rwkv_k_k': 0.85 + nrm((DEPTH, RWKV_WIDTH), 0.05),
        'rwkv_k_a': 1.0 + nrm((DEPTH, RWKV_WIDTH), 0.05),
        'rwkv_r_k': nrm((DEPTH, RWKV_HEADS, RWKV_HEAD_DIM), 0.1),
        'rwkv_ln_w': 1.0 + nrm((DEPTH, RWKV_WIDTH), 0.02),
        'rwkv_ln_b': nrm((DEPTH, RWKV_WIDTH), 0.02),
        'hgrn_lb_logits': nrm((DEPTH, HGRN_WIDTH), 0.5),
        'hgrn_norm_w': 1.0 + nrm((DEPTH, HGRN_WIDTH), 0.02),
        'rglru_conv_w': nrm((DEPTH, LRU_CONV, LRU_WIDTH), LRU_CONV ** -0.5),
        'rglru_conv_b': nrm((DEPTH, LRU_WIDTH), 0.02),
        'rglru_wa': nrm((DEPTH, LRU_BLOCKS, LRU_BLOCK_DIM, LRU_BLOCK_DIM), LRU_BLOCK_DIM ** -0.5),
        'rglru_ba': nrm((DEPTH, LRU_WIDTH), 0.02),
        'rglru_wx': nrm((DEPTH, LRU_BLOCKS, LRU_BLOCK_DIM, LRU_BLOCK_DIM), LRU_BLOCK_DIM ** -0.5),
        'rglru_bx': nrm((DEPTH, LRU_WIDTH), 0.02),
        'rglru_lambda': jnp.log(lam_s) - jnp.log1p(-lam_s),
        'w_out': nrm((DEPTH, MIX_WIDTH, D_MODEL), MIX_WIDTH ** -0.5),
        'norm_ffn': 1.0 + nrm((DEPTH, D_MODEL), 0.02),
        'ffn_w_up': nrm((DEPTH, D_MODEL, 2 * D_FF), D_MODEL ** -0.5),
        'ffn_conv_w': nrm((DEPTH, FFN_CONV, D_FF), FFN_CONV ** -0.5),
        'ffn_conv_b': nrm((DEPTH, D_FF), 0.02),
        'ffn_w_down': nrm((DEPTH, D_FF, D_MODEL), D_FF ** -0.5),
        'norm_final': 1.0 + nrm((D_MODEL,), 0.02),
    }


def reference(x_prompt, x_sample, state_rwkv, state_rwkv_shift, state_hgrn, state_rglru,
              cache_rglru_conv, cache_ffn_conv, norm_mix, w_in, rwkv_mu, rwkv_w0, rwkv_w2,
              rwkv_a0, rwkv_a2, rwkv_g2, rwkv_k_k, rwkv_k_a, rwkv_r_k, rwkv_ln_w, rwkv_ln_b,
              hgrn_lb_logits, hgrn_norm_w, rglru_conv_w, rglru_conv_b, rglru_wa, rglru_ba,
              rglru_wx, rglru_bx, rglru_lambda, w_out, norm_ffn, ffn_w_up, ffn_conv_w,
              ffn_conv_b, ffn_w_down, norm_final):
    gam = jax.nn.softmax(hgrn_lb_logits.astype(jnp.float32), axis=0)
    lower_bounds = jnp.cumsum(gam, axis=0) - gam[0]
    x_p, x_s = x_prompt, x_sample
    new_p, new_s = [], []
    for l in range(DEPTH):
        lp = {
            'norm_mix': norm_mix[l], 'w_in': w_in[l], 'rwkv_mu': rwkv_mu[l],
            'rwkv_w0': rwkv_w0[l], 'rwkv_w2': rwkv_w2[l], 'rwkv_a0': rwkv_a0[l],
            'rwkv_a2': rwkv_a2[l], 'rwkv_g2': rwkv_g2[l], 'rwkv_k_k': rwkv_k_k[l],
            'rwkv_k_a': rwkv_k_a[l], 'rwkv_r_k': rwkv_r_k[l], 'rwkv_ln_w': rwkv_ln_w[l],
            'rwkv_ln_b': rwkv_ln_b[l], 'hgrn_lb': lower_bounds[l], 'hgrn_norm_w': hgrn_norm_w[l],
            'rglru_conv_w': rglru_conv_w[l], 'rglru_conv_b': rglru_conv_b[l],
            'rglru_wa': rglru_wa[l], 'rglru_ba': rglru_ba[l], 'rglru_wx': rglru_wx[l],
            'rglru_bx': rglru_bx[l], 'rglru_lambda': rglru_lambda[l], 'w_out': w_out[l],
            'norm_ffn': norm_ffn[l], 'ffn_w_up': ffn_w_up[l], 'ffn_conv_w': ffn_conv_w[l],
            'ffn_conv_b': ffn_conv_b[l], 'ffn_w_down': ffn_w_down[l],
        }
        x_p, sp = trunk_layer(x_p, zero_states(x_p.shape[0], x_p.dtype), lp)
        x_s, ss = trunk_layer(x_s, (state_rwkv[l], state_rwkv_shift[l], state_hgrn[l],
                                    state_rglru[l], cache_rglru_conv[l], cache_ffn_conv[l]), lp)
        new_p.append(sp)
        new_s.append(ss)
    y_prompt = rms_norm(x_p, norm_final)
    y_sample = rms_norm(x_s, norm_final)
    p_rwkv, p_rwkv_shift, p_hgrn, p_rglru, p_rglru_conv, p_ffn_conv = [
        jnp.stack(s, axis=0) for s in zip(*new_p)]
    s_rwkv, s_rwkv_shift, s_hgrn, s_rglru, s_rglru_conv, s_ffn_conv = [
        jnp.stack(s, axis=0) for s in zip(*new_s)]
    return (y_prompt, y_sample,
            p_rwkv, p_rwkv_shift, p_hgrn, p_rglru, p_rglru_conv, p_ffn_conv,
            s_rwkv, s_rwkv_shift, s_hgrn, s_rglru, s_rglru_conv, s_ffn_conv)
```

```python
import contextlib
import numpy as np
import concourse.bass as bass
import concourse.mybir as mybir
from concourse.bass_utils import run_bass_kernel_spmd

F32 = mybir.dt.float32
BF16 = mybir.dt.bfloat16
AF = mybir.ActivationFunctionType
ALU = mybir.AluOpType
AX = mybir.AxisListType

D = 2048
DEPTH = 2
NSEQ = 16
TS = 8
RW_COLS = 2560
HG_COLS = 3072
LRU_COLS = 1024
IN_COLS = 6656
DFF = 5632
EPS = 1e-6
RW_EPS = 64e-5
ENG = ('pe', 'act', 'dve', 'pool', 'sp')
NDSEM = 8
NDT = BF16
P_NM, P_NF, P_MU, P_W0, P_A0, P_KK, P_KA, P_OMKA, P_RK, P_LNW, P_LNB = 0, 16, 32, 52, 58, 64, 70, 76, 82, 88, 94
P_LB, P_OMLB, P_HNW, P_CW, P_CB, P_BA, P_BX, P_LAM, P_C8, P_2C8 = 100, 106, 112, 118, 134, 138, 142, 146, 150, 154


class Sched:
    def __init__(self, nc):
        self.nc = nc
        self.ops = {e: [] for e in ENG}
        self.res = {}
        self.bar = {e: {} for e in ENG}
        self.dma_since_bar = []

    def emit(self, eng, fn, reads=(), writes=()):
        deps = {}

        def add(d):
            e, i = d
            if e == 'sp':
                deps.setdefault(('sp', i), i)
            else:
                if deps.get(e, -1) < i:
                    deps[e] = i

        for r in reads:
            st = self.res.get(r)
            if st and st['w'] is not None:
                add(st['w'])
        for w in writes:
            st = self.res.get(w)
            if st:
                if st['w'] is not None:
                    add(st['w'])
                for e, i in st['r'].items():
                    if e == 'sp':
                        for ii in i:
                            add(('sp', ii))
                    else:
                        add((e, i))
        for k, v in self.bar[eng].items():
            if isinstance(k, tuple):
                deps.setdefault(k, v)
            elif deps.get(k, -1) < v:
                deps[k] = v
        self.bar[eng] = {}
        idx = len(self.ops[eng])
        if eng == 'pe':
            deps.pop('pe', None)
        self.ops[eng].append(dict(fn=fn, deps=deps, need=False))
        if eng == 'sp':
            self.dma_since_bar.append(idx)
        for r in reads:
            st = self.res.setdefault(r, {'w': None, 'r': {}})
            if eng == 'sp':
                st['r'].setdefault('sp', []).append(idx)
            else:
                st['r'][eng] = idx
        for w in writes:
            self.res[w] = {'w': (eng, idx), 'r': {}}

    def barrier(self):
        last = {}
        for e in ENG:
            if e == 'sp':
                continue
            if self.ops[e]:
                last[e] = len(self.ops[e]) - 1
        for i in self.dma_since_bar:
            last[('sp', i)] = i
        self.dma_since_bar = []
        for e in ENG:
            d = dict(last)
            d.pop(e, None)
            for k, v in d.items():
                if isinstance(k, tuple):
                    self.bar[e].setdefault(k, v)
                elif self.bar[e].get(k, -1) < v:
                    self.bar[e][k] = v

    def finalize(self):
        nc = self.nc
        ops = self.ops
        for e in ENG:
            for op in ops[e]:
                for k, v in op['deps'].items():
                    if isinstance(k, tuple):
                        continue
                    ops[k][v]['need'] = True
        for e in ENG:
            if e == 'sp':
                continue
            c = 0
            for op in ops[e]:
                if op['need']:
                    c += 1
                op['val'] = c
        cnt = [0] * NDSEM
        for k, op in enumerate(ops['sp']):
            j = k % NDSEM
            op['prev'] = cnt[j]
            cnt[j] += 16
            op['sem'] = j
            op['val'] = cnt[j]
        sems = {e: nc.alloc_semaphore('s_' + e) for e in ENG if e != 'sp'}
        dsems = [nc.alloc_semaphore('s_d%d' % j) for j in range(NDSEM)]

        def run(e, engobj):
            emitted = {}

            def need(key, val):
                if emitted.get(key, 0) < val:
                    sm = dsems[key[1]] if isinstance(key, tuple) else sems[key]
                    engobj.wait_ge(sm, val)
                    emitted[key] = val

            for op in ops[e]:
                for k, v in op['deps'].items():
                    if isinstance(k, tuple):
                        p = ops['sp'][v]
                        need(('d', p['sem']), p['val'])
                    else:
                        need(k, ops[k][v]['val'])
                if e == 'sp' and op['prev'] > 0:
                    need(('d', op['sem']), op['prev'])
                inst = op['fn'](engobj)
                if e == 'sp':
                    inst.then_inc(dsems[op['sem']], 16)
                elif op['need']:
                    inst.then_inc(sems[e], 1)
            if e == 'sp':
                for j in range(NDSEM):
                    if cnt[j] > 0:
                        need(('d', j), cnt[j])

        self._emitted = {}
        with nc.Block() as block:
            @block.tensor
            def _(t):
                run('pe', t)

            @block.scalar
            def _(t):
                run('act', t)

            @block.vector
            def _(t):
                run('dve', t)

            @block.gpsimd
            def _(t):
                run('pool', t)

            @block.sync
            def _(t):
                run('sp', t)


def make_consts():
    c = {}
    c['ident'] = np.eye(128, dtype=np.float32)
    c['ones'] = np.ones((128, 128), np.float32)
    bo = np.zeros((128, 128), np.float32)
    bo[:64, :64] = 1
    bo[64:, 64:] = 1
    c['blkones'] = bo
    i = np.arange(128)
    for name, sub in (('1', 128), ('4', 32), ('16', 8)):
        same = (i[:, None] // sub) == (i[None, :] // sub)
        low_s = ((i[None, :] < i[:, None]) & same).astype(np.float32)
        up_s = ((i[:, None] < i[None, :]) & same).astype(np.float32)
        up_i = ((i[:, None] <= i[None, :]) & same).astype(np.float32)
        c['low_s' + name] = low_s
        c['up_si' + name] = np.concatenate([up_s, up_i], axis=1)
        c['up_i' + name] = up_i
        nsub = 128 // sub
        rst = np.ones((128, 128), np.float32)
        rst[:, ::sub] = 0
        c['rst' + name] = rst
        cm = np.zeros((128, nsub, 128), np.float32)
        rm = np.zeros((128, nsub), np.float32)
        for sb in range(nsub):
            cm[:, sb, sb * sub:(sb + 1) * sub] = 1
            rm[sb * sub:(sb + 1) * sub, sb] = 1
        c['cm' + name] = cm.reshape(128, nsub * 128)
        c['rm' + name] = rm
    return c


class Builder:
    def __init__(self, TP, depth, stop=None, dbg=()):
        self.TP = TP
        self.depth = depth
        self.NT = TP + NSEQ * TS
        self.ntile = self.NT // 128
        self.stop = stop
        self.rw_stop = None
        self.dbg = dbg
        nc = bass.Bass('TRN2', target_bir_lowering=False)
        self.nc = nc
        self.S = Sched(nc)
        self.inputs = {}
        self.outputs = {}
        self.tg = []
        t = 0
        while t < TP:
            n = min(512, TP - t)
            self.tg.append((t, n))
            t += n
        self.tg.append((TP, NSEQ * TS))

    def din(self, name, shape, dt=F32):
        t = self.nc.dram_tensor(name, list(shape), dt, kind='ExternalInput')
        self.inputs[name] = t
        return t.ap()

    def dout(self, name, shape, dt=F32):
        t = self.nc.dram_tensor(name, list(shape), dt, kind='ExternalOutput')
        self.outputs[name] = t
        return t.ap()

    def dscr(self, name, shape, dt=F32):
        return self.nc.dram_tensor(name, list(shape), dt, kind='Internal').ap()

    def dma(self, out, in_, reads, writes, slow=False):
        if slow:
            self.S.emit('sp', lambda e: e.dma_start(out=out, in_=in_, allow_slow_non_contiguous=True), reads, writes)
        else:
            self.S.emit('sp', lambda e: e.dma_start(out=out, in_=in_), reads, writes)

    def E(self, eng, fn, reads, writes):
        self.S.emit(eng, fn, reads, writes)

    def psum(self):
        k = self._pk
        self._pk = (k + 1) % 8
        return self.ps[k], 'ps%d' % k

    def build(self):
        nc = self.nc
        TP, NT, L = self.TP, self.NT, self.depth
        I = {}
        I['x'] = self.din('x', [NT, D])
        shapes = dict(
            state_rwkv=[L, NSEQ, 12, 64, 64], state_rwkv_shift=[L, NSEQ, RW_COLS], state_hgrn=[L, NSEQ, 6, 128, 128],
            state_rglru=[L, NSEQ, 512], cache_rglru_conv=[L, NSEQ, 3, 512], cache_ffn_conv=[L, NSEQ, 2, DFF],
            norm_mix=[L, D], w_in=[L, D, IN_COLS], rwkv_mu=[L, RW_COLS], rwkv_w0=[L, 768], rwkv_w2=[L, 64, 768],
            rwkv_a0=[L, 768], rwkv_a2=[L, 64, 768], rwkv_g2=[L, 128, 768], rwkv_k_k=[L, 768], rwkv_k_a=[L, 768],
            rwkv_r_k=[L, 12, 64], rwkv_ln_w=[L, 768], rwkv_ln_b=[L, 768], hgrn_lb_logits=[L, 768], hgrn_norm_w=[L, 768],
            rglru_conv_w=[L, 4, 512], rglru_conv_b=[L, 512], rglru_wa=[L, 8, 64, 64], rglru_ba=[L, 512],
            rglru_wx=[L, 8, 64, 64], rglru_bx=[L, 512], rglru_lambda=[L, 512], w_out=[L, D, D], norm_ffn=[L, D],
            ffn_w_up=[L, D, 2 * DFF], ffn_conv_w=[L, 3, DFF], ffn_conv_b=[L, DFF], ffn_w_down=[L, DFF, D], norm_final=[D])
        for k, shp in shapes.items():
            I[k] = self.din(k, shp)
        self.cst = {k: self.din('c_' + k, v.shape) for k, v in make_consts().items()}
        O_ = {}
        O_['y'] = self.dout('y', [NT, D])
        oshapes = dict(p_rwkv=[L, 12, 64, 64], p_rwkv_shift=[L, RW_COLS], p_hgrn=[L, 6, 128, 128], p_rglru=[L, 512],
                       p_rglru_conv=[L, 3, 512], p_ffn_conv=[L, 2, DFF],
                       s_rwkv=[L, NSEQ, 12, 64, 64], s_rwkv_shift=[L, NSEQ, RW_COLS], s_hgrn=[L, NSEQ, 6, 128, 128],
                       s_rglru=[L, NSEQ, 512], s_rglru_conv=[L, NSEQ, 3, 512], s_ffn_conv=[L, NSEQ, 2, DFF])
        for k, shp in oshapes.items():
            O_[k] = self.dout(k, shp)
        self.xT = self.dscr('xT', [D, NT])
        self.projT = self.dscr('projT', [IN_COLS, NT])
        self.yT = self.dscr('yT', [D, NT], BF16)
        self.hT = self.dscr('hT', [DFF, NT], BF16)
        dbg = {k: self.dout('d_' + k, shp, dt) for k, (shp, dt) in dict(
            projT=([IN_COLS, NT], F32), xT=([D, NT], F32), yT=([D, NT], BF16), hT=([DFF, NT], BF16)).items() if k in self.dbg}

        with contextlib.ExitStack() as es:
            A = lambda name, shape, dt=F32: es.enter_context(nc.sbuf_tensor(name, list(shape), dt))
            self.ps = [es.enter_context(nc.psum_tensor('ps%d' % k, [128, 512], F32)) for k in range(8)]
            self._pk = 0
            self.ident = A('ident', [128, 128])
            self.ones_bf = A('ones_bf', [128, 128], BF16)
            self.dma(self.ident[:], self.cst['ident'], [], ['ident'])
            self.MEMSET('dve', self.ones_bf[:], 1.0, ['ones_bf'])
            self.prm = A('prm', [128, L, 160])
            self.fprm = A('fprm', [128, L, 176])
            self.prmf = A('prmf', [128, 16])
            self.ptmp = A('ptmp', [128, L + 1, 16])
            self.load_params(I)
            self.phase_transpose_in(I['x'])
            self.S.barrier()
            stop = self.stop
            for l in range(L):
                def sub(fn):
                    with contextlib.ExitStack() as es2:
                        A2 = lambda name, shape, dt=F32: es2.enter_context(nc.sbuf_tensor(name, list(shape), dt))
                        fn(A2)
                    self.S.barrier()

                def p_in(A2):
                    actT = A2('actT_a%d' % l, [128, 16, NT], BF16)
                    self.phase_norm(l, actT, P_NM, A2)
                    self.S.barrier()
                    self.phase_inproj(l, actT, I, A2)
                sub(p_in)
                if stop == 'inproj':
                    break
                self.phase_rwkv(l, I, O_)
                self.S.barrier()
                if stop == 'rwkv':
                    break
                self.phase_hgrn(l, I, O_, side=self.lru_gen(l, I, O_))
                self.S.barrier()
                if stop == 'mix':
                    break

                def p_out(A2):
                    actT = A2('actT_b%d' % l, [128, 16, NT], BF16)
                    self.phase_outproj(l, actT, I, A2)
                sub(p_out)

                def p_up(A2):
                    actT = A2('actT_c%d' % l, [128, 16, NT], BF16)
                    self.phase_norm(l, actT, P_NF, A2)
                    self.S.barrier()
                    self.phase_ffn_up(l, actT, I, O_, A2)
                sub(p_up)
                if stop == 'ffnup':
                    break
                sub(lambda A2: self.phase_ffn_down(l, I, A2))
            if stop is None:
                self.phase_final(O_['y'])
            self.S.barrier()
            for k, ap in dbg.items():
                if k == 'yT' and stop == 'rwkv':
                    self.dma(ap[0:768], self.yT[0:768], [], [])
                else:
                    self.dma(ap, getattr(self, k), [], [])
            self.S.finalize()
        return nc

    def phase_transpose_in(self, x_in):
        nc = self.nc
        with contextlib.ExitStack() as es:
            A = lambda name, shape, dt=F32: es.enter_context(nc.sbuf_tensor(name, list(shape), dt))
            xt = [A('ti_x%d' % i, [128, D]) for i in range(3)]
            xo = [A('ti_o%d' % i, [128, 16, 512]) for i in range(2)]
            k = 0
            for gi, (g0, gn) in enumerate(self.tg):
                ob = gi % 2
                for j in range(gn // 128):
                    b = k % 3
                    k += 1
                    t0 = g0 + j * 128
                    self.dma(xt[b][:], x_in[t0:t0 + 128, :], [], ['ti_x%d' % b])
                    for q in range(4):
                        ps, pn = self.psum()
                        for jj in range(4):
                            kc = q * 4 + jj
                            self.TR(ps[:, jj * 128:(jj + 1) * 128], xt[b][:, kc * 128:(kc + 1) * 128], ['ti_x%d' % b], [pn])
                        self.CP('act' if q % 2 == 0 else 'dve', xo[ob][:, q * 4:(q + 1) * 4, j * 128:(j + 1) * 128], ps[:].rearrange('p (a t) -> p a t', a=4), [pn], ['ti_o%d' % ob])
                self.dma(self.xT[:, g0:g0 + gn].rearrange('(c p) t -> p c t', p=128), xo[ob][:, :, 0:gn], ['ti_o%d' % ob], ['xT'])

    def phase_norm(self, l, actT, poff, A_unused):
        with contextlib.ExitStack() as es:
            self._phase_norm(l, actT, poff, lambda name, shape, dt=F32: es.enter_context(self.nc.sbuf_tensor('%s_%d' % (name, poff), list(shape), dt)))

    def _phase_norm(self, l, actT, poff, A):
        xb = [A('nm_x%d_%d' % (l, i), [128, 16, 512]) for i in range(2)]
        sqs = [A('nm_sq%d_%d' % (l, i), [128, 16, 512], BF16) for i in range(2)]
        sds = [A('nm_sd%d_%d' % (l, i), [128, 512]) for i in range(2)]
        rss = [A('nm_rs%d_%d' % (l, i), [128, 512]) for i in range(2)]
        for gi, (t0, tn) in enumerate(self.tg):
            b = gi % 2
            sq, sd, rs = sqs[b], sds[b], rss[b]
            xn = 'nm_x%d' % b
            self.dma(xb[b][:, :, 0:tn], self.xT[:, t0:t0 + tn].rearrange('(c p) t -> p c t', p=128), ['xT'], [xn])
            self.E('act', lambda e, b=b, tn=tn, sq=sq: e.activation(sq[:, :, 0:tn], xb[b][:, :, 0:tn], AF.Square), [xn], ['nm_sq%d' % b])
            ps, pn = self.psum()
            for kc in range(16):
                self.E('pe', lambda e, ps=ps, kc=kc, tn=tn, sq=sq: e.matmul(ps[:, 0:tn], self.ones_bf[:], sq[:, kc, 0:tn], start=(kc == 0), stop=(kc == 15)),
                       ['nm_sq%d' % b, 'ones_bf'], [pn])
            self.E('act', lambda e, ps=ps, tn=tn, sd=sd: e.activation(sd[:, 0:tn], ps[:, 0:tn], AF.Sqrt, bias=EPS, scale=1.0 / D), [pn], ['nm_sd%d' % b])
            self.E('dve', lambda e, tn=tn, rs=rs, sd=sd: e.reciprocal(rs[:, 0:tn], sd[:, 0:tn]), ['nm_sd%d' % b], ['nm_rs%d' % b])
            for kc in range(16):
                self.E('dve', lambda e, b=b, kc=kc, t0=t0, tn=tn, rs=rs: e.scalar_tensor_tensor(
                    actT[:, kc, t0:t0 + tn], xb[b][:, kc, 0:tn], self.prm[:, l, poff + kc:poff + kc + 1], rs[:, 0:tn], ALU.mult, ALU.mult),
                    [xn, 'nm_rs%d' % b, 'prm'], ['actT'])

    def TT(self, eng, out, in0, in1, op, r, w):
        self.E(eng, lambda e: e.tensor_tensor(out, in0, in1, op), r, w)

    def TSC(self, eng, out, in0, s1, s2, op0, op1, r, w):
        if op1 is None:
            self.E(eng, lambda e: e.tensor_scalar(out, in0, s1, None, op0), r, w)
        else:
            self.E(eng, lambda e: e.tensor_scalar(out, in0, s1, s2, op0, op1), r, w)

    def STT(self, out, in0, sc, in1, op0, op1, r, w):
        self.E('dve', lambda e: e.scalar_tensor_tensor(out, in0, sc, in1, op0, op1), r, w)

    def ACT(self, out, in_, func, r, w, bias=None, scale=None, accum=None):
        kw = {}
        if bias is not None:
            kw['bias'] = bias
        if scale is not None:
            kw['scale'] = scale
        if accum is not None:
            kw['accum_out'] = accum
        self.E('act', lambda e: e.activation(out, in_, func, **kw), r, w)

    def CP(self, eng, out, in_, r, w):
        if eng == 'act':
            self.ACT(out, in_, AF.Copy, r, w)
        else:
            self.E(eng, lambda e: e.tensor_copy(out, in_), r, w)

    def MM(self, out, lhsT, rhs, start, stop, r, w, sgc=False):
        self.E('pe', lambda e: e.matmul(out, lhsT, rhs, start=start, stop=stop, skip_group_check=sgc), r, w)

    def TR(self, out, in_, r, w, n=128):
        self.E('pe', lambda e: e.transpose(out, in_, self.ident[0:n, 0:n]), list(r) + ['ident'], w)

    def RECIP(self, out, in_, r, w):
        self.E('dve', lambda e: e.reciprocal(out, in_), r, w)

    def MEMSET(self, eng, ap, val, w):
        self.E(eng, lambda e: e.memset(ap, val), [], w)

    def SCAN(self, out, d0, d1, init, r, w):
        self.E('dve', lambda e: e.tensor_tensor_scan(out, d0, d1, init, ALU.mult, ALU.add), r, w)

    def RED(self, out, in_, r, w):
        self.E('dve', lambda e: e.tensor_reduce(out, in_, AX.X, ALU.add), r, w)

    def load_params(self, I):
        L = self.depth
        prm = self.prm

        def ld(idx, n, ap_fn):
            for l in range(L):
                self.dma(prm[:, l, idx:idx + n], ap_fn(l), [], ['prm'], slow=True)

        cp = lambda name: (lambda l: I[name][l].rearrange('(c p) -> p c', p=128))
        ld(P_NM, 16, cp('norm_mix'))
        ld(P_NF, 16, cp('norm_ffn'))
        ld(P_MU, 20, cp('rwkv_mu'))
        ld(P_W0, 6, cp('rwkv_w0'))
        ld(P_A0, 6, cp('rwkv_a0'))
        ld(P_KK, 6, cp('rwkv_k_k'))
        ld(P_KA, 6, cp('rwkv_k_a'))
        ld(P_RK, 6, lambda l: I['rwkv_r_k'][l].rearrange('(c h2) k -> (h2 k) c', h2=2))
        ld(P_LNW, 6, cp('rwkv_ln_w'))
        ld(P_LNB, 6, cp('rwkv_ln_b'))
        ld(P_LB, 6, cp('hgrn_lb_logits'))
        ld(P_HNW, 6, cp('hgrn_norm_w'))
        for j in range(4):
            ld(P_CW + 4 * j, 4, lambda l, j=j: I['rglru_conv_w'][l, j].rearrange('(c p) -> p c', p=128))
        ld(P_CB, 4, cp('rglru_conv_b'))
        ld(P_BA, 4, cp('rglru_ba'))
        ld(P_BX, 4, cp('rglru_bx'))
        ld(P_LAM, 4, cp('rglru_lambda'))
        for l in range(L):
            for j in range(3):
                self.dma(self.fprm[:, l, j * 44:(j + 1) * 44], I['ffn_conv_w'][l, j].rearrange('(c p) -> p c', p=128), [], ['fprm'], slow=True)
            self.dma(self.fprm[:, l, 132:176], I['ffn_conv_b'][l].rearrange('(c p) -> p c', p=128), [], ['fprm'], slow=True)
        self.dma(self.prmf[:, 0:16], I['norm_final'].rearrange('(c p) -> p c', p=128), [], ['prmf'], slow=True)
        for l in range(L):
            self.TSC('dve', prm[:, l, P_OMKA:P_OMKA + 6], prm[:, l, P_KA:P_KA + 6], -1.0, 1.0, ALU.mult, ALU.add, ['prm'], ['prm'])
        ex = self.ptmp
        for l in range(L):
            self.ACT(ex[:, l, 0:6], prm[:, l, P_LB:P_LB + 6], AF.Exp, ['prm'], ['ptmp'])
        self.CP('dve', ex[:, L, 0:6], ex[:, 0, 0:6], ['ptmp'], ['ptmp'])
        for l in range(1, L):
            self.TT('dve', ex[:, L, 0:6], ex[:, L, 0:6], ex[:, l, 0:6], ALU.add, ['ptmp'], ['ptmp'])
        self.RECIP(ex[:, L, 0:6], ex[:, L, 0:6], ['ptmp'], ['ptmp'])
        for l in range(L):
            self.TT('dve', ex[:, l, 0:6], ex[:, l, 0:6], ex[:, L, 0:6], ALU.mult, ['ptmp'], ['ptmp'])
        self.MEMSET('dve', prm[:, 0, P_LB:P_LB + 6], 0.0, ['prm'])
        for l in range(1, L):
            self.TT('dve', prm[:, l, P_LB:P_LB + 6], prm[:, l - 1, P_LB:P_LB + 6], ex[:, l, 0:6], ALU.add, ['prm', 'ptmp'], ['prm'])
        for l in range(L):
            self.TSC('dve', prm[:, l, P_OMLB:P_OMLB + 6], prm[:, l, P_LB:P_LB + 6], -1.0, 1.0, ALU.mult, ALU.add, ['prm'], ['prm'])
            self.ACT(ex[:, l, 8:12], prm[:, l, P_LAM:P_LAM + 4], AF.Exp, ['prm'], ['ptmp'], scale=-1.0)
            self.ACT(ex[:, l, 8:12], ex[:, l, 8:12], AF.Ln, ['ptmp'], ['ptmp'], bias=1.0)
            self.TSC('dve', prm[:, l, P_C8:P_C8 + 4], ex[:, l, 8:12], -8.0, None, ALU.mult, None, ['ptmp'], ['prm'])
            self.TSC('dve', prm[:, l, P_2C8:P_2C8 + 4], ex[:, l, 8:12], -16.0, None, ALU.mult, None, ['ptmp'], ['prm'])
    def phase_rwkv(self, l, I, O_):
        nc = self.nc
        TP, NT = self.TP, self.NT
        prm = self.prm
        C0 = 0.6065306597126334
        with contextlib.ExitStack() as es:
            A = lambda name, shape, dt=F32: es.enter_context(nc.sbuf_tensor('rw%d_%s' % (l, name), list(shape), dt))
            w2a2 = A('w2a2', [128, 768])
            g2 = A('g2', [128, 768])
            self.dma(w2a2[0:64, :], I['rwkv_w2'][l], [], ['w2a2'])
            self.dma(w2a2[64:128, :], I['rwkv_a2'][l], [], ['w2a2'])
            self.dma(g2[:], I['rwkv_g2'][l], [], ['g2'])
            blk = A('blk', [128, 128])
            self.dma(blk[:], self.cst['blkones'], [], ['blk'])
            masks = {}
            for mk in ('1', '16'):
                mbk = A('mbk' + mk, [128, 512])
                ml4 = A('ml4' + mk, [128, 512])
                rst = A('rst' + mk, [128, 128])
                for j in range(2):
                    self.dma(mbk[:, j * 256:(j + 1) * 256], self.cst['up_si' + mk], [], ['mbk' + mk])
                for j in range(4):
                    self.dma(ml4[:, j * 128:(j + 1) * 128], self.cst['low_s' + mk], [], ['ml4' + mk])
                self.dma(rst[:], self.cst['rst' + mk], [], ['rst' + mk])
                masks[mk] = (mbk, ml4, rst)
            cm16 = A('cm16', [128, 16, 128])
            rm16 = A('rm16', [128, 16])
            self.dma(cm16[:], self.cst['cm16'].rearrange('p (s t) -> p s t', s=16), [], ['cm16'])
            self.dma(rm16[:], self.cst['rm16'], [], ['rm16'])

            pbuf = A('pbuf', [128, 20, 144])
            xs = A('xs', [128, 20, 128])
            dd = A('dd', [128, 20, 128])
            tx = A('tx', [128, 128])
            sgx = A('sgx', [128, 128])
            sgw = A('sgw', [128, 6, 128])
            aa = A('aa', [128, 6, 128])
            gT = A('gT', [128, 6, 128])
            kk = A('kk', [128, 6, 128])
            t1 = A('t1', [128, 6, 128])
            t2 = A('t2', [128, 6, 128])
            kf = A('kf', [128, 6, 128])
            nb = A('nb', [128, 6, 128])
            cs = A('cs', [128, 6, 128])
            Gm = A('Gm', [128, 6, 128])
            Gi = A('Gi', [128, 6, 128])
            Gp = A('Gp', [128, 6, 128])
            AR = A('AR', [128, 6, 256])
            KT = A('KT', [128, 6, 128])
            BT = A('BT', [128, 6, 128])
            Vtm = A('Vtm', [128, 768])
            Btm = A('Btm', [128, 768])
            Ktm = A('Ktm', [128, 768])
            M4 = A('M4', [128, 12, 384])
            Pm = [A('Pm%d' % i, [128, 12, 128], NDT) for i in range(2)]
            PTm = [A('PTm%d' % i, [128, 12, 128], NDT) for i in range(2)]
            X = [A('X%d' % i, [128, 12, 128], NDT) for i in range(2)]
            Osb = A('Osb', [128, 12, 64])
            st = A('st', [128, 64])
            bst = A('bst', [128, 12, 6])
            ybf = A('ybf', [128, 6, 128], BF16)
            Hc = A('Hc', [128, 1, 6, 64])
            Hin = A('Hin', [128, 16, 6, 64])
            sin = [A('sin0', [64, 6, 2, 64])] * 2
            sout = sin
            htmp = A('htmp', [128, 64])

            self.MEMSET('dve', Hc[:], 0.0, ['Hc'])
            tiles = [('p', ti) for ti in range(TP // 128)] + [('s', 0)]
            for kind, ti in tiles:
                sample = kind == 's'
                t0 = TP if sample else ti * 128
                mk = '16' if sample else '1'
                nsub = 16 if sample else 1
                mbk, ml4, rst = masks[mk]
                rmk = ['mbk' + mk, 'ml4' + mk, 'rst' + mk]
                if sample:
                    pv = pbuf[:, :, 0:144].rearrange('p c (s t) -> p c s t', t=9)
                    self.dma(dd[:], self.projT[0:RW_COLS, t0:t0 + 128].rearrange('(c p) t -> p c t', p=128), ['projT'], ['dd'])
                    self.CP('dve', pv[:, :, :, 1:9], dd[:].rearrange('p c (s t) -> p c s t', t=8), ['dd'], ['pbuf'])
                    for c in range(20):
                        self.dma(pv[:, c, :, 0], I['state_rwkv_shift'][l, :, c * 128:(c + 1) * 128].rearrange('s p -> p s'), [], ['pbuf'], slow=True)
                    prev = lambda c: pv[:, c, :, 0:8]
                    cur = lambda c: pv[:, c, :, 1:9]
                    v3 = lambda ap: ap.rearrange('p (s t) -> p s t', t=8)
                else:
                    if ti == 0:
                        self.MEMSET('dve', pbuf[:, :, 0:1], 0.0, ['pbuf'])
                        self.dma(pbuf[:, :, 1:129], self.projT[0:RW_COLS, t0:t0 + 128].rearrange('(c p) t -> p c t', p=128), ['projT'], ['pbuf'])
                    else:
                        self.dma(pbuf[:, :, 0:129], self.projT[0:RW_COLS, t0 - 1:t0 + 128].rearrange('(c p) t -> p c t', p=128), ['projT'], ['pbuf'])
                    prev = lambda c: pbuf[:, c, 0:128]
                    cur = lambda c: pbuf[:, c, 1:129]
                    v3 = lambda ap: ap
                if sample:
                    pall, call = pv[:, :, :, 0:8], pv[:, :, :, 1:9]
                    va = lambda t: t[:].rearrange('p c (s t) -> p c s t', t=8)
                    mub = prm[:, l, P_MU:P_MU + 20].rearrange('p (c a b) -> p c a b', a=1, b=1).to_broadcast([128, 20, 16, 8])
                else:
                    pall, call = pbuf[:, :, 0:128], pbuf[:, :, 1:129]
                    va = lambda t: t[:]
                    mub = prm[:, l, P_MU:P_MU + 20].rearrange('p (c a) -> p c a', a=1).to_broadcast([128, 20, 128])
                self.TT('dve', va(dd), pall, call, ALU.subtract, ['pbuf'], ['dd'])
                self.TT('dve', va(dd), va(dd), mub, ALU.mult, ['dd', 'prm'], ['dd'])
                self.TT('dve', va(xs), va(dd), call, ALU.add, ['dd', 'pbuf'], ['xs'])
                self.ACT(tx[0:64, :], xs[0:64, 18, :], AF.Tanh, ['xs'], ['tx'])
                self.ACT(sgx[:], xs[:, 19, :], AF.Sigmoid, ['xs'], ['sgx'])
                for (rows, rhs, rres, out, bidx) in ((slice(0, 64), tx[0:64, :], 'tx', sgw, P_W0), (slice(64, 128), xs[64:128, 18, :], 'xs', aa, P_A0)):
                    for half in range(2):
                        ps, pn = self.psum()
                        cs_ = range(0, 4) if half == 0 else range(4, 6)
                        for c in cs_:
                            self.MM(ps[:, (c % 4) * 128:(c % 4 + 1) * 128], w2a2[rows, c * 128:(c + 1) * 128], rhs, True, True, ['w2a2', rres], [pn])
                        for c in cs_:
                            self.ACT(out[:, c, :], ps[:, (c % 4) * 128:(c % 4 + 1) * 128], AF.Sigmoid, [pn, 'prm'], ['sgw' if out is sgw else 'aa'], bias=prm[:, l, bidx + c:bidx + c + 1])
                for half in range(2):
                    ps, pn = self.psum()
                    cs_ = range(0, 4) if half == 0 else range(4, 6)
                    for c in cs_:
                        self.MM(ps[:, (c % 4) * 128:(c % 4 + 1) * 128], g2[:, c * 128:(c + 1) * 128], sgx[:], True, True, ['g2', 'sgx'], [pn])
                    n = len(cs_)
                    self.CP('act', gT[:, cs_[0]:cs_[0] + n, :], ps[:, 0:n * 128].rearrange('p (c t) -> p c t', t=128), [pn], ['gT'])
                SGW, AA = 'sgw', 'aa'
                for c in range(6):
                    self.ACT(kk[:, c, :], xs[:, 6 + c, :], AF.Copy, ['xs', 'prm'], ['kk'], scale=prm[:, l, P_KK + c:P_KK + c + 1])
                self.TT('dve', t1[:], kk[:], kk[:], ALU.mult, ['kk'], ['t1'])
                for half in range(2):
                    ps, pn = self.psum()
                    cs_ = range(0, 4) if half == 0 else range(4, 6)
                    for c in cs_:
                        self.MM(ps[:, (c % 4) * 128:(c % 4 + 1) * 128], blk[:], t1[:, c, :], True, True, ['blk', 't1'], [pn])
                    n = len(cs_)
                    self.TSC('dve', t2[:, cs_[0]:cs_[0] + n, :], ps[:, 0:n * 128].rearrange('p (c t) -> p c t', t=128), 1e-24, None, ALU.max, None, [pn], ['t2'])
                self.ACT(t2[:], t2[:], AF.Sqrt, ['t2'], ['t2'])
                self.RECIP(t2[:], t2[:], ['t2'], ['t2'])
                self.TT('dve', kk[:], kk[:], t2[:], ALU.mult, ['kk', 't2'], ['kk'])
                for c in range(6):
                    self.TSC('dve', t1[:, c, :], aa[:, c, :], prm[:, l, P_KA + c:P_KA + c + 1], prm[:, l, P_OMKA + c:P_OMKA + c + 1], ALU.mult, ALU.add, [AA, 'prm', 't1'], ['t1'])
                self.TT('dve', kf[:], xs[:, 6:12, :], t1[:], ALU.mult, ['xs', 't1'], ['kf'])
                self.TT('dve', nb[:], kk[:], aa[:], ALU.mult, ['kk', AA], ['nb'])
                for c in range(6):
                    self.SCAN(cs[:, c, :], rst[:], sgw[:, c, :], 0.0, [SGW, 'rst' + mk], ['cs'])
                self.ACT(Gm[:], cs[:], AF.Exp, ['cs'], ['Gm'], scale=-C0)
                self.ACT(Gi[:], cs[:], AF.Exp, ['cs'], ['Gi'], scale=C0)
                self.TT('dve', t2[:], cs[:], sgw[:], ALU.subtract, ['cs', SGW], ['t2'])
                self.ACT(Gp[:], t2[:], AF.Exp, ['t2'], ['Gp'], scale=-C0)
                self.STT(AR[:, :, 0:128], kk[:], -1.0, Gp[:], ALU.mult, ALU.mult, ['kk', 'Gp'], ['AR0'])
                self.TT('pool', AR[:, :, 128:256], xs[:, 0:6, :], Gm[:], ALU.mult, ['xs', 'Gm'], ['AR1'])
                self.TT('dve', KT[:], kf[:], Gi[:], ALU.mult, ['kf', 'Gi'], ['KT'])
                self.TT('pool', BT[:], nb[:], Gi[:], ALU.mult, ['nb', 'Gi'], ['BT'])
                if self.rw_stop is not None and self.rw_stop < 10:
                    continue
                for (src_fn, sres, dst, dres) in ((lambda c: xs[:, 12 + c, :], 'xs', Vtm, 'Vtm'), (lambda c: AR[:, c, 0:128], 'AR0', cs[:].rearrange('p c t -> p (c t)'), 'cs'),
                                                  (lambda c: BT[:, c, :], 'BT', Btm, 'Btm'), (lambda c: KT[:, c, :], 'KT', Ktm, 'Ktm')):
                    for half in range(2):
                        ps, pn = self.psum()
                        cs_ = range(0, 4) if half == 0 else range(4, 6)
                        for c in cs_:
                            self.TR(ps[:, (c % 4) * 128:(c % 4 + 1) * 128], src_fn(c), [sres], [pn])
                        n = len(cs_)
                        self.CP('act' if half == 0 else 'dve', dst[:, cs_[0] * 128:(cs_[0] + n) * 128], ps[:, 0:n * 128], [pn], ['cs'] if dres == 'cs' else [dres + str(half)])
                VT_R = ['Vtm0', 'Vtm1']
                if self.rw_stop is not None and self.rw_stop < 11:
                    continue
                for h in range(12):
                    c, pb = h // 2, 64 * (h % 2)
                    ps, pn = self.psum()
                    self.MM(ps[:, 0:256], BT[pb:pb + 64, c, :], AR[pb:pb + 64, c, :], True, True, ['BT', 'AR0', 'AR1'], [pn])
                    self.MM(ps[:, 256:512], KT[pb:pb + 64, c, :], AR[pb:pb + 64, c, :], True, True, ['KT', 'AR0', 'AR1'], [pn])
                    self.TT('dve', PTm[0][:, h, :], ps[:, 0:128], mbk[:, 0:128], ALU.mult, [pn, 'mbk' + mk], ['PTm0_%d' % (h // 4)])
                    self.TT('dve', M4[:, h, :], ps[:, 128:512], mbk[:, 128:512], ALU.mult, [pn, 'mbk' + mk], ['M4_%d' % h])
                Pv = Pm[0][:].rearrange('p (c a) t -> p c a t', a=2)
                for h2 in range(2):
                    pb = 64 * h2
                    for (c0, c1) in ((0, 4), (4, 6)):
                        ps, pn = self.psum()
                        for c in range(c0, c1):
                            self.MM(ps[:, (c - c0) * 128:(c - c0 + 1) * 128], AR[pb:pb + 64, c, 0:128], BT[pb:pb + 64, c, :], True, True, ['AR0', 'BT'], [pn])
                        n = c1 - c0
                        self.TT('dve', Pv[:, c0:c1, h2, :], ps[:, 0:n * 128].rearrange('p (h t) -> p h t', t=128), ml4[:, 0:n * 128].rearrange('p (h t) -> p h t', t=128), ALU.mult,
                                [pn, 'ml4' + mk], sorted(set('Pm0_%d' % ((2 * c + h2) // 4) for c in range(c0, c1))))
                if self.rw_stop is not None and self.rw_stop < 12:
                    continue
                self.CP('act', X[0][:, :, 0:64], cs[:].rearrange('p c (a k) -> p (c a) k', k=64), ['cs'], ['X0_0', 'X0_1', 'X0_2'])
                for half in range(2):
                    ps, pn = self.psum()
                    hs = range(0, 8) if half == 0 else range(8, 12)
                    for h in hs:
                        self.MM(ps[:, (h % 8) * 64:(h % 8 + 1) * 64], M4[:, h, 128:256], Vtm[:, h * 64:(h + 1) * 64], True, True, ['M4_%d' % h] + VT_R, [pn])
                    n = len(hs)
                    self.CP('act', X[0][:, hs[0]:hs[0] + n, 64:128], ps[:, 0:n * 64].rearrange('p (h v) -> p h v', v=64), [pn], ['X0_0', 'X0_1'] if half == 0 else ['X0_2'])
                xres = {0: ['X0_0', 'X0_1', 'X0_2'], 1: ['X1_0', 'X1_1', 'X1_2']}
                if self.rw_stop is not None and self.rw_stop < 13:
                    continue
                NST = 3 if sample else 7
                for s_ in range(NST):
                    a_, b_ = s_ % 2, (s_ + 1) % 2
                    pres = ['Pm%d_%d' % (a_, q) for q in range(3)]
                    ptres = ['PTm%d_%d' % (a_, q) for q in range(3)]
                    for q in range(3):
                        ps, pn = self.psum()
                        for j in range(4):
                            h = q * 4 + j
                            self.MM(ps[:, j * 128:(j + 1) * 128], PTm[a_][:, h, :], X[a_][:, h, :], True, True, ptres + xres[a_], [pn])
                        self.TT('dve', X[b_][:, q * 4:(q + 1) * 4, :], ps[:].rearrange('p (h t) -> p h t', t=128), X[a_][:, q * 4:(q + 1) * 4, :], ALU.add,
                                [pn] + xres[a_], ['X%d_%d' % (b_, q)])
                    if s_ < NST - 1:
                        for q in range(3):
                            ps, pn = self.psum()
                            ps2, pn2 = self.psum()
                            for j in range(4):
                                h = q * 4 + j
                                self.MM(ps[:, j * 128:(j + 1) * 128], PTm[a_][:, h, :], Pm[a_][:, h, :], True, True, pres + ptres, [pn])
                                self.MM(ps2[:, j * 128:(j + 1) * 128], Pm[a_][:, h, :], PTm[a_][:, h, :], True, True, pres + ptres, [pn2])
                            self.CP('act', Pm[b_][:, q * 4:(q + 1) * 4, :], ps[:].rearrange('p (h t) -> p h t', t=128), [pn], ['Pm%d_%d' % (b_, q)])
                            self.CP('act', PTm[b_][:, q * 4:(q + 1) * 4, :], ps2[:].rearrange('p (h t) -> p h t', t=128), [pn2], ['PTm%d_%d' % (b_, q)])
                fin = NST % 2
                Xf = X[fin]
                xfr = xres[fin]
                if self.rw_stop is not None and self.rw_stop < 14:
                    continue
                self.CP('act', Gi[:].rearrange('p c (a k) -> p (c a) k', k=64), Xf[:, :, 0:64], xfr, ['Gi'])
                for half in range(2):
                    ps, pn = self.psum()
                    cs_ = range(0, 4) if half == 0 else range(4, 6)
                    for c in cs_:
                        self.TR(ps[:, (c % 4) * 128:(c % 4 + 1) * 128], Gi[:, c, :], ['Gi'], [pn])
                    n = len(cs_)
                    self.CP('act', Gp[:, cs_[0]:cs_[0] + n, :], ps[:, 0:n * 128].rearrange('p (c t) -> p c t', t=128), [pn], ['Gp'])
                gtr = ['Gp']
                if sample:
                    for s in range(16):
                        b = s % 2
                        self.dma(sin[b][:], I['state_rwkv'][l, s].rearrange('(c h2) v k -> v c h2 k', h2=2), [], ['sin0'])
                        ps, pn = self.psum()
                        for c in range(6):
                            self.TR(ps[:, c * 64:(c + 1) * 64], sin[b][:, c, :, :], ['sin0'], [pn], n=64)
                        self.CP('act' if s % 2 == 0 else 'dve', Hin[:, s, :, :], ps[:, 0:384].rearrange('p (c v) -> p c v', v=64), [pn], ['Hin%d' % s])
                    Hi = lambda sb: Hin[:, sb, :, :]
                    Ho = Hi
                    hir = lambda sb: 'Hin%d' % sb
                    hor = hir
                else:
                    Hi = lambda sb: Hc[:, 0, :, :]
                    Ho = Hi
                    hir = lambda sb: 'Hc'
                    hor = hir
                if self.rw_stop is not None and self.rw_stop < 15:
                    continue
                def inter(which):
                    psl = {0: self.psum(), 1: self.psum()}
                    for c in range(6):
                        if which == 'G':
                            src, sres = Gp[:, c, :], gtr
                        else:
                            src, sres = AR[:, c, 128:256], ['AR1']
                        if sample:
                            for sb in range(16):
                                self.TT('pool' if sb % 4 == 3 else 'dve', dd[:, sb, :], src, cm16[:, sb, :], ALU.mult, sres + ['cm16'], ['dd'])
                        for h2 in range(2):
                            pb = 64 * h2
                            ps, pn = psl[h2]
                            o = ps[:, c * 64:(c + 1) * 64]
                            for sb in range(nsub):
                                if sample:
                                    lh, lr = dd[pb:pb + 64, sb, :], ['dd']
                                else:
                                    lh, lr = src[pb:pb + 64, :], sres
                                self.MM(o, lh, Hi(sb)[pb:pb + 64, c, :], sb == 0, sb == nsub - 1, lr + [hir(sb)], [pn])
                    return psl
                Uv = sgw[:].rearrange('p c (a v) -> p c a v', a=2)
                Xv = Xf[:].rearrange('p (c a) t -> p c a t', a=2)
                psl = inter('G')
                for h2 in range(2):
                    ps, pn = psl[h2]
                    self.TT('dve', Uv[:, :, h2, :], ps[:, 0:384].rearrange('p (c v) -> p c v', v=64), Xv[:, :, h2, 64:128], ALU.add, [pn] + xfr, ['sgw'])
                ur = ['sgw']
                psl = inter('R')
                psO = [self.psum(), self.psum()]
                for h in range(12):
                    ps, pn = psO[h // 8]
                    o = ps[:, (h % 8) * 64:(h % 8 + 1) * 64]
                    self.MM(o, M4[:, h, 0:128], sgw[:].rearrange('p c t -> p (c t)')[:, h * 64:(h + 1) * 64], True, False, ['M4_%d' % h] + ur, [pn])
                    self.MM(o, M4[:, h, 256:384], Vtm[:, h * 64:(h + 1) * 64], False, True, ['M4_%d' % h] + VT_R, [pn])
                for half in range(2):
                    ps, pn = psO[half]
                    hs = range(0, 8) if half == 0 else range(8, 12)
                    n = len(hs)
                    self.CP('act', Osb[:, hs[0]:hs[0] + n, :], ps[:, 0:n * 64].rearrange('p (h v) -> p h v', v=64), [pn], ['Osb%d' % half])
                Ov = Osb[:].rearrange('p (c a) v -> p c a v', a=2)
                for h2 in range(2):
                    ps, pn = psl[h2]
                    self.TT('dve', Ov[:, :, h2, :], Ov[:, :, h2, :], ps[:, 0:384].rearrange('p (c v) -> p c v', v=64), ALU.add, [pn, 'Osb0', 'Osb1'], ['Osb0', 'Osb1'])
                osr = ['Osb0', 'Osb1']
                for h in range(12):
                    self.E('dve', (lambda e, h=h: e.bn_stats(bst[:, h, :], Osb[:, h, :])), osr, ['bst'])
                for h in range(12):
                    self.E('dve', (lambda e, h=h: e.bn_aggr(st[:, 2 * h:2 * h + 2], bst[:, h, :])), ['bst'], ['st'])
                stv = st[:, 0:24].rearrange('p (h a) -> p h a', a=2)
                self.ACT(st[:, 36:48], stv[:, :, 1], AF.Sqrt, ['st'], ['st'], bias=RW_EPS)
                self.RECIP(st[:, 36:48], st[:, 36:48], ['st'], ['st'])
                for h in range(12):
                    self.TSC('dve', t2[:, h // 2, (h % 2) * 64:(h % 2) * 64 + 64], Osb[:, h, :], st[:, 2 * h:2 * h + 1], st[:, 36 + h:37 + h], ALU.subtract, ALU.mult, osr + ['st'], ['t2'])
                for half in range(2):
                    ps, pn = self.psum()
                    cs_ = range(0, 4) if half == 0 else range(4, 6)
                    for c in cs_:
                        self.TR(ps[:, (c % 4) * 128:(c % 4 + 1) * 128], t2[:, c, :], ['t2'], [pn])
                    for c in cs_:
                        self.TSC('dve', cs[:, c, :], ps[:, (c % 4) * 128:(c % 4 + 1) * 128], prm[:, l, P_LNW + c:P_LNW + c + 1], prm[:, l, P_LNB + c:P_LNB + c + 1], ALU.mult, ALU.add, [pn, 'prm'], ['cs'])
                if self.rw_stop is not None and self.rw_stop < 17:
                    continue
                self.TT('pool', t1[:], xs[:, 0:6, :], kf[:], ALU.mult, ['xs', 'kf'], ['t1'])
                for c in range(6):
                    self.TSC('pool', t1[:, c, :], t1[:, c, :], prm[:, l, P_RK + c:P_RK + c + 1], None, ALU.mult, None, ['t1', 'prm'], ['t1'])
                for half in range(2):
                    ps, pn = self.psum()
                    cs_ = range(0, 4) if half == 0 else range(4, 6)
                    for c in cs_:
                        self.MM(ps[:, (c % 4) * 128:(c % 4 + 1) * 128], blk[:], t1[:, c, :], True, True, ['blk', 't1'], [pn])
                    n = len(cs_)
                    self.TT('dve', t2[:, cs_[0]:cs_[0] + n, :], ps[:, 0:n * 128].rearrange('p (c t) -> p c t', t=128), xs[:, 12 + cs_[0]:12 + cs_[0] + n, :], ALU.mult, [pn, 'xs'], ['t2'])
                self.TT('dve', cs[:], cs[:], t2[:], ALU.add, ['cs', 't2'], ['cs'])
                self.TT('dve', ybf[:], cs[:], gT[:], ALU.mult, ['cs', 'gT'], ['ybf'])
                self.dma(self.yT[0:768, t0:t0 + 128].rearrange('(c p) t -> p c t', p=128), ybf[:], ['ybf'], ['yT_rw'])
                if self.rw_stop is not None and self.rw_stop < 18:
                    continue
                for sb in range(nsub):
                    if sample:
                        HG = M4[:, sb % 2, :].rearrange('p (c v) -> p c v', v=64)
                        hgr = 'M4_%d' % (sb % 2)
                        gend = Gm[:, :, sb * 8 + 7:sb * 8 + 8].to_broadcast([128, 6, 64])
                        self.TT('dve', HG, Hin[:, sb, :, :], gend, ALU.mult, ['Hin%d' % sb, 'Gm'], [hgr])
                        self.TSC('dve', kk[:].rearrange('p c t -> p (c t)'), Btm[:], rm16[:, sb:sb + 1], None, ALU.mult, None, ['Btm0', 'Btm1', 'rm16'], ['kk'])
                        self.ACT(nb[:].rearrange('p c t -> p (c t)'), Ktm[:], AF.Copy, ['Ktm0', 'Ktm1', 'rm16'], ['nb'], scale=rm16[:, sb:sb + 1])
                        bl, kl, blr, klr = kk[:].rearrange('p c t -> p (c t)'), nb[:].rearrange('p c t -> p (c t)'), ['kk'], ['nb']
                        tend = sb * 8 + 7
                    else:
                        bl, kl, blr, klr = Btm[:], Ktm[:], ['Btm0', 'Btm1'], ['Ktm0', 'Ktm1']
                        tend = 127
                    for c in range(6):
                        ps, pn = self.psum()
                        self.MM(ps[:, 0:128], bl[:, c * 128:(c + 1) * 128], sgw[:].rearrange('p c t -> p (c t)')[:, c * 128:(c + 1) * 128], True, False, blr + ur, [pn])
                        self.MM(ps[:, 0:128], kl[:, c * 128:(c + 1) * 128], Vtm[:, c * 128:(c + 1) * 128], False, True, klr + VT_R, [pn])
                        for h2 in range(2):
                            pr = slice(64 * h2, 64 * h2 + 64)
                            if sample:
                                self.STT(Ho(sb)[pr, c, :], ps[pr, 64 * h2:64 * h2 + 64], Gm[pr, c, tend:tend + 1], HG[pr, c, :], ALU.mult, ALU.add, [pn, 'Gm', hgr], [hor(sb)])
                            else:
                                self.TT('dve', htmp[pr, :], ps[pr, 64 * h2:64 * h2 + 64], Hi(sb)[pr, c, :], ALU.add, [pn, hir(sb)], ['htmp'])
                                self.TSC('dve', Ho(sb)[pr, c, :], htmp[pr, :], Gm[pr, c, tend:tend + 1], None, ALU.mult, None, ['htmp', 'Gm'], [hor(sb)])
                if self.rw_stop is not None and self.rw_stop < 19:
                    continue
                last_prompt = (not sample) and ti == TP // 128 - 1
                if sample or last_prompt:
                    for sb in range(nsub):
                        b = sb % 2
                        ps, pn = self.psum()
                        ps2, pn2 = self.psum()
                        for c in range(4):
                            self.TR(ps[0:64, c * 128:(c + 1) * 128], Ho(sb)[:, c, :], [hor(sb)], [pn])
                        for c in range(4, 6):
                            self.TR(ps2[0:64, (c - 4) * 128:(c - 3) * 128], Ho(sb)[:, c, :], [hor(sb)], [pn2])
                        self.CP('act', sout[b][:, 0:4, :, :], ps[0:64, :].rearrange('p (c h k) -> p c h k', h=2, k=64), [pn], ['sin0'])
                        self.CP('dve', sout[b][:, 4:6, :, :], ps2[0:64, 0:256].rearrange('p (c h k) -> p c h k', h=2, k=64), [pn2], ['sin0'])
                        dst = (O_['s_rwkv'][l, sb] if sample else O_['p_rwkv'][l]).rearrange('(c h2) v k -> v c h2 k', h2=2)
                        self.dma(dst, sout[b][:], ['sin0'], [])
                    if sample:
                        for c in range(20):
                            self.dma(O_['s_rwkv_shift'][l, :, c * 128:(c + 1) * 128].rearrange('s p -> p s'), pv[:, c, :, 8], ['pbuf'], [], slow=True)
                    else:
                        self.dma(O_['p_rwkv_shift'][l].rearrange('(c p) -> p c', p=128), pbuf[:, :, 128], ['pbuf'], [], slow=True)
    def phase_hgrn(self, l, I, O_, side=None):
        nc = self.nc
        TP = self.TP
        prm = self.prm
        with contextlib.ExitStack() as es:
            A = lambda name, shape, dt=F32: es.enter_context(nc.sbuf_tensor('hg%d_%s' % (l, name), list(shape), dt))
            masks = {}
            for mk, ns in (('4', 4), ('16', 16)):
                up = A('up' + mk, [128, 128])
                rst = A('rst' + mk, [128, 128])
                cm = A('cm' + mk, [128, ns, 128])
                rm = A('rm' + mk, [128, ns])
                self.dma(up[:], self.cst['up_i' + mk], [], ['up' + mk])
                self.dma(rst[:], self.cst['rst' + mk], [], ['rst' + mk])
                self.dma(cm[:], self.cst['cm' + mk].rearrange('p (s t) -> p s t', s=ns), [], ['cm' + mk])
                self.dma(rm[:], self.cst['rm' + mk], [], ['rm' + mk])
                masks[mk] = (up, rst, cm, rm)
            pb = A('pb', [128, 24, 128])
            q = A('q', [128, 6, 128])
            fg = A('fg', [128, 6, 128])
            lf = A('lf', [128, 6, 128])
            kx = A('kx', [128, 6, 128])
            cs = A('cs', [128, 6, 128])
            Gm = A('Gm', [128, 6, 128])
            Gi = A('Gi', [128, 6, 128])
            QT = A('QT', [128, 6, 128])
            KT = A('KT', [128, 6, 128])
            sg = A('sg', [128, 6, 128])
            Vtm = A('Vtm', [128, 6, 128])
            Ktm = A('Ktm', [128, 6, 128])
            KM = A('KM', [128, 16, 128])
            QM = A('QM', [128, 16, 128])
            AT = A('AT', [128, 128])
            Hs = A('Hs', [128, 16, 128])
            Ho = A('Ho', [128, 16, 128])
            Hc = A('Hc', [128, 6, 128])
            htmp = A('htmp', [128, 128])
            Hs6 = A('Hs6', [128, 6, 5, 128])
            AT6 = A('AT6', [128, 6, 128])
            tmp6 = A('tmp6', [128, 6, 128])
            on6 = A('on6', [128, 6, 128])
            ybf6 = A('ybf6', [128, 6, 128], BF16)
            ss6 = A('ss6', [128, 24])
            junk = htmp
            ss = A('ss', [128, 4])
            on = A('on', [128, 128])
            ybf = A('ybf', [128, 128], BF16)
            self.MEMSET('dve', Hc[:], 0.0, ['Hc'])
            tiles = [('p', ti) for ti in range(TP // 128)] + [('s', 0)]
            for kind, ti in tiles:
                sample = kind == 's'
                t0 = TP if sample else ti * 128
                mk = '16' if sample else '4'
                nsub = 16 if sample else 4
                sub = 128 // nsub
                up, rst, cm, rm = masks[mk]
                self.dma(pb[:], self.projT[RW_COLS:RW_COLS + HG_COLS, t0:t0 + 128].rearrange('(c p) t -> p c t', p=128), ['projT'], ['pb'])
                self.ACT(q[:], pb[:, 0:6, :], AF.Silu, ['pb'], ['q'])
                self.ACT(sg[:], pb[:, 18:24, :], AF.Silu, ['pb'], ['sg'])
                self.ACT(fg[:], pb[:, 6:12, :], AF.Sigmoid, ['pb'], ['fg'])
                for h in range(6):
                    self.TSC('dve', fg[:, h, :], fg[:, h, :], prm[:, l, P_OMLB + h:P_OMLB + h + 1], prm[:, l, P_LB + h:P_LB + h + 1], ALU.mult, ALU.add, ['fg', 'prm'], ['fg'])
                self.ACT(lf[:], fg[:], AF.Ln, ['fg'], ['lf'])
                self.TSC('dve', kx[:], fg[:], -1.0, 1.0, ALU.mult, ALU.add, ['fg'], ['kx'])
                for h in range(6):
                    self.SCAN(cs[:, h, :], rst[:], lf[:, h, :], 0.0, ['lf', 'rst' + mk], ['cs'])
                self.ACT(Gm[:], cs[:], AF.Exp, ['cs'], ['Gm'])
                self.ACT(Gi[:], cs[:], AF.Exp, ['cs'], ['Gi'], scale=-1.0)
                self.TT('dve', QT[:], q[:], Gm[:], ALU.mult, ['q', 'Gm'], ['QT'])
                self.TT('dve', KT[:], kx[:], Gi[:], ALU.mult, ['kx', 'Gi'], ['KT'])
                for (src_fn, sres, dst, dres) in ((lambda h: pb[:, 12 + h, :], 'pb', Vtm, 'Vtm'), (lambda h: KT[:, h, :], 'KT', Ktm, 'Ktm')):
                    for half in range(2):
                        ps, pn = self.psum()
                        hs = range(0, 4) if half == 0 else range(4, 6)
                        for h in hs:
                            self.TR(ps[:, (h % 4) * 128:(h % 4 + 1) * 128], src_fn(h), [sres], [pn])
                        n = len(hs)
                        self.CP('act' if half == 0 else 'dve', dst[:, hs[0]:hs[0] + n, :], ps[:, 0:n * 128].rearrange('p (h t) -> p h t', t=128), [pn], [dres + str(half)])
                vr, kr = ['Vtm0', 'Vtm1'], ['Ktm0', 'Ktm1']
                if not sample:
                    for half in range(2):
                        ps, pn = self.psum()
                        hs = range(0, 4) if half == 0 else range(4, 6)
                        for h in hs:
                            self.MM(ps[:, (h % 4) * 128:(h % 4 + 1) * 128], KT[:, h, :], QT[:, h, :], True, True, ['KT', 'QT'], [pn])
                        n = len(hs)
                        self.TT('dve', AT6[:, hs[0]:hs[0] + n, :], ps[:, 0:n * 128].rearrange('p (h t) -> p h t', t=128),
                                up[:, None, :].to_broadcast([128, n, 128]) if False else up[:].rearrange('p (a t) -> p a t', a=1).to_broadcast([128, n, 128]), ALU.mult, [pn, 'up' + mk], ['AT6_%d' % half])
                    atr = ['AT6_0', 'AT6_1']
                    self.TSC('dve', KM[:, 0:6, :], Ktm[:], rm[:, 3:4], None, ALU.mult, None, kr + ['rm' + mk], ['KM3'])
                    self.TT('dve', QM[:, 0:6, :], QT[:], cm[:, 3:4, :].to_broadcast([128, 6, 128]), ALU.mult, ['QT', 'cm' + mk], ['QM3'])
                    self.CP('act', Hs6[:, :, 0, :], Hc[:], ['Hc'], ['Hs6_0'])
                    for sb in range(4):
                        pss = [self.psum(), self.psum()]
                        for h in range(6):
                            ps, pn = pss[h // 4]
                            o = ps[:, (h % 4) * 128:(h % 4 + 1) * 128]
                            if sb == 3:
                                self.MM(o, KM[:, h, :], Vtm[:, h, :], True, True, ['KM3'] + vr, [pn])
                            else:
                                pr = slice(sb * 32, (sb + 1) * 32)
                                self.MM(o, Ktm[pr, h, :], Vtm[pr, h, :], True, True, kr + vr, [pn])
                        for half in range(2):
                            ps, pn = pss[half]
                            hs = range(0, 4) if half == 0 else range(4, 6)
                            n = len(hs)
                            self.TT('dve', tmp6[:, hs[0]:hs[0] + n, :], ps[:, 0:n * 128].rearrange('p (h t) -> p h t', t=128), Hs6[:, hs[0]:hs[0] + n, sb, :], ALU.add,
                                    [pn, 'Hs6_%d' % sb], ['tmp6_%d' % half])
                        tend = sb * 32 + 31
                        self.TT('dve', Hs6[:, :, sb + 1, :], tmp6[:], Gm[:, :, tend:tend + 1].to_broadcast([128, 6, 128]), ALU.mult, ['tmp6_0', 'tmp6_1', 'Gm'], ['Hs6_%d' % (sb + 1)])
                    psO = [self.psum(), self.psum()]
                    for h in range(6):
                        ps, pn = psO[h // 4]
                        o = ps[:, (h % 4) * 128:(h % 4 + 1) * 128]
                        self.MM(o, QM[:, h, :], Hs6[:, h, 3, :], True, False, ['QM3', 'Hs6_3'], [pn], sgc=True)
                        for sb in range(3):
                            pr = slice(sb * 32, (sb + 1) * 32)
                            self.MM(ps[pr, (h % 4) * 128:(h % 4 + 1) * 128], QT[:, h, pr], Hs6[:, h, sb, :], True, False, ['QT', 'Hs6_%d' % sb], [pn], sgc=True)
                        self.MM(o, AT6[:, h, :], Vtm[:, h, :], False, True, atr + vr, [pn], sgc=True)
                    for h in range(6):
                        ps, pn = psO[h // 4]
                        self.ACT(junk[:], ps[:, (h % 4) * 128:(h % 4 + 1) * 128], AF.Square, [pn], ['htmp', 'ss6'], accum=ss6[:, h:h + 1])
                    self.ACT(ss6[:, 6:12], ss6[:, 0:6], AF.Sqrt, ['ss6'], ['ss6'], bias=EPS, scale=1.0 / 128)
                    self.RECIP(ss6[:, 12:18], ss6[:, 6:12], ['ss6'], ['ss6'])
                    for half in range(2):
                        ps, pn = psO[half]
                        hs = range(0, 4) if half == 0 else range(4, 6)
                        n = len(hs)
                        self.TT('dve', on6[:, hs[0]:hs[0] + n, :], ps[:, 0:n * 128].rearrange('p (h t) -> p h t', t=128),
                                ss6[:, 12 + hs[0]:12 + hs[0] + n].rearrange('p (h a) -> p h a', a=1).to_broadcast([128, n, 128]), ALU.mult, [pn, 'ss6'], ['on6_%d' % half])
                    for half in range(2):
                        ps2, pn2 = self.psum()
                        hs = range(0, 4) if half == 0 else range(4, 6)
                        for h in hs:
                            self.TR(ps2[:, (h % 4) * 128:(h % 4 + 1) * 128], on6[:, h, :], ['on6_0', 'on6_1'], [pn2])
                        for h in hs:
                            self.STT(ybf6[:, h, :], ps2[:, (h % 4) * 128:(h % 4 + 1) * 128], prm[:, l, P_HNW + h:P_HNW + h + 1], sg[:, h, :], ALU.mult, ALU.mult, [pn2, 'prm', 'sg'], ['ybf6'])
                    self.dma(self.yT[768:1536, t0:t0 + 128].rearrange('(h p) t -> p h t', p=128), ybf6[:], ['ybf6'], ['yT_hg'])
                    self.CP('act', Hc[:], Hs6[:, :, 4, :], ['Hs6_4'], ['Hc'])
                    if ti == TP // 128 - 1:
                        self.dma(O_['p_hgrn'][l].rearrange('h k v -> k h v'), Hs6[:, :, 4, :], ['Hs6_4'], [])
                    if side is not None and (ti % 2 == 1 or ti == 0):
                        next(side, None)
                    continue
                for h in range(6):
                    ps, pn = self.psum()
                    self.MM(ps[:, 0:128], KT[:, h, :], QT[:, h, :], True, True, ['KT', 'QT'], [pn])
                    self.TT('dve', AT[:], ps[:, 0:128], up[:], ALU.mult, [pn, 'up' + mk], ['AT'])
                    if sample:
                        self.dma(Hs[:, 0:16, :], I['state_hgrn'][l, :, h, :, :].rearrange('s k v -> k s v'), [], ['Hs%d' % sb for sb in range(16)])
                        hin = lambda sb: (Hs[:, sb, :], 'Hs%d' % sb)
                        hout = lambda sb: (Ho[:, sb, :], 'Ho%d' % sb)
                    else:
                        self.CP('pool', Hs[:, 0, :], Hc[:, h, :], ['Hc'], ['Hs0'])
                        hin = lambda sb: (Hs[:, sb, :], 'Hs%d' % sb)
                        hout = lambda sb: (Hs[:, sb + 1, :], 'Hs%d' % (sb + 1))
                    msk = (lambda sb: True) if sample else (lambda sb: sb == 3)
                    for sb in range(nsub):
                        if msk(sb):
                            self.TSC('dve', KM[:, sb, :], Ktm[:, h, :], rm[:, sb:sb + 1], None, ALU.mult, None, kr + ['rm' + mk], ['KM%d' % sb])
                            self.TT('pool' if (sample and sb % 4 == 3) else 'dve', QM[:, sb, :], QT[:, h, :], cm[:, sb, :], ALU.mult, ['QT', 'cm' + mk], ['QM%d' % sb])
                    for sb in range(nsub):
                        ps, pn = self.psum()
                        if msk(sb):
                            self.MM(ps[:, 0:128], KM[:, sb, :], Vtm[:, h, :], True, True, ['KM%d' % sb] + vr, [pn])
                        else:
                            pr = slice(sb * sub, (sb + 1) * sub)
                            self.MM(ps[:, 0:128], Ktm[pr, h, :], Vtm[pr, h, :], True, True, kr + vr, [pn])
                        hi, hir = hin(sb)
                        ho, hor = hout(sb)
                        tend = sb * sub + sub - 1
                        self.TT('dve', htmp[:], ps[:, 0:128], hi, ALU.add, [pn, hir], ['htmp'])
                        self.TSC('dve', ho, htmp[:], Gm[:, h, tend:tend + 1], None, ALU.mult, None, ['htmp', 'Gm'], [hor])
                    ps, pn = self.psum()
                    for sb in ([3, 0, 1, 2] if not sample else range(nsub)):
                        hi, hir = hin(sb)
                        if sample:
                            self.MM(ps[:, 0:128], QM[:, sb, :], hi, sb == 0, False, ['QM%d' % sb, hir], [pn])
                        elif sb == 3:
                            self.MM(ps[:, 0:128], QM[:, sb, :], hi, True, False, ['QM%d' % sb, hir], [pn], sgc=True)
                        else:
                            pr = slice(sb * sub, (sb + 1) * sub)
                            self.MM(ps[pr, 0:128], QT[:, h, pr], hi, True, False, ['QT', hir], [pn], sgc=True)
                    self.MM(ps[:, 0:128], AT[:], Vtm[:, h, :], False, True, ['AT'] + vr, [pn], sgc=not sample)
                    self.ACT(junk[:], ps[:, 0:128], AF.Square, [pn], ['htmp', 'ss'], accum=ss[:, 0:1])
                    self.ACT(ss[:, 1:2], ss[:, 0:1], AF.Sqrt, ['ss'], ['ss'], bias=EPS, scale=1.0 / 128)
                    self.RECIP(ss[:, 2:3], ss[:, 1:2], ['ss'], ['ss'])
                    self.TSC('dve', on[:], ps[:, 0:128], ss[:, 2:3], None, ALU.mult, None, [pn, 'ss'], ['on'])
                    ps2, pn2 = self.psum()
                    self.TR(ps2[:, 0:128], on[:], ['on'], [pn2])
                    self.STT(ybf[:], ps2[:, 0:128], prm[:, l, P_HNW + h:P_HNW + h + 1], sg[:, h, :], ALU.mult, ALU.mult, [pn2, 'prm', 'sg'], ['ybf'])
                    self.dma(self.yT[768 + h * 128:768 + (h + 1) * 128, t0:t0 + 128], ybf[:], ['ybf'], ['yT_hg'])
                    if sample:
                        self.dma(O_['s_hgrn'][l, :, h, :, :].rearrange('s k v -> k s v'), Ho[:], ['Ho%d' % sb for sb in range(16)], [])
                    else:
                        self.CP('pool', Hc[:, h, :], Hs[:, nsub, :], ['Hs%d' % nsub], ['Hc'])
                        if ti == TP // 128 - 1:
                            self.dma(O_['p_hgrn'][l, h, :, :], Hs[:, nsub, :], ['Hs%d' % nsub], [])
                if side is not None and (ti % 2 == 1 or sample or ti == 0):
                    next(side, None)
            if side is not None:
                for _ in side:
                    pass

    def phase_lru(self, l, I, O_):
        for _ in self.lru_gen(l, I, O_):
            pass

    def lru_gen(self, l, I, O_):
        nc = self.nc
        TP = self.TP
        prm = self.prm
        base = RW_COLS + HG_COLS
        with contextlib.ExitStack() as es:
            A = lambda name, shape, dt=F32: es.enter_context(nc.sbuf_tensor('lr%d_%s' % (l, name), list(shape), dt))
            wabd = A('wabd', [128, 4, 128])
            wxbd = A('wxbd', [128, 4, 128])
            self.MEMSET('dve', wabd[:], 0.0, ['wabd'])
            self.MEMSET('dve', wxbd[:], 0.0, ['wxbd'])
            for blk in range(8):
                pr = slice(64 * (blk % 2), 64 * (blk % 2) + 64)
                self.dma(wabd[pr, blk // 2, 64 * (blk % 2):64 * (blk % 2) + 64], I['rglru_wa'][l, blk], [], ['wabd'])
                self.dma(wxbd[pr, blk // 2, 64 * (blk % 2):64 * (blk % 2) + 64], I['rglru_wx'][l, blk], [], ['wxbd'])
            TM = max(TP, 128)
            xpb = [A('xp_p%d' % c, [128, TP + 3]) for c in range(2)]
            gtb = [A('gt_p%d' % c, [128, TP]) for c in range(2)]
            xps = {('p', c): xpb[c % 2] for c in range(4)}
            gts = {('p', c): gtb[c % 2] for c in range(4)}
            for c in range(4):
                xps[('s', c)] = A('xp_s%d' % c, [128, 16 * 11])
                gts[('s', c)] = A('gt_s%d' % c, [128, 128])
            h0s = [A('h0_%d' % c, [128, 16]) for c in range(4)]
            xc = A('xc', [128, TM])
            rr = A('rr', [128, TM])
            ig = A('ig', [128, TM])
            aa = A('aa', [128, TM])
            uu = A('uu', [128, TM])
            hh = xc
            ybf = aa[:].bitcast(BF16)
            def prefetch(kind, c):
                    sample = kind == 's'
                    nseq, T, t0 = (16, 8, TP) if sample else (1, TP, 0)
                    N = nseq * T
                    xp, gt, h0 = xps[(kind, c)], gts[(kind, c)], h0s[c]
                    xn, gn_ = 'xp%s%d' % (kind, c % 2 if kind == 'p' else c), 'gt%s%d' % (kind, c % 2 if kind == 'p' else c)
                    xpv = xp[:, 0:nseq * (T + 3)].rearrange('p (s t) -> p s t', t=T + 3)
                    if sample:
                        for j in range(3):
                            self.dma(xpv[:, :, j], I['cache_rglru_conv'][l, :, j, c * 128:(c + 1) * 128].rearrange('s p -> p s'), [], [xn], slow=True)
                        self.dma(h0[:], I['state_rglru'][l, :, c * 128:(c + 1) * 128].rearrange('s p -> p s'), [], ['h0_%d' % c], slow=True)
                    else:
                        self.MEMSET('pool', xpv[:, :, 0:3], 0.0, [xn])
                    r0 = base + c * 128
                    self.dma(xpv[:, :, 3:T + 3], self.projT[r0:r0 + 128, t0:t0 + N].rearrange('p (s t) -> p s t', t=T), ['projT'], [xn])
                    self.dma(gt[:, 0:N], self.projT[r0 + 512:r0 + 640, t0:t0 + N], ['projT'], [gn_])
            for c in range(2):
                prefetch('p', c)
            for c in range(4):
                prefetch('s', c)
            yield
            for kind in ('p', 's'):
                sample = kind == 's'
                nseq, T, t0 = (16, 8, TP) if sample else (1, TP, 0)
                N = nseq * T
                for c in range(4):
                    xp, gt, h0 = xps[(kind, c)], gts[(kind, c)], h0s[c]
                    xn, gn_ = 'xp%s%d' % (kind, c % 2 if kind == 'p' else c), 'gt%s%d' % (kind, c % 2 if kind == 'p' else c)
                    xpv = xp[:, 0:nseq * (T + 3)].rearrange('p (s t) -> p s t', t=T + 3)
                    v3 = lambda ap: ap[:, 0:N].rearrange('p (s t) -> p s t', t=T)
                    r0 = base + c * 128
                    cw = lambda j: prm[:, l, P_CW + 4 * j + c:P_CW + 4 * j + c + 1]
                    self.TSC('dve', v3(xc), xpv[:, :, 0:T], cw(0), prm[:, l, P_CB + c:P_CB + c + 1], ALU.mult, ALU.add, [xn, 'prm'], ['xc'])
                    for j in range(1, 4):
                        self.STT(v3(xc), xpv[:, :, j:j + T], cw(j), v3(xc), ALU.mult, ALU.add, [xn, 'prm', 'xc'], ['xc'])
                    g0 = 0
                    while g0 < N:
                        gn = min(512, N - g0)
                        ps, pn = self.psum()
                        self.MM(ps[:, 0:gn], wabd[:, c, :], xc[:, g0:g0 + gn], True, True, ['wabd', 'xc'], [pn])
                        self.ACT(rr[:, g0:g0 + gn], ps[:, 0:gn], AF.Sigmoid, [pn, 'prm'], ['rr'], bias=prm[:, l, P_BA + c:P_BA + c + 1])
                        ps, pn = self.psum()
                        self.MM(ps[:, 0:gn], wxbd[:, c, :], xc[:, g0:g0 + gn], True, True, ['wxbd', 'xc'], [pn])
                        self.ACT(ig[:, g0:g0 + gn], ps[:, 0:gn], AF.Sigmoid, [pn, 'prm'], ['ig'], bias=prm[:, l, P_BX + c:P_BX + c + 1])
                        g0 += gn
                    self.ACT(aa[:, 0:N], rr[:, 0:N], AF.Exp, ['rr', 'prm'], ['aa'], scale=prm[:, l, P_C8 + c:P_C8 + c + 1])
                    self.ACT(uu[:, 0:N], rr[:, 0:N], AF.Exp, ['rr', 'prm'], ['uu'], scale=prm[:, l, P_2C8 + c:P_2C8 + c + 1])
                    self.ACT(uu[:, 0:N], uu[:, 0:N], AF.Sqrt, ['uu'], ['uu'], bias=1.0, scale=-1.0)
                    self.TT('pool', ig[:, 0:N], ig[:, 0:N], xc[:, 0:N], ALU.mult, ['ig', 'xc'], ['ig'])
                    self.TT('dve', uu[:, 0:N], uu[:, 0:N], ig[:, 0:N], ALU.mult, ['uu', 'ig'], ['uu'])
                    if sample:
                        self.TT('dve', h0[:], h0[:], v3(aa)[:, :, 0], ALU.mult, ['h0_%d' % c, 'aa'], ['h0_%d' % c])
                        self.TT('dve', v3(uu)[:, :, 0], v3(uu)[:, :, 0], h0[:], ALU.add, ['uu', 'h0_%d' % c], ['uu'])
                        self.MEMSET('dve', v3(aa)[:, :, 0:1], 0.0, ['aa'])
                    self.SCAN2(hh[:, 0:N], aa[:, 0:N], uu[:, 0:N], ['aa', 'uu'], ['xc'])
                    self.TT('pool', rr[:, 0:N], gt[:, 0:N], gt[:, 0:N], ALU.mult, [gn_, 'rr'], ['rr'])
                    self.TSC('pool', rr[:, 0:N], rr[:, 0:N], 0.044715, 1.0, ALU.mult, ALU.add, ['rr'], ['rr'])
                    self.TT('pool', rr[:, 0:N], rr[:, 0:N], gt[:, 0:N], ALU.mult, ['rr', gn_], ['rr'])
                    self.ACT(rr[:, 0:N], rr[:, 0:N], AF.Sigmoid, ['rr'], ['rr'], scale=1.5957691216057308)
                    self.TT('dve', ig[:, 0:N], hh[:, 0:N], gt[:, 0:N], ALU.mult, ['xc', gn_, 'ig'], ['ig'])
                    self.TT('dve', ybf[:, 0:N], ig[:, 0:N], rr[:, 0:N], ALU.mult, ['ig', 'rr'], ['aa'])
                    self.dma(self.yT[1536 + c * 128:1536 + (c + 1) * 128, t0:t0 + N], ybf[:, 0:N], ['aa'], ['yT_lru'])
                    if sample:
                        for j in range(3):
                            self.dma(O_['s_rglru_conv'][l, :, j, c * 128:(c + 1) * 128].rearrange('s p -> p s'), xpv[:, :, T + j], [xn], [], slow=True)
                        self.dma(O_['s_rglru'][l, :, c * 128:(c + 1) * 128].rearrange('s p -> p s'), v3(hh)[:, :, T - 1], ['xc'], [], slow=True)
                    else:
                        self.dma(O_['p_rglru_conv'][l, :, c * 128:(c + 1) * 128].rearrange('j p -> p j'), xpv[:, 0, T:T + 3], [xn], [], slow=True)
                        self.dma(O_['p_rglru'][l, c * 128:(c + 1) * 128].rearrange('(p o) -> p o', o=1), hh[:, T - 1:T], ['xc'], [], slow=True)
                    if kind == 'p' and c < 2:
                        prefetch('p', c + 2)
                    yield

    def SCAN2(self, out, d0, d1, r, w):
        self.E('dve', lambda e: e.tensor_tensor_scan(out, d0, d1, 0.0, ALU.mult, ALU.add), r, w)
    def gemm(self, tag, A, KC, blocks, rhs_fn, rhs_res, epilogue, cbw=256, tgs=None, bufs=None, single=False):
        tgs = self.tg if tgs is None else tgs
        if bufs is None:
            wst = [A('%s_wst%d' % (tag, i), [128, KC, cbw]) for i in range(1 if single else 2)]
            if single:
                wst = wst * 2
            wbf = [A('%s_wbf%d' % (tag, i), [128, KC, cbw], BF16) for i in range(2)]
        else:
            wst, wbf, tag = bufs

        sb_ = (lambda b: 0) if single else (lambda b: b)
        ka = (KC * 7) // 16
        kparts = [(0, ka), (ka, 2 * ka), (2 * ka, KC)]
        kp_of = lambda kc: 0 if kc < ka else (1 if kc < 2 * ka else 2)

        def load_dma(bi):
            b = bi % 2
            off = 0
            for ap, w in blocks[bi]:
                self.dma(wst[b][:, :, off:off + w], ap.rearrange('(c p) n -> p c n', p=128), [], ['%s_wst%d' % (tag, sb_(b))])
                off += w
            return off

        def load_cast(bi, off):
            b = bi % 2
            for p, (k0, k1) in enumerate(kparts):
                self.CP(('dve', 'act', 'pool')[p], wbf[b][:, k0:k1, 0:off], wst[b][:, k0:k1, 0:off], ['%s_wst%d' % (tag, sb_(b))], ['%s_wbf%d_%d' % (tag, b, p)])

        widths = {0: load_dma(0)}
        load_cast(0, widths[0])
        for bi in range(len(blocks)):
            if bi + 1 < len(blocks):
                widths[bi + 1] = load_dma(bi + 1)
            b = bi % 2
            nu = widths[bi] // 128
            work = [(u, gi, t0, tn) for u in range(nu) for gi, (t0, tn) in enumerate(tgs)]
            cast_at = (len(work) * 3) // 5
            for wi, (u, gi, t0, tn) in enumerate(work):
                if wi == cast_at and bi + 1 < len(blocks):
                    load_cast(bi + 1, widths[bi + 1])
                ps, pn = self.psum()
                for kc in range(KC):
                    self.MM(ps[:, 0:tn], wbf[b][:, kc, u * 128:(u + 1) * 128], rhs_fn(kc, t0, tn), kc == 0, kc == KC - 1,
                            ['%s_wbf%d_%d' % (tag, b, kp_of(kc))] + rhs_res, [pn])
                epilogue(bi, u, gi, t0, tn, ps, pn)

    def phase_inproj(self, l, actT, I, A):
        st = [A('ip_st%d_%d' % (l, i), [128, 512]) for i in range(4)]
        w_in = I['w_in']
        blocks = [[(w_in[l, :, c0:c0 + 512], 512)] for c0 in range(0, IN_COLS, 512)]
        self._k = 0

        def epi(bi, u, gi, t0, tn, ps, pn):
            k = self._k % 4
            self._k += 1
            c0 = bi * 512 + u * 128
            self.CP('act' if k % 2 == 0 else 'dve', st[k][:, 0:tn], ps[:, 0:tn], [pn], ['ip_st%d' % k])
            self.dma(self.projT[c0:c0 + 128, t0:t0 + tn], st[k][:, 0:tn], ['ip_st%d' % k], ['projT'])

        self.gemm('ip%d' % l, A, 16, blocks, lambda kc, t0, tn: actT[:, kc, t0:t0 + tn], ['actT'], epi, cbw=512)

    def residual_epi(self, A, tag):
        xs = [A('%s_x%d' % (tag, i), [128, 512]) for i in range(4)]
        self._k = 0

        def epi(c0, t0, tn, ps, pn):
            k = self._k % 4
            self._k += 1
            rn = '%s_x%d' % (tag, k)
            self.dma(xs[k][:, 0:tn], self.xT[c0:c0 + 128, t0:t0 + tn], ['xT'], [rn])
            self.TT('dve', xs[k][:, 0:tn], xs[k][:, 0:tn], ps[:, 0:tn], ALU.add, [rn, pn], [rn])
            self.dma(self.xT[c0:c0 + 128, t0:t0 + tn], xs[k][:, 0:tn], [rn], ['xT'])
        return epi

    def phase_outproj(self, l, actT, I, A):
        self.dma(actT[:], self.yT.rearrange('(c p) t -> p c t', p=128), ['yT_rw', 'yT_hg', 'yT_lru'], ['actT'])
        w = I['w_out']
        blocks = [[(w[l, :, c0:c0 + 512], 512)] for c0 in range(0, D, 512)]
        repi = self.residual_epi(A, 'op%d' % l)
        self.gemm('op%d' % l, A, 16, blocks, lambda kc, t0, tn: actT[:, kc, t0:t0 + tn], ['actT'],
                  lambda bi, u, gi, t0, tn, ps, pn: repi(bi * 512 + u * 128, t0, tn, ps, pn), cbw=512)

    def phase_ffn_up(self, l, actT, I, O_, A):
        TP, NT = self.TP, self.NT
        fp = self.fprm
        w = I['ffn_w_up']
        blocks = [[(w[l, :, j * 256:(j + 1) * 256], 256), (w[l, :, DFF + j * 256:DFF + (j + 1) * 256], 256)] for j in range(DFF // 256)]
        gp = [A('fu%d_gp%d' % (l, i), [128, TP + 2]) for i in range(2)]
        gs = [A('fu%d_gs%d' % (l, i), [128, 16, 10]) for i in range(2)]
        vb = [A('fu%d_vb%d' % (l, i), [128, NT]) for i in range(2)]
        cv = A('fu%d_cv' % l, [128, NT])
        hb = [A('fu%d_hb%d' % (l, i), [128, NT], BF16) for i in range(2)]
        for i in range(2):
            self.MEMSET('pool', gp[i][:, 0:2], 0.0, ['gp%d' % i])
        ng = len(self.tg)
        CI = [A('fu%d_ci%d' % (l, i), [32, 128]) for i in range(2)]
        OS = [A('fu%d_os%d' % (l, i), [34, 128]) for i in range(2)]
        tmpo = A('fu%d_tmpo' % l, [128, 34])
        cin = I['cache_ffn_conv'][l].rearrange('s j c -> (s j) c')
        for b0 in range(2):
            self.dma(CI[b0][:], cin[:, b0 * 128:(b0 + 1) * 128], [], ['fu_ci%d' % b0])

        def epi(bi, u, gi, t0, tn, ps, pn):
            b = u % 2
            blk = 2 * bi + b
            if u < 2:
                if gi == 0:
                    ps2, pn2 = self.psum()
                    self.TR(ps2[:, 0:32], CI[b][:], ['fu_ci%d' % b], [pn2], n=32)
                    self.CP('dve', gs[b][:, :, 0:2], ps2[:, 0:32].rearrange('p (s j) -> p s j', j=2), [pn2], ['gs%d' % b])
                if t0 < TP:
                    self.CP('act', gp[b][:, 2 + t0:2 + t0 + tn], ps[:, 0:tn], [pn], ['gp%d' % b])
                else:
                    self.CP('act', gs[b][:, :, 2:10], ps[:, 0:tn].rearrange('p (s t) -> p s t', t=8), [pn], ['gs%d' % b])
            else:
                self.CP('act' if gi % 2 == 0 else 'dve', vb[b][:, t0:t0 + tn], ps[:, 0:tn], [pn], ['vb%d' % b])
                if gi == 0 and bi + 1 < len(blocks):
                    nblk = 2 * (bi + 1) + b
                    self.dma(CI[b][:], cin[:, nblk * 128:(nblk + 1) * 128], [], ['fu_ci%d' % b])
                if gi == ng - 1:
                    self.CP('dve', tmpo[:, 0:32].rearrange('p (s j) -> p s j', j=2), gs[b][:, :, 8:10], ['gs%d' % b], ['fu_tmpo'])
                    self.CP('dve', tmpo[:, 32:34], gp[b][:, TP:TP + 2], ['gp%d' % b], ['fu_tmpo'])
                    ps2, pn2 = self.psum()
                    self.TR(ps2[0:34, 0:128], tmpo[:], ['fu_tmpo'], [pn2])
                    self.CP('act', OS[b][:], ps2[0:34, 0:128], [pn2], ['fu_os%d' % b])
                    self.dma(O_['s_ffn_conv'][l].rearrange('s j c -> (s j) c')[:, blk * 128:(blk + 1) * 128], OS[b][0:32, :], ['fu_os%d' % b], [])
                    self.dma(O_['p_ffn_conv'][l][:, blk * 128:(blk + 1) * 128], OS[b][32:34, :], ['fu_os%d' % b], [])
                    w_ = lambda j: fp[:, l, j * 44 + blk:j * 44 + blk + 1]
                    bb = fp[:, l, 132 + blk:133 + blk]
                    cvs = cv[:, TP:NT].rearrange('p (s t) -> p s t', t=8)
                    self.TSC('dve', cv[:, 0:TP], gp[b][:, 0:TP], w_(0), bb, ALU.mult, ALU.add, ['gp%d' % b, 'fprm'], ['cv'])
                    self.STT(cv[:, 0:TP], gp[b][:, 1:TP + 1], w_(1), cv[:, 0:TP], ALU.mult, ALU.add, ['gp%d' % b, 'fprm', 'cv'], ['cv'])
                    self.STT(cv[:, 0:TP], gp[b][:, 2:TP + 2], w_(2), cv[:, 0:TP], ALU.mult, ALU.add, ['gp%d' % b, 'fprm', 'cv'], ['cv'])
                    self.TSC('dve', cvs, gs[b][:, :, 0:8], w_(0), bb, ALU.mult, ALU.add, ['gs%d' % b, 'fprm', 'cv'], ['cv'])
                    self.STT(cvs, gs[b][:, :, 1:9], w_(1), cvs, ALU.mult, ALU.add, ['gs%d' % b, 'fprm', 'cv'], ['cv'])
                    self.STT(cvs, gs[b][:, :, 2:10], w_(2), cvs, ALU.mult, ALU.add, ['gs%d' % b, 'fprm', 'cv'], ['cv'])
                    self.ACT(cv[:], cv[:], AF.Silu, ['cv'], ['cv'])
                    self.TT('dve', hb[b][:], cv[:], vb[b][:], ALU.mult, ['cv', 'vb%d' % b], ['hb%d' % b])
                    self.dma(self.hT[blk * 128:(blk + 1) * 128, :], hb[b][:], ['hb%d' % b], ['hT'])

        self.gemm('fu%d' % l, A, 16, blocks, lambda kc, t0, tn: actT[:, kc, t0:t0 + tn], ['actT'], epi, cbw=512, single=True)

    def phase_ffn_down(self, l, I, A):
        NT = self.NT
        w = I['ffn_w_down']
        KC = DFF // 128
        parts = [self.tg[0:2], self.tg[2:]] if len(self.tg) > 2 else [self.tg]
        nmax = max(sum(n for _, n in p) for p in parts if p)
        hres = A('fd%d_h' % l, [128, KC, nmax], BF16)
        repi = self.residual_epi(A, 'fd%d' % l)
        blocks = [[(w[l, :, c0:c0 + 256], 256)] for c0 in range(0, D, 256)]
        tag = 'fd%d' % l
        bufs = ([A('%s_wst0' % tag, [128, KC, 256])] * 2, [A('%s_wbf%d' % (tag, i), [128, KC, 256], BF16) for i in range(2)], tag)
        for pi, part in enumerate(parts):
            if not part:
                continue
            s0 = part[0][0]
            n = sum(nn for _, nn in part)
            self.dma(hres[:, :, 0:n], self.hT[:, s0:s0 + n].rearrange('(c p) t -> p c t', p=128), ['hT'], ['hres'])
            self.gemm('fd%d_%d' % (l, pi), A, KC, blocks, lambda kc, t0, tn, s0=s0: hres[:, kc, t0 - s0:t0 - s0 + tn], ['hres'],
                      lambda bi, u, gi, t0, tn, ps, pn: repi(bi * 256 + u * 128, t0, tn, ps, pn), cbw=256, tgs=part, bufs=bufs, single=True)

    def phase_final(self, y_out):
        nc = self.nc
        with contextlib.ExitStack() as es:
            A = lambda name, shape, dt=F32: es.enter_context(nc.sbuf_tensor('fin_' + name, list(shape), dt))
            xg = [A('xg%d' % i, [128, 16, 512]) for i in range(2)]
            sqs = [A('sq%d' % i, [128, 16, 128], BF16) for i in range(2)]
            sds = [A('sd%d' % i, [128, 128]) for i in range(2)]
            rss = [A('rs%d' % i, [128, 128]) for i in range(2)]
            hns = [A('hn%d' % i, [128, 16, 128]) for i in range(2)]
            yo = [A('yo%d' % i, [128, D]) for i in range(2)]
            k = 0
            for gi, (g0, gn) in enumerate(self.tg):
                gb = gi % 2
                xn = 'fxg%d' % gb
                self.dma(xg[gb][:, :, 0:gn], self.xT[:, g0:g0 + gn].rearrange('(c p) t -> p c t', p=128), ['xT'], [xn])
                for j in range(gn // 128):
                    b = k % 2
                    k += 1
                    sq, sd, rs, hn = sqs[b], sds[b], rss[b], hns[b]
                    t0 = g0 + j * 128
                    xv = xg[gb][:, :, j * 128:(j + 1) * 128]
                    self.ACT(sq[:], xv, AF.Square, [xn], ['fsq%d' % b])
                    ps, pn = self.psum()
                    for kc in range(16):
                        self.MM(ps[:, 0:128], self.ones_bf[:], sq[:, kc, :], kc == 0, kc == 15, ['fsq%d' % b, 'ones_bf'], [pn])
                    self.ACT(sd[:], ps[:, 0:128], AF.Sqrt, [pn], ['fsd%d' % b], bias=EPS, scale=1.0 / D)
                    self.RECIP(rs[:], sd[:], ['fsd%d' % b], ['frs%d' % b])
                    for kc in range(16):
                        self.STT(hn[:, kc, :], xg[gb][:, kc, j * 128:(j + 1) * 128], self.prmf[:, kc:kc + 1], rs[:], ALU.mult, ALU.mult, [xn, 'frs%d' % b, 'prmf'], ['fhn%d' % b])
                    for q in range(4):
                        ps, pn = self.psum()
                        for jj in range(4):
                            kc = q * 4 + jj
                            self.TR(ps[:, jj * 128:(jj + 1) * 128], hn[:, kc, :], ['fhn%d' % b], [pn])
                        self.CP('act' if q % 2 == 0 else 'dve', yo[b][:, q * 512:(q + 1) * 512], ps[:], [pn], ['fyo%d_%d' % (b, q)])
                    self.dma(y_out[t0:t0 + 128, :], yo[b][:], ['fyo%d_%d' % (b, q) for q in range(4)], [])


_CACHE = {}
TP_FULL = 2048
W_NAMES = ['norm_mix', 'w_in', 'rwkv_mu', 'rwkv_w0', 'rwkv_w2', 'rwkv_a0', 'rwkv_a2', 'rwkv_g2', 'rwkv_k_k', 'rwkv_k_a',
           'rwkv_r_k', 'rwkv_ln_w', 'rwkv_ln_b', 'hgrn_lb_logits', 'hgrn_norm_w', 'rglru_conv_w', 'rglru_conv_b', 'rglru_wa',
           'rglru_ba', 'rglru_wx', 'rglru_bx', 'rglru_lambda', 'w_out', 'norm_ffn', 'ffn_w_up', 'ffn_conv_w', 'ffn_conv_b',
           'ffn_w_down', 'norm_final']
S_NAMES = ['state_rwkv', 'state_rwkv_shift', 'state_hgrn', 'state_rglru', 'cache_rglru_conv', 'cache_ffn_conv']
O_NAMES = ['rwkv', 'rwkv_shift', 'hgrn', 'rglru', 'rglru_conv', 'ffn_conv']


def kernel(**inputs):
    if 'nc' not in _CACHE:
        b = Builder(TP_FULL, DEPTH)
        _CACHE['nc'] = b.build()
        _CACHE['b'] = b
    nc = _CACHE['nc']
    f32 = lambda a: np.ascontiguousarray(np.asarray(a, dtype=np.float32))
    consts = make_consts()
    shared = {k: f32(inputs[k]) for k in W_NAMES}
    for k, v in consts.items():
        shared['c_' + k] = v
    xp = f32(inputs['x_prompt'])
    xs = f32(inputs['x_sample'])
    states = {k: f32(inputs[k]) for k in S_NAMES}
    ncore = 8
    in_maps = []
    for i in range(ncore):
        m = dict(shared)
        m['x'] = np.ascontiguousarray(np.concatenate([xp[i % 4], xs[NSEQ * i:NSEQ * (i + 1)].reshape(NSEQ * TS, D)], axis=0))
        for k in S_NAMES:
            m[k] = np.ascontiguousarray(states[k][:, NSEQ * i:NSEQ * (i + 1)])
        in_maps.append(m)
    res = run_bass_kernel_spmd(nc, in_maps, core_ids=list(range(ncore)))
    R = res.results
    y_prompt = np.stack([np.asarray(R[b]['y'])[:TP_FULL] for b in range(4)], axis=0).astype(np.float32)
    y_sample = np.concatenate([np.asarray(R[i]['y'])[TP_FULL:].reshape(NSEQ, TS, D) for i in range(ncore)], axis=0).astype(np.float32)
    outs = [y_prompt, y_sample]
    for n in O_NAMES:
        outs.append(np.stack([np.asarray(R[b]['p_' + n]) for b in range(4)], axis=1).astype(np.float32))
    for n in O_NAMES:
        outs.append(np.concatenate([np.asarray(R[i]['s_' + n]) for i in range(ncore)], axis=1).astype(np.float32))
    return tuple(outs)
```

```python
import contextlib
import numpy as np
import concourse.bass as bass
import concourse.mybir as mybir
from concourse.bass_utils import run_bass_kernel_spmd

F32 = mybir.dt.float32
BF16 = mybir.dt.bfloat16
AF = mybir.ActivationFunctionType
ALU = mybir.AluOpType
AX = mybir.AxisListType

D = 2048
DEPTH = 2
NSEQ = 16
TS = 8
RW_COLS = 2560
HG_COLS = 3072
LRU_COLS = 1024
IN_COLS = 6656
DFF = 5632
EPS = 1e-6
RW_EPS = 64e-5
ENG = ('pe', 'act', 'dve', 'pool', 'sp')
NDSEM = 8
NDT = BF16
P_NM, P_NF, P_MU, P_W0, P_A0, P_KK, P_KA, P_OMKA, P_RK, P_LNW, P_LNB = 0, 16, 32, 52, 58, 64, 70, 76, 82, 88, 94
P_LB, P_OMLB, P_HNW, P_CW, P_CB, P_BA, P_BX, P_LAM, P_C8, P_2C8 = 100, 106, 112, 118, 134, 138, 142, 146, 150, 154


class Sched:
    def __init__(self, nc):
        self.nc = nc
        self.ops = {e: [] for e in ENG}
        self.res = {}
        self.bar = {e: {} for e in ENG}
        self.dma_since_bar = []

    def emit(self, eng, fn, reads=(), writes=()):
        deps = {}

        def add(d):
            e, i = d
            if e == 'sp':
                deps.setdefault(('sp', i), i)
            else:
                if deps.get(e, -1) < i:
                    deps[e] = i

        for r in reads:
            st = self.res.get(r)
            if st and st['w'] is not None:
                add(st['w'])
        for w in writes:
            st = self.res.get(w)
            if st:
                if st['w'] is not None:
                    add(st['w'])
                for e, i in st['r'].items():
                    if e == 'sp':
                        for ii in i:
                            add(('sp', ii))
                    else:
                        add((e, i))
        for k, v in self.bar[eng].items():
            if isinstance(k, tuple):
                deps.setdefault(k, v)
            elif deps.get(k, -1) < v:
                deps[k] = v
        self.bar[eng] = {}
        idx = len(self.ops[eng])
        if eng == 'pe':
            deps.pop('pe', None)
        self.ops[eng].append(dict(fn=fn, deps=deps, need=False))
        if eng == 'sp':
            self.dma_since_bar.append(idx)
        for r in reads:
            st = self.res.setdefault(r, {'w': None, 'r': {}})
            if eng == 'sp':
                st['r'].setdefault('sp', []).append(idx)
            else:
                st['r'][eng] = idx
        for w in writes:
            self.res[w] = {'w': (eng, idx), 'r': {}}

    def barrier(self):
        last = {}
        for e in ENG:
            if e == 'sp':
                continue
            if self.ops[e]:
                last[e] = len(self.ops[e]) - 1
        for i in self.dma_since_bar:
            last[('sp', i)] = i
        self.dma_since_bar = []
        for e in ENG:
            d = dict(last)
            d.pop(e, None)
            for k, v in d.items():
                if isinstance(k, tuple):
                    self.bar[e].setdefault(k, v)
                elif self.bar[e].get(k, -1) < v:
                    self.bar[e][k] = v

    def finalize(self):
        nc = self.nc
        ops = self.ops
        for e in ENG:
            for op in ops[e]:
                for k, v in op['deps'].items():
                    if isinstance(k, tuple):
                        continue
                    ops[k][v]['need'] = True
        for e in ENG:
            if e == 'sp':
                continue
            c = 0
            for op in ops[e]:
                if op['need']:
                    c += 1
                op['val'] = c
        cnt = [0] * NDSEM
        for k, op in enumerate(ops['sp']):
            j = k % NDSEM
            op['prev'] = cnt[j]
            cnt[j] += 16
            op['sem'] = j
            op['val'] = cnt[j]
        sems = {e: nc.alloc_semaphore('s_' + e) for e in ENG if e != 'sp'}
        dsems = [nc.alloc_semaphore('s_d%d' % j) for j in range(NDSEM)]

        def run(e, engobj):
            emitted = {}

            def need(key, val):
                if emitted.get(key, 0) < val:
                    sm = dsems[key[1]] if isinstance(key, tuple) else sems[key]
                    engobj.wait_ge(sm, val)
                    emitted[key] = val

            for op in ops[e]:
                for k, v in op['deps'].items():
                    if isinstance(k, tuple):
                        p = ops['sp'][v]
                        need(('d', p['sem']), p['val'])
                    else:
                        need(k, ops[k][v]['val'])
                if e == 'sp' and op['prev'] > 0:
                    need(('d', op['sem']), op['prev'])
                inst = op['fn'](engobj)
                if e == 'sp':
                    inst.then_inc(dsems[op['sem']], 16)
                elif op['need']:
                    inst.then_inc(sems[e], 1)
            if e == 'sp':
                for j in range(NDSEM):
                    if cnt[j] > 0:
                        need(('d', j), cnt[j])

        self._emitted = {}
        with nc.Block() as block:
            @block.tensor
            def _(t):
                run('pe', t)

            @block.scalar
            def _(t):
                run('act', t)

            @block.vector
            def _(t):
                run('dve', t)

            @block.gpsimd
            def _(t):
                run('pool', t)

            @block.sync
            def _(t):
                run('sp', t)


def make_consts():
    c = {}
    c['ident'] = np.eye(128, dtype=np.float32)
    c['ones'] = np.ones((128, 128), np.float32)
    bo = np.zeros((128, 128), np.float32)
    bo[:64, :64] = 1
    bo[64:, 64:] = 1
    c['blkones'] = bo
    i = np.arange(128)
    for name, sub in (('1', 128), ('4', 32), ('16', 8)):
        same = (i[:, None] // sub) == (i[None, :] // sub)
        low_s = ((i[None, :] < i[:, None]) & same).astype(np.float32)
        up_s = ((i[:, None] < i[None, :]) & same).astype(np.float32)
        up_i = ((i[:, None] <= i[None, :]) & same).astype(np.float32)
        c['low_s' + name] = low_s
        c['up_si' + name] = np.concatenate([up_s, up_i], axis=1)
        c['up_i' + name] = up_i
        nsub = 128 // sub
        rst = np.ones((128, 128), np.float32)
        rst[:, ::sub] = 0
        c['rst' + name] = rst
        cm = np.zeros((128, nsub, 128), np.float32)
        rm = np.zeros((128, nsub), np.float32)
        for sb in range(nsub):
            cm[:, sb, sb * sub:(sb + 1) * sub] = 1
            rm[sb * sub:(sb + 1) * sub, sb] = 1
        c['cm' + name] = cm.reshape(128, nsub * 128)
        c['rm' + name] = rm
    return c


class Builder:
    def __init__(self, TP, depth, stop=None, dbg=()):
        self.TP = TP
        self.depth = depth
        self.NT = TP + NSEQ * TS
        self.ntile = self.NT // 128
        self.stop = stop
        self.rw_stop = None
        self.dbg = dbg
        nc = bass.Bass('TRN2', target_bir_lowering=False)
        self.nc = nc
        self.S = Sched(nc)
        self.inputs = {}
        self.outputs = {}
        self.tg = []
        t = 0
        while t < TP:
            n = min(512, TP - t)
            self.tg.append((t, n))
            t += n
        self.tg.append((TP, NSEQ * TS))

    def din(self, name, shape, dt=F32):
        t = self.nc.dram_tensor(name, list(shape), dt, kind='ExternalInput')
        self.inputs[name] = t
        return t.ap()

    def dout(self, name, shape, dt=F32):
        t = self.nc.dram_tensor(name, list(shape), dt, kind='ExternalOutput')
        self.outputs[name] = t
        return t.ap()

    def dscr(self, name, shape, dt=F32):
        return self.nc.dram_tensor(name, list(shape), dt, kind='Internal').ap()

    def dma(self, out, in_, reads, writes, slow=False):
        if slow:
            self.S.emit('sp', lambda e: e.dma_start(out=out, in_=in_, allow_slow_non_contiguous=True), reads, writes)
        else:
            self.S.emit('sp', lambda e: e.dma_start(out=out, in_=in_), reads, writes)

    def E(self, eng, fn, reads, writes):
        self.S.emit(eng, fn, reads, writes)

    def psum(self):
        k = self._pk
        self._pk = (k + 1) % 8
        return self.ps[k], 'ps%d' % k

    def build(self):
        nc = self.nc
        TP, NT, L = self.TP, self.NT, self.depth
        I = {}
        I['x'] = self.din('x', [NT, D])
        shapes = dict(
            state_rwkv=[L, NSEQ, 12, 64, 64], state_rwkv_shift=[L, NSEQ, RW_COLS], state_hgrn=[L, NSEQ, 6, 128, 128],
            state_rglru=[L, NSEQ, 512], cache_rglru_conv=[L, NSEQ, 3, 512], cache_ffn_conv=[L, NSEQ, 2, DFF],
            norm_mix=[L, D], w_in=[L, D, IN_COLS], rwkv_mu=[L, RW_COLS], rwkv_w0=[L, 768], rwkv_w2=[L, 64, 768],
            rwkv_a0=[L, 768], rwkv_a2=[L, 64, 768], rwkv_g2=[L, 128, 768], rwkv_k_k=[L, 768], rwkv_k_a=[L, 768],
            rwkv_r_k=[L, 12, 64], rwkv_ln_w=[L, 768], rwkv_ln_b=[L, 768], hgrn_lb_logits=[L, 768], hgrn_norm_w=[L, 768],
            rglru_conv_w=[L, 4, 512], rglru_conv_b=[L, 512], rglru_wa=[L, 8, 64, 64], rglru_ba=[L, 512],
            rglru_wx=[L, 8, 64, 64], rglru_bx=[L, 512], rglru_lambda=[L, 512], w_out=[L, D, D], norm_ffn=[L, D],
            ffn_w_up=[L, D, 2 * DFF], ffn_conv_w=[L, 3, DFF], ffn_conv_b=[L, DFF], ffn_w_down=[L, DFF, D], norm_final=[D])
        for k, shp in shapes.items():
            I[k] = self.din(k, shp)
        self.cst = {k: self.din('c_' + k, v.shape) for k, v in make_consts().items()}
        O_ = {}
        O_['y'] = self.dout('y', [NT, D])
        oshapes = dict(p_rwkv=[L, 12, 64, 64], p_rwkv_shift=[L, RW_COLS], p_hgrn=[L, 6, 128, 128], p_rglru=[L, 512],
                       p_rglru_conv=[L, 3, 512], p_ffn_conv=[L, 2, DFF],
                       s_rwkv=[L, NSEQ, 12, 64, 64], s_rwkv_shift=[L, NSEQ, RW_COLS], s_hgrn=[L, NSEQ, 6, 128, 128],
                       s_rglru=[L, NSEQ, 512], s_rglru_conv=[L, NSEQ, 3, 512], s_ffn_conv=[L, NSEQ, 2, DFF])
        for k, shp in oshapes.items():
            O_[k] = self.dout(k, shp)
        self.xT = self.dscr('xT', [D, NT])
        self.projT = self.dscr('projT', [IN_COLS, NT])
        self.yT = self.dscr('yT', [D, NT], BF16)
        self.hT = self.dscr('hT', [DFF, NT], BF16)
        dbg = {k: self.dout('d_' + k, shp, dt) for k, (shp, dt) in dict(
            projT=([IN_COLS, NT], F32), xT=([D, NT], F32), yT=([D, NT], BF16), hT=([DFF, NT], BF16)).items() if k in self.dbg}

        with contextlib.ExitStack() as es:
            A = lambda name, shape, dt=F32: es.enter_context(nc.sbuf_tensor(name, list(shape), dt))
            self.ps = [es.enter_context(nc.psum_tensor('ps%d' % k, [128, 512], F32)) for k in range(8)]
            self._pk = 0
            self.ident = A('ident', [128, 128])
            self.ones_bf = A('ones_bf', [128, 128], BF16)
            self.dma(self.ident[:], self.cst['ident'], [], ['ident'])
            self.MEMSET('dve', self.ones_bf[:], 1.0, ['ones_bf'])
            self.prm = A('prm', [128, L, 160])
            self.fprm = A('fprm', [128, L, 176])
            self.prmf = A('prmf', [128, 16])
            self.ptmp = A('ptmp', [128, L + 1, 16])
            self.load_params(I)
            self.phase_transpose_in(I['x'])
            self.S.barrier()
            stop = self.stop
            for l in range(L):
                def sub(fn):
                    with contextlib.ExitStack() as es2:
                        A2 = lambda name, shape, dt=F32: es2.enter_context(nc.sbuf_tensor(name, list(shape), dt))
                        fn(A2)
                    self.S.barrier()

                def p_in(A2):
                    actT = A2('actT_a%d' % l, [128, 16, NT], BF16)
                    self.phase_norm(l, actT, P_NM, A2)
                    self.S.barrier()
                    self.phase_inproj(l, actT, I, A2)
                sub(p_in)
                if stop == 'inproj':
                    break
                self.phase_rwkv(l, I, O_)
                self.S.barrier()
                if stop == 'rwkv':
                    break
                self.phase_hgrn(l, I, O_, side=self.lru_gen(l, I, O_))
                self.S.barrier()
                if stop == 'mix':
                    break

                def p_out(A2):
                    actT = A2('actT_b%d' % l, [128, 16, NT], BF16)
                    self.phase_outproj(l, actT, I, A2)
                sub(p_out)

                def p_up(A2):
                    actT = A2('actT_c%d' % l, [128, 16, NT], BF16)
                    self.phase_norm(l, actT, P_NF, A2)
                    self.S.barrier()
                    self.phase_ffn_up(l, actT, I, O_, A2)
                sub(p_up)
                if stop == 'ffnup':
                    break
                sub(lambda A2: self.phase_ffn_down(l, I, A2))
            if stop is None:
                self.phase_final(O_['y'])
            self.S.barrier()
            for k, ap in dbg.items():
                if k == 'yT' and stop == 'rwkv':
                    self.dma(ap[0:768], self.yT[0:768], [], [])
                else:
                    self.dma(ap, getattr(self, k), [], [])
            self.S.finalize()
        return nc

    def phase_transpose_in(self, x_in):
        nc = self.nc
        with contextlib.ExitStack() as es:
            A = lambda name, shape, dt=F32: es.enter_context(nc.sbuf_tensor(name, list(shape), dt))
            xt = [A('ti_x%d' % i, [128, D]) for i in range(3)]
            xo = [A('ti_o%d' % i, [128, 16, 512]) for i in range(2)]
            k = 0
            for gi, (g0, gn) in enumerate(self.tg):
                ob = gi % 2
                for j in range(gn // 128):
                    b = k % 3
                    k += 1
                    t0 = g0 + j * 128
                    self.dma(xt[b][:], x_in[t0:t0 + 128, :], [], ['ti_x%d' % b])
                    for q in range(4):
                        ps, pn = self.psum()
                        for jj in range(4):
                            kc = q * 4 + jj
                            self.TR(ps[:, jj * 128:(jj + 1) * 128], xt[b][:, kc * 128:(kc + 1) * 128], ['ti_x%d' % b], [pn])
                        self.CP('act' if q % 2 == 0 else 'dve', xo[ob][:, q * 4:(q + 1) * 4, j * 128:(j + 1) * 128], ps[:].rearrange('p (a t) -> p a t', a=4), [pn], ['ti_o%d' % ob])
                self.dma(self.xT[:, g0:g0 + gn].rearrange('(c p) t -> p c t', p=128), xo[ob][:, :, 0:gn], ['ti_o%d' % ob], ['xT'])

    def phase_norm(self, l, actT, poff, A_unused):
        with contextlib.ExitStack() as es:
            self._phase_norm(l, actT, poff, lambda name, shape, dt=F32: es.enter_context(self.nc.sbuf_tensor('%s_%d' % (name, poff), list(shape), dt)))

    def _phase_norm(self, l, actT, poff, A):
        xb = [A('nm_x%d_%d' % (l, i), [128, 16, 512]) for i in range(2)]
        sqs = [A('nm_sq%d_%d' % (l, i), [128, 16, 512], BF16) for i in range(2)]
        sds = [A('nm_sd%d_%d' % (l, i), [128, 512]) for i in range(2)]
        rss = [A('nm_rs%d_%d' % (l, i), [128, 512]) for i in range(2)]
        for gi, (t0, tn) in enumerate(self.tg):
            b = gi % 2
            sq, sd, rs = sqs[b], sds[b], rss[b]
            xn = 'nm_x%d' % b
            self.dma(xb[b][:, :, 0:tn], self.xT[:, t0:t0 + tn].rearrange('(c p) t -> p c t', p=128), ['xT'], [xn])
            self.E('act', lambda e, b=b, tn=tn, sq=sq: e.activation(sq[:, :, 0:tn], xb[b][:, :, 0:tn], AF.Square), [xn], ['nm_sq%d' % b])
            ps, pn = self.psum()
            for kc in range(16):
                self.E('pe', lambda e, ps=ps, kc=kc, tn=tn, sq=sq: e.matmul(ps[:, 0:tn], self.ones_bf[:], sq[:, kc, 0:tn], start=(kc == 0), stop=(kc == 15)),
                       ['nm_sq%d' % b, 'ones_bf'], [pn])
            self.E('act', lambda e, ps=ps, tn=tn, sd=sd: e.activation(sd[:, 0:tn], ps[:, 0:tn], AF.Sqrt, bias=EPS, scale=1.0 / D), [pn], ['nm_sd%d' % b])
            self.E('dve', lambda e, tn=tn, rs=rs, sd=sd: e.reciprocal(rs[:, 0:tn], sd[:, 0:tn]), ['nm_sd%d' % b], ['nm_rs%d' % b])
            for kc in range(16):
                self.E('dve', lambda e, b=b, kc=kc, t0=t0, tn=tn, rs=rs: e.scalar_tensor_tensor(
                    actT[:, kc, t0:t0 + tn], xb[b][:, kc, 0:tn], self.prm[:, l, poff + kc:poff + kc + 1], rs[:, 0:tn], ALU.mult, ALU.mult),
                    [xn, 'nm_rs%d' % b, 'prm'], ['actT'])

    def TT(self, eng, out, in0, in1, op, r, w):
        self.E(eng, lambda e: e.tensor_tensor(out, in0, in1, op), r, w)

    def TSC(self, eng, out, in0, s1, s2, op0, op1, r, w):
        if op1 is None:
            self.E(eng, lambda e: e.tensor_scalar(out, in0, s1, None, op0), r, w)
        else:
            self.E(eng, lambda e: e.tensor_scalar(out, in0, s1, s2, op0, op1), r, w)

    def STT(self, out, in0, sc, in1, op0, op1, r, w):
        self.E('dve', lambda e: e.scalar_tensor_tensor(out, in0, sc, in1, op0, op1), r, w)

    def ACT(self, out, in_, func, r, w, bias=None, scale=None, accum=None):
        kw = {}
        if bias is not None:
            kw['bias'] = bias
        if scale is not None:
            kw['scale'] = scale
        if accum is not None:
            kw['accum_out'] = accum
        self.E('act', lambda e: e.activation(out, in_, func, **kw), r, w)

    def CP(self, eng, out, in_, r, w):
        if eng == 'act':
            self.ACT(out, in_, AF.Copy, r, w)
        else:
            self.E(eng, lambda e: e.tensor_copy(out, in_), r, w)

    def MM(self, out, lhsT, rhs, start, stop, r, w, sgc=False):
        self.E('pe', lambda e: e.matmul(out, lhsT, rhs, start=start, stop=stop, skip_group_check=sgc), r, w)

    def TR(self, out, in_, r, w, n=128):
        self.E('pe', lambda e: e.transpose(out, in_, self.ident[0:n, 0:n]), list(r) + ['ident'], w)

    def RECIP(self, out, in_, r, w):
        self.E('dve', lambda e: e.reciprocal(out, in_), r, w)

    def MEMSET(self, eng, ap, val, w):
        self.E(eng, lambda e: e.memset(ap, val), [], w)

    def SCAN(self, out, d0, d1, init, r, w):
        self.E('dve', lambda e: e.tensor_tensor_scan(out, d0, d1, init, ALU.mult, ALU.add), r, w)

    def RED(self, out, in_, r, w):
        self.E('dve', lambda e: e.tensor_reduce(out, in_, AX.X, ALU.add), r, w)

    def load_params(self, I):
        L = self.depth
        prm = self.prm

        def ld(idx, n, ap_fn):
            for l in range(L):
                self.dma(prm[:, l, idx:idx + n], ap_fn(l), [], ['prm'], slow=True)

        cp = lambda name: (lambda l: I[name][l].rearrange('(c p) -> p c', p=128))
        ld(P_NM, 16, cp('norm_mix'))
        ld(P_NF, 16, cp('norm_ffn'))
        ld(P_MU, 20, cp('rwkv_mu'))
        ld(P_W0, 6, cp('rwkv_w0'))
        ld(P_A0, 6, cp('rwkv_a0'))
        ld(P_KK, 6, cp('rwkv_k_k'))
        ld(P_KA, 6, cp('rwkv_k_a'))
        ld(P_RK, 6, lambda l: I['rwkv_r_k'][l].rearrange('(c h2) k -> (h2 k) c', h2=2))
        ld(P_LNW, 6, cp('rwkv_ln_w'))
        ld(P_LNB, 6, cp('rwkv_ln_b'))
        ld(P_LB, 6, cp('hgrn_lb_logits'))
        ld(P_HNW, 6, cp('hgrn_norm_w'))
        for j in range(4):
            ld(P_CW + 4 * j, 4, lambda l, j=j: I['rglru_conv_w'][l, j].rearrange('(c p) -> p c', p=128))
        ld(P_CB, 4, cp('rglru_conv_b'))
        ld(P_BA, 4, cp('rglru_ba'))
        ld(P_BX, 4, cp('rglru_bx'))
        ld(P_LAM, 4, cp('rglru_lambda'))
        for l in range(L):
            for j in range(3):
                self.dma(self.fprm[:, l, j * 44:(j + 1) * 44], I['ffn_conv_w'][l, j].rearrange('(c p) -> p c', p=128), [], ['fprm'], slow=True)
            self.dma(self.fprm[:, l, 132:176], I['ffn_conv_b'][l].rearrange('(c p) -> p c', p=128), [], ['fprm'], slow=True)
        self.dma(self.prmf[:, 0:16], I['norm_final'].rearrange('(c p) -> p c', p=128), [], ['prmf'], slow=True)
        for l in range(L):
            self.TSC('dve', prm[:, l, P_OMKA:P_OMKA + 6], prm[:, l, P_KA:P_KA + 6], -1.0, 1.0, ALU.mult, ALU.add, ['prm'], ['prm'])
        ex = self.ptmp
        for l in range(L):
            self.ACT(ex[:, l, 0:6], prm[:, l, P_LB:P_LB + 6], AF.Exp, ['prm'], ['ptmp'])
        self.CP('dve', ex[:, L, 0:6], ex[:, 0, 0:6], ['ptmp'], ['ptmp'])
        for l in range(1, L):
            self.TT('dve', ex[:, L, 0:6], ex[:, L, 0:6], ex[:, l, 0:6], ALU.add, ['ptmp'], ['ptmp'])
        self.RECIP(ex[:, L, 0:6], ex[:, L, 0:6], ['ptmp'], ['ptmp'])
        for l in range(L):
            self.TT('dve', ex[:, l, 0:6], ex[:, l, 0:6], ex[:, L, 0:6], ALU.mult, ['ptmp'], ['ptmp'])
        self.MEMSET('dve', prm[:, 0, P_LB:P_LB + 6], 0.0, ['prm'])
        for l in range(1, L):
            self.TT('dve', prm[:, l, P_LB:P_LB + 6], prm[:, l - 1, P_LB:P_LB + 6], ex[:, l, 0:6], ALU.add, ['prm', 'ptmp'], ['prm'])
        for l in range(L):
            self.TSC('dve', prm[:, l, P_OMLB:P_OMLB + 6], prm[:, l, P_LB:P_LB + 6], -1.0, 1.0, ALU.mult, ALU.add, ['prm'], ['prm'])
            self.ACT(ex[:, l, 8:12], prm[:, l, P_LAM:P_LAM + 4], AF.Exp, ['prm'], ['ptmp'], scale=-1.0)
            self.ACT(ex[:, l, 8:12], ex[:, l, 8:12], AF.Ln, ['ptmp'], ['ptmp'], bias=1.0)
            self.TSC('dve', prm[:, l, P_C8:P_C8 + 4], ex[:, l, 8:12], -8.0, None, ALU.mult, None, ['ptmp'], ['prm'])
            self.TSC('dve', prm[:, l, P_2C8:P_2C8 + 4], ex[:, l, 8:12], -16.0, None, ALU.mult, None, ['ptmp'], ['prm'])
    def phase_rwkv(self, l, I, O_):
        nc = self.nc
        TP, NT = self.TP, self.NT
        prm = self.prm
        C0 = 0.6065306597126334
        with contextlib.ExitStack() as es:
            A = lambda name, shape, dt=F32: es.enter_context(nc.sbuf_tensor('rw%d_%s' % (l, name), list(shape), dt))
            w2a2 = A('w2a2', [128, 768])
            g2 = A('g2', [128, 768])
            self.dma(w2a2[0:64, :], I['rwkv_w2'][l], [], ['w2a2'])
            self.dma(w2a2[64:128, :], I['rwkv_a2'][l], [], ['w2a2'])
            self.dma(g2[:], I['rwkv_g2'][l], [], ['g2'])
            blk = A('blk', [128, 128])
            self.dma(blk[:], self.cst['blkones'], [], ['blk'])
            masks = {}
            for mk in ('1', '16'):
                mbk = A('mbk' + mk, [128, 512])
                ml4 = A('ml4' + mk, [128, 512])
                rst = A('rst' + mk, [128, 128])
                for j in range(2):
                    self.dma(mbk[:, j * 256:(j + 1) * 256], self.cst['up_si' + mk], [], ['mbk' + mk])
                for j in range(4):
                    self.dma(ml4[:, j * 128:(j + 1) * 128], self.cst['low_s' + mk], [], ['ml4' + mk])
                self.dma(rst[:], self.cst['rst' + mk], [], ['rst' + mk])
                masks[mk] = (mbk, ml4, rst)
            cm16 = A('cm16', [128, 16, 128])
            rm16 = A('rm16', [128, 16])
            self.dma(cm16[:], self.cst['cm16'].rearrange('p (s t) -> p s t', s=16), [], ['cm16'])
            self.dma(rm16[:], self.cst['rm16'], [], ['rm16'])

            pbuf = A('pbuf', [128, 20, 144])
            xs = A('xs', [128, 20, 128])
            dd = A('dd', [128, 20, 128])
            tx = A('tx', [128, 128])
            sgx = A('sgx', [128, 128])
            sgw = A('sgw', [128, 6, 128])
            aa = A('aa', [128, 6, 128])
            gT = A('gT', [128, 6, 128])
            kk = A('kk', [128, 6, 128])
            t1 = A('t1', [128, 6, 128])
            t2 = A('t2', [128, 6, 128])
            kf = A('kf', [128, 6, 128])
            nb = A('nb', [128, 6, 128])
            cs = A('cs', [128, 6, 128])
            Gm = A('Gm', [128, 6, 128])
            Gi = A('Gi', [128, 6, 128])
            Gp = A('Gp', [128, 6, 128])
            AR = A('AR', [128, 6, 256])
            KT = A('KT', [128, 6, 128])
            BT = A('BT', [128, 6, 128])
            Vtm = A('Vtm', [128, 768])
            Btm = A('Btm', [128, 768])
            Ktm = A('Ktm', [128, 768])
            M4 = A('M4', [128, 12, 384])
            Pm = [A('Pm%d' % i, [128, 12, 128], NDT) for i in range(2)]
            PTm = [A('PTm%d' % i, [128, 12, 128], NDT) for i in range(2)]
            X = [A('X%d' % i, [128, 12, 128], NDT) for i in range(2)]
            Osb = A('Osb', [128, 12, 64])
            st = A('st', [128, 64])
            bst = A('bst', [128, 12, 6])
            ybf = A('ybf', [128, 6, 128], BF16)
            Hc = A('Hc', [128, 1, 6, 64])
            Hin = A('Hin', [128, 16, 6, 64])
            sin = [A('sin0', [64, 6, 2, 64])] * 2
            sout = sin
            htmp = A('htmp', [128, 64])

            self.MEMSET('dve', Hc[:], 0.0, ['Hc'])
            tiles = [('p', ti) for ti in range(TP // 128)] + [('s', 0)]
            for kind, ti in tiles:
                sample = kind == 's'
                t0 = TP if sample else ti * 128
                mk = '16' if sample else '1'
                nsub = 16 if sample else 1
                mbk, ml4, rst = masks[mk]
                rmk = ['mbk' + mk, 'ml4' + mk, 'rst' + mk]
                if sample:
                    pv = pbuf[:, :, 0:144].rearrange('p c (s t) -> p c s t', t=9)
                    self.dma(dd[:], self.projT[0:RW_COLS, t0:t0 + 128].rearrange('(c p) t -> p c t', p=128), ['projT'], ['dd'])
                    self.CP('dve', pv[:, :, :, 1:9], dd[:].rearrange('p c (s t) -> p c s t', t=8), ['dd'], ['pbuf'])
                    for g in range(5):
                        a0 = 2 * (g % 2)
                        stg = M4[0:16, a0:a0 + 2, :].rearrange('p a b -> p (a b)')[:, 0:512]
                        sres = ['M4_%d' % a0, 'M4_%d' % (a0 + 1)]
                        self.dma(stg, I['state_rwkv_shift'][l, :, g * 512:(g + 1) * 512], [], sres)
                        ps, pn = self.psum()
                        for j in range(4):
                            self.TR(ps[:, j * 16:(j + 1) * 16], stg[:, j * 128:(j + 1) * 128], sres, [pn], n=16)
                        self.CP('dve', pv[:, g * 4:(g + 1) * 4, :, 0], ps[:, 0:64].rearrange('p (c s) -> p c s', s=16), [pn], ['pbuf'])
                    prev = lambda c: pv[:, c, :, 0:8]
                    cur = lambda c: pv[:, c, :, 1:9]
                    v3 = lambda ap: ap.rearrange('p (s t) -> p s t', t=8)
                else:
                    if ti == 0:
                        self.MEMSET('dve', pbuf[:, :, 0:1], 0.0, ['pbuf'])
                        self.dma(pbuf[:, :, 1:129], self.projT[0:RW_COLS, t0:t0 + 128].rearrange('(c p) t -> p c t', p=128), ['projT'], ['pbuf'])
                    else:
                        self.dma(pbuf[:, :, 0:129], self.projT[0:RW_COLS, t0 - 1:t0 + 128].rearrange('(c p) t -> p c t', p=128), ['projT'], ['pbuf'])
                    prev = lambda c: pbuf[:, c, 0:128]
                    cur = lambda c: pbuf[:, c, 1:129]
                    v3 = lambda ap: ap
                if sample:
                    pall, call = pv[:, :, :, 0:8], pv[:, :, :, 1:9]
                    va = lambda t: t[:].rearrange('p c (s t) -> p c s t', t=8)
                    mub = prm[:, l, P_MU:P_MU + 20].rearrange('p (c a b) -> p c a b', a=1, b=1).to_broadcast([128, 20, 16, 8])
                else:
                    pall, call = pbuf[:, :, 0:128], pbuf[:, :, 1:129]
                    va = lambda t: t[:]
                    mub = prm[:, l, P_MU:P_MU + 20].rearrange('p (c a) -> p c a', a=1).to_broadcast([128, 20, 128])
                self.TT('dve', va(dd), pall, call, ALU.subtract, ['pbuf'], ['dd'])
                self.TT('dve', va(dd), va(dd), mub, ALU.mult, ['dd', 'prm'], ['dd'])
                self.TT('dve', va(xs), va(dd), call, ALU.add, ['dd', 'pbuf'], ['xs'])
                self.ACT(tx[0:64, :], xs[0:64, 18, :], AF.Tanh, ['xs'], ['tx'])
                self.ACT(sgx[:], xs[:, 19, :], AF.Sigmoid, ['xs'], ['sgx'])
                for (rows, rhs, rres, out, bidx) in ((slice(0, 64), tx[0:64, :], 'tx', sgw, P_W0), (slice(64, 128), xs[64:128, 18, :], 'xs', aa, P_A0)):
                    for half in range(2):
                        ps, pn = self.psum()
                        cs_ = range(0, 4) if half == 0 else range(4, 6)
                        for c in cs_:
                            self.MM(ps[:, (c % 4) * 128:(c % 4 + 1) * 128], w2a2[rows, c * 128:(c + 1) * 128], rhs, True, True, ['w2a2', rres], [pn])
                        for c in cs_:
                            self.ACT(out[:, c, :], ps[:, (c % 4) * 128:(c % 4 + 1) * 128], AF.Sigmoid, [pn, 'prm'], ['sgw' if out is sgw else 'aa'], bias=prm[:, l, bidx + c:bidx + c + 1])
                for half in range(2):
                    ps, pn = self.psum()
                    cs_ = range(0, 4) if half == 0 else range(4, 6)
                    for c in cs_:
                        self.MM(ps[:, (c % 4) * 128:(c % 4 + 1) * 128], g2[:, c * 128:(c + 1) * 128], sgx[:], True, True, ['g2', 'sgx'], [pn])
                    n = len(cs_)
                    self.CP('act', gT[:, cs_[0]:cs_[0] + n, :], ps[:, 0:n * 128].rearrange('p (c t) -> p c t', t=128), [pn], ['gT'])
                SGW, AA = 'sgw', 'aa'
                for c in range(6):
                    self.ACT(kk[:, c, :], xs[:, 6 + c, :], AF.Copy, ['xs', 'prm'], ['kk'], scale=prm[:, l, P_KK + c:P_KK + c + 1])
                self.TT('dve', t1[:], kk[:], kk[:], ALU.mult, ['kk'], ['t1'])
                for half in range(2):
                    ps, pn = self.psum()
                    cs_ = range(0, 4) if half == 0 else range(4, 6)
                    for c in cs_:
                        self.MM(ps[:, (c % 4) * 128:(c % 4 + 1) * 128], blk[:], t1[:, c, :], True, True, ['blk', 't1'], [pn])
                    n = len(cs_)
                    self.TSC('dve', t2[:, cs_[0]:cs_[0] + n, :], ps[:, 0:n * 128].rearrange('p (c t) -> p c t', t=128), 1e-24, None, ALU.max, None, [pn], ['t2'])
                self.ACT(t2[:], t2[:], AF.Sqrt, ['t2'], ['t2'])
                self.RECIP(t2[:], t2[:], ['t2'], ['t2'])
                self.TT('dve', kk[:], kk[:], t2[:], ALU.mult, ['kk', 't2'], ['kk'])
                for c in range(6):
                    self.TSC('dve', t1[:, c, :], aa[:, c, :], prm[:, l, P_KA + c:P_KA + c + 1], prm[:, l, P_OMKA + c:P_OMKA + c + 1], ALU.mult, ALU.add, [AA, 'prm', 't1'], ['t1'])
                self.TT('dve', kf[:], xs[:, 6:12, :], t1[:], ALU.mult, ['xs', 't1'], ['kf'])
                self.TT('dve', nb[:], kk[:], aa[:], ALU.mult, ['kk', AA], ['nb'])
                for c in range(6):
                    self.SCAN(cs[:, c, :], rst[:], sgw[:, c, :], 0.0, [SGW, 'rst' + mk], ['cs'])
                self.ACT(Gm[:], cs[:], AF.Exp, ['cs'], ['Gm'], scale=-C0)
                self.ACT(Gi[:], cs[:], AF.Exp, ['cs'], ['Gi'], scale=C0)
                self.TT('dve', t2[:], cs[:], sgw[:], ALU.subtract, ['cs', SGW], ['t2'])
                self.ACT(Gp[:], t2[:], AF.Exp, ['t2'], ['Gp'], scale=-C0)
                self.STT(AR[:, :, 0:128], kk[:], -1.0, Gp[:], ALU.mult, ALU.mult, ['kk', 'Gp'], ['AR0'])
                self.TT('pool', AR[:, :, 128:256], xs[:, 0:6, :], Gm[:], ALU.mult, ['xs', 'Gm'], ['AR1'])
                self.TT('dve', KT[:], kf[:], Gi[:], ALU.mult, ['kf', 'Gi'], ['KT'])
                self.TT('pool', BT[:], nb[:], Gi[:], ALU.mult, ['nb', 'Gi'], ['BT'])
                if self.rw_stop is not None and self.rw_stop < 10:
                    continue
                for (src_fn, sres, dst, dres) in ((lambda c: xs[:, 12 + c, :], 'xs', Vtm, 'Vtm'), (lambda c: AR[:, c, 0:128], 'AR0', cs[:].rearrange('p c t -> p (c t)'), 'cs'),
                                                  (lambda c: BT[:, c, :], 'BT', Btm, 'Btm'), (lambda c: KT[:, c, :], 'KT', Ktm, 'Ktm')):
                    for half in range(2):
                        ps, pn = self.psum()
                        cs_ = range(0, 4) if half == 0 else range(4, 6)
                        for c in cs_:
                            self.TR(ps[:, (c % 4) * 128:(c % 4 + 1) * 128], src_fn(c), [sres], [pn])
                        n = len(cs_)
                        self.CP('act' if half == 0 else 'dve', dst[:, cs_[0] * 128:(cs_[0] + n) * 128], ps[:, 0:n * 128], [pn], ['cs'] if dres == 'cs' else [dres + str(half)])
                VT_R = ['Vtm0', 'Vtm1']
                if self.rw_stop is not None and self.rw_stop < 11:
                    continue
                for h in range(12):
                    c, pb = h // 2, 64 * (h % 2)
                    ps, pn = self.psum()
                    self.MM(ps[:, 0:256], BT[pb:pb + 64, c, :], AR[pb:pb + 64, c, :], True, True, ['BT', 'AR0', 'AR1'], [pn])
                    self.MM(ps[:, 256:512], KT[pb:pb + 64, c, :], AR[pb:pb + 64, c, :], True, True, ['KT', 'AR0', 'AR1'], [pn])
                    self.TT('dve', PTm[0][:, h, :], ps[:, 0:128], mbk[:, 0:128], ALU.mult, [pn, 'mbk' + mk], ['PTm0_%d' % (h // 4)])
                    self.TT('dve', M4[:, h, :], ps[:, 128:512], mbk[:, 128:512], ALU.mult, [pn, 'mbk' + mk], ['M4_%d' % h])
                Pv = Pm[0][:].rearrange('p (c a) t -> p c a t', a=2)
                for h2 in range(2):
                    pb = 64 * h2
                    for (c0, c1) in ((0, 4), (4, 6)):
                        ps, pn = self.psum()
                        for c in range(c0, c1):
                            self.MM(ps[:, (c - c0) * 128:(c - c0 + 1) * 128], AR[pb:pb + 64, c, 0:128], BT[pb:pb + 64, c, :], True, True, ['AR0', 'BT'], [pn])
                        n = c1 - c0
                        self.TT('dve', Pv[:, c0:c1, h2, :], ps[:, 0:n * 128].rearrange('p (h t) -> p h t', t=128), ml4[:, 0:n * 128].rearrange('p (h t) -> p h t', t=128), ALU.mult,
                                [pn, 'ml4' + mk], sorted(set('Pm0_%d' % ((2 * c + h2) // 4) for c in range(c0, c1))))
                if self.rw_stop is not None and self.rw_stop < 12:
                    continue
                self.CP('act', X[0][:, :, 0:64], cs[:].rearrange('p c (a k) -> p (c a) k', k=64), ['cs'], ['X0_0', 'X0_1', 'X0_2'])
                for half in range(2):
                    ps, pn = self.psum()
                    hs = range(0, 8) if half == 0 else range(8, 12)
                    for h in hs:
                        self.MM(ps[:, (h % 8) * 64:(h % 8 + 1) * 64], M4[:, h, 128:256], Vtm[:, h * 64:(h + 1) * 64], True, True, ['M4_%d' % h] + VT_R, [pn])
                    n = len(hs)
                    self.CP('act', X[0][:, hs[0]:hs[0] + n, 64:128], ps[:, 0:n * 64].rearrange('p (h v) -> p h v', v=64), [pn], ['X0_0', 'X0_1'] if half == 0 else ['X0_2'])
                xres = {0: ['X0_0', 'X0_1', 'X0_2'], 1: ['X1_0', 'X1_1', 'X1_2']}
                if self.rw_stop is not None and self.rw_stop < 13:
                    continue
                NST = 3 if sample else 7
                for s_ in range(NST):
                    a_, b_ = s_ % 2, (s_ + 1) % 2
                    pres = ['Pm%d_%d' % (a_, q) for q in range(3)]
                    ptres = ['PTm%d_%d' % (a_, q) for q in range(3)]
                    for q in range(3):
                        ps, pn = self.psum()
                        for j in range(4):
                            h = q * 4 + j
                            self.MM(ps[:, j * 128:(j + 1) * 128], PTm[a_][:, h, :], X[a_][:, h, :], True, True, ptres + xres[a_], [pn])
                        self.TT('dve', X[b_][:, q * 4:(q + 1) * 4, :], ps[:].rearrange('p (h t) -> p h t', t=128), X[a_][:, q * 4:(q + 1) * 4, :], ALU.add,
                                [pn] + xres[a_], ['X%d_%d' % (b_, q)])
                    if s_ < NST - 1:
                        for q in range(3):
                            ps, pn = self.psum()
                            ps2, pn2 = self.psum()
                            for j in range(4):
                                h = q * 4 + j
                                self.MM(ps[:, j * 128:(j + 1) * 128], PTm[a_][:, h, :], Pm[a_][:, h, :], True, True, pres + ptres, [pn])
                                self.MM(ps2[:, j * 128:(j + 1) * 128], Pm[a_][:, h, :], PTm[a_][:, h, :], True, True, pres + ptres, [pn2])
                            self.CP('act', Pm[b_][:, q * 4:(q + 1) * 4, :], ps[:].rearrange('p (h t) -> p h t', t=128), [pn], ['Pm%d_%d' % (b_, q)])
                            self.CP('act', PTm[b_][:, q * 4:(q + 1) * 4, :], ps2[:].rearrange('p (h t) -> p h t', t=128), [pn2], ['PTm%d_%d' % (b_, q)])
                fin = NST % 2
                Xf = X[fin]
                xfr = xres[fin]
                if self.rw_stop is not None and self.rw_stop < 14:
                    continue
                self.CP('act', Gi[:].rearrange('p c (a k) -> p (c a) k', k=64), Xf[:, :, 0:64], xfr, ['Gi'])
                for half in range(2):
                    ps, pn = self.psum()
                    cs_ = range(0, 4) if half == 0 else range(4, 6)
                    for c in cs_:
                        self.TR(ps[:, (c % 4) * 128:(c % 4 + 1) * 128], Gi[:, c, :], ['Gi'], [pn])
                    n = len(cs_)
                    self.CP('act', Gp[:, cs_[0]:cs_[0] + n, :], ps[:, 0:n * 128].rearrange('p (c t) -> p c t', t=128), [pn], ['Gp'])
                gtr = ['Gp']
                if sample:
                    for s in range(16):
                        b = s % 2
                        self.dma(sin[b][:], I['state_rwkv'][l, s].rearrange('(c h2) v k -> v c h2 k', h2=2), [], ['sin0'])
                        ps, pn = self.psum()
                        for c in range(6):
                            self.TR(ps[:, c * 64:(c + 1) * 64], sin[b][:, c, :, :], ['sin0'], [pn], n=64)
                        self.CP('act' if s % 2 == 0 else 'dve', Hin[:, s, :, :], ps[:, 0:384].rearrange('p (c v) -> p c v', v=64), [pn], ['Hin%d' % s])
                    Hi = lambda sb: Hin[:, sb, :, :]
                    Ho = Hi
                    hir = lambda sb: 'Hin%d' % sb
                    hor = hir
                else:
                    Hi = lambda sb: Hc[:, 0, :, :]
                    Ho = Hi
                    hir = lambda sb: 'Hc'
                    hor = hir
                if self.rw_stop is not None and self.rw_stop < 15:
                    continue
                def inter(which):
                    psl = {0: self.psum(), 1: self.psum()}
                    for c in range(6):
                        if which == 'G':
                            src, sres = Gp[:, c, :], gtr
                        else:
                            src, sres = AR[:, c, 128:256], ['AR1']
                        if sample:
                            for sb in range(16):
                                self.TT('pool' if sb % 4 == 3 else 'dve', dd[:, sb, :], src, cm16[:, sb, :], ALU.mult, sres + ['cm16'], ['dd'])
                        for h2 in range(2):
                            pb = 64 * h2
                            ps, pn = psl[h2]
                            o = ps[:, c * 64:(c + 1) * 64]
                            for sb in range(nsub):
                                if sample:
                                    lh, lr = dd[pb:pb + 64, sb, :], ['dd']
                                else:
                                    lh, lr = src[pb:pb + 64, :], sres
                                self.MM(o, lh, Hi(sb)[pb:pb + 64, c, :], sb == 0, sb == nsub - 1, lr + [hir(sb)], [pn])
                    return psl
                Uv = sgw[:].rearrange('p c (a v) -> p c a v', a=2)
                Xv = Xf[:].rearrange('p (c a) t -> p c a t', a=2)
                psl = inter('G')
                for h2 in range(2):
                    ps, pn = psl[h2]
                    self.TT('dve', Uv[:, :, h2, :], ps[:, 0:384].rearrange('p (c v) -> p c v', v=64), Xv[:, :, h2, 64:128], ALU.add, [pn] + xfr, ['sgw'])
                ur = ['sgw']
                psl = inter('R')
                psO = [self.psum(), self.psum()]
                for h in range(12):
                    ps, pn = psO[h // 8]
                    o = ps[:, (h % 8) * 64:(h % 8 + 1) * 64]
                    self.MM(o, M4[:, h, 0:128], sgw[:].rearrange('p c t -> p (c t)')[:, h * 64:(h + 1) * 64], True, False, ['M4_%d' % h] + ur, [pn])
                    self.MM(o, M4[:, h, 256:384], Vtm[:, h * 64:(h + 1) * 64], False, True, ['M4_%d' % h] + VT_R, [pn])
                for half in range(2):
                    ps, pn = psO[half]
                    hs = range(0, 8) if half == 0 else range(8, 12)
                    n = len(hs)
                    self.CP('act', Osb[:, hs[0]:hs[0] + n, :], ps[:, 0:n * 64].rearrange('p (h v) -> p h v', v=64), [pn], ['Osb%d' % half])
                Ov = Osb[:].rearrange('p (c a) v -> p c a v', a=2)
                for h2 in range(2):
                    ps, pn = psl[h2]
                    self.TT('dve', Ov[:, :, h2, :], Ov[:, :, h2, :], ps[:, 0:384].rearrange('p (c v) -> p c v', v=64), ALU.add, [pn, 'Osb0', 'Osb1'], ['Osb0', 'Osb1'])
                osr = ['Osb0', 'Osb1']
                for h in range(12):
                    self.E('dve', (lambda e, h=h: e.bn_stats(bst[:, h, :], Osb[:, h, :])), osr, ['bst'])
                for h in range(12):
                    self.E('dve', (lambda e, h=h: e.bn_aggr(st[:, 2 * h:2 * h + 2], bst[:, h, :])), ['bst'], ['st'])
                stv = st[:, 0:24].rearrange('p (h a) -> p h a', a=2)
                self.ACT(st[:, 36:48], stv[:, :, 1], AF.Sqrt, ['st'], ['st'], bias=RW_EPS)
                self.RECIP(st[:, 36:48], st[:, 36:48], ['st'], ['st'])
                for h in range(12):
                    self.TSC('dve', t2[:, h // 2, (h % 2) * 64:(h % 2) * 64 + 64], Osb[:, h, :], st[:, 2 * h:2 * h + 1], st[:, 36 + h:37 + h], ALU.subtract, ALU.mult, osr + ['st'], ['t2'])
                for half in range(2):
                    ps, pn = self.psum()
                    cs_ = range(0, 4) if half == 0 else range(4, 6)
                    for c in cs_:
                        self.TR(ps[:, (c % 4) * 128:(c % 4 + 1) * 128], t2[:, c, :], ['t2'], [pn])
                    for c in cs_:
                        self.TSC('dve', cs[:, c, :], ps[:, (c % 4) * 128:(c % 4 + 1) * 128], prm[:, l, P_LNW + c:P_LNW + c + 1], prm[:, l, P_LNB + c:P_LNB + c + 1], ALU.mult, ALU.add, [pn, 'prm'], ['cs'])
                if self.rw_stop is not None and self.rw_stop < 17:
                    continue
                self.TT('pool', t1[:], xs[:, 0:6, :], kf[:], ALU.mult, ['xs', 'kf'], ['t1'])
                for c in range(6):
                    self.TSC('pool', t1[:, c, :], t1[:, c, :], prm[:, l, P_RK + c:P_RK + c + 1], None, ALU.mult, None, ['t1', 'prm'], ['t1'])
                for half in range(2):
                    ps, pn = self.psum()
                    cs_ = range(0, 4) if half == 0 else range(4, 6)
                    for c in cs_:
                        self.MM(ps[:, (c % 4) * 128:(c % 4 + 1) * 128], blk[:], t1[:, c, :], True, True, ['blk', 't1'], [pn])
                    n = len(cs_)
                    self.TT('dve', t2[:, cs_[0]:cs_[0] + n, :], ps[:, 0:n * 128].rearrange('p (c t) -> p c t', t=128), xs[:, 12 + cs_[0]:12 + cs_[0] + n, :], ALU.mult, [pn, 'xs'], ['t2'])
                self.TT('dve', cs[:], cs[:], t2[:], ALU.add, ['cs', 't2'], ['cs'])
                self.TT('dve', ybf[:], cs[:], gT[:], ALU.mult, ['cs', 'gT'], ['ybf'])
                self.dma(self.yT[0:768, t0:t0 + 128].rearrange('(c p) t -> p c t', p=128), ybf[:], ['ybf'], ['yT_rw'])
                if self.rw_stop is not None and self.rw_stop < 18:
                    continue
                for sb in range(nsub):
                    if sample:
                        HG = M4[:, sb % 2, :].rearrange('p (c v) -> p c v', v=64)
                        hgr = 'M4_%d' % (sb % 2)
                        gend = Gm[:, :, sb * 8 + 7:sb * 8 + 8].to_broadcast([128, 6, 64])
                        self.TT('dve', HG, Hin[:, sb, :, :], gend, ALU.mult, ['Hin%d' % sb, 'Gm'], [hgr])
                        self.TSC('dve', kk[:].rearrange('p c t -> p (c t)'), Btm[:], rm16[:, sb:sb + 1], None, ALU.mult, None, ['Btm0', 'Btm1', 'rm16'], ['kk'])
                        self.ACT(nb[:].rearrange('p c t -> p (c t)'), Ktm[:], AF.Copy, ['Ktm0', 'Ktm1', 'rm16'], ['nb'], scale=rm16[:, sb:sb + 1])
                        bl, kl, blr, klr = kk[:].rearrange('p c t -> p (c t)'), nb[:].rearrange('p c t -> p (c t)'), ['kk'], ['nb']
                        tend = sb * 8 + 7
                    else:
                        bl, kl, blr, klr = Btm[:], Ktm[:], ['Btm0', 'Btm1'], ['Ktm0', 'Ktm1']
                        tend = 127
                    for c in range(6):
                        ps, pn = self.psum()
                        self.MM(ps[:, 0:128], bl[:, c * 128:(c + 1) * 128], sgw[:].rearrange('p c t -> p (c t)')[:, c * 128:(c + 1) * 128], True, False, blr + ur, [pn])
                        self.MM(ps[:, 0:128], kl[:, c * 128:(c + 1) * 128], Vtm[:, c * 128:(c + 1) * 128], False, True, klr + VT_R, [pn])
                        for h2 in range(2):
                            pr = slice(64 * h2, 64 * h2 + 64)
                            if sample:
                                self.STT(Ho(sb)[pr, c, :], ps[pr, 64 * h2:64 * h2 + 64], Gm[pr, c, tend:tend + 1], HG[pr, c, :], ALU.mult, ALU.add, [pn, 'Gm', hgr], [hor(sb)])
                            else:
                                self.TT('dve', htmp[pr, :], ps[pr, 64 * h2:64 * h2 + 64], Hi(sb)[pr, c, :], ALU.add, [pn, hir(sb)], ['htmp'])
                                self.TSC('dve', Ho(sb)[pr, c, :], htmp[pr, :], Gm[pr, c, tend:tend + 1], None, ALU.mult, None, ['htmp', 'Gm'], [hor(sb)])
                if self.rw_stop is not None and self.rw_stop < 19:
                    continue
                last_prompt = (not sample) and ti == TP // 128 - 1
                if sample or last_prompt:
                    for sb in range(nsub):
                        b = sb % 2
                        ps, pn = self.psum()
                        ps2, pn2 = self.psum()
                        for c in range(4):
                            self.TR(ps[0:64, c * 128:(c + 1) * 128], Ho(sb)[:, c, :], [hor(sb)], [pn])
                        for c in range(4, 6):
                            self.TR(ps2[0:64, (c - 4) * 128:(c - 3) * 128], Ho(sb)[:, c, :], [hor(sb)], [pn2])
                        self.CP('act', sout[b][:, 0:4, :, :], ps[0:64, :].rearrange('p (c h k) -> p c h k', h=2, k=64), [pn], ['sin0'])
                        self.CP('dve', sout[b][:, 4:6, :, :], ps2[0:64, 0:256].rearrange('p (c h k) -> p c h k', h=2, k=64), [pn2], ['sin0'])
                        dst = (O_['s_rwkv'][l, sb] if sample else O_['p_rwkv'][l]).rearrange('(c h2) v k -> v c h2 k', h2=2)
                        self.dma(dst, sout[b][:], ['sin0'], [])
                    if sample:
                        for g in range(5):
                            a0 = 2 * (g % 2)
                            stg = M4[0:16, a0:a0 + 2, :].rearrange('p a b -> p (a b)')[:, 0:512]
                            sres = ['M4_%d' % a0, 'M4_%d' % (a0 + 1)]
                            ps, pn = self.psum()
                            for j in range(4):
                                self.TR(ps[0:16, j * 128:(j + 1) * 128], pv[:, g * 4 + j, :, 8], ['pbuf'], [pn])
                            self.CP('act', stg, ps[0:16, 0:512], [pn], sres)
                            self.dma(O_['s_rwkv_shift'][l, :, g * 512:(g + 1) * 512], stg, sres, [])
                    else:
                        self.dma(O_['p_rwkv_shift'][l].rearrange('(c p) -> p c', p=128), pbuf[:, :, 128], ['pbuf'], [], slow=True)
    def phase_hgrn(self, l, I, O_, side=None):
        nc = self.nc
        TP = self.TP
        prm = self.prm
        with contextlib.ExitStack() as es:
            A = lambda name, shape, dt=F32: es.enter_context(nc.sbuf_tensor('hg%d_%s' % (l, name), list(shape), dt))
            masks = {}
            for mk, ns in (('4', 4), ('16', 16)):
                up = A('up' + mk, [128, 128])
                rst = A('rst' + mk, [128, 128])
                cm = A('cm' + mk, [128, ns, 128])
                rm = A('rm' + mk, [128, ns])
                self.dma(up[:], self.cst['up_i' + mk], [], ['up' + mk])
                self.dma(rst[:], self.cst['rst' + mk], [], ['rst' + mk])
                self.dma(cm[:], self.cst['cm' + mk].rearrange('p (s t) -> p s t', s=ns), [], ['cm' + mk])
                self.dma(rm[:], self.cst['rm' + mk], [], ['rm' + mk])
                masks[mk] = (up, rst, cm, rm)
            pb = A('pb', [128, 24, 128])
            q = A('q', [128, 6, 128])
            fg = A('fg', [128, 6, 128])
            lf = A('lf', [128, 6, 128])
            kx = A('kx', [128, 6, 128])
            cs = A('cs', [128, 6, 128])
            Gm = A('Gm', [128, 6, 128])
            Gi = A('Gi', [128, 6, 128])
            QT = A('QT', [128, 6, 128])
            KT = A('KT', [128, 6, 128])
            sg = A('sg', [128, 6, 128])
            Vtm = A('Vtm', [128, 6, 128])
            Ktm = A('Ktm', [128, 6, 128])
            KM = A('KM', [128, 16, 128])
            QM = A('QM', [128, 16, 128])
            AT = A('AT', [128, 128])
            Hs = A('Hs', [128, 16, 128])
            Ho = A('Ho', [128, 16, 128])
            Hc = A('Hc', [128, 6, 128])
            htmp = A('htmp', [128, 128])
            Hs6 = A('Hs6', [128, 6, 5, 128])
            AT6 = A('AT6', [128, 6, 128])
            tmp6 = A('tmp6', [128, 6, 128])
            on6 = A('on6', [128, 6, 128])
            ybf6 = A('ybf6', [128, 6, 128], BF16)
            ss6 = A('ss6', [128, 24])
            junk = htmp
            ss = A('ss', [128, 4])
            on = A('on', [128, 128])
            ybf = A('ybf', [128, 128], BF16)
            self.MEMSET('dve', Hc[:], 0.0, ['Hc'])
            tiles = [('p', ti) for ti in range(TP // 128)] + [('s', 0)]
            for kind, ti in tiles:
                sample = kind == 's'
                t0 = TP if sample else ti * 128
                mk = '16' if sample else '4'
                nsub = 16 if sample else 4
                sub = 128 // nsub
                up, rst, cm, rm = masks[mk]
                self.dma(pb[:], self.projT[RW_COLS:RW_COLS + HG_COLS, t0:t0 + 128].rearrange('(c p) t -> p c t', p=128), ['projT'], ['pb'])
                self.ACT(q[:], pb[:, 0:6, :], AF.Silu, ['pb'], ['q'])
                self.ACT(sg[:], pb[:, 18:24, :], AF.Silu, ['pb'], ['sg'])
                self.ACT(fg[:], pb[:, 6:12, :], AF.Sigmoid, ['pb'], ['fg'])
                for h in range(6):
                    self.TSC('dve', fg[:, h, :], fg[:, h, :], prm[:, l, P_OMLB + h:P_OMLB + h + 1], prm[:, l, P_LB + h:P_LB + h + 1], ALU.mult, ALU.add, ['fg', 'prm'], ['fg'])
                self.ACT(lf[:], fg[:], AF.Ln, ['fg'], ['lf'])
                self.TSC('dve', kx[:], fg[:], -1.0, 1.0, ALU.mult, ALU.add, ['fg'], ['kx'])
                for h in range(6):
                    self.SCAN(cs[:, h, :], rst[:], lf[:, h, :], 0.0, ['lf', 'rst' + mk], ['cs'])
                self.ACT(Gm[:], cs[:], AF.Exp, ['cs'], ['Gm'])
                self.ACT(Gi[:], cs[:], AF.Exp, ['cs'], ['Gi'], scale=-1.0)
                self.TT('dve', QT[:], q[:], Gm[:], ALU.mult, ['q', 'Gm'], ['QT'])
                self.TT('dve', KT[:], kx[:], Gi[:], ALU.mult, ['kx', 'Gi'], ['KT'])
                for (src_fn, sres, dst, dres) in ((lambda h: pb[:, 12 + h, :], 'pb', Vtm, 'Vtm'), (lambda h: KT[:, h, :], 'KT', Ktm, 'Ktm')):
                    for half in range(2):
                        ps, pn = self.psum()
                        hs = range(0, 4) if half == 0 else range(4, 6)
                        for h in hs:
                            self.TR(ps[:, (h % 4) * 128:(h % 4 + 1) * 128], src_fn(h), [sres], [pn])
                        n = len(hs)
                        self.CP('act' if half == 0 else 'dve', dst[:, hs[0]:hs[0] + n, :], ps[:, 0:n * 128].rearrange('p (h t) -> p h t', t=128), [pn], [dres + str(half)])
                vr, kr = ['Vtm0', 'Vtm1'], ['Ktm0', 'Ktm1']
                if not sample:
                    for half in range(2):
                        ps, pn = self.psum()
                        hs = range(0, 4) if half == 0 else range(4, 6)
                        for h in hs:
                            self.MM(ps[:, (h % 4) * 128:(h % 4 + 1) * 128], KT[:, h, :], QT[:, h, :], True, True, ['KT', 'QT'], [pn])
                        n = len(hs)
                        self.TT('dve', AT6[:, hs[0]:hs[0] + n, :], ps[:, 0:n * 128].rearrange('p (h t) -> p h t', t=128),
                                up[:, None, :].to_broadcast([128, n, 128]) if False else up[:].rearrange('p (a t) -> p a t', a=1).to_broadcast([128, n, 128]), ALU.mult, [pn, 'up' + mk], ['AT6_%d' % half])
                    atr = ['AT6_0', 'AT6_1']
                    self.TSC('dve', KM[:, 0:6, :], Ktm[:], rm[:, 3:4], None, ALU.mult, None, kr + ['rm' + mk], ['KM3'])
                    self.TT('dve', QM[:, 0:6, :], QT[:], cm[:, 3:4, :].to_broadcast([128, 6, 128]), ALU.mult, ['QT', 'cm' + mk], ['QM3'])
                    self.CP('act', Hs6[:, :, 0, :], Hc[:], ['Hc'], ['Hs6_0'])
                    for sb in range(4):
                        pss = [self.psum(), self.psum()]
                        for h in range(6):
                            ps, pn = pss[h // 4]
                            o = ps[:, (h % 4) * 128:(h % 4 + 1) * 128]
                            if sb == 3:
                                self.MM(o, KM[:, h, :], Vtm[:, h, :], True, True, ['KM3'] + vr, [pn])
                            else:
                                pr = slice(sb * 32, (sb + 1) * 32)
                                self.MM(o, Ktm[pr, h, :], Vtm[pr, h, :], True, True, kr + vr, [pn])
                        for half in range(2):
                            ps, pn = pss[half]
                            hs = range(0, 4) if half == 0 else range(4, 6)
                            n = len(hs)
                            self.TT('dve', tmp6[:, hs[0]:hs[0] + n, :], ps[:, 0:n * 128].rearrange('p (h t) -> p h t', t=128), Hs6[:, hs[0]:hs[0] + n, sb, :], ALU.add,
                                    [pn, 'Hs6_%d' % sb], ['tmp6_%d' % half])
                        tend = sb * 32 + 31
                        self.TT('dve', Hs6[:, :, sb + 1, :], tmp6[:], Gm[:, :, tend:tend + 1].to_broadcast([128, 6, 128]), ALU.mult, ['tmp6_0', 'tmp6_1', 'Gm'], ['Hs6_%d' % (sb + 1)])
                    psO = [self.psum(), self.psum()]
                    for h in range(6):
                        ps, pn = psO[h // 4]
                        o = ps[:, (h % 4) * 128:(h % 4 + 1) * 128]
                        self.MM(o, QM[:, h, :], Hs6[:, h, 3, :], True, False, ['QM3', 'Hs6_3'], [pn], sgc=True)
                        for sb in range(3):
                            pr = slice(sb * 32, (sb + 1) * 32)
                            self.MM(ps[pr, (h % 4) * 128:(h % 4 + 1) * 128], QT[:, h, pr], Hs6[:, h, sb, :], True, False, ['QT', 'Hs6_%d' % sb], [pn], sgc=True)
                        self.MM(o, AT6[:, h, :], Vtm[:, h, :], False, True, atr + vr, [pn], sgc=True)
                    for h in range(6):
                        ps, pn = psO[h // 4]
                        self.ACT(junk[:], ps[:, (h % 4) * 128:(h % 4 + 1) * 128], AF.Square, [pn], ['htmp', 'ss6'], accum=ss6[:, h:h + 1])
                    self.ACT(ss6[:, 6:12], ss6[:, 0:6], AF.Sqrt, ['ss6'], ['ss6'], bias=EPS, scale=1.0 / 128)
                    self.RECIP(ss6[:, 12:18], ss6[:, 6:12], ['ss6'], ['ss6'])
                    for half in range(2):
                        ps, pn = psO[half]
                        hs = range(0, 4) if half == 0 else range(4, 6)
                        n = len(hs)
                        self.TT('dve', on6[:, hs[0]:hs[0] + n, :], ps[:, 0:n * 128].rearrange('p (h t) -> p h t', t=128),
                                ss6[:, 12 + hs[0]:12 + hs[0] + n].rearrange('p (h a) -> p h a', a=1).to_broadcast([128, n, 128]), ALU.mult, [pn, 'ss6'], ['on6_%d' % half])
                    for half in range(2):
                        ps2, pn2 = self.psum()
                        hs = range(0, 4) if half == 0 else range(4, 6)
                        for h in hs:
                            self.TR(ps2[:, (h % 4) * 128:(h % 4 + 1) * 128], on6[:, h, :], ['on6_0', 'on6_1'], [pn2])
                        for h in hs:
                            self.STT(ybf6[:, h, :], ps2[:, (h % 4) * 128:(h % 4 + 1) * 128], prm[:, l, P_HNW + h:P_HNW + h + 1], sg[:, h, :], ALU.mult, ALU.mult, [pn2, 'prm', 'sg'], ['ybf6'])
                    self.dma(self.yT[768:1536, t0:t0 + 128].rearrange('(h p) t -> p h t', p=128), ybf6[:], ['ybf6'], ['yT_hg'])
                    self.CP('act', Hc[:], Hs6[:, :, 4, :], ['Hs6_4'], ['Hc'])
                    if ti == TP // 128 - 1:
                        self.dma(O_['p_hgrn'][l].rearrange('h k v -> k h v'), Hs6[:, :, 4, :], ['Hs6_4'], [])
                    if side is not None and (ti % 2 == 1 or ti == 0):
                        next(side, None)
                    continue
                for h in range(6):
                    ps, pn = self.psum()
                    self.MM(ps[:, 0:128], KT[:, h, :], QT[:, h, :], True, True, ['KT', 'QT'], [pn])
                    self.TT('dve', AT[:], ps[:, 0:128], up[:], ALU.mult, [pn, 'up' + mk], ['AT'])
                    if sample:
                        self.dma(Hs[:, 0:16, :], I['state_hgrn'][l, :, h, :, :].rearrange('s k v -> k s v'), [], ['Hs%d' % sb for sb in range(16)])
                        hin = lambda sb: (Hs[:, sb, :], 'Hs%d' % sb)
                        hout = lambda sb: (Ho[:, sb, :], 'Ho%d' % sb)
                    else:
                        self.CP('pool', Hs[:, 0, :], Hc[:, h, :], ['Hc'], ['Hs0'])
                        hin = lambda sb: (Hs[:, sb, :], 'Hs%d' % sb)
                        hout = lambda sb: (Hs[:, sb + 1, :], 'Hs%d' % (sb + 1))
                    msk = (lambda sb: True) if sample else (lambda sb: sb == 3)
                    for sb in range(nsub):
                        if msk(sb):
                            self.TSC('dve', KM[:, sb, :], Ktm[:, h, :], rm[:, sb:sb + 1], None, ALU.mult, None, kr + ['rm' + mk], ['KM%d' % sb])
                            self.TT('pool' if (sample and sb % 4 == 3) else 'dve', QM[:, sb, :], QT[:, h, :], cm[:, sb, :], ALU.mult, ['QT', 'cm' + mk], ['QM%d' % sb])
                    for sb in range(nsub):
                        ps, pn = self.psum()
                        if msk(sb):
                            self.MM(ps[:, 0:128], KM[:, sb, :], Vtm[:, h, :], True, True, ['KM%d' % sb] + vr, [pn])
                        else:
                            pr = slice(sb * sub, (sb + 1) * sub)
                            self.MM(ps[:, 0:128], Ktm[pr, h, :], Vtm[pr, h, :], True, True, kr + vr, [pn])
                        hi, hir = hin(sb)
                        ho, hor = hout(sb)
                        tend = sb * sub + sub - 1
                        self.TT('dve', htmp[:], ps[:, 0:128], hi, ALU.add, [pn, hir], ['htmp'])
                        self.TSC('dve', ho, htmp[:], Gm[:, h, tend:tend + 1], None, ALU.mult, None, ['htmp', 'Gm'], [hor])
                    ps, pn = self.psum()
                    for sb in ([3, 0, 1, 2] if not sample else range(nsub)):
                        hi, hir = hin(sb)
                        if sample:
                            self.MM(ps[:, 0:128], QM[:, sb, :], hi, sb == 0, False, ['QM%d' % sb, hir], [pn])
                        elif sb == 3:
                            self.MM(ps[:, 0:128], QM[:, sb, :], hi, True, False, ['QM%d' % sb, hir], [pn], sgc=True)
                        else:
                            pr = slice(sb * sub, (sb + 1) * sub)
                            self.MM(ps[pr, 0:128], QT[:, h, pr], hi, True, False, ['QT', hir], [pn], sgc=True)
                    self.MM(ps[:, 0:128], AT[:], Vtm[:, h, :], False, True, ['AT'] + vr, [pn], sgc=not sample)
                    self.ACT(junk[:], ps[:, 0:128], AF.Square, [pn], ['htmp', 'ss'], accum=ss[:, 0:1])
                    self.ACT(ss[:, 1:2], ss[:, 0:1], AF.Sqrt, ['ss'], ['ss'], bias=EPS, scale=1.0 / 128)
                    self.RECIP(ss[:, 2:3], ss[:, 1:2], ['ss'], ['ss'])
                    self.TSC('dve', on[:], ps[:, 0:128], ss[:, 2:3], None, ALU.mult, None, [pn, 'ss'], ['on'])
                    ps2, pn2 = self.psum()
                    self.TR(ps2[:, 0:128], on[:], ['on'], [pn2])
                    self.STT(ybf[:], ps2[:, 0:128], prm[:, l, P_HNW + h:P_HNW + h + 1], sg[:, h, :], ALU.mult, ALU.mult, [pn2, 'prm', 'sg'], ['ybf'])
                    self.dma(self.yT[768 + h * 128:768 + (h + 1) * 128, t0:t0 + 128], ybf[:], ['ybf'], ['yT_hg'])
                    if sample:
                        self.dma(O_['s_hgrn'][l, :, h, :, :].rearrange('s k v -> k s v'), Ho[:], ['Ho%d' % sb for sb in range(16)], [])
                    else:
                        self.CP('pool', Hc[:, h, :], Hs[:, nsub, :], ['Hs%d' % nsub], ['Hc'])
                        if ti == TP // 128 - 1:
                            self.dma(O_['p_hgrn'][l, h, :, :], Hs[:, nsub, :], ['Hs%d' % nsub], [])
                if side is not None and (ti % 2 == 1 or sample or ti == 0):
                    next(side, None)
            if side is not None:
                for _ in side:
                    pass

    def phase_lru(self, l, I, O_):
        for _ in self.lru_gen(l, I, O_):
            pass

    def lru_gen(self, l, I, O_):
        nc = self.nc
        TP = self.TP
        prm = self.prm
        base = RW_COLS + HG_COLS
        with contextlib.ExitStack() as es:
            A = lambda name, shape, dt=F32: es.enter_context(nc.sbuf_tensor('lr%d_%s' % (l, name), list(shape), dt))
            wabd = A('wabd', [128, 4, 128])
            wxbd = A('wxbd', [128, 4, 128])
            self.MEMSET('dve', wabd[:], 0.0, ['wabd'])
            self.MEMSET('dve', wxbd[:], 0.0, ['wxbd'])
            for blk in range(8):
                pr = slice(64 * (blk % 2), 64 * (blk % 2) + 64)
                self.dma(wabd[pr, blk // 2, 64 * (blk % 2):64 * (blk % 2) + 64], I['rglru_wa'][l, blk], [], ['wabd'])
                self.dma(wxbd[pr, blk // 2, 64 * (blk % 2):64 * (blk % 2) + 64], I['rglru_wx'][l, blk], [], ['wxbd'])
            TM = max(TP, 128)
            xpb = [A('xp_p%d' % c, [128, TP + 3]) for c in range(2)]
            gtb = [A('gt_p%d' % c, [128, TP]) for c in range(2)]
            xps = {('p', c): xpb[c % 2] for c in range(4)}
            gts = {('p', c): gtb[c % 2] for c in range(4)}
            for c in range(4):
                xps[('s', c)] = A('xp_s%d' % c, [128, 16 * 11])
                gts[('s', c)] = A('gt_s%d' % c, [128, 128])
            h0s = [A('h0_%d' % c, [128, 16]) for c in range(4)]
            xc = A('xc', [128, TM])
            rr = A('rr', [128, TM])
            ig = A('ig', [128, TM])
            aa = A('aa', [128, TM])
            uu = A('uu', [128, TM])
            hh = xc
            ybf = aa[:].bitcast(BF16)
            def prefetch(kind, c):
                    sample = kind == 's'
                    nseq, T, t0 = (16, 8, TP) if sample else (1, TP, 0)
                    N = nseq * T
                    xp, gt, h0 = xps[(kind, c)], gts[(kind, c)], h0s[c]
                    xn, gn_ = 'xp%s%d' % (kind, c % 2 if kind == 'p' else c), 'gt%s%d' % (kind, c % 2 if kind == 'p' else c)
                    xpv = xp[:, 0:nseq * (T + 3)].rearrange('p (s t) -> p s t', t=T + 3)
                    if sample:
                        s48 = rr[0:48, 0:128]
                        s16 = rr[0:16, 128:256]
                        self.dma(s48, I['cache_rglru_conv'][l, :, :, c * 128:(c + 1) * 128].rearrange('s j p -> (s j) p'), [], ['rr'])
                        self.dma(s16, I['state_rglru'][l, :, c * 128:(c + 1) * 128], [], ['rr'])
                        ps, pn = self.psum()
                        self.TR(ps[:, 0:48], s48, ['rr'], [pn], n=48)
                        self.TR(ps[:, 64:80], s16, ['rr'], [pn], n=16)
                        self.CP('dve', xpv[:, :, 0:3], ps[:, 0:48].rearrange('p (s j) -> p s j', j=3), [pn], [xn])
                        self.CP('dve', h0[:], ps[:, 64:80], [pn], ['h0_%d' % c])
                    else:
                        self.MEMSET('pool', xpv[:, :, 0:3], 0.0, [xn])
                    r0 = base + c * 128
                    self.dma(xpv[:, :, 3:T + 3], self.projT[r0:r0 + 128, t0:t0 + N].rearrange('p (s t) -> p s t', t=T), ['projT'], [xn])
                    self.dma(gt[:, 0:N], self.projT[r0 + 512:r0 + 640, t0:t0 + N], ['projT'], [gn_])
            for c in range(2):
                prefetch('p', c)
            for c in range(4):
                prefetch('s', c)
            yield
            for kind in ('p', 's'):
                sample = kind == 's'
                nseq, T, t0 = (16, 8, TP) if sample else (1, TP, 0)
                N = nseq * T
                for c in range(4):
                    xp, gt, h0 = xps[(kind, c)], gts[(kind, c)], h0s[c]
                    xn, gn_ = 'xp%s%d' % (kind, c % 2 if kind == 'p' else c), 'gt%s%d' % (kind, c % 2 if kind == 'p' else c)
                    xpv = xp[:, 0:nseq * (T + 3)].rearrange('p (s t) -> p s t', t=T + 3)
                    v3 = lambda ap: ap[:, 0:N].rearrange('p (s t) -> p s t', t=T)
                    r0 = base + c * 128
                    cw = lambda j: prm[:, l, P_CW + 4 * j + c:P_CW + 4 * j + c + 1]
                    self.TSC('dve', v3(xc), xpv[:, :, 0:T], cw(0), prm[:, l, P_CB + c:P_CB + c + 1], ALU.mult, ALU.add, [xn, 'prm'], ['xc'])
                    for j in range(1, 4):
                        self.STT(v3(xc), xpv[:, :, j:j + T], cw(j), v3(xc), ALU.mult, ALU.add, [xn, 'prm', 'xc'], ['xc'])
                    g0 = 0
                    while g0 < N:
                        gn = min(512, N - g0)
                        ps, pn = self.psum()
                        self.MM(ps[:, 0:gn], wabd[:, c, :], xc[:, g0:g0 + gn], True, True, ['wabd', 'xc'], [pn])
                        self.ACT(rr[:, g0:g0 + gn], ps[:, 0:gn], AF.Sigmoid, [pn, 'prm'], ['rr'], bias=prm[:, l, P_BA + c:P_BA + c + 1])
                        ps, pn = self.psum()
                        self.MM(ps[:, 0:gn], wxbd[:, c, :], xc[:, g0:g0 + gn], True, True, ['wxbd', 'xc'], [pn])
                        self.ACT(ig[:, g0:g0 + gn], ps[:, 0:gn], AF.Sigmoid, [pn, 'prm'], ['ig'], bias=prm[:, l, P_BX + c:P_BX + c + 1])
                        g0 += gn
                    self.ACT(aa[:, 0:N], rr[:, 0:N], AF.Exp, ['rr', 'prm'], ['aa'], scale=prm[:, l, P_C8 + c:P_C8 + c + 1])
                    self.ACT(uu[:, 0:N], rr[:, 0:N], AF.Exp, ['rr', 'prm'], ['uu'], scale=prm[:, l, P_2C8 + c:P_2C8 + c + 1])
                    self.ACT(uu[:, 0:N], uu[:, 0:N], AF.Sqrt, ['uu'], ['uu'], bias=1.0, scale=-1.0)
                    self.TT('pool', ig[:, 0:N], ig[:, 0:N], xc[:, 0:N], ALU.mult, ['ig', 'xc'], ['ig'])
                    self.TT('dve', uu[:, 0:N], uu[:, 0:N], ig[:, 0:N], ALU.mult, ['uu', 'ig'], ['uu'])
                    if sample:
                        self.TT('dve', h0[:], h0[:], v3(aa)[:, :, 0], ALU.mult, ['h0_%d' % c, 'aa'], ['h0_%d' % c])
                        self.TT('dve', v3(uu)[:, :, 0], v3(uu)[:, :, 0], h0[:], ALU.add, ['uu', 'h0_%d' % c], ['uu'])
                        self.MEMSET('dve', v3(aa)[:, :, 0:1], 0.0, ['aa'])
                    self.SCAN2(hh[:, 0:N], aa[:, 0:N], uu[:, 0:N], ['aa', 'uu'], ['xc'])
                    self.TT('pool', rr[:, 0:N], gt[:, 0:N], gt[:, 0:N], ALU.mult, [gn_, 'rr'], ['rr'])
                    self.TSC('pool', rr[:, 0:N], rr[:, 0:N], 0.044715, 1.0, ALU.mult, ALU.add, ['rr'], ['rr'])
                    self.TT('pool', rr[:, 0:N], rr[:, 0:N], gt[:, 0:N], ALU.mult, ['rr', gn_], ['rr'])
                    self.ACT(rr[:, 0:N], rr[:, 0:N], AF.Sigmoid, ['rr'], ['rr'], scale=1.5957691216057308)
                    self.TT('dve', ig[:, 0:N], hh[:, 0:N], gt[:, 0:N], ALU.mult, ['xc', gn_, 'ig'], ['ig'])
                    self.TT('dve', ybf[:, 0:N], ig[:, 0:N], rr[:, 0:N], ALU.mult, ['ig', 'rr'], ['aa'])
                    self.dma(self.yT[1536 + c * 128:1536 + (c + 1) * 128, t0:t0 + N], ybf[:, 0:N], ['aa'], ['yT_lru'])
                    if sample:
                        self.CP('dve', rr[:, 0:48].rearrange('p (s j) -> p s j', j=3), xpv[:, :, T:T + 3], [xn, 'rr'], ['rr'])
                        self.CP('dve', rr[:, 64:80], v3(hh)[:, :, T - 1], ['xc', 'rr'], ['rr'])
                        ps, pn = self.psum()
                        self.TR(ps[0:48, 0:128], rr[:, 0:48], ['rr'], [pn])
                        ps2, pn2 = self.psum()
                        self.TR(ps2[0:16, 0:128], rr[:, 64:80], ['rr'], [pn2])
                        self.CP('act', ig[0:48, 0:128], ps[0:48, 0:128], [pn, 'ig'], ['ig'])
                        self.CP('act', ig[0:16, 128:256], ps2[0:16, 0:128], [pn2, 'ig'], ['ig'])
                        self.dma(O_['s_rglru_conv'][l, :, :, c * 128:(c + 1) * 128].rearrange('s j p -> (s j) p'), ig[0:48, 0:128], ['ig'], [])
                        self.dma(O_['s_rglru'][l, :, c * 128:(c + 1) * 128], ig[0:16, 128:256], ['ig'], [])
                    else:
                        self.dma(O_['p_rglru_conv'][l, :, c * 128:(c + 1) * 128].rearrange('j p -> p j'), xpv[:, 0, T:T + 3], [xn], [], slow=True)
                        self.dma(O_['p_rglru'][l, c * 128:(c + 1) * 128].rearrange('(p o) -> p o', o=1), hh[:, T - 1:T], ['xc'], [], slow=True)
                    if kind == 'p' and c < 2:
                        prefetch('p', c + 2)
                    yield

    def SCAN2(self, out, d0, d1, r, w):
        self.E('dve', lambda e: e.tensor_tensor_scan(out, d0, d1, 0.0, ALU.mult, ALU.add), r, w)
    def gemm(self, tag, A, KC, blocks, rhs_fn, rhs_res, epilogue, cbw=256, tgs=None, bufs=None, single=False):
        tgs = self.tg if tgs is None else tgs
        if bufs is None:
            wst = [A('%s_wst%d' % (tag, i), [128, KC, cbw]) for i in range(1 if single else 2)]
            if single:
                wst = wst * 2
            wbf = [A('%s_wbf%d' % (tag, i), [128, KC, cbw], BF16) for i in range(2)]
        else:
            wst, wbf, tag = bufs

        sb_ = (lambda b: 0) if single else (lambda b: b)
        ka = (KC * 7) // 16
        kparts = [(0, ka), (ka, 2 * ka), (2 * ka, KC)]
        kp_of = lambda kc: 0 if kc < ka else (1 if kc < 2 * ka else 2)

        def load_dma(bi):
            b = bi % 2
            off = 0
            for ap, w in blocks[bi]:
                self.dma(wst[b][:, :, off:off + w], ap.rearrange('(c p) n -> p c n', p=128), [], ['%s_wst%d' % (tag, sb_(b))])
                off += w
            return off

        def load_cast(bi, off):
            b = bi % 2
            for p, (k0, k1) in enumerate(kparts):
                self.CP(('dve', 'act', 'pool')[p], wbf[b][:, k0:k1, 0:off], wst[b][:, k0:k1, 0:off], ['%s_wst%d' % (tag, sb_(b))], ['%s_wbf%d_%d' % (tag, b, p)])

        widths = {0: load_dma(0)}
        load_cast(0, widths[0])
        for bi in range(len(blocks)):
            if bi + 1 < len(blocks):
                widths[bi + 1] = load_dma(bi + 1)
            b = bi % 2
            nu = widths[bi] // 128
            work = [(u, gi, t0, tn) for u in range(nu) for gi, (t0, tn) in enumerate(tgs)]
            cast_at = (len(work) * 3) // 5
            for wi, (u, gi, t0, tn) in enumerate(work):
                if wi == cast_at and bi + 1 < len(blocks):
                    load_cast(bi + 1, widths[bi + 1])
                ps, pn = self.psum()
                for kc in range(KC):
                    self.MM(ps[:, 0:tn], wbf[b][:, kc, u * 128:(u + 1) * 128], rhs_fn(kc, t0, tn), kc == 0, kc == KC - 1,
                            ['%s_wbf%d_%d' % (tag, b, kp_of(kc))] + rhs_res, [pn])
                epilogue(bi, u, gi, t0, tn, ps, pn)

    def phase_inproj(self, l, actT, I, A):
        st = [A('ip_st%d_%d' % (l, i), [128, 512]) for i in range(4)]
        w_in = I['w_in']
        blocks = [[(w_in[l, :, c0:c0 + 512], 512)] for c0 in range(0, IN_COLS, 512)]
        self._k = 0

        def epi(bi, u, gi, t0, tn, ps, pn):
            k = self._k % 4
            self._k += 1
            c0 = bi * 512 + u * 128
            self.CP('act' if k % 2 == 0 else 'dve', st[k][:, 0:tn], ps[:, 0:tn], [pn], ['ip_st%d' % k])
            self.dma(self.projT[c0:c0 + 128, t0:t0 + tn], st[k][:, 0:tn], ['ip_st%d' % k], ['projT'])

        self.gemm('ip%d' % l, A, 16, blocks, lambda kc, t0, tn: actT[:, kc, t0:t0 + tn], ['actT'], epi, cbw=512)

    def residual_epi(self, A, tag):
        xs = [A('%s_x%d' % (tag, i), [128, 512]) for i in range(4)]
        self._k = 0

        def epi(c0, t0, tn, ps, pn):
            k = self._k % 4
            self._k += 1
            rn = '%s_x%d' % (tag, k)
            self.dma(xs[k][:, 0:tn], self.xT[c0:c0 + 128, t0:t0 + tn], ['xT'], [rn])
            self.TT('dve', xs[k][:, 0:tn], xs[k][:, 0:tn], ps[:, 0:tn], ALU.add, [rn, pn], [rn])
            self.dma(self.xT[c0:c0 + 128, t0:t0 + tn], xs[k][:, 0:tn], [rn], ['xT'])
        return epi

    def phase_outproj(self, l, actT, I, A):
        self.dma(actT[:], self.yT.rearrange('(c p) t -> p c t', p=128), ['yT_rw', 'yT_hg', 'yT_lru'], ['actT'])
        w = I['w_out']
        blocks = [[(w[l, :, c0:c0 + 512], 512)] for c0 in range(0, D, 512)]
        repi = self.residual_epi(A, 'op%d' % l)
        self.gemm('op%d' % l, A, 16, blocks, lambda kc, t0, tn: actT[:, kc, t0:t0 + tn], ['actT'],
                  lambda bi, u, gi, t0, tn, ps, pn: repi(bi * 512 + u * 128, t0, tn, ps, pn), cbw=512)

    def phase_ffn_up(self, l, actT, I, O_, A):
        TP, NT = self.TP, self.NT
        fp = self.fprm
        w = I['ffn_w_up']
        blocks = [[(w[l, :, j * 256:(j + 1) * 256], 256), (w[l, :, DFF + j * 256:DFF + (j + 1) * 256], 256)] for j in range(DFF // 256)]
        gp = [A('fu%d_gp%d' % (l, i), [128, TP + 2]) for i in range(2)]
        gs = [A('fu%d_gs%d' % (l, i), [128, 16, 10]) for i in range(2)]
        vb = [A('fu%d_vb%d' % (l, i), [128, NT]) for i in range(2)]
        cv = A('fu%d_cv' % l, [128, NT])
        hb = [A('fu%d_hb%d' % (l, i), [128, NT], BF16) for i in range(2)]
        for i in range(2):
            self.MEMSET('pool', gp[i][:, 0:2], 0.0, ['gp%d' % i])
        ng = len(self.tg)
        CI = [A('fu%d_ci%d' % (l, i), [32, 128]) for i in range(2)]
        OS = [A('fu%d_os%d' % (l, i), [34, 128]) for i in range(2)]
        tmpo = A('fu%d_tmpo' % l, [128, 34])
        cin = I['cache_ffn_conv'][l].rearrange('s j c -> (s j) c')
        for b0 in range(2):
            self.dma(CI[b0][:], cin[:, b0 * 128:(b0 + 1) * 128], [], ['fu_ci%d' % b0])

        def epi(bi, u, gi, t0, tn, ps, pn):
            b = u % 2
            blk = 2 * bi + b
            if u < 2:
                if gi == 0:
                    ps2, pn2 = self.psum()
                    self.TR(ps2[:, 0:32], CI[b][:], ['fu_ci%d' % b], [pn2], n=32)
                    self.CP('dve', gs[b][:, :, 0:2], ps2[:, 0:32].rearrange('p (s j) -> p s j', j=2), [pn2], ['gs%d' % b])
                if t0 < TP:
                    self.CP('act', gp[b][:, 2 + t0:2 + t0 + tn], ps[:, 0:tn], [pn], ['gp%d' % b])
                else:
                    self.CP('act', gs[b][:, :, 2:10], ps[:, 0:tn].rearrange('p (s t) -> p s t', t=8), [pn], ['gs%d' % b])
            else:
                self.CP('act' if gi % 2 == 0 else 'dve', vb[b][:, t0:t0 + tn], ps[:, 0:tn], [pn], ['vb%d' % b])
                if gi == 0 and bi + 1 < len(blocks):
                    nblk = 2 * (bi + 1) + b
                    self.dma(CI[b][:], cin[:, nblk * 128:(nblk + 1) * 128], [], ['fu_ci%d' % b])
                if gi == ng - 1:
                    self.CP('dve', tmpo[:, 0:32].rearrange('p (s j) -> p s j', j=2), gs[b][:, :, 8:10], ['gs%d' % b], ['fu_tmpo'])
                    self.CP('dve', tmpo[:, 32:34], gp[b][:, TP:TP + 2], ['gp%d' % b], ['fu_tmpo'])
                    ps2, pn2 = self.psum()
                    self.TR(ps2[0:34, 0:128], tmpo[:], ['fu_tmpo'], [pn2])
                    self.CP('act', OS[b][:], ps2[0:34, 0:128], [pn2], ['fu_os%d' % b])
                    self.dma(O_['s_ffn_conv'][l].rearrange('s j c -> (s j) c')[:, blk * 128:(blk + 1) * 128], OS[b][0:32, :], ['fu_os%d' % b], [])
                    self.dma(O_['p_ffn_conv'][l][:, blk * 128:(blk + 1) * 128], OS[b][32:34, :], ['fu_os%d' % b], [])
                    w_ = lambda j: fp[:, l, j * 44 + blk:j * 44 + blk + 1]
                    bb = fp[:, l, 132 + blk:133 + blk]
                    cvs = cv[:, TP:NT].rearrange('p (s t) -> p s t', t=8)
                    self.TSC('dve', cv[:, 0:TP], gp[b][:, 0:TP], w_(0), bb, ALU.mult, ALU.add, ['gp%d' % b, 'fprm'], ['cv'])
                    self.STT(cv[:, 0:TP], gp[b][:, 1:TP + 1], w_(1), cv[:, 0:TP], ALU.mult, ALU.add, ['gp%d' % b, 'fprm', 'cv'], ['cv'])
                    self.STT(cv[:, 0:TP], gp[b][:, 2:TP + 2], w_(2), cv[:, 0:TP], ALU.mult, ALU.add, ['gp%d' % b, 'fprm', 'cv'], ['cv'])
                    self.TSC('dve', cvs, gs[b][:, :, 0:8], w_(0), bb, ALU.mult, ALU.add, ['gs%d' % b, 'fprm', 'cv'], ['cv'])
                    self.STT(cvs, gs[b][:, :, 1:9], w_(1), cvs, ALU.mult, ALU.add, ['gs%d' % b, 'fprm', 'cv'], ['cv'])
                    self.STT(cvs, gs[b][:, :, 2:10], w_(2), cvs, ALU.mult, ALU.add, ['gs%d' % b, 'fprm', 'cv'], ['cv'])
                    self.ACT(cv[:], cv[:], AF.Silu, ['cv'], ['cv'])
                    self.TT('dve', hb[b][:], cv[:], vb[b][:], ALU.mult, ['cv', 'vb%d' % b], ['hb%d' % b])
                    self.dma(self.hT[blk * 128:(blk + 1) * 128, :], hb[b][:], ['hb%d' % b], ['hT'])

        self.gemm('fu%d' % l, A, 16, blocks, lambda kc, t0, tn: actT[:, kc, t0:t0 + tn], ['actT'], epi, cbw=512, single=True)

    def phase_ffn_down(self, l, I, A):
        NT = self.NT
        w = I['ffn_w_down']
        KC = DFF // 128
        parts = [self.tg[0:2], self.tg[2:]] if len(self.tg) > 2 else [self.tg]
        nmax = max(sum(n for _, n in p) for p in parts if p)
        hres = A('fd%d_h' % l, [128, KC, nmax], BF16)
        repi = self.residual_epi(A, 'fd%d' % l)
        blocks = [[(w[l, :, c0:c0 + 256], 256)] for c0 in range(0, D, 256)]
        tag = 'fd%d' % l
        bufs = ([A('%s_wst0' % tag, [128, KC, 256])] * 2, [A('%s_wbf%d' % (tag, i), [128, KC, 256], BF16) for i in range(2)], tag)
        for pi, part in enumerate(parts):
            if not part:
                continue
            s0 = part[0][0]
            n = sum(nn for _, nn in part)
            self.dma(hres[:, :, 0:n], self.hT[:, s0:s0 + n].rearrange('(c p) t -> p c t', p=128), ['hT'], ['hres'])
            self.gemm('fd%d_%d' % (l, pi), A, KC, blocks, lambda kc, t0, tn, s0=s0: hres[:, kc, t0 - s0:t0 - s0 + tn], ['hres'],
                      lambda bi, u, gi, t0, tn, ps, pn: repi(bi * 256 + u * 128, t0, tn, ps, pn), cbw=256, tgs=part, bufs=bufs, single=True)

    def phase_final(self, y_out):
        nc = self.nc
        with contextlib.ExitStack() as es:
            A = lambda name, shape, dt=F32: es.enter_context(nc.sbuf_tensor('fin_' + name, list(shape), dt))
            xg = [A('xg%d' % i, [128, 16, 512]) for i in range(2)]
            sqs = [A('sq%d' % i, [128, 16, 128], BF16) for i in range(2)]
            sds = [A('sd%d' % i, [128, 128]) for i in range(2)]
            rss = [A('rs%d' % i, [128, 128]) for i in range(2)]
            hns = [A('hn%d' % i, [128, 16, 128]) for i in range(2)]
            yo = [A('yo%d' % i, [128, D]) for i in range(2)]
            k = 0
            for gi, (g0, gn) in enumerate(self.tg):
                gb = gi % 2
                xn = 'fxg%d' % gb
                self.dma(xg[gb][:, :, 0:gn], self.xT[:, g0:g0 + gn].rearrange('(c p) t -> p c t', p=128), ['xT'], [xn])
                for j in range(gn // 128):
                    b = k % 2
                    k += 1
                    sq, sd, rs, hn = sqs[b], sds[b], rss[b], hns[b]
                    t0 = g0 + j * 128
                    xv = xg[gb][:, :, j * 128:(j + 1) * 128]
                    self.ACT(sq[:], xv, AF.Square, [xn], ['fsq%d' % b])
                    ps, pn = self.psum()
                    for kc in range(16):
                        self.MM(ps[:, 0:128], self.ones_bf[:], sq[:, kc, :], kc == 0, kc == 15, ['fsq%d' % b, 'ones_bf'], [pn])
                    self.ACT(sd[:], ps[:, 0:128], AF.Sqrt, [pn], ['fsd%d' % b], bias=EPS, scale=1.0 / D)
                    self.RECIP(rs[:], sd[:], ['fsd%d' % b], ['frs%d' % b])
                    for kc in range(16):
                        self.STT(hn[:, kc, :], xg[gb][:, kc, j * 128:(j + 1) * 128], self.prmf[:, kc:kc + 1], rs[:], ALU.mult, ALU.mult, [xn, 'frs%d' % b, 'prmf'], ['fhn%d' % b])
                    for q in range(4):
                        ps, pn = self.psum()
                        for jj in range(4):
                            kc = q * 4 + jj
                            self.TR(ps[:, jj * 128:(jj + 1) * 128], hn[:, kc, :], ['fhn%d' % b], [pn])
                        self.CP('act' if q % 2 == 0 else 'dve', yo[b][:, q * 512:(q + 1) * 512], ps[:], [pn], ['fyo%d_%d' % (b, q)])
                    self.dma(y_out[t0:t0 + 128, :], yo[b][:], ['fyo%d_%d' % (b, q) for q in range(4)], [])


_CACHE = {}
TP_FULL = 2048
W_NAMES = ['norm_mix', 'w_in', 'rwkv_mu', 'rwkv_w0', 'rwkv_w2', 'rwkv_a0', 'rwkv_a2', 'rwkv_g2', 'rwkv_k_k', 'rwkv_k_a',
           'rwkv_r_k', 'rwkv_ln_w', 'rwkv_ln_b', 'hgrn_lb_logits', 'hgrn_norm_w', 'rglru_conv_w', 'rglru_conv_b', 'rglru_wa',
           'rglru_ba', 'rglru_wx', 'rglru_bx', 'rglru_lambda', 'w_out', 'norm_ffn', 'ffn_w_up', 'ffn_conv_w', 'ffn_conv_b',
           'ffn_w_down', 'norm_final']
S_NAMES = ['state_rwkv', 'state_rwkv_shift', 'state_hgrn', 'state_rglru', 'cache_rglru_conv', 'cache_ffn_conv']
O_NAMES = ['rwkv', 'rwkv_shift', 'hgrn', 'rglru', 'rglru_conv', 'ffn_conv']


def kernel(**inputs):
    if 'nc' not in _CACHE:
        b = Builder(TP_FULL, DEPTH)
        _CACHE['nc'] = b.build()
        _CACHE['b'] = b
    nc = _CACHE['nc']
    f32 = lambda a: np.ascontiguousarray(np.asarray(a, dtype=np.float32))
    consts = make_consts()
    shared = {k: f32(inputs[k]) for k in W_NAMES}
    for k, v in consts.items():
        shared['c_' + k] = v
    xp = f32(inputs['x_prompt'])
    xs = f32(inputs['x_sample'])
    states = {k: f32(inputs[k]) for k in S_NAMES}
    ncore = 8
    in_maps = []
    for i in range(ncore):
        m = dict(shared)
        m['x'] = np.ascontiguousarray(np.concatenate([xp[i % 4], xs[NSEQ * i:NSEQ * (i + 1)].reshape(NSEQ * TS, D)], axis=0))
        for k in S_NAMES:
            m[k] = np.ascontiguousarray(states[k][:, NSEQ * i:NSEQ * (i + 1)])
        in_maps.append(m)
    res = run_bass_kernel_spmd(nc, in_maps, core_ids=list(range(ncore)))
    R = res.results
    y_prompt = np.stack([np.asarray(R[b]['y'])[:TP_FULL] for b in range(4)], axis=0).astype(np.float32)
    y_sample = np.concatenate([np.asarray(R[i]['y'])[TP_FULL:].reshape(NSEQ, TS, D) for i in range(ncore)], axis=0).astype(np.float32)
    outs = [y_prompt, y_sample]
    for n in O_NAMES:
        outs.append(np.stack([np.asarray(R[b]['p_' + n]) for b in range(4)], axis=1).astype(np.float32))
    for n in O_NAMES:
        outs.append(np.concatenate([np.asarray(R[i]['s_' + n]) for i in range(ncore)], axis=1).astype(np.float32))
    return tuple(outs)
```

```python
import contextlib
import numpy as np
import concourse.bass as bass
import concourse.mybir as mybir
from concourse.bass_utils import run_bass_kernel_spmd

F32 = mybir.dt.float32
BF16 = mybir.dt.bfloat16
AF = mybir.ActivationFunctionType
ALU = mybir.AluOpType
AX = mybir.AxisListType

D = 2048
DEPTH = 2
NSEQ = 16
TS = 8
RW_COLS = 2560
HG_COLS = 3072
LRU_COLS = 1024
IN_COLS = 6656
DFF = 5632
EPS = 1e-6
RW_EPS = 64e-5
ENG = ('pe', 'act', 'dve', 'pool', 'sp')
NDSEM = 8
NDT = BF16
P_NM, P_NF, P_MU, P_W0, P_A0, P_KK, P_KA, P_OMKA, P_RK, P_LNW, P_LNB = 0, 16, 32, 52, 58, 64, 70, 76, 82, 88, 94
P_LB, P_OMLB, P_HNW, P_CW, P_CB, P_BA, P_BX, P_LAM, P_C8, P_2C8 = 100, 106, 112, 118, 134, 138, 142, 146, 150, 154


class Sched:
    def __init__(self, nc):
        self.nc = nc
        self.ops = {e: [] for e in ENG}
        self.res = {}
        self.bar = {e: {} for e in ENG}
        self.dma_since_bar = []

    def emit(self, eng, fn, reads=(), writes=()):
        deps = {}

        def add(d):
            e, i = d
            if e == 'sp':
                deps.setdefault(('sp', i), i)
            else:
                if deps.get(e, -1) < i:
                    deps[e] = i

        for r in reads:
            st = self.res.get(r)
            if st and st['w'] is not None:
                add(st['w'])
        for w in writes:
            st = self.res.get(w)
            if st:
                if st['w'] is not None:
                    add(st['w'])
                for e, i in st['r'].items():
                    if e == 'sp':
                        for ii in i:
                            add(('sp', ii))
                    else:
                        add((e, i))
        for k, v in self.bar[eng].items():
            if isinstance(k, tuple):
                deps.setdefault(k, v)
            elif deps.get(k, -1) < v:
                deps[k] = v
        self.bar[eng] = {}
        idx = len(self.ops[eng])
        if eng == 'pe':
            deps.pop('pe', None)
        self.ops[eng].append(dict(fn=fn, deps=deps, need=False))
        if eng == 'sp':
            self.dma_since_bar.append(idx)
        for r in reads:
            st = self.res.setdefault(r, {'w': None, 'r': {}})
            if eng == 'sp':
                st['r'].setdefault('sp', []).append(idx)
            else:
                st['r'][eng] = idx
        for w in writes:
            self.res[w] = {'w': (eng, idx), 'r': {}}

    def barrier(self):
        last = {}
        for e in ENG:
            if e == 'sp':
                continue
            if self.ops[e]:
                last[e] = len(self.ops[e]) - 1
        for i in self.dma_since_bar:
            last[('sp', i)] = i
        self.dma_since_bar = []
        for e in ENG:
            d = dict(last)
            d.pop(e, None)
            for k, v in d.items():
                if isinstance(k, tuple):
                    self.bar[e].setdefault(k, v)
                elif self.bar[e].get(k, -1) < v:
                    self.bar[e][k] = v

    def finalize(self):
        nc = self.nc
        ops = self.ops
        for e in ENG:
            for op in ops[e]:
                for k, v in op['deps'].items():
                    if isinstance(k, tuple):
                        continue
                    ops[k][v]['need'] = True
        for e in ENG:
            if e == 'sp':
                continue
            c = 0
            for op in ops[e]:
                if op['need']:
                    c += 1
                op['val'] = c
        cnt = [0] * NDSEM
        for k, op in enumerate(ops['sp']):
            j = k % NDSEM
            op['prev'] = cnt[j]
            cnt[j] += 16
            op['sem'] = j
            op['val'] = cnt[j]
        sems = {e: nc.alloc_semaphore('s_' + e) for e in ENG if e != 'sp'}
        dsems = [nc.alloc_semaphore('s_d%d' % j) for j in range(NDSEM)]

        def run(e, engobj):
            emitted = {}

            def need(key, val):
                if emitted.get(key, 0) < val:
                    sm = dsems[key[1]] if isinstance(key, tuple) else sems[key]
                    engobj.wait_ge(sm, val)
                    emitted[key] = val

            for op in ops[e]:
                for k, v in op['deps'].items():
                    if isinstance(k, tuple):
                        p = ops['sp'][v]
                        need(('d', p['sem']), p['val'])
                    else:
                        need(k, ops[k][v]['val'])
                if e == 'sp' and op['prev'] > 0:
                    need(('d', op['sem']), op['prev'])
                inst = op['fn'](engobj)
                if e == 'sp':
                    inst.then_inc(dsems[op['sem']], 16)
                elif op['need']:
                    inst.then_inc(sems[e], 1)
            if e == 'sp':
                for j in range(NDSEM):
                    if cnt[j] > 0:
                        need(('d', j), cnt[j])

        self._emitted = {}
        with nc.Block() as block:
            @block.tensor
            def _(t):
                run('pe', t)

            @block.scalar
            def _(t):
                run('act', t)

            @block.vector
            def _(t):
                run('dve', t)

            @block.gpsimd
            def _(t):
                run('pool', t)

            @block.sync
            def _(t):
                run('sp', t)


def make_consts():
    c = {}
    c['ident'] = np.eye(128, dtype=np.float32)
    c['ones'] = np.ones((128, 128), np.float32)
    bo = np.zeros((128, 128), np.float32)
    bo[:64, :64] = 1
    bo[64:, 64:] = 1
    c['blkones'] = bo
    i = np.arange(128)
    for name, sub in (('1', 128), ('4', 32), ('16', 8)):
        same = (i[:, None] // sub) == (i[None, :] // sub)
        low_s = ((i[None, :] < i[:, None]) & same).astype(np.float32)
        up_s = ((i[:, None] < i[None, :]) & same).astype(np.float32)
        up_i = ((i[:, None] <= i[None, :]) & same).astype(np.float32)
        c['low_s' + name] = low_s
        c['up_si' + name] = np.concatenate([up_s, up_i], axis=1)
        c['up_i' + name] = up_i
        nsub = 128 // sub
        rst = np.ones((128, 128), np.float32)
        rst[:, ::sub] = 0
        c['rst' + name] = rst
        cm = np.zeros((128, nsub, 128), np.float32)
        rm = np.zeros((128, nsub), np.float32)
        for sb in range(nsub):
            cm[:, sb, sb * sub:(sb + 1) * sub] = 1
            rm[sb * sub:(sb + 1) * sub, sb] = 1
        c['cm' + name] = cm.reshape(128, nsub * 128)
        c['rm' + name] = rm
    return c


class Builder:
    def __init__(self, TP, depth, stop=None, dbg=()):
        self.TP = TP
        self.depth = depth
        self.NT = TP + NSEQ * TS
        self.ntile = self.NT // 128
        self.stop = stop
        self.rw_stop = None
        self.dbg = dbg
        nc = bass.Bass('TRN2', target_bir_lowering=False)
        self.nc = nc
        self.S = Sched(nc)
        self.inputs = {}
        self.outputs = {}
        self.tg = []
        t = 0
        while t < TP:
            n = min(512, TP - t)
            self.tg.append((t, n))
            t += n
        self.tg.append((TP, NSEQ * TS))

    def din(self, name, shape, dt=F32):
        t = self.nc.dram_tensor(name, list(shape), dt, kind='ExternalInput')
        self.inputs[name] = t
        return t.ap()

    def dout(self, name, shape, dt=F32):
        t = self.nc.dram_tensor(name, list(shape), dt, kind='ExternalOutput')
        self.outputs[name] = t
        return t.ap()

    def dscr(self, name, shape, dt=F32):
        return self.nc.dram_tensor(name, list(shape), dt, kind='Internal').ap()

    def dma(self, out, in_, reads, writes, slow=False):
        if slow:
            self.S.emit('sp', lambda e: e.dma_start(out=out, in_=in_, allow_slow_non_contiguous=True), reads, writes)
        else:
            self.S.emit('sp', lambda e: e.dma_start(out=out, in_=in_), reads, writes)

    def E(self, eng, fn, reads, writes):
        self.S.emit(eng, fn, reads, writes)

    def psum(self):
        k = self._pk
        self._pk = (k + 1) % 8
        return self.ps[k], 'ps%d' % k

    def build(self):
        nc = self.nc
        TP, NT, L = self.TP, self.NT, self.depth
        I = {}
        I['x'] = self.din('x', [NT, D])
        shapes = dict(
            state_rwkv=[L, NSEQ, 12, 64, 64], state_rwkv_shift=[L, NSEQ, RW_COLS], state_hgrn=[L, NSEQ, 6, 128, 128],
            state_rglru=[L, NSEQ, 512], cache_rglru_conv=[L, NSEQ, 3, 512], cache_ffn_conv=[L, NSEQ, 2, DFF],
            norm_mix=[L, D], w_in=[L, D, IN_COLS], rwkv_mu=[L, RW_COLS], rwkv_w0=[L, 768], rwkv_w2=[L, 64, 768],
            rwkv_a0=[L, 768], rwkv_a2=[L, 64, 768], rwkv_g2=[L, 128, 768], rwkv_k_k=[L, 768], rwkv_k_a=[L, 768],
            rwkv_r_k=[L, 12, 64], rwkv_ln_w=[L, 768], rwkv_ln_b=[L, 768], hgrn_lb_logits=[L, 768], hgrn_norm_w=[L, 768],
            rglru_conv_w=[L, 4, 512], rglru_conv_b=[L, 512], rglru_wa=[L, 8, 64, 64], rglru_ba=[L, 512],
            rglru_wx=[L, 8, 64, 64], rglru_bx=[L, 512], rglru_lambda=[L, 512], w_out=[L, D, D], norm_ffn=[L, D],
            ffn_w_up=[L, D, 2 * DFF], ffn_conv_w=[L, 3, DFF], ffn_conv_b=[L, DFF], ffn_w_down=[L, DFF, D], norm_final=[D])
        for k, shp in shapes.items():
            I[k] = self.din(k, shp)
        self.cst = {k: self.din('c_' + k, v.shape) for k, v in make_consts().items()}
        O_ = {}
        O_['y'] = self.dout('y', [NT, D])
        oshapes = dict(p_rwkv=[L, 12, 64, 64], p_rwkv_shift=[L, RW_COLS], p_hgrn=[L, 6, 128, 128], p_rglru=[L, 512],
                       p_rglru_conv=[L, 3, 512], p_ffn_conv=[L, 2, DFF],
                       s_rwkv=[L, NSEQ, 12, 64, 64], s_rwkv_shift=[L, NSEQ, RW_COLS], s_hgrn=[L, NSEQ, 6, 128, 128],
                       s_rglru=[L, NSEQ, 512], s_rglru_conv=[L, NSEQ, 3, 512], s_ffn_conv=[L, NSEQ, 2, DFF])
        for k, shp in oshapes.items():
            O_[k] = self.dout(k, shp)
        self.xT = self.dscr('xT', [D, NT])
        self.projT = self.dscr('projT', [IN_COLS, NT])
        self.yT = self.dscr('yT', [D, NT], BF16)
        self.hT = self.dscr('hT', [DFF, NT], BF16)
        dbg = {k: self.dout('d_' + k, shp, dt) for k, (shp, dt) in dict(
            projT=([IN_COLS, NT], F32), xT=([D, NT], F32), yT=([D, NT], BF16), hT=([DFF, NT], BF16)).items() if k in self.dbg}

        with contextlib.ExitStack() as es:
            A = lambda name, shape, dt=F32: es.enter_context(nc.sbuf_tensor(name, list(shape), dt))
            self.ps = [es.enter_context(nc.psum_tensor('ps%d' % k, [128, 512], F32)) for k in range(8)]
            self._pk = 0
            self.ident = A('ident', [128, 128])
            self.ones_bf = A('ones_bf', [128, 128], BF16)
            self.dma(self.ident[:], self.cst['ident'], [], ['ident'])
            self.MEMSET('dve', self.ones_bf[:], 1.0, ['ones_bf'])
            self.prm = A('prm', [128, L, 160])
            self.fprm = A('fprm', [128, L, 176])
            self.prmf = A('prmf', [128, 16])
            self.ptmp = A('ptmp', [128, L + 1, 16])
            self.load_params(I)
            self.phase_transpose_in(I['x'])
            self.S.barrier()
            stop = self.stop
            for l in range(L):
                def sub(fn):
                    with contextlib.ExitStack() as es2:
                        A2 = lambda name, shape, dt=F32: es2.enter_context(nc.sbuf_tensor(name, list(shape), dt))
                        fn(A2)
                    self.S.barrier()

                def p_in(A2):
                    actT = A2('actT_a%d' % l, [128, 16, NT], BF16)
                    self.phase_norm(l, actT, P_NM, A2)
                    self.S.barrier()
                    self.phase_inproj(l, actT, I, A2)
                sub(p_in)
                if stop == 'inproj':
                    break
                self.phase_rwkv(l, I, O_)
                self.S.barrier()
                if stop == 'rwkv':
                    break
                self.phase_hgrn(l, I, O_, side=self.lru_gen(l, I, O_))
                self.S.barrier()
                if stop == 'mix':
                    break

                def p_out(A2):
                    actT = A2('actT_b%d' % l, [128, 16, NT], BF16)
                    self.phase_outproj(l, actT, I, A2)
                sub(p_out)

                def p_up(A2):
                    actT = A2('actT_c%d' % l, [128, 16, NT], BF16)
                    self.phase_norm(l, actT, P_NF, A2)
                    self.S.barrier()
                    self.phase_ffn_up(l, actT, I, O_, A2)
                sub(p_up)
                if stop == 'ffnup':
                    break
                sub(lambda A2: self.phase_ffn_down(l, I, A2))
            if stop is None:
                self.phase_final(O_['y'])
            self.S.barrier()
            for k, ap in dbg.items():
                if k == 'yT' and stop == 'rwkv':
                    self.dma(ap[0:768], self.yT[0:768], [], [])
                else:
                    self.dma(ap, getattr(self, k), [], [])
            self.S.finalize()
        return nc

    def phase_transpose_in(self, x_in):
        nc = self.nc
        with contextlib.ExitStack() as es:
            A = lambda name, shape, dt=F32: es.enter_context(nc.sbuf_tensor(name, list(shape), dt))
            xt = [A('ti_x%d' % i, [128, D]) for i in range(3)]
            xo = [A('ti_o%d' % i, [128, 16, 512]) for i in range(2)]
            k = 0
            for gi, (g0, gn) in enumerate(self.tg):
                ob = gi % 2
                for j in range(gn // 128):
                    b = k % 3
                    k += 1
                    t0 = g0 + j * 128
                    self.dma(xt[b][:], x_in[t0:t0 + 128, :], [], ['ti_x%d' % b])
                    for q in range(4):
                        ps, pn = self.psum()
                        for jj in range(4):
                            kc = q * 4 + jj
                            self.TR(ps[:, jj * 128:(jj + 1) * 128], xt[b][:, kc * 128:(kc + 1) * 128], ['ti_x%d' % b], [pn])
                        self.CP('act' if q % 2 == 0 else 'dve', xo[ob][:, q * 4:(q + 1) * 4, j * 128:(j + 1) * 128], ps[:].rearrange('p (a t) -> p a t', a=4), [pn], ['ti_o%d' % ob])
                self.dma(self.xT[:, g0:g0 + gn].rearrange('(c p) t -> p c t', p=128), xo[ob][:, :, 0:gn], ['ti_o%d' % ob], ['xT'])

    def phase_norm(self, l, actT, poff, A_unused):
        with contextlib.ExitStack() as es:
            self._phase_norm(l, actT, poff, lambda name, shape, dt=F32: es.enter_context(self.nc.sbuf_tensor('%s_%d' % (name, poff), list(shape), dt)))

    def _phase_norm(self, l, actT, poff, A):
        xb = [A('nm_x%d_%d' % (l, i), [128, 16, 512]) for i in range(2)]
        sqs = [A('nm_sq%d_%d' % (l, i), [128, 16, 512], BF16) for i in range(2)]
        sds = [A('nm_sd%d_%d' % (l, i), [128, 512]) for i in range(2)]
        rss = [A('nm_rs%d_%d' % (l, i), [128, 512]) for i in range(2)]
        for gi, (t0, tn) in enumerate(self.tg):
            b = gi % 2
            sq, sd, rs = sqs[b], sds[b], rss[b]
            xn = 'nm_x%d' % b
            self.dma(xb[b][:, :, 0:tn], self.xT[:, t0:t0 + tn].rearrange('(c p) t -> p c t', p=128), ['xT'], [xn])
            self.E('act', lambda e, b=b, tn=tn, sq=sq: e.activation(sq[:, :, 0:tn], xb[b][:, :, 0:tn], AF.Square), [xn], ['nm_sq%d' % b])
            ps, pn = self.psum()
            for kc in range(16):
                self.E('pe', lambda e, ps=ps, kc=kc, tn=tn, sq=sq: e.matmul(ps[:, 0:tn], self.ones_bf[:], sq[:, kc, 0:tn], start=(kc == 0), stop=(kc == 15)),
                       ['nm_sq%d' % b, 'ones_bf'], [pn])
            self.E('act', lambda e, ps=ps, tn=tn, sd=sd: e.activation(sd[:, 0:tn], ps[:, 0:tn], AF.Sqrt, bias=EPS, scale=1.0 / D), [pn], ['nm_sd%d' % b])
            self.E('dve', lambda e, tn=tn, rs=rs, sd=sd: e.reciprocal(rs[:, 0:tn], sd[:, 0:tn]), ['nm_sd%d' % b], ['nm_rs%d' % b])
            for kc in range(16):
                self.E('dve', lambda e, b=b, kc=kc, t0=t0, tn=tn, rs=rs: e.scalar_tensor_tensor(
                    actT[:, kc, t0:t0 + tn], xb[b][:, kc, 0:tn], self.prm[:, l, poff + kc:poff + kc + 1], rs[:, 0:tn], ALU.mult, ALU.mult),
                    [xn, 'nm_rs%d' % b, 'prm'], ['actT'])

    def TT(self, eng, out, in0, in1, op, r, w):
        self.E(eng, lambda e: e.tensor_tensor(out, in0, in1, op), r, w)

    def TSC(self, eng, out, in0, s1, s2, op0, op1, r, w):
        if op1 is None:
            self.E(eng, lambda e: e.tensor_scalar(out, in0, s1, None, op0), r, w)
        else:
            self.E(eng, lambda e: e.tensor_scalar(out, in0, s1, s2, op0, op1), r, w)

    def STT(self, out, in0, sc, in1, op0, op1, r, w):
        self.E('dve', lambda e: e.scalar_tensor_tensor(out, in0, sc, in1, op0, op1), r, w)

    def ACT(self, out, in_, func, r, w, bias=None, scale=None, accum=None):
        kw = {}
        if bias is not None:
            kw['bias'] = bias
        if scale is not None:
            kw['scale'] = scale
        if accum is not None:
            kw['accum_out'] = accum
        self.E('act', lambda e: e.activation(out, in_, func, **kw), r, w)

    def CP(self, eng, out, in_, r, w):
        if eng == 'act':
            self.ACT(out, in_, AF.Copy, r, w)
        else:
            self.E(eng, lambda e: e.tensor_copy(out, in_), r, w)

    def MM(self, out, lhsT, rhs, start, stop, r, w, sgc=False):
        self.E('pe', lambda e: e.matmul(out, lhsT, rhs, start=start, stop=stop, skip_group_check=sgc), r, w)

    def TR(self, out, in_, r, w, n=128):
        self.E('pe', lambda e: e.transpose(out, in_, self.ident[0:n, 0:n]), list(r) + ['ident'], w)

    def RECIP(self, out, in_, r, w):
        self.E('dve', lambda e: e.reciprocal(out, in_), r, w)

    def MEMSET(self, eng, ap, val, w):
        self.E(eng, lambda e: e.memset(ap, val), [], w)

    def SCAN(self, out, d0, d1, init, r, w):
        self.E('dve', lambda e: e.tensor_tensor_scan(out, d0, d1, init, ALU.mult, ALU.add), r, w)

    def RED(self, out, in_, r, w):
        self.E('dve', lambda e: e.tensor_reduce(out, in_, AX.X, ALU.add), r, w)

    def load_params(self, I):
        L = self.depth
        prm = self.prm

        def ld(idx, n, ap_fn):
            for l in range(L):
                self.dma(prm[:, l, idx:idx + n], ap_fn(l), [], ['prm'], slow=True)

        cp = lambda name: (lambda l: I[name][l].rearrange('(c p) -> p c', p=128))
        ld(P_NM, 16, cp('norm_mix'))
        ld(P_NF, 16, cp('norm_ffn'))
        ld(P_MU, 20, cp('rwkv_mu'))
        ld(P_W0, 6, cp('rwkv_w0'))
        ld(P_A0, 6, cp('rwkv_a0'))
        ld(P_KK, 6, cp('rwkv_k_k'))
        ld(P_KA, 6, cp('rwkv_k_a'))
        ld(P_RK, 6, lambda l: I['rwkv_r_k'][l].rearrange('(c h2) k -> (h2 k) c', h2=2))
        ld(P_LNW, 6, cp('rwkv_ln_w'))
        ld(P_LNB, 6, cp('rwkv_ln_b'))
        ld(P_LB, 6, cp('hgrn_lb_logits'))
        ld(P_HNW, 6, cp('hgrn_norm_w'))
        for j in range(4):
            ld(P_CW + 4 * j, 4, lambda l, j=j: I['rglru_conv_w'][l, j].rearrange('(c p) -> p c', p=128))
        ld(P_CB, 4, cp('rglru_conv_b'))
        ld(P_BA, 4, cp('rglru_ba'))
        ld(P_BX, 4, cp('rglru_bx'))
        ld(P_LAM, 4, cp('rglru_lambda'))
        for l in range(L):
            for j in range(3):
                self.dma(self.fprm[:, l, j * 44:(j + 1) * 44], I['ffn_conv_w'][l, j].rearrange('(c p) -> p c', p=128), [], ['fprm'], slow=True)
            self.dma(self.fprm[:, l, 132:176], I['ffn_conv_b'][l].rearrange('(c p) -> p c', p=128), [], ['fprm'], slow=True)
        self.dma(self.prmf[:, 0:16], I['norm_final'].rearrange('(c p) -> p c', p=128), [], ['prmf'], slow=True)
        for l in range(L):
            self.TSC('dve', prm[:, l, P_OMKA:P_OMKA + 6], prm[:, l, P_KA:P_KA + 6], -1.0, 1.0, ALU.mult, ALU.add, ['prm'], ['prm'])
        ex = self.ptmp
        for l in range(L):
            self.ACT(ex[:, l, 0:6], prm[:, l, P_LB:P_LB + 6], AF.Exp, ['prm'], ['ptmp'])
        self.CP('dve', ex[:, L, 0:6], ex[:, 0, 0:6], ['ptmp'], ['ptmp'])
        for l in range(1, L):
            self.TT('dve', ex[:, L, 0:6], ex[:, L, 0:6], ex[:, l, 0:6], ALU.add, ['ptmp'], ['ptmp'])
        self.RECIP(ex[:, L, 0:6], ex[:, L, 0:6], ['ptmp'], ['ptmp'])
        for l in range(L):
            self.TT('dve', ex[:, l, 0:6], ex[:, l, 0:6], ex[:, L, 0:6], ALU.mult, ['ptmp'], ['ptmp'])
        self.MEMSET('dve', prm[:, 0, P_LB:P_LB + 6], 0.0, ['prm'])
        for l in range(1, L):
            self.TT('dve', prm[:, l, P_LB:P_LB + 6], prm[:, l - 1, P_LB:P_LB + 6], ex[:, l, 0:6], ALU.add, ['prm', 'ptmp'], ['prm'])
        for l in range(L):
            self.TSC('dve', prm[:, l, P_OMLB:P_OMLB + 6], prm[:, l, P_LB:P_LB + 6], -1.0, 1.0, ALU.mult, ALU.add, ['prm'], ['prm'])
            self.ACT(ex[:, l, 8:12], prm[:, l, P_LAM:P_LAM + 4], AF.Exp, ['prm'], ['ptmp'], scale=-1.0)
            self.ACT(ex[:, l, 8:12], ex[:, l, 8:12], AF.Ln, ['ptmp'], ['ptmp'], bias=1.0)
            self.TSC('dve', prm[:, l, P_C8:P_C8 + 4], ex[:, l, 8:12], -8.0, None, ALU.mult, None, ['ptmp'], ['prm'])
            self.TSC('dve', prm[:, l, P_2C8:P_2C8 + 4], ex[:, l, 8:12], -16.0, None, ALU.mult, None, ['ptmp'], ['prm'])
    def phase_rwkv(self, l, I, O_):
        nc = self.nc
        TP, NT = self.TP, self.NT
        prm = self.prm
        C0 = 0.6065306597126334
        with contextlib.ExitStack() as es:
            A = lambda name, shape, dt=F32: es.enter_context(nc.sbuf_tensor('rw%d_%s' % (l, name), list(shape), dt))
            w2a2 = A('w2a2', [128, 768])
            g2 = A('g2', [128, 768])
            self.dma(w2a2[0:64, :], I['rwkv_w2'][l], [], ['w2a2'])
            self.dma(w2a2[64:128, :], I['rwkv_a2'][l], [], ['w2a2'])
            self.dma(g2[:], I['rwkv_g2'][l], [], ['g2'])
            blk = A('blk', [128, 128])
            self.dma(blk[:], self.cst['blkones'], [], ['blk'])
            masks = {}
            for mk in ('1', '16'):
                mbk = A('mbk' + mk, [128, 512])
                ml4 = A('ml4' + mk, [128, 512])
                rst = A('rst' + mk, [128, 128])
                for j in range(2):
                    self.dma(mbk[:, j * 256:(j + 1) * 256], self.cst['up_si' + mk], [], ['mbk' + mk])
                for j in range(4):
                    self.dma(ml4[:, j * 128:(j + 1) * 128], self.cst['low_s' + mk], [], ['ml4' + mk])
                self.dma(rst[:], self.cst['rst' + mk], [], ['rst' + mk])
                masks[mk] = (mbk, ml4, rst)
            cm16 = A('cm16', [128, 16, 128])
            rm16 = A('rm16', [128, 16])
            self.dma(cm16[:], self.cst['cm16'].rearrange('p (s t) -> p s t', s=16), [], ['cm16'])
            self.dma(rm16[:], self.cst['rm16'], [], ['rm16'])

            pbuf = A('pbuf', [128, 20, 144])
            xs = A('xs', [128, 20, 128])
            dd = A('dd', [128, 20, 128])
            tx = A('tx', [128, 128])
            sgx = A('sgx', [128, 128])
            sgw = A('sgw', [128, 6, 128])
            aa = A('aa', [128, 6, 128])
            gT = A('gT', [128, 6, 128])
            kk = A('kk', [128, 6, 128])
            t1 = A('t1', [128, 6, 128])
            t2 = A('t2', [128, 6, 128])
            kf = A('kf', [128, 6, 128])
            nb = A('nb', [128, 6, 128])
            cs = A('cs', [128, 6, 128])
            Gm = A('Gm', [128, 6, 128])
            Gi = A('Gi', [128, 6, 128])
            Gp = A('Gp', [128, 6, 128])
            AR = A('AR', [128, 6, 256])
            KT = A('KT', [128, 6, 128])
            BT = A('BT', [128, 6, 128])
            Vtm = A('Vtm', [128, 768])
            Btm = A('Btm', [128, 768])
            Ktm = A('Ktm', [128, 768])
            M4 = A('M4', [128, 12, 384])
            Pm = [A('Pm%d' % i, [128, 12, 128], NDT) for i in range(2)]
            PTm = [A('PTm%d' % i, [128, 12, 128], NDT) for i in range(2)]
            X = [A('X%d' % i, [128, 12, 128], NDT) for i in range(2)]
            Osb = A('Osb', [128, 12, 64])
            st = A('st', [128, 64])
            bst = A('bst', [128, 12, 6])
            ybf = A('ybf', [128, 6, 128], BF16)
            Hc = A('Hc', [128, 1, 6, 64])
            Hin = A('Hin', [128, 16, 6, 64])
            sin = [A('sin0', [64, 6, 2, 64])] * 2
            sout = sin
            htmp = A('htmp', [128, 64])

            self.MEMSET('dve', Hc[:], 0.0, ['Hc'])
            tiles = [('p', ti) for ti in range(TP // 128)] + [('s', 0)]
            for kind, ti in tiles:
                sample = kind == 's'
                t0 = TP if sample else ti * 128
                mk = '16' if sample else '1'
                nsub = 16 if sample else 1
                mbk, ml4, rst = masks[mk]
                rmk = ['mbk' + mk, 'ml4' + mk, 'rst' + mk]
                if sample:
                    pv = pbuf[:, :, 0:144].rearrange('p c (s t) -> p c s t', t=9)
                    self.dma(dd[:], self.projT[0:RW_COLS, t0:t0 + 128].rearrange('(c p) t -> p c t', p=128), ['projT'], ['dd'])
                    self.CP('dve', pv[:, :, :, 1:9], dd[:].rearrange('p c (s t) -> p c s t', t=8), ['dd'], ['pbuf'])
                    for g in range(5):
                        a0 = 2 * (g % 2)
                        stg = M4[0:16, a0:a0 + 2, :].rearrange('p a b -> p (a b)')[:, 0:512]
                        sres = ['M4_%d' % a0, 'M4_%d' % (a0 + 1)]
                        self.dma(stg, I['state_rwkv_shift'][l, :, g * 512:(g + 1) * 512], [], sres)
                        ps, pn = self.psum()
                        for j in range(4):
                            self.TR(ps[:, j * 16:(j + 1) * 16], stg[:, j * 128:(j + 1) * 128], sres, [pn], n=16)
                        self.CP('dve', pv[:, g * 4:(g + 1) * 4, :, 0], ps[:, 0:64].rearrange('p (c s) -> p c s', s=16), [pn], ['pbuf'])
                    prev = lambda c: pv[:, c, :, 0:8]
                    cur = lambda c: pv[:, c, :, 1:9]
                    v3 = lambda ap: ap.rearrange('p (s t) -> p s t', t=8)
                else:
                    if ti == 0:
                        self.MEMSET('dve', pbuf[:, :, 0:1], 0.0, ['pbuf'])
                        self.dma(pbuf[:, :, 1:129], self.projT[0:RW_COLS, t0:t0 + 128].rearrange('(c p) t -> p c t', p=128), ['projT'], ['pbuf'])
                    else:
                        self.dma(pbuf[:, :, 0:129], self.projT[0:RW_COLS, t0 - 1:t0 + 128].rearrange('(c p) t -> p c t', p=128), ['projT'], ['pbuf'])
                    prev = lambda c: pbuf[:, c, 0:128]
                    cur = lambda c: pbuf[:, c, 1:129]
                    v3 = lambda ap: ap
                if sample:
                    pall, call = pv[:, :, :, 0:8], pv[:, :, :, 1:9]
                    va = lambda t: t[:].rearrange('p c (s t) -> p c s t', t=8)
                    mub = prm[:, l, P_MU:P_MU + 20].rearrange('p (c a b) -> p c a b', a=1, b=1).to_broadcast([128, 20, 16, 8])
                else:
                    pall, call = pbuf[:, :, 0:128], pbuf[:, :, 1:129]
                    va = lambda t: t[:]
                    mub = prm[:, l, P_MU:P_MU + 20].rearrange('p (c a) -> p c a', a=1).to_broadcast([128, 20, 128])
                self.TT('dve', va(dd), pall, call, ALU.subtract, ['pbuf'], ['dd'])
                self.TT('dve', va(dd), va(dd), mub, ALU.mult, ['dd', 'prm'], ['dd'])
                self.TT('dve', va(xs), va(dd), call, ALU.add, ['dd', 'pbuf'], ['xs'])
                self.ACT(tx[0:64, :], xs[0:64, 18, :], AF.Tanh, ['xs'], ['tx'])
                self.ACT(sgx[:], xs[:, 19, :], AF.Sigmoid, ['xs'], ['sgx'])
                for (rows, rhs, rres, out, bidx) in ((slice(0, 64), tx[0:64, :], 'tx', sgw, P_W0), (slice(64, 128), xs[64:128, 18, :], 'xs', aa, P_A0)):
                    for half in range(2):
                        ps, pn = self.psum()
                        cs_ = range(0, 4) if half == 0 else range(4, 6)
                        for c in cs_:
                            self.MM(ps[:, (c % 4) * 128:(c % 4 + 1) * 128], w2a2[rows, c * 128:(c + 1) * 128], rhs, True, True, ['w2a2', rres], [pn])
                        for c in cs_:
                            self.ACT(out[:, c, :], ps[:, (c % 4) * 128:(c % 4 + 1) * 128], AF.Sigmoid, [pn, 'prm'], ['sgw' if out is sgw else 'aa'], bias=prm[:, l, bidx + c:bidx + c + 1])
                for half in range(2):
                    ps, pn = self.psum()
                    cs_ = range(0, 4) if half == 0 else range(4, 6)
                    for c in cs_:
                        self.MM(ps[:, (c % 4) * 128:(c % 4 + 1) * 128], g2[:, c * 128:(c + 1) * 128], sgx[:], True, True, ['g2', 'sgx'], [pn])
                    n = len(cs_)
                    self.CP('act', gT[:, cs_[0]:cs_[0] + n, :], ps[:, 0:n * 128].rearrange('p (c t) -> p c t', t=128), [pn], ['gT'])
                SGW, AA = 'sgw', 'aa'
                for c in range(6):
                    self.ACT(kk[:, c, :], xs[:, 6 + c, :], AF.Copy, ['xs', 'prm'], ['kk'], scale=prm[:, l, P_KK + c:P_KK + c + 1])
                self.TT('dve', t1[:], kk[:], kk[:], ALU.mult, ['kk'], ['t1'])
                for half in range(2):
                    ps, pn = self.psum()
                    cs_ = range(0, 4) if half == 0 else range(4, 6)
                    for c in cs_:
                        self.MM(ps[:, (c % 4) * 128:(c % 4 + 1) * 128], blk[:], t1[:, c, :], True, True, ['blk', 't1'], [pn])
                    n = len(cs_)
                    self.TSC('dve', t2[:, cs_[0]:cs_[0] + n, :], ps[:, 0:n * 128].rearrange('p (c t) -> p c t', t=128), 1e-24, None, ALU.max, None, [pn], ['t2'])
                self.ACT(t2[:], t2[:], AF.Sqrt, ['t2'], ['t2'])
                self.RECIP(t2[:], t2[:], ['t2'], ['t2'])
                self.TT('dve', kk[:], kk[:], t2[:], ALU.mult, ['kk', 't2'], ['kk'])
                for c in range(6):
                    self.TSC('dve', t1[:, c, :], aa[:, c, :], prm[:, l, P_KA + c:P_KA + c + 1], prm[:, l, P_OMKA + c:P_OMKA + c + 1], ALU.mult, ALU.add, [AA, 'prm', 't1'], ['t1'])
                self.TT('dve', kf[:], xs[:, 6:12, :], t1[:], ALU.mult, ['xs', 't1'], ['kf'])
                self.TT('dve', nb[:], kk[:], aa[:], ALU.mult, ['kk', AA], ['nb'])
                for c in range(6):
                    self.SCAN(cs[:, c, :], rst[:], sgw[:, c, :], 0.0, [SGW, 'rst' + mk], ['cs'])
                self.ACT(Gm[:], cs[:], AF.Exp, ['cs'], ['Gm'], scale=-C0)
                self.ACT(Gi[:], cs[:], AF.Exp, ['cs'], ['Gi'], scale=C0)
                self.TT('dve', t2[:], cs[:], sgw[:], ALU.subtract, ['cs', SGW], ['t2'])
                self.ACT(Gp[:], t2[:], AF.Exp, ['t2'], ['Gp'], scale=-C0)
                self.STT(AR[:, :, 0:128], kk[:], -1.0, Gp[:], ALU.mult, ALU.mult, ['kk', 'Gp'], ['AR0'])
                self.TT('pool', AR[:, :, 128:256], xs[:, 0:6, :], Gm[:], ALU.mult, ['xs', 'Gm'], ['AR1'])
                self.TT('dve', KT[:], kf[:], Gi[:], ALU.mult, ['kf', 'Gi'], ['KT'])
                self.TT('pool', BT[:], nb[:], Gi[:], ALU.mult, ['nb', 'Gi'], ['BT'])
                if self.rw_stop is not None and self.rw_stop < 10:
                    continue
                for (src_fn, sres, dst, dres) in ((lambda c: xs[:, 12 + c, :], 'xs', Vtm, 'Vtm'), (lambda c: AR[:, c, 0:128], 'AR0', cs[:].rearrange('p c t -> p (c t)'), 'cs'),
                                                  (lambda c: BT[:, c, :], 'BT', Btm, 'Btm'), (lambda c: KT[:, c, :], 'KT', Ktm, 'Ktm')):
                    for half in range(2):
                        ps, pn = self.psum()
                        cs_ = range(0, 4) if half == 0 else range(4, 6)
                        for c in cs_:
                            self.TR(ps[:, (c % 4) * 128:(c % 4 + 1) * 128], src_fn(c), [sres], [pn])
                        n = len(cs_)
                        self.CP('act' if half == 0 else 'dve', dst[:, cs_[0] * 128:(cs_[0] + n) * 128], ps[:, 0:n * 128], [pn], ['cs'] if dres == 'cs' else [dres + str(half)])
                VT_R = ['Vtm0', 'Vtm1']
                if self.rw_stop is not None and self.rw_stop < 11:
                    continue
                for h in range(12):
                    c, pb = h // 2, 64 * (h % 2)
                    ps, pn = self.psum()
                    self.MM(ps[:, 0:256], BT[pb:pb + 64, c, :], AR[pb:pb + 64, c, :], True, True, ['BT', 'AR0', 'AR1'], [pn])
                    self.MM(ps[:, 256:512], KT[pb:pb + 64, c, :], AR[pb:pb + 64, c, :], True, True, ['KT', 'AR0', 'AR1'], [pn])
                    self.TT('dve', PTm[0][:, h, :], ps[:, 0:128], mbk[:, 0:128], ALU.mult, [pn, 'mbk' + mk], ['PTm0_%d' % (h // 4)])
                    self.TT('dve', M4[:, h, :], ps[:, 128:512], mbk[:, 128:512], ALU.mult, [pn, 'mbk' + mk], ['M4_%d' % h])
                Pv = Pm[0][:].rearrange('p (c a) t -> p c a t', a=2)
                for h2 in range(2):
                    pb = 64 * h2
                    for (c0, c1) in ((0, 4), (4, 6)):
                        ps, pn = self.psum()
                        for c in range(c0, c1):
                            self.MM(ps[:, (c - c0) * 128:(c - c0 + 1) * 128], AR[pb:pb + 64, c, 0:128], BT[pb:pb + 64, c, :], True, True, ['AR0', 'BT'], [pn])
                        n = c1 - c0
                        self.TT('dve', Pv[:, c0:c1, h2, :], ps[:, 0:n * 128].rearrange('p (h t) -> p h t', t=128), ml4[:, 0:n * 128].rearrange('p (h t) -> p h t', t=128), ALU.mult,
                                [pn, 'ml4' + mk], sorted(set('Pm0_%d' % ((2 * c + h2) // 4) for c in range(c0, c1))))
                if self.rw_stop is not None and self.rw_stop < 12:
                    continue
                self.CP('act', X[0][:, :, 0:64], cs[:].rearrange('p c (a k) -> p (c a) k', k=64), ['cs'], ['X0_0', 'X0_1', 'X0_2'])
                for half in range(2):
                    ps, pn = self.psum()
                    hs = range(0, 8) if half == 0 else range(8, 12)
                    for h in hs:
                        self.MM(ps[:, (h % 8) * 64:(h % 8 + 1) * 64], M4[:, h, 128:256], Vtm[:, h * 64:(h + 1) * 64], True, True, ['M4_%d' % h] + VT_R, [pn])
                    n = len(hs)
                    self.CP('act', X[0][:, hs[0]:hs[0] + n, 64:128], ps[:, 0:n * 64].rearrange('p (h v) -> p h v', v=64), [pn], ['X0_0', 'X0_1'] if half == 0 else ['X0_2'])
                xres = {0: ['X0_0', 'X0_1', 'X0_2'], 1: ['X1_0', 'X1_1', 'X1_2']}
                if self.rw_stop is not None and self.rw_stop < 13:
                    continue
                NST = 3 if sample else 7
                for s_ in range(NST):
                    a_, b_ = s_ % 2, (s_ + 1) % 2
                    pres = ['Pm%d_%d' % (a_, q) for q in range(3)]
                    ptres = ['PTm%d_%d' % (a_, q) for q in range(3)]
                    for q in range(3):
                        ps, pn = self.psum()
                        for j in range(4):
                            h = q * 4 + j
                            self.MM(ps[:, j * 128:(j + 1) * 128], PTm[a_][:, h, :], X[a_][:, h, :], True, True, ptres + xres[a_], [pn])
                        self.TT('dve', X[b_][:, q * 4:(q + 1) * 4, :], ps[:].rearrange('p (h t) -> p h t', t=128), X[a_][:, q * 4:(q + 1) * 4, :], ALU.add,
                                [pn] + xres[a_], ['X%d_%d' % (b_, q)])
                    if s_ < NST - 1:
                        for q in range(3):
                            ps, pn = self.psum()
                            ps2, pn2 = self.psum()
                            for j in range(4):
                                h = q * 4 + j
                                self.MM(ps[:, j * 128:(j + 1) * 128], PTm[a_][:, h, :], Pm[a_][:, h, :], True, True, pres + ptres, [pn])
                                self.MM(ps2[:, j * 128:(j + 1) * 128], Pm[a_][:, h, :], PTm[a_][:, h, :], True, True, pres + ptres, [pn2])
                            self.CP('act', Pm[b_][:, q * 4:(q + 1) * 4, :], ps[:].rearrange('p (h t) -> p h t', t=128), [pn], ['Pm%d_%d' % (b_, q)])
                            self.CP('act', PTm[b_][:, q * 4:(q + 1) * 4, :], ps2[:].rearrange('p (h t) -> p h t', t=128), [pn2], ['PTm%d_%d' % (b_, q)])
                fin = NST % 2
                Xf = X[fin]
                xfr = xres[fin]
                if self.rw_stop is not None and self.rw_stop < 14:
                    continue
                self.CP('act', Gi[:].rearrange('p c (a k) -> p (c a) k', k=64), Xf[:, :, 0:64], xfr, ['Gi'])
                for half in range(2):
                    ps, pn = self.psum()
                    cs_ = range(0, 4) if half == 0 else range(4, 6)
                    for c in cs_:
                        self.TR(ps[:, (c % 4) * 128:(c % 4 + 1) * 128], Gi[:, c, :], ['Gi'], [pn])
                    n = len(cs_)
                    self.CP('act', Gp[:, cs_[0]:cs_[0] + n, :], ps[:, 0:n * 128].rearrange('p (c t) -> p c t', t=128), [pn], ['Gp'])
                gtr = ['Gp']
                if sample:
                    for s in range(16):
                        b = s % 2
                        self.dma(sin[b][:], I['state_rwkv'][l, s].rearrange('(c h2) v k -> v c h2 k', h2=2), [], ['sin0'])
                        ps, pn = self.psum()
                        for c in range(6):
                            self.TR(ps[:, c * 64:(c + 1) * 64], sin[b][:, c, :, :], ['sin0'], [pn], n=64)
                        self.CP('act' if s % 2 == 0 else 'dve', Hin[:, s, :, :], ps[:, 0:384].rearrange('p (c v) -> p c v', v=64), [pn], ['Hin%d' % s])
                    Hi = lambda sb: Hin[:, sb, :, :]
                    Ho = Hi
                    hir = lambda sb: 'Hin%d' % sb
                    hor = hir
                else:
                    Hi = lambda sb: Hc[:, 0, :, :]
                    Ho = Hi
                    hir = lambda sb: 'Hc'
                    hor = hir
                if self.rw_stop is not None and self.rw_stop < 15:
                    continue
                def inter(which):
                    psl = {0: self.psum(), 1: self.psum()}
                    for c in range(6):
                        if which == 'G':
                            src, sres = Gp[:, c, :], gtr
                        else:
                            src, sres = AR[:, c, 128:256], ['AR1']
                        if sample:
                            for sb in range(16):
                                self.TT('pool' if sb % 4 == 3 else 'dve', dd[:, sb, :], src, cm16[:, sb, :], ALU.mult, sres + ['cm16'], ['dd'])
                        for h2 in range(2):
                            pb = 64 * h2
                            ps, pn = psl[h2]
                            o = ps[:, c * 64:(c + 1) * 64]
                            for sb in range(nsub):
                                if sample:
                                    lh, lr = dd[pb:pb + 64, sb, :], ['dd']
                                else:
                                    lh, lr = src[pb:pb + 64, :], sres
                                self.MM(o, lh, Hi(sb)[pb:pb + 64, c, :], sb == 0, sb == nsub - 1, lr + [hir(sb)], [pn])
                    return psl
                Uv = sgw[:].rearrange('p c (a v) -> p c a v', a=2)
                Xv = Xf[:].rearrange('p (c a) t -> p c a t', a=2)
                psl = inter('G')
                for h2 in range(2):
                    ps, pn = psl[h2]
                    self.TT('dve', Uv[:, :, h2, :], ps[:, 0:384].rearrange('p (c v) -> p c v', v=64), Xv[:, :, h2, 64:128], ALU.add, [pn] + xfr, ['sgw'])
                ur = ['sgw']
                psl = inter('R')
                psO = [self.psum(), self.psum()]
                for h in range(12):
                    ps, pn = psO[h // 8]
                    o = ps[:, (h % 8) * 64:(h % 8 + 1) * 64]
                    self.MM(o, M4[:, h, 0:128], sgw[:].rearrange('p c t -> p (c t)')[:, h * 64:(h + 1) * 64], True, False, ['M4_%d' % h] + ur, [pn])
                    self.MM(o, M4[:, h, 256:384], Vtm[:, h * 64:(h + 1) * 64], False, True, ['M4_%d' % h] + VT_R, [pn])
                for half in range(2):
                    ps, pn = psO[half]
                    hs = range(0, 8) if half == 0 else range(8, 12)
                    n = len(hs)
                    self.CP('act', Osb[:, hs[0]:hs[0] + n, :], ps[:, 0:n * 64].rearrange('p (h v) -> p h v', v=64), [pn], ['Osb%d' % half])
                Ov = Osb[:].rearrange('p (c a) v -> p c a v', a=2)
                for h2 in range(2):
                    ps, pn = psl[h2]
                    self.TT('dve', Ov[:, :, h2, :], Ov[:, :, h2, :], ps[:, 0:384].rearrange('p (c v) -> p c v', v=64), ALU.add, [pn, 'Osb0', 'Osb1'], ['Osb0', 'Osb1'])
                osr = ['Osb0', 'Osb1']
                for h in range(12):
                    self.E('dve', (lambda e, h=h: e.bn_stats(bst[:, h, :], Osb[:, h, :])), osr, ['bst'])
                for h in range(12):
                    self.E('dve', (lambda e, h=h: e.bn_aggr(st[:, 2 * h:2 * h + 2], bst[:, h, :])), ['bst'], ['st'])
                stv = st[:, 0:24].rearrange('p (h a) -> p h a', a=2)
                self.ACT(st[:, 36:48], stv[:, :, 1], AF.Sqrt, ['st'], ['st'], bias=RW_EPS)
                self.RECIP(st[:, 36:48], st[:, 36:48], ['st'], ['st'])
                for h in range(12):
                    self.TSC('dve', t2[:, h // 2, (h % 2) * 64:(h % 2) * 64 + 64], Osb[:, h, :], st[:, 2 * h:2 * h + 1], st[:, 36 + h:37 + h], ALU.subtract, ALU.mult, osr + ['st'], ['t2'])
                for half in range(2):
                    ps, pn = self.psum()
                    cs_ = range(0, 4) if half == 0 else range(4, 6)
                    for c in cs_:
                        self.TR(ps[:, (c % 4) * 128:(c % 4 + 1) * 128], t2[:, c, :], ['t2'], [pn])
                    for c in cs_:
                        self.TSC('dve', cs[:, c, :], ps[:, (c % 4) * 128:(c % 4 + 1) * 128], prm[:, l, P_LNW + c:P_LNW + c + 1], prm[:, l, P_LNB + c:P_LNB + c + 1], ALU.mult, ALU.add, [pn, 'prm'], ['cs'])
                if self.rw_stop is not None and self.rw_stop < 17:
                    continue
                self.TT('pool', t1[:], xs[:, 0:6, :], kf[:], ALU.mult, ['xs', 'kf'], ['t1'])
                for c in range(6):
                    self.TSC('pool', t1[:, c, :], t1[:, c, :], prm[:, l, P_RK + c:P_RK + c + 1], None, ALU.mult, None, ['t1', 'prm'], ['t1'])
                for half in range(2):
                    ps, pn = self.psum()
                    cs_ = range(0, 4) if half == 0 else range(4, 6)
                    for c in cs_:
                        self.MM(ps[:, (c % 4) * 128:(c % 4 + 1) * 128], blk[:], t1[:, c, :], True, True, ['blk', 't1'], [pn])
                    n = len(cs_)
                    self.TT('dve', t2[:, cs_[0]:cs_[0] + n, :], ps[:, 0:n * 128].rearrange('p (c t) -> p c t', t=128), xs[:, 12 + cs_[0]:12 + cs_[0] + n, :], ALU.mult, [pn, 'xs'], ['t2'])
                self.TT('dve', cs[:], cs[:], t2[:], ALU.add, ['cs', 't2'], ['cs'])
                self.TT('dve', ybf[:], cs[:], gT[:], ALU.mult, ['cs', 'gT'], ['ybf'])
                self.dma(self.yT[0:768, t0:t0 + 128].rearrange('(c p) t -> p c t', p=128), ybf[:], ['ybf'], ['yT_rw'])
                if self.rw_stop is not None and self.rw_stop < 18:
                    continue
                for sb in range(nsub):
                    if sample:
                        HG = M4[:, sb % 2, :].rearrange('p (c v) -> p c v', v=64)
                        hgr = 'M4_%d' % (sb % 2)
                        gend = Gm[:, :, sb * 8 + 7:sb * 8 + 8].to_broadcast([128, 6, 64])
                        self.TT('dve', HG, Hin[:, sb, :, :], gend, ALU.mult, ['Hin%d' % sb, 'Gm'], [hgr])
                        self.TSC('dve', kk[:].rearrange('p c t -> p (c t)'), Btm[:], rm16[:, sb:sb + 1], None, ALU.mult, None, ['Btm0', 'Btm1', 'rm16'], ['kk'])
                        self.ACT(nb[:].rearrange('p c t -> p (c t)'), Ktm[:], AF.Copy, ['Ktm0', 'Ktm1', 'rm16'], ['nb'], scale=rm16[:, sb:sb + 1])
                        bl, kl, blr, klr = kk[:].rearrange('p c t -> p (c t)'), nb[:].rearrange('p c t -> p (c t)'), ['kk'], ['nb']
                        tend = sb * 8 + 7
                    else:
                        bl, kl, blr, klr = Btm[:], Ktm[:], ['Btm0', 'Btm1'], ['Ktm0', 'Ktm1']
                        tend = 127
                    for c in range(6):
                        ps, pn = self.psum()
                        self.MM(ps[:, 0:128], bl[:, c * 128:(c + 1) * 128], sgw[:].rearrange('p c t -> p (c t)')[:, c * 128:(c + 1) * 128], True, False, blr + ur, [pn])
                        self.MM(ps[:, 0:128], kl[:, c * 128:(c + 1) * 128], Vtm[:, c * 128:(c + 1) * 128], False, True, klr + VT_R, [pn])
                        for h2 in range(2):
                            pr = slice(64 * h2, 64 * h2 + 64)
                            if sample:
                                self.STT(Ho(sb)[pr, c, :], ps[pr, 64 * h2:64 * h2 + 64], Gm[pr, c, tend:tend + 1], HG[pr, c, :], ALU.mult, ALU.add, [pn, 'Gm', hgr], [hor(sb)])
                            else:
                                self.TT('dve', htmp[pr, :], ps[pr, 64 * h2:64 * h2 + 64], Hi(sb)[pr, c, :], ALU.add, [pn, hir(sb)], ['htmp'])
                                self.TSC('dve', Ho(sb)[pr, c, :], htmp[pr, :], Gm[pr, c, tend:tend + 1], None, ALU.mult, None, ['htmp', 'Gm'], [hor(sb)])
                if self.rw_stop is not None and self.rw_stop < 19:
                    continue
                last_prompt = (not sample) and ti == TP // 128 - 1
                if sample or last_prompt:
                    for sb in range(nsub):
                        b = sb % 2
                        ps, pn = self.psum()
                        ps2, pn2 = self.psum()
                        for c in range(4):
                            self.TR(ps[0:64, c * 128:(c + 1) * 128], Ho(sb)[:, c, :], [hor(sb)], [pn])
                        for c in range(4, 6):
                            self.TR(ps2[0:64, (c - 4) * 128:(c - 3) * 128], Ho(sb)[:, c, :], [hor(sb)], [pn2])
                        self.CP('act', sout[b][:, 0:4, :, :], ps[0:64, :].rearrange('p (c h k) -> p c h k', h=2, k=64), [pn], ['sin0'])
                        self.CP('dve', sout[b][:, 4:6, :, :], ps2[0:64, 0:256].rearrange('p (c h k) -> p c h k', h=2, k=64), [pn2], ['sin0'])
                        dst = (O_['s_rwkv'][l, sb] if sample else O_['p_rwkv'][l]).rearrange('(c h2) v k -> v c h2 k', h2=2)
                        self.dma(dst, sout[b][:], ['sin0'], [])
                    if sample:
                        for g in range(5):
                            a0 = 2 * (g % 2)
                            stg = M4[0:16, a0:a0 + 2, :].rearrange('p a b -> p (a b)')[:, 0:512]
                            sres = ['M4_%d' % a0, 'M4_%d' % (a0 + 1)]
                            ps, pn = self.psum()
                            for j in range(4):
                                self.TR(ps[0:16, j * 128:(j + 1) * 128], pv[:, g * 4 + j, :, 8], ['pbuf'], [pn])
                            self.CP('act', stg, ps[0:16, 0:512], [pn], sres)
                            self.dma(O_['s_rwkv_shift'][l, :, g * 512:(g + 1) * 512], stg, sres, [])
                    else:
                        self.dma(O_['p_rwkv_shift'][l].rearrange('(c p) -> p c', p=128), pbuf[:, :, 128], ['pbuf'], [], slow=True)
    def phase_hgrn(self, l, I, O_, side=None):
        nc = self.nc
        TP = self.TP
        prm = self.prm
        with contextlib.ExitStack() as es:
            A = lambda name, shape, dt=F32: es.enter_context(nc.sbuf_tensor('hg%d_%s' % (l, name), list(shape), dt))
            masks = {}
            for mk, ns in (('4', 4), ('16', 16)):
                up = A('up' + mk, [128, 128])
                rst = A('rst' + mk, [128, 128])
                cm = A('cm' + mk, [128, ns, 128])
                rm = A('rm' + mk, [128, ns])
                self.dma(up[:], self.cst['up_i' + mk], [], ['up' + mk])
                self.dma(rst[:], self.cst['rst' + mk], [], ['rst' + mk])
                self.dma(cm[:], self.cst['cm' + mk].rearrange('p (s t) -> p s t', s=ns), [], ['cm' + mk])
                self.dma(rm[:], self.cst['rm' + mk], [], ['rm' + mk])
                masks[mk] = (up, rst, cm, rm)
            pb = A('pb', [128, 24, 128])
            q = A('q', [128, 6, 128])
            fg = A('fg', [128, 6, 128])
            lf = A('lf', [128, 6, 128])
            kx = A('kx', [128, 6, 128])
            cs = A('cs', [128, 6, 128])
            Gm = A('Gm', [128, 6, 128])
            Gi = A('Gi', [128, 6, 128])
            QT = A('QT', [128, 6, 128])
            KT = A('KT', [128, 6, 128])
            sg = A('sg', [128, 6, 128])
            Vtm = A('Vtm', [128, 6, 128])
            Ktm = A('Ktm', [128, 6, 128])
            KM = A('KM', [128, 16, 128])
            QM = A('QM', [128, 16, 128])
            AT = A('AT', [128, 128])
            Hs = A('Hs', [128, 16, 128])
            Ho = A('Ho', [128, 16, 128])
            Hc = A('Hc', [128, 6, 128])
            htmp = A('htmp', [128, 128])
            Hs6 = A('Hs6', [128, 6, 5, 128])
            AT6 = A('AT6', [128, 6, 128])
            tmp6 = A('tmp6', [128, 6, 128])
            on6 = A('on6', [128, 6, 128])
            ybf6 = A('ybf6', [128, 6, 128], BF16)
            ss6 = A('ss6', [128, 24])
            junk = htmp
            ss = A('ss', [128, 4])
            on = A('on', [128, 128])
            ybf = A('ybf', [128, 128], BF16)
            self.MEMSET('dve', Hc[:], 0.0, ['Hc'])
            tiles = [('p', ti) for ti in range(TP // 128)] + [('s', 0)]
            for kind, ti in tiles:
                sample = kind == 's'
                t0 = TP if sample else ti * 128
                mk = '16' if sample else '4'
                nsub = 16 if sample else 4
                sub = 128 // nsub
                up, rst, cm, rm = masks[mk]
                self.dma(pb[:], self.projT[RW_COLS:RW_COLS + HG_COLS, t0:t0 + 128].rearrange('(c p) t -> p c t', p=128), ['projT'], ['pb'])
                self.ACT(q[:], pb[:, 0:6, :], AF.Silu, ['pb'], ['q'])
                self.ACT(sg[:], pb[:, 18:24, :], AF.Silu, ['pb'], ['sg'])
                self.ACT(fg[:], pb[:, 6:12, :], AF.Sigmoid, ['pb'], ['fg'])
                for h in range(6):
                    self.TSC('dve', fg[:, h, :], fg[:, h, :], prm[:, l, P_OMLB + h:P_OMLB + h + 1], prm[:, l, P_LB + h:P_LB + h + 1], ALU.mult, ALU.add, ['fg', 'prm'], ['fg'])
                self.ACT(lf[:], fg[:], AF.Ln, ['fg'], ['lf'])
                self.TSC('dve', kx[:], fg[:], -1.0, 1.0, ALU.mult, ALU.add, ['fg'], ['kx'])
                for h in range(6):
                    self.SCAN(cs[:, h, :], rst[:], lf[:, h, :], 0.0, ['lf', 'rst' + mk], ['cs'])
                self.TSC('dve', cs[:], cs[:], -80.0, None, ALU.max, None, ['cs'], ['cs'])
                self.ACT(Gm[:], cs[:], AF.Exp, ['cs'], ['Gm'])
                self.ACT(Gi[:], cs[:], AF.Exp, ['cs'], ['Gi'], scale=-1.0)
                self.TT('dve', QT[:], q[:], Gm[:], ALU.mult, ['q', 'Gm'], ['QT'])
                self.TT('dve', KT[:], kx[:], Gi[:], ALU.mult, ['kx', 'Gi'], ['KT'])
                for (src_fn, sres, dst, dres) in ((lambda h: pb[:, 12 + h, :], 'pb', Vtm, 'Vtm'), (lambda h: KT[:, h, :], 'KT', Ktm, 'Ktm')):
                    for half in range(2):
                        ps, pn = self.psum()
                        hs = range(0, 4) if half == 0 else range(4, 6)
                        for h in hs:
                            self.TR(ps[:, (h % 4) * 128:(h % 4 + 1) * 128], src_fn(h), [sres], [pn])
                        n = len(hs)
                        self.CP('act' if half == 0 else 'dve', dst[:, hs[0]:hs[0] + n, :], ps[:, 0:n * 128].rearrange('p (h t) -> p h t', t=128), [pn], [dres + str(half)])
                vr, kr = ['Vtm0', 'Vtm1'], ['Ktm0', 'Ktm1']
                if not sample:
                    for half in range(2):
                        ps, pn = self.psum()
                        hs = range(0, 4) if half == 0 else range(4, 6)
                        for h in hs:
                            self.MM(ps[:, (h % 4) * 128:(h % 4 + 1) * 128], KT[:, h, :], QT[:, h, :], True, True, ['KT', 'QT'], [pn])
                        n = len(hs)
                        self.TT('dve', AT6[:, hs[0]:hs[0] + n, :], ps[:, 0:n * 128].rearrange('p (h t) -> p h t', t=128),
                                up[:, None, :].to_broadcast([128, n, 128]) if False else up[:].rearrange('p (a t) -> p a t', a=1).to_broadcast([128, n, 128]), ALU.mult, [pn, 'up' + mk], ['AT6_%d' % half])
                    atr = ['AT6_0', 'AT6_1']
                    self.TSC('dve', KM[:, 0:6, :], Ktm[:], rm[:, 3:4], None, ALU.mult, None, kr + ['rm' + mk], ['KM3'])
                    self.TT('dve', QM[:, 0:6, :], QT[:], cm[:, 3:4, :].to_broadcast([128, 6, 128]), ALU.mult, ['QT', 'cm' + mk], ['QM3'])
                    self.CP('act', Hs6[:, :, 0, :], Hc[:], ['Hc'], ['Hs6_0'])
                    for sb in range(4):
                        pss = [self.psum(), self.psum()]
                        for h in range(6):
                            ps, pn = pss[h // 4]
                            o = ps[:, (h % 4) * 128:(h % 4 + 1) * 128]
                            if sb == 3:
                                self.MM(o, KM[:, h, :], Vtm[:, h, :], True, True, ['KM3'] + vr, [pn])
                            else:
                                pr = slice(sb * 32, (sb + 1) * 32)
                                self.MM(o, Ktm[pr, h, :], Vtm[pr, h, :], True, True, kr + vr, [pn])
                        for half in range(2):
                            ps, pn = pss[half]
                            hs = range(0, 4) if half == 0 else range(4, 6)
                            n = len(hs)
                            self.TT('dve', tmp6[:, hs[0]:hs[0] + n, :], ps[:, 0:n * 128].rearrange('p (h t) -> p h t', t=128), Hs6[:, hs[0]:hs[0] + n, sb, :], ALU.add,
                                    [pn, 'Hs6_%d' % sb], ['tmp6_%d' % half])
                        tend = sb * 32 + 31
                        self.TT('dve', Hs6[:, :, sb + 1, :], tmp6[:], Gm[:, :, tend:tend + 1].to_broadcast([128, 6, 128]), ALU.mult, ['tmp6_0', 'tmp6_1', 'Gm'], ['Hs6_%d' % (sb + 1)])
                    psO = [self.psum(), self.psum()]
                    for h in range(6):
                        ps, pn = psO[h // 4]
                        o = ps[:, (h % 4) * 128:(h % 4 + 1) * 128]
                        self.MM(o, QM[:, h, :], Hs6[:, h, 3, :], True, False, ['QM3', 'Hs6_3'], [pn], sgc=True)
                        for sb in range(3):
                            pr = slice(sb * 32, (sb + 1) * 32)
                            self.MM(ps[pr, (h % 4) * 128:(h % 4 + 1) * 128], QT[:, h, pr], Hs6[:, h, sb, :], True, False, ['QT', 'Hs6_%d' % sb], [pn], sgc=True)
                        self.MM(o, AT6[:, h, :], Vtm[:, h, :], False, True, atr + vr, [pn], sgc=True)
                    for h in range(6):
                        ps, pn = psO[h // 4]
                        self.ACT(junk[:], ps[:, (h % 4) * 128:(h % 4 + 1) * 128], AF.Square, [pn], ['htmp', 'ss6'], accum=ss6[:, h:h + 1])
                    self.ACT(ss6[:, 6:12], ss6[:, 0:6], AF.Sqrt, ['ss6'], ['ss6'], bias=EPS, scale=1.0 / 128)
                    self.RECIP(ss6[:, 12:18], ss6[:, 6:12], ['ss6'], ['ss6'])
                    for half in range(2):
                        ps, pn = psO[half]
                        hs = range(0, 4) if half == 0 else range(4, 6)
                        n = len(hs)
                        self.TT('dve', on6[:, hs[0]:hs[0] + n, :], ps[:, 0:n * 128].rearrange('p (h t) -> p h t', t=128),
                                ss6[:, 12 + hs[0]:12 + hs[0] + n].rearrange('p (h a) -> p h a', a=1).to_broadcast([128, n, 128]), ALU.mult, [pn, 'ss6'], ['on6_%d' % half])
                    for half in range(2):
                        ps2, pn2 = self.psum()
                        hs = range(0, 4) if half == 0 else range(4, 6)
                        for h in hs:
                            self.TR(ps2[:, (h % 4) * 128:(h % 4 + 1) * 128], on6[:, h, :], ['on6_0', 'on6_1'], [pn2])
                        for h in hs:
                            self.STT(ybf6[:, h, :], ps2[:, (h % 4) * 128:(h % 4 + 1) * 128], prm[:, l, P_HNW + h:P_HNW + h + 1], sg[:, h, :], ALU.mult, ALU.mult, [pn2, 'prm', 'sg'], ['ybf6'])
                    self.dma(self.yT[768:1536, t0:t0 + 128].rearrange('(h p) t -> p h t', p=128), ybf6[:], ['ybf6'], ['yT_hg'])
                    self.CP('act', Hc[:], Hs6[:, :, 4, :], ['Hs6_4'], ['Hc'])
                    if ti == TP // 128 - 1:
                        self.dma(O_['p_hgrn'][l].rearrange('h k v -> k h v'), Hs6[:, :, 4, :], ['Hs6_4'], [])
                    if side is not None and (ti % 2 == 1 or ti == 0):
                        next(side, None)
                    continue
                for h in range(6):
                    ps, pn = self.psum()
                    self.MM(ps[:, 0:128], KT[:, h, :], QT[:, h, :], True, True, ['KT', 'QT'], [pn])
                    self.TT('dve', AT[:], ps[:, 0:128], up[:], ALU.mult, [pn, 'up' + mk], ['AT'])
                    if sample:
                        self.dma(Hs[:, 0:16, :], I['state_hgrn'][l, :, h, :, :].rearrange('s k v -> k s v'), [], ['Hs%d' % sb for sb in range(16)])
                        hin = lambda sb: (Hs[:, sb, :], 'Hs%d' % sb)
                        hout = lambda sb: (Ho[:, sb, :], 'Ho%d' % sb)
                    else:
                        self.CP('pool', Hs[:, 0, :], Hc[:, h, :], ['Hc'], ['Hs0'])
                        hin = lambda sb: (Hs[:, sb, :], 'Hs%d' % sb)
                        hout = lambda sb: (Hs[:, sb + 1, :], 'Hs%d' % (sb + 1))
                    msk = (lambda sb: True) if sample else (lambda sb: sb == 3)
                    for sb in range(nsub):
                        if msk(sb):
                            self.TSC('dve', KM[:, sb, :], Ktm[:, h, :], rm[:, sb:sb + 1], None, ALU.mult, None, kr + ['rm' + mk], ['KM%d' % sb])
                            self.TT('pool' if (sample and sb % 4 == 3) else 'dve', QM[:, sb, :], QT[:, h, :], cm[:, sb, :], ALU.mult, ['QT', 'cm' + mk], ['QM%d' % sb])
                    for sb in range(nsub):
                        ps, pn = self.psum()
                        if msk(sb):
                            self.MM(ps[:, 0:128], KM[:, sb, :], Vtm[:, h, :], True, True, ['KM%d' % sb] + vr, [pn])
                        else:
                            pr = slice(sb * sub, (sb + 1) * sub)
                            self.MM(ps[:, 0:128], Ktm[pr, h, :], Vtm[pr, h, :], True, True, kr + vr, [pn])
                        hi, hir = hin(sb)
                        ho, hor = hout(sb)
                        tend = sb * sub + sub - 1
                        self.TT('dve', htmp[:], ps[:, 0:128], hi, ALU.add, [pn, hir], ['htmp'])
                        self.TSC('dve', ho, htmp[:], Gm[:, h, tend:tend + 1], None, ALU.mult, None, ['htmp', 'Gm'], [hor])
                    ps, pn = self.psum()
                    for sb in ([3, 0, 1, 2] if not sample else range(nsub)):
                        hi, hir = hin(sb)
                        if sample:
                            self.MM(ps[:, 0:128], QM[:, sb, :], hi, sb == 0, False, ['QM%d' % sb, hir], [pn])
                        elif sb == 3:
                            self.MM(ps[:, 0:128], QM[:, sb, :], hi, True, False, ['QM%d' % sb, hir], [pn], sgc=True)
                        else:
                            pr = slice(sb * sub, (sb + 1) * sub)
                            self.MM(ps[pr, 0:128], QT[:, h, pr], hi, True, False, ['QT', hir], [pn], sgc=True)
                    self.MM(ps[:, 0:128], AT[:], Vtm[:, h, :], False, True, ['AT'] + vr, [pn], sgc=not sample)
                    self.ACT(junk[:], ps[:, 0:128], AF.Square, [pn], ['htmp', 'ss'], accum=ss[:, 0:1])
                    self.ACT(ss[:, 1:2], ss[:, 0:1], AF.Sqrt, ['ss'], ['ss'], bias=EPS, scale=1.0 / 128)
                    self.RECIP(ss[:, 2:3], ss[:, 1:2], ['ss'], ['ss'])
                    self.TSC('dve', on[:], ps[:, 0:128], ss[:, 2:3], None, ALU.mult, None, [pn, 'ss'], ['on'])
                    ps2, pn2 = self.psum()
                    self.TR(ps2[:, 0:128], on[:], ['on'], [pn2])
                    self.STT(ybf[:], ps2[:, 0:128], prm[:, l, P_HNW + h:P_HNW + h + 1], sg[:, h, :], ALU.mult, ALU.mult, [pn2, 'prm', 'sg'], ['ybf'])
                    self.dma(self.yT[768 + h * 128:768 + (h + 1) * 128, t0:t0 + 128], ybf[:], ['ybf'], ['yT_hg'])
                    if sample:
                        self.dma(O_['s_hgrn'][l, :, h, :, :].rearrange('s k v -> k s v'), Ho[:], ['Ho%d' % sb for sb in range(16)], [])
                    else:
                        self.CP('pool', Hc[:, h, :], Hs[:, nsub, :], ['Hs%d' % nsub], ['Hc'])
                        if ti == TP // 128 - 1:
                            self.dma(O_['p_hgrn'][l, h, :, :], Hs[:, nsub, :], ['Hs%d' % nsub], [])
                if side is not None and (ti % 2 == 1 or sample or ti == 0):
                    next(side, None)
            if side is not None:
                for _ in side:
                    pass

    def phase_lru(self, l, I, O_):
        for _ in self.lru_gen(l, I, O_):
            pass

    def lru_gen(self, l, I, O_):
        nc = self.nc
        TP = self.TP
        prm = self.prm
        base = RW_COLS + HG_COLS
        with contextlib.ExitStack() as es:
            A = lambda name, shape, dt=F32: es.enter_context(nc.sbuf_tensor('lr%d_%s' % (l, name), list(shape), dt))
            wabd = A('wabd', [128, 4, 128])
            wxbd = A('wxbd', [128, 4, 128])
            self.MEMSET('dve', wabd[:], 0.0, ['wabd'])
            self.MEMSET('dve', wxbd[:], 0.0, ['wxbd'])
            for blk in range(8):
                pr = slice(64 * (blk % 2), 64 * (blk % 2) + 64)
                self.dma(wabd[pr, blk // 2, 64 * (blk % 2):64 * (blk % 2) + 64], I['rglru_wa'][l, blk], [], ['wabd'])
                self.dma(wxbd[pr, blk // 2, 64 * (blk % 2):64 * (blk % 2) + 64], I['rglru_wx'][l, blk], [], ['wxbd'])
            TM = max(TP, 128)
            xpb = [A('xp_p%d' % c, [128, TP + 3]) for c in range(2)]
            gtb = [A('gt_p%d' % c, [128, TP]) for c in range(2)]
            xps = {('p', c): xpb[c % 2] for c in range(4)}
            gts = {('p', c): gtb[c % 2] for c in range(4)}
            for c in range(4):
                xps[('s', c)] = A('xp_s%d' % c, [128, 16 * 11])
                gts[('s', c)] = A('gt_s%d' % c, [128, 128])
            h0s = [A('h0_%d' % c, [128, 16]) for c in range(4)]
            xc = A('xc', [128, TM])
            rr = A('rr', [128, TM])
            ig = A('ig', [128, TM])
            aa = A('aa', [128, TM])
            uu = A('uu', [128, TM])
            hh = xc
            ybf = aa[:].bitcast(BF16)
            def prefetch(kind, c):
                    sample = kind == 's'
                    nseq, T, t0 = (16, 8, TP) if sample else (1, TP, 0)
                    N = nseq * T
                    xp, gt, h0 = xps[(kind, c)], gts[(kind, c)], h0s[c]
                    xn, gn_ = 'xp%s%d' % (kind, c % 2 if kind == 'p' else c), 'gt%s%d' % (kind, c % 2 if kind == 'p' else c)
                    xpv = xp[:, 0:nseq * (T + 3)].rearrange('p (s t) -> p s t', t=T + 3)
                    if sample:
                        s48 = rr[0:48, 0:128]
                        s16 = rr[0:16, 128:256]
                        self.dma(s48, I['cache_rglru_conv'][l, :, :, c * 128:(c + 1) * 128].rearrange('s j p -> (s j) p'), [], ['rr'])
                        self.dma(s16, I['state_rglru'][l, :, c * 128:(c + 1) * 128], [], ['rr'])
                        ps, pn = self.psum()
                        self.TR(ps[:, 0:48], s48, ['rr'], [pn], n=48)
                        self.TR(ps[:, 64:80], s16, ['rr'], [pn], n=16)
                        self.CP('dve', xpv[:, :, 0:3], ps[:, 0:48].rearrange('p (s j) -> p s j', j=3), [pn], [xn])
                        self.CP('dve', h0[:], ps[:, 64:80], [pn], ['h0_%d' % c])
                    else:
                        self.MEMSET('pool', xpv[:, :, 0:3], 0.0, [xn])
                    r0 = base + c * 128
                    self.dma(xpv[:, :, 3:T + 3], self.projT[r0:r0 + 128, t0:t0 + N].rearrange('p (s t) -> p s t', t=T), ['projT'], [xn])
                    self.dma(gt[:, 0:N], self.projT[r0 + 512:r0 + 640, t0:t0 + N], ['projT'], [gn_])
            for c in range(2):
                prefetch('p', c)
            for c in range(4):
                prefetch('s', c)
            yield
            for kind in ('p', 's'):
                sample = kind == 's'
                nseq, T, t0 = (16, 8, TP) if sample else (1, TP, 0)
                N = nseq * T
                for c in range(4):
                    xp, gt, h0 = xps[(kind, c)], gts[(kind, c)], h0s[c]
                    xn, gn_ = 'xp%s%d' % (kind, c % 2 if kind == 'p' else c), 'gt%s%d' % (kind, c % 2 if kind == 'p' else c)
                    xpv = xp[:, 0:nseq * (T + 3)].rearrange('p (s t) -> p s t', t=T + 3)
                    v3 = lambda ap: ap[:, 0:N].rearrange('p (s t) -> p s t', t=T)
                    r0 = base + c * 128
                    cw = lambda j: prm[:, l, P_CW + 4 * j + c:P_CW + 4 * j + c + 1]
                    self.TSC('dve', v3(xc), xpv[:, :, 0:T], cw(0), prm[:, l, P_CB + c:P_CB + c + 1], ALU.mult, ALU.add, [xn, 'prm'], ['xc'])
                    for j in range(1, 4):
                        self.STT(v3(xc), xpv[:, :, j:j + T], cw(j), v3(xc), ALU.mult, ALU.add, [xn, 'prm', 'xc'], ['xc'])
                    g0 = 0
                    while g0 < N:
                        gn = min(512, N - g0)
                        ps, pn = self.psum()
                        self.MM(ps[:, 0:gn], wabd[:, c, :], xc[:, g0:g0 + gn], True, True, ['wabd', 'xc'], [pn])
                        self.ACT(rr[:, g0:g0 + gn], ps[:, 0:gn], AF.Sigmoid, [pn, 'prm'], ['rr'], bias=prm[:, l, P_BA + c:P_BA + c + 1])
                        ps, pn = self.psum()
                        self.MM(ps[:, 0:gn], wxbd[:, c, :], xc[:, g0:g0 + gn], True, True, ['wxbd', 'xc'], [pn])
                        self.ACT(ig[:, g0:g0 + gn], ps[:, 0:gn], AF.Sigmoid, [pn, 'prm'], ['ig'], bias=prm[:, l, P_BX + c:P_BX + c + 1])
                        g0 += gn
                    self.ACT(aa[:, 0:N], rr[:, 0:N], AF.Exp, ['rr', 'prm'], ['aa'], scale=prm[:, l, P_C8 + c:P_C8 + c + 1])
                    self.ACT(uu[:, 0:N], rr[:, 0:N], AF.Exp, ['rr', 'prm'], ['uu'], scale=prm[:, l, P_2C8 + c:P_2C8 + c + 1])
                    self.ACT(uu[:, 0:N], uu[:, 0:N], AF.Sqrt, ['uu'], ['uu'], bias=1.0, scale=-1.0)
                    self.TT('pool', ig[:, 0:N], ig[:, 0:N], xc[:, 0:N], ALU.mult, ['ig', 'xc'], ['ig'])
                    self.TT('dve', uu[:, 0:N], uu[:, 0:N], ig[:, 0:N], ALU.mult, ['uu', 'ig'], ['uu'])
                    if sample:
                        self.TT('dve', h0[:], h0[:], v3(aa)[:, :, 0], ALU.mult, ['h0_%d' % c, 'aa'], ['h0_%d' % c])
                        self.TT('dve', v3(uu)[:, :, 0], v3(uu)[:, :, 0], h0[:], ALU.add, ['uu', 'h0_%d' % c], ['uu'])
                        self.MEMSET('dve', v3(aa)[:, :, 0:1], 0.0, ['aa'])
                    self.SCAN2(hh[:, 0:N], aa[:, 0:N], uu[:, 0:N], ['aa', 'uu'], ['xc'])
                    self.TT('pool', rr[:, 0:N], gt[:, 0:N], gt[:, 0:N], ALU.mult, [gn_, 'rr'], ['rr'])
                    self.TSC('pool', rr[:, 0:N], rr[:, 0:N], 0.044715, 1.0, ALU.mult, ALU.add, ['rr'], ['rr'])
                    self.TT('pool', rr[:, 0:N], rr[:, 0:N], gt[:, 0:N], ALU.mult, ['rr', gn_], ['rr'])
                    self.ACT(rr[:, 0:N], rr[:, 0:N], AF.Sigmoid, ['rr'], ['rr'], scale=1.5957691216057308)
                    self.TT('dve', ig[:, 0:N], hh[:, 0:N], gt[:, 0:N], ALU.mult, ['xc', gn_, 'ig'], ['ig'])
                    self.TT('dve', ybf[:, 0:N], ig[:, 0:N], rr[:, 0:N], ALU.mult, ['ig', 'rr'], ['aa'])
                    self.dma(self.yT[1536 + c * 128:1536 + (c + 1) * 128, t0:t0 + N], ybf[:, 0:N], ['aa'], ['yT_lru'])
                    if sample:
                        self.CP('dve', rr[:, 0:48].rearrange('p (s j) -> p s j', j=3), xpv[:, :, T:T + 3], [xn, 'rr'], ['rr'])
                        self.CP('dve', rr[:, 64:80], v3(hh)[:, :, T - 1], ['xc', 'rr'], ['rr'])
                        ps, pn = self.psum()
                        self.TR(ps[0:48, 0:128], rr[:, 0:48], ['rr'], [pn])
                        ps2, pn2 = self.psum()
                        self.TR(ps2[0:16, 0:128], rr[:, 64:80], ['rr'], [pn2])
                        self.CP('act', ig[0:48, 0:128], ps[0:48, 0:128], [pn, 'ig'], ['ig'])
                        self.CP('act', ig[0:16, 128:256], ps2[0:16, 0:128], [pn2, 'ig'], ['ig'])
                        self.dma(O_['s_rglru_conv'][l, :, :, c * 128:(c + 1) * 128].rearrange('s j p -> (s j) p'), ig[0:48, 0:128], ['ig'], [])
                        self.dma(O_['s_rglru'][l, :, c * 128:(c + 1) * 128], ig[0:16, 128:256], ['ig'], [])
                    else:
                        self.dma(O_['p_rglru_conv'][l, :, c * 128:(c + 1) * 128].rearrange('j p -> p j'), xpv[:, 0, T:T + 3], [xn], [], slow=True)
                        self.dma(O_['p_rglru'][l, c * 128:(c + 1) * 128].rearrange('(p o) -> p o', o=1), hh[:, T - 1:T], ['xc'], [], slow=True)
                    if kind == 'p' and c < 2:
                        prefetch('p', c + 2)
                    yield

    def SCAN2(self, out, d0, d1, r, w):
        self.E('dve', lambda e: e.tensor_tensor_scan(out, d0, d1, 0.0, ALU.mult, ALU.add), r, w)
    def gemm(self, tag, A, KC, blocks, rhs_fn, rhs_res, epilogue, cbw=256, tgs=None, bufs=None, single=False):
        tgs = self.tg if tgs is None else tgs
        if bufs is None:
            wst = [A('%s_wst%d' % (tag, i), [128, KC, cbw]) for i in range(1 if single else 2)]
            if single:
                wst = wst * 2
            wbf = [A('%s_wbf%d' % (tag, i), [128, KC, cbw], BF16) for i in range(2)]
        else:
            wst, wbf, tag = bufs

        sb_ = (lambda b: 0) if single else (lambda b: b)
        ka = (KC * 7) // 16
        kparts = [(0, ka), (ka, 2 * ka), (2 * ka, KC)]
        kp_of = lambda kc: 0 if kc < ka else (1 if kc < 2 * ka else 2)

        def load_dma(bi):
            b = bi % 2
            off = 0
            for ap, w in blocks[bi]:
                self.dma(wst[b][:, :, off:off + w], ap.rearrange('(c p) n -> p c n', p=128), [], ['%s_wst%d' % (tag, sb_(b))])
                off += w
            return off

        def load_cast(bi, off):
            b = bi % 2
            for p, (k0, k1) in enumerate(kparts):
                self.CP(('dve', 'act', 'pool')[p], wbf[b][:, k0:k1, 0:off], wst[b][:, k0:k1, 0:off], ['%s_wst%d' % (tag, sb_(b))], ['%s_wbf%d_%d' % (tag, b, p)])

        widths = {0: load_dma(0)}
        load_cast(0, widths[0])
        for bi in range(len(blocks)):
            if bi + 1 < len(blocks):
                widths[bi + 1] = load_dma(bi + 1)
            b = bi % 2
            nu = widths[bi] // 128
            work = [(u, gi, t0, tn) for u in range(nu) for gi, (t0, tn) in enumerate(tgs)]
            cast_at = (len(work) * 3) // 5
            for wi, (u, gi, t0, tn) in enumerate(work):
                if wi == cast_at and bi + 1 < len(blocks):
                    load_cast(bi + 1, widths[bi + 1])
                ps, pn = self.psum()
                for kc in range(KC):
                    self.MM(ps[:, 0:tn], wbf[b][:, kc, u * 128:(u + 1) * 128], rhs_fn(kc, t0, tn), kc == 0, kc == KC - 1,
                            ['%s_wbf%d_%d' % (tag, b, kp_of(kc))] + rhs_res, [pn])
                epilogue(bi, u, gi, t0, tn, ps, pn)

    def phase_inproj(self, l, actT, I, A):
        st = [A('ip_st%d_%d' % (l, i), [128, 512]) for i in range(4)]
        w_in = I['w_in']
        blocks = [[(w_in[l, :, c0:c0 + 512], 512)] for c0 in range(0, IN_COLS, 512)]
        self._k = 0

        def epi(bi, u, gi, t0, tn, ps, pn):
            k = self._k % 4
            self._k += 1
            c0 = bi * 512 + u * 128
            self.CP('act' if k % 2 == 0 else 'dve', st[k][:, 0:tn], ps[:, 0:tn], [pn], ['ip_st%d' % k])
            self.dma(self.projT[c0:c0 + 128, t0:t0 + tn], st[k][:, 0:tn], ['ip_st%d' % k], ['projT'])

        self.gemm('ip%d' % l, A, 16, blocks, lambda kc, t0, tn: actT[:, kc, t0:t0 + tn], ['actT'], epi, cbw=512)

    def residual_epi(self, A, tag):
        xs = [A('%s_x%d' % (tag, i), [128, 512]) for i in range(4)]
        self._k = 0

        def epi(c0, t0, tn, ps, pn):
            k = self._k % 4
            self._k += 1
            rn = '%s_x%d' % (tag, k)
            self.dma(xs[k][:, 0:tn], self.xT[c0:c0 + 128, t0:t0 + tn], ['xT'], [rn])
            self.TT('dve', xs[k][:, 0:tn], xs[k][:, 0:tn], ps[:, 0:tn], ALU.add, [rn, pn], [rn])
            self.dma(self.xT[c0:c0 + 128, t0:t0 + tn], xs[k][:, 0:tn], [rn], ['xT'])
        return epi

    def phase_outproj(self, l, actT, I, A):
        self.dma(actT[:], self.yT.rearrange('(c p) t -> p c t', p=128), ['yT_rw', 'yT_hg', 'yT_lru'], ['actT'])
        w = I['w_out']
        blocks = [[(w[l, :, c0:c0 + 512], 512)] for c0 in range(0, D, 512)]
        repi = self.residual_epi(A, 'op%d' % l)
        self.gemm('op%d' % l, A, 16, blocks, lambda kc, t0, tn: actT[:, kc, t0:t0 + tn], ['actT'],
                  lambda bi, u, gi, t0, tn, ps, pn: repi(bi * 512 + u * 128, t0, tn, ps, pn), cbw=512)

    def phase_ffn_up(self, l, actT, I, O_, A):
        TP, NT = self.TP, self.NT
        fp = self.fprm
        w = I['ffn_w_up']
        blocks = [[(w[l, :, j * 256:(j + 1) * 256], 256), (w[l, :, DFF + j * 256:DFF + (j + 1) * 256], 256)] for j in range(DFF // 256)]
        gp = [A('fu%d_gp%d' % (l, i), [128, TP + 2]) for i in range(2)]
        gs = [A('fu%d_gs%d' % (l, i), [128, 16, 10]) for i in range(2)]
        vb = [A('fu%d_vb%d' % (l, i), [128, NT]) for i in range(2)]
        cv = A('fu%d_cv' % l, [128, NT])
        hb = [A('fu%d_hb%d' % (l, i), [128, NT], BF16) for i in range(2)]
        for i in range(2):
            self.MEMSET('pool', gp[i][:, 0:2], 0.0, ['gp%d' % i])
        ng = len(self.tg)
        CI = [A('fu%d_ci%d' % (l, i), [32, 128]) for i in range(2)]
        OS = [A('fu%d_os%d' % (l, i), [34, 128]) for i in range(2)]
        tmpo = A('fu%d_tmpo' % l, [128, 34])
        cin = I['cache_ffn_conv'][l].rearrange('s j c -> (s j) c')
        for b0 in range(2):
            self.dma(CI[b0][:], cin[:, b0 * 128:(b0 + 1) * 128], [], ['fu_ci%d' % b0])

        def epi(bi, u, gi, t0, tn, ps, pn):
            b = u % 2
            blk = 2 * bi + b
            if u < 2:
                if gi == 0:
                    ps2, pn2 = self.psum()
                    self.TR(ps2[:, 0:32], CI[b][:], ['fu_ci%d' % b], [pn2], n=32)
                    self.CP('dve', gs[b][:, :, 0:2], ps2[:, 0:32].rearrange('p (s j) -> p s j', j=2), [pn2], ['gs%d' % b])
                if t0 < TP:
                    self.CP('act', gp[b][:, 2 + t0:2 + t0 + tn], ps[:, 0:tn], [pn], ['gp%d' % b])
                else:
                    self.CP('act', gs[b][:, :, 2:10], ps[:, 0:tn].rearrange('p (s t) -> p s t', t=8), [pn], ['gs%d' % b])
            else:
                self.CP('act' if gi % 2 == 0 else 'dve', vb[b][:, t0:t0 + tn], ps[:, 0:tn], [pn], ['vb%d' % b])
                if gi == 0 and bi + 1 < len(blocks):
                    nblk = 2 * (bi + 1) + b
                    self.dma(CI[b][:], cin[:, nblk * 128:(nblk + 1) * 128], [], ['fu_ci%d' % b])
                if gi == ng - 1:
                    self.CP('dve', tmpo[:, 0:32].rearrange('p (s j) -> p s j', j=2), gs[b][:, :, 8:10], ['gs%d' % b], ['fu_tmpo'])
                    self.CP('dve', tmpo[:, 32:34], gp[b][:, TP:TP + 2], ['gp%d' % b], ['fu_tmpo'])
                    ps2, pn2 = self.psum()
                    self.TR(ps2[0:34, 0:128], tmpo[:], ['fu_tmpo'], [pn2])
                    self.CP('act', OS[b][:], ps2[0:34, 0:128], [pn2], ['fu_os%d' % b])
                    self.dma(O_['s_ffn_conv'][l].rearrange('s j c -> (s j) c')[:, blk * 128:(blk + 1) * 128], OS[b][0:32, :], ['fu_os%d' % b], [])
                    self.dma(O_['p_ffn_conv'][l][:, blk * 128:(blk + 1) * 128], OS[b][32:34, :], ['fu_os%d' % b], [])
                    w_ = lambda j: fp[:, l, j * 44 + blk:j * 44 + blk + 1]
                    bb = fp[:, l, 132 + blk:133 + blk]
                    cvs = cv[:, TP:NT].rearrange('p (s t) -> p s t', t=8)
                    self.TSC('dve', cv[:, 0:TP], gp[b][:, 0:TP], w_(0), bb, ALU.mult, ALU.add, ['gp%d' % b, 'fprm'], ['cv'])
                    self.STT(cv[:, 0:TP], gp[b][:, 1:TP + 1], w_(1), cv[:, 0:TP], ALU.mult, ALU.add, ['gp%d' % b, 'fprm', 'cv'], ['cv'])
                    self.STT(cv[:, 0:TP], gp[b][:, 2:TP + 2], w_(2), cv[:, 0:TP], ALU.mult, ALU.add, ['gp%d' % b, 'fprm', 'cv'], ['cv'])
                    self.TSC('dve', cvs, gs[b][:, :, 0:8], w_(0), bb, ALU.mult, ALU.add, ['gs%d' % b, 'fprm', 'cv'], ['cv'])
                    self.STT(cvs, gs[b][:, :, 1:9], w_(1), cvs, ALU.mult, ALU.add, ['gs%d' % b, 'fprm', 'cv'], ['cv'])
                    self.STT(cvs, gs[b][:, :, 2:10], w_(2), cvs, ALU.mult, ALU.add, ['gs%d' % b, 'fprm', 'cv'], ['cv'])
                    self.ACT(cv[:], cv[:], AF.Silu, ['cv'], ['cv'])
                    self.TT('dve', hb[b][:], cv[:], vb[b][:], ALU.mult, ['cv', 'vb%d' % b], ['hb%d' % b])
                    self.dma(self.hT[blk * 128:(blk + 1) * 128, :], hb[b][:], ['hb%d' % b], ['hT'])

        self.gemm('fu%d' % l, A, 16, blocks, lambda kc, t0, tn: actT[:, kc, t0:t0 + tn], ['actT'], epi, cbw=512, single=True)

    def phase_ffn_down(self, l, I, A):
        NT = self.NT
        w = I['ffn_w_down']
        KC = DFF // 128
        parts = [self.tg[0:2], self.tg[2:]] if len(self.tg) > 2 else [self.tg]
        nmax = max(sum(n for _, n in p) for p in parts if p)
        hres = A('fd%d_h' % l, [128, KC, nmax], BF16)
        repi = self.residual_epi(A, 'fd%d' % l)
        blocks = [[(w[l, :, c0:c0 + 256], 256)] for c0 in range(0, D, 256)]
        tag = 'fd%d' % l
        bufs = ([A('%s_wst0' % tag, [128, KC, 256])] * 2, [A('%s_wbf%d' % (tag, i), [128, KC, 256], BF16) for i in range(2)], tag)
        for pi, part in enumerate(parts):
            if not part:
                continue
            s0 = part[0][0]
            n = sum(nn for _, nn in part)
            self.dma(hres[:, :, 0:n], self.hT[:, s0:s0 + n].rearrange('(c p) t -> p c t', p=128), ['hT'], ['hres'])
            self.gemm('fd%d_%d' % (l, pi), A, KC, blocks, lambda kc, t0, tn, s0=s0: hres[:, kc, t0 - s0:t0 - s0 + tn], ['hres'],
                      lambda bi, u, gi, t0, tn, ps, pn: repi(bi * 256 + u * 128, t0, tn, ps, pn), cbw=256, tgs=part, bufs=bufs, single=True)

    def phase_final(self, y_out):
        nc = self.nc
        with contextlib.ExitStack() as es:
            A = lambda name, shape, dt=F32: es.enter_context(nc.sbuf_tensor('fin_' + name, list(shape), dt))
            xg = [A('xg%d' % i, [128, 16, 512]) for i in range(2)]
            sqs = [A('sq%d' % i, [128, 16, 128], BF16) for i in range(2)]
            sds = [A('sd%d' % i, [128, 128]) for i in range(2)]
            rss = [A('rs%d' % i, [128, 128]) for i in range(2)]
            hns = [A('hn%d' % i, [128, 16, 128]) for i in range(2)]
            yo = [A('yo%d' % i, [128, D]) for i in range(2)]
            k = 0
            for gi, (g0, gn) in enumerate(self.tg):
                gb = gi % 2
                xn = 'fxg%d' % gb
                self.dma(xg[gb][:, :, 0:gn], self.xT[:, g0:g0 + gn].rearrange('(c p) t -> p c t', p=128), ['xT'], [xn])
                for j in range(gn // 128):
                    b = k % 2
                    k += 1
                    sq, sd, rs, hn = sqs[b], sds[b], rss[b], hns[b]
                    t0 = g0 + j * 128
                    xv = xg[gb][:, :, j * 128:(j + 1) * 128]
                    self.ACT(sq[:], xv, AF.Square, [xn], ['fsq%d' % b])
                    ps, pn = self.psum()
                    for kc in range(16):
                        self.MM(ps[:, 0:128], self.ones_bf[:], sq[:, kc, :], kc == 0, kc == 15, ['fsq%d' % b, 'ones_bf'], [pn])
                    self.ACT(sd[:], ps[:, 0:128], AF.Sqrt, [pn], ['fsd%d' % b], bias=EPS, scale=1.0 / D)
                    self.RECIP(rs[:], sd[:], ['fsd%d' % b], ['frs%d' % b])
                    for kc in range(16):
                        self.STT(hn[:, kc, :], xg[gb][:, kc, j * 128:(j + 1) * 128], self.prmf[:, kc:kc + 1], rs[:], ALU.mult, ALU.mult, [xn, 'frs%d' % b, 'prmf'], ['fhn%d' % b])
                    for q in range(4):
                        ps, pn = self.psum()
                        for jj in range(4):
                            kc = q * 4 + jj
                            self.TR(ps[:, jj * 128:(jj + 1) * 128], hn[:, kc, :], ['fhn%d' % b], [pn])
                        self.CP('act' if q % 2 == 0 else 'dve', yo[b][:, q * 512:(q + 1) * 512], ps[:], [pn], ['fyo%d_%d' % (b, q)])
                    self.dma(y_out[t0:t0 + 128, :], yo[b][:], ['fyo%d_%d' % (b, q) for q in range(4)], [])


_CACHE = {}
TP_FULL = 2048
W_NAMES = ['norm_mix', 'w_in', 'rwkv_mu', 'rwkv_w0', 'rwkv_w2', 'rwkv_a0', 'rwkv_a2', 'rwkv_g2', 'rwkv_k_k', 'rwkv_k_a',
           'rwkv_r_k', 'rwkv_ln_w', 'rwkv_ln_b', 'hgrn_lb_logits', 'hgrn_norm_w', 'rglru_conv_w', 'rglru_conv_b', 'rglru_wa',
           'rglru_ba', 'rglru_wx', 'rglru_bx', 'rglru_lambda', 'w_out', 'norm_ffn', 'ffn_w_up', 'ffn_conv_w', 'ffn_conv_b',
           'ffn_w_down', 'norm_final']
S_NAMES = ['state_rwkv', 'state_rwkv_shift', 'state_hgrn', 'state_rglru', 'cache_rglru_conv', 'cache_ffn_conv']
O_NAMES = ['rwkv', 'rwkv_shift', 'hgrn', 'rglru', 'rglru_conv', 'ffn_conv']


def kernel(**inputs):
    if 'nc' not in _CACHE:
        b = Builder(TP_FULL, DEPTH)
        _CACHE['nc'] = b.build()
        _CACHE['b'] = b
    nc = _CACHE['nc']
    f32 = lambda a: np.ascontiguousarray(np.asarray(a, dtype=np.float32))
    consts = make_consts()
    shared = {k: f32(inputs[k]) for k in W_NAMES}
    for k, v in consts.items():
        shared['c_' + k] = v
    xp = f32(inputs['x_prompt'])
    xs = f32(inputs['x_sample'])
    states = {k: f32(inputs[k]) for k in S_NAMES}
    ncore = 8
    in_maps = []
    for i in range(ncore):
        m = dict(shared)
        m['x'] = np.ascontiguousarray(np.concatenate([xp[i % 4], xs[NSEQ * i:NSEQ * (i + 1)].reshape(NSEQ * TS, D)], axis=0))
        for k in S_NAMES:
            m[k] = np.ascontiguousarray(states[k][:, NSEQ * i:NSEQ * (i + 1)])
        in_maps.append(m)
    res = run_bass_kernel_spmd(nc, in_maps, core_ids=list(range(ncore)))
    R = res.results
    y_prompt = np.stack([np.asarray(R[b]['y'])[:TP_FULL] for b in range(4)], axis=0).astype(np.float32)
    y_sample = np.concatenate([np.asarray(R[i]['y'])[TP_FULL:].reshape(NSEQ, TS, D) for i in range(ncore)], axis=0).astype(np.float32)
    outs = [y_prompt, y_sample]
    for n in O_NAMES:
        outs.append(np.stack([np.asarray(R[b]['p_' + n]) for b in range(4)], axis=1).astype(np.float32))
    for n in O_NAMES:
        outs.append(np.concatenate([np.asarray(R[i]['s_' + n]) for i in range(ncore)], axis=1).astype(np.float32))
    return tuple(outs)
```

```python
import contextlib
import numpy as np
import concourse.bass as bass
import concourse.mybir as mybir
from concourse.bass_utils import run_bass_kernel_spmd

F32 = mybir.dt.float32
BF16 = mybir.dt.bfloat16
AF = mybir.ActivationFunctionType
ALU = mybir.AluOpType
AX = mybir.AxisListType

D = 2048
DEPTH = 2
NSEQ = 16
TS = 8
RW_COLS = 2560
HG_COLS = 3072
LRU_COLS = 1024
IN_COLS = 6656
DFF = 5632
EPS = 1e-6
RW_EPS = 64e-5
ENG = ('pe', 'act', 'dve', 'pool', 'sp')
NDSEM = 8
NDT = BF16
P_NM, P_NF, P_MU, P_W0, P_A0, P_KK, P_KA, P_OMKA, P_RK, P_LNW, P_LNB = 0, 16, 32, 52, 58, 64, 70, 76, 82, 88, 94
P_LB, P_OMLB, P_HNW, P_CW, P_CB, P_BA, P_BX, P_LAM, P_C8, P_2C8 = 100, 106, 112, 118, 134, 138, 142, 146, 150, 154


class Sched:
    def __init__(self, nc):
        self.nc = nc
        self.ops = {e: [] for e in ENG}
        self.res = {}
        self.bar = {e: {} for e in ENG}
        self.dma_since_bar = []

    def emit(self, eng, fn, reads=(), writes=()):
        deps = {}

        def add(d):
            e, i = d
            if e == 'sp':
                deps.setdefault(('sp', i), i)
            else:
                if deps.get(e, -1) < i:
                    deps[e] = i

        for r in reads:
            st = self.res.get(r)
            if st and st['w'] is not None:
                add(st['w'])
        for w in writes:
            st = self.res.get(w)
            if st:
                if st['w'] is not None:
                    add(st['w'])
                for e, i in st['r'].items():
                    if e == 'sp':
                        for ii in i:
                            add(('sp', ii))
                    else:
                        add((e, i))
        for k, v in self.bar[eng].items():
            if isinstance(k, tuple):
                deps.setdefault(k, v)
            elif deps.get(k, -1) < v:
                deps[k] = v
        self.bar[eng] = {}
        idx = len(self.ops[eng])
        if eng == 'pe':
            deps.pop('pe', None)
        self.ops[eng].append(dict(fn=fn, deps=deps, need=False))
        if eng == 'sp':
            self.dma_since_bar.append(idx)
        for r in reads:
            st = self.res.setdefault(r, {'w': None, 'r': {}})
            if eng == 'sp':
                st['r'].setdefault('sp', []).append(idx)
            else:
                st['r'][eng] = idx
        for w in writes:
            self.res[w] = {'w': (eng, idx), 'r': {}}

    def barrier(self):
        last = {}
        for e in ENG:
            if e == 'sp':
                continue
            if self.ops[e]:
                last[e] = len(self.ops[e]) - 1
        for i in self.dma_since_bar:
            last[('sp', i)] = i
        self.dma_since_bar = []
        for e in ENG:
            d = dict(last)
            d.pop(e, None)
            for k, v in d.items():
                if isinstance(k, tuple):
                    self.bar[e].setdefault(k, v)
                elif self.bar[e].get(k, -1) < v:
                    self.bar[e][k] = v

    def finalize(self):
        nc = self.nc
        ops = self.ops
        for e in ENG:
            for op in ops[e]:
                for k, v in op['deps'].items():
                    if isinstance(k, tuple):
                        continue
                    ops[k][v]['need'] = True
        for e in ENG:
            if e == 'sp':
                continue
            c = 0
            for op in ops[e]:
                if op['need']:
                    c += 1
                op['val'] = c
        cnt = [0] * NDSEM
        for k, op in enumerate(ops['sp']):
            j = k % NDSEM
            op['prev'] = cnt[j]
            cnt[j] += 16
            op['sem'] = j
            op['val'] = cnt[j]
        sems = {e: nc.alloc_semaphore('s_' + e) for e in ENG if e != 'sp'}
        dsems = [nc.alloc_semaphore('s_d%d' % j) for j in range(NDSEM)]

        def run(e, engobj):
            emitted = {}

            def need(key, val):
                if emitted.get(key, 0) < val:
                    sm = dsems[key[1]] if isinstance(key, tuple) else sems[key]
                    engobj.wait_ge(sm, val)
                    emitted[key] = val

            for op in ops[e]:
                for k, v in op['deps'].items():
                    if isinstance(k, tuple):
                        p = ops['sp'][v]
                        need(('d', p['sem']), p['val'])
                    else:
                        need(k, ops[k][v]['val'])
                if e == 'sp' and op['prev'] > 0:
                    need(('d', op['sem']), op['prev'])
                inst = op['fn'](engobj)
                if e == 'sp':
                    inst.then_inc(dsems[op['sem']], 16)
                elif op['need']:
                    inst.then_inc(sems[e], 1)
            if e == 'sp':
                for j in range(NDSEM):
                    if cnt[j] > 0:
                        need(('d', j), cnt[j])

        self._emitted = {}
        with nc.Block() as block:
            @block.tensor
            def _(t):
                run('pe', t)

            @block.scalar
            def _(t):
                run('act', t)

            @block.vector
            def _(t):
                run('dve', t)

            @block.gpsimd
            def _(t):
                run('pool', t)

            @block.sync
            def _(t):
                run('sp', t)


def make_consts():
    c = {}
    c['ident'] = np.eye(128, dtype=np.float32)
    c['ones'] = np.ones((128, 128), np.float32)
    bo = np.zeros((128, 128), np.float32)
    bo[:64, :64] = 1
    bo[64:, 64:] = 1
    c['blkones'] = bo
    i = np.arange(128)
    for name, sub in (('1', 128), ('4', 32), ('16', 8)):
        same = (i[:, None] // sub) == (i[None, :] // sub)
        low_s = ((i[None, :] < i[:, None]) & same).astype(np.float32)
        up_s = ((i[:, None] < i[None, :]) & same).astype(np.float32)
        up_i = ((i[:, None] <= i[None, :]) & same).astype(np.float32)
        c['low_s' + name] = low_s
        c['up_si' + name] = np.concatenate([up_s, up_i], axis=1)
        c['up_i' + name] = up_i
        nsub = 128 // sub
        rst = np.ones((128, 128), np.float32)
        rst[:, ::sub] = 0
        c['rst' + name] = rst
        cm = np.zeros((128, nsub, 128), np.float32)
        rm = np.zeros((128, nsub), np.float32)
        for sb in range(nsub):
            cm[:, sb, sb * sub:(sb + 1) * sub] = 1
            rm[sb * sub:(sb + 1) * sub, sb] = 1
        c['cm' + name] = cm.reshape(128, nsub * 128)
        c['rm' + name] = rm
    return c


class Builder:
    def __init__(self, TP, depth, stop=None, dbg=()):
        self.TP = TP
        self.depth = depth
        self.NT = TP + NSEQ * TS
        self.ntile = self.NT // 128
        self.stop = stop
        self.rw_stop = None
        self.dbg = dbg
        nc = bass.Bass('TRN2', target_bir_lowering=False)
        self.nc = nc
        self.S = Sched(nc)
        self.inputs = {}
        self.outputs = {}
        self.tg = []
        t = 0
        while t < TP:
            n = min(512, TP - t)
            self.tg.append((t, n))
            t += n
        self.tg.append((TP, NSEQ * TS))

    def din(self, name, shape, dt=F32):
        t = self.nc.dram_tensor(name, list(shape), dt, kind='ExternalInput')
        self.inputs[name] = t
        return t.ap()

    def dout(self, name, shape, dt=F32):
        t = self.nc.dram_tensor(name, list(shape), dt, kind='ExternalOutput')
        self.outputs[name] = t
        return t.ap()

    def dscr(self, name, shape, dt=F32):
        return self.nc.dram_tensor(name, list(shape), dt, kind='Internal').ap()

    def dma(self, out, in_, reads, writes, slow=False):
        if slow:
            self.S.emit('sp', lambda e: e.dma_start(out=out, in_=in_, allow_slow_non_contiguous=True), reads, writes)
        else:
            self.S.emit('sp', lambda e: e.dma_start(out=out, in_=in_), reads, writes)

    def E(self, eng, fn, reads, writes):
        self.S.emit(eng, fn, reads, writes)

    def psum(self):
        k = self._pk
        self._pk = (k + 1) % 8
        return self.ps[k], 'ps%d' % k

    def build(self):
        nc = self.nc
        TP, NT, L = self.TP, self.NT, self.depth
        I = {}
        I['x'] = self.din('x', [NT, D])
        shapes = dict(
            state_rwkv=[L, NSEQ, 12, 64, 64], state_rwkv_shift=[L, NSEQ, RW_COLS], state_hgrn=[L, NSEQ, 6, 128, 128],
            state_rglru=[L, NSEQ, 512], cache_rglru_conv=[L, NSEQ, 3, 512], cache_ffn_conv=[L, NSEQ, 2, DFF],
            norm_mix=[L, D], w_in=[L, D, IN_COLS], rwkv_mu=[L, RW_COLS], rwkv_w0=[L, 768], rwkv_w2=[L, 64, 768],
            rwkv_a0=[L, 768], rwkv_a2=[L, 64, 768], rwkv_g2=[L, 128, 768], rwkv_k_k=[L, 768], rwkv_k_a=[L, 768],
            rwkv_r_k=[L, 12, 64], rwkv_ln_w=[L, 768], rwkv_ln_b=[L, 768], hgrn_lb_logits=[L, 768], hgrn_norm_w=[L, 768],
            rglru_conv_w=[L, 4, 512], rglru_conv_b=[L, 512], rglru_wa=[L, 8, 64, 64], rglru_ba=[L, 512],
            rglru_wx=[L, 8, 64, 64], rglru_bx=[L, 512], rglru_lambda=[L, 512], w_out=[L, D, D], norm_ffn=[L, D],
            ffn_w_up=[L, D, 2 * DFF], ffn_conv_w=[L, 3, DFF], ffn_conv_b=[L, DFF], ffn_w_down=[L, DFF, D], norm_final=[D])
        for k, shp in shapes.items():
            I[k] = self.din(k, shp)
        self.cst = {k: self.din('c_' + k, v.shape) for k, v in make_consts().items()}
        O_ = {}
        O_['y'] = self.dout('y', [NT, D])
        oshapes = dict(p_rwkv=[L, 12, 64, 64], p_rwkv_shift=[L, RW_COLS], p_hgrn=[L, 6, 128, 128], p_rglru=[L, 512],
                       p_rglru_conv=[L, 3, 512], p_ffn_conv=[L, 2, DFF],
                       s_rwkv=[L, NSEQ, 12, 64, 64], s_rwkv_shift=[L, NSEQ, RW_COLS], s_hgrn=[L, NSEQ, 6, 128, 128],
                       s_rglru=[L, NSEQ, 512], s_rglru_conv=[L, NSEQ, 3, 512], s_ffn_conv=[L, NSEQ, 2, DFF])
        for k, shp in oshapes.items():
            O_[k] = self.dout(k, shp)
        self.xT = self.dscr('xT', [D, NT])
        self.projT = self.dscr('projT', [IN_COLS, NT])
        self.yT = self.dscr('yT', [D, NT], BF16)
        self.hT = self.dscr('hT', [DFF, NT], BF16)
        dbg = {k: self.dout('d_' + k, shp, dt) for k, (shp, dt) in dict(
            projT=([IN_COLS, NT], F32), xT=([D, NT], F32), yT=([D, NT], BF16), hT=([DFF, NT], BF16)).items() if k in self.dbg}

        with contextlib.ExitStack() as es:
            A = lambda name, shape, dt=F32: es.enter_context(nc.sbuf_tensor(name, list(shape), dt))
            self.ps = [es.enter_context(nc.psum_tensor('ps%d' % k, [128, 512], F32)) for k in range(8)]
            self._pk = 0
            self.ident = A('ident', [128, 128])
            self.ones_bf = A('ones_bf', [128, 128], BF16)
            self.dma(self.ident[:], self.cst['ident'], [], ['ident'])
            self.MEMSET('dve', self.ones_bf[:], 1.0, ['ones_bf'])
            self.prm = A('prm', [128, L, 160])
            self.fprm = A('fprm', [128, L, 176])
            self.prmf = A('prmf', [128, 16])
            self.ptmp = A('ptmp', [128, L + 1, 16])
            self.load_params(I)
            self.phase_transpose_in(I['x'])
            self.S.barrier()
            stop = self.stop
            for l in range(L):
                def sub(fn):
                    with contextlib.ExitStack() as es2:
                        A2 = lambda name, shape, dt=F32: es2.enter_context(nc.sbuf_tensor(name, list(shape), dt))
                        fn(A2)
                    self.S.barrier()

                def p_in(A2):
                    actT = A2('actT_a%d' % l, [128, 16, NT], BF16)
                    self.phase_norm(l, actT, P_NM, A2)
                    self.S.barrier()
                    self.phase_inproj(l, actT, I, A2)
                sub(p_in)
                if stop == 'inproj':
                    break
                self.phase_rwkv(l, I, O_)
                self.S.barrier()
                if stop == 'rwkv':
                    break
                self.phase_hgrn(l, I, O_, side=self.lru_gen(l, I, O_))
                self.S.barrier()
                if stop == 'mix':
                    break

                def p_out(A2):
                    actT = A2('actT_b%d' % l, [128, 16, NT], BF16)
                    self.phase_outproj(l, actT, I, A2)
                sub(p_out)

                def p_up(A2):
                    actT = A2('actT_c%d' % l, [128, 16, NT], BF16)
                    self.phase_norm(l, actT, P_NF, A2)
                    self.S.barrier()
                    self.phase_ffn_up(l, actT, I, O_, A2)
                sub(p_up)
                if stop == 'ffnup':
                    break
                sub(lambda A2: self.phase_ffn_down(l, I, A2))
            if stop is None:
                self.phase_final(O_['y'])
            self.S.barrier()
            for k, ap in dbg.items():
                if k == 'yT' and stop == 'rwkv':
                    self.dma(ap[0:768], self.yT[0:768], [], [])
                else:
                    self.dma(ap, getattr(self, k), [], [])
            self.S.finalize()
        return nc

    def phase_transpose_in(self, x_in):
        nc = self.nc
        with contextlib.ExitStack() as es:
            A = lambda name, shape, dt=F32: es.enter_context(nc.sbuf_tensor(name, list(shape), dt))
            xt = [A('ti_x%d' % i, [128, D]) for i in range(3)]
            xo = [A('ti_o%d' % i, [128, 16, 512]) for i in range(2)]
            k = 0
            for gi, (g0, gn) in enumerate(self.tg):
                ob = gi % 2
                for j in range(gn // 128):
                    b = k % 3
                    k += 1
                    t0 = g0 + j * 128
                    self.dma(xt[b][:], x_in[t0:t0 + 128, :], [], ['ti_x%d' % b])
                    for q in range(4):
                        ps, pn = self.psum()
                        for jj in range(4):
                            kc = q * 4 + jj
                            self.TR(ps[:, jj * 128:(jj + 1) * 128], xt[b][:, kc * 128:(kc + 1) * 128], ['ti_x%d' % b], [pn])
                        self.CP('act' if q % 2 == 0 else 'dve', xo[ob][:, q * 4:(q + 1) * 4, j * 128:(j + 1) * 128], ps[:].rearrange('p (a t) -> p a t', a=4), [pn], ['ti_o%d' % ob])
                self.dma(self.xT[:, g0:g0 + gn].rearrange('(c p) t -> p c t', p=128), xo[ob][:, :, 0:gn], ['ti_o%d' % ob], ['xT'])

    def phase_norm(self, l, actT, poff, A_unused):
        with contextlib.ExitStack() as es:
            self._phase_norm(l, actT, poff, lambda name, shape, dt=F32: es.enter_context(self.nc.sbuf_tensor('%s_%d' % (name, poff), list(shape), dt)))

    def _phase_norm(self, l, actT, poff, A):
        xb = [A('nm_x%d_%d' % (l, i), [128, 16, 512]) for i in range(2)]
        sqs = [A('nm_sq%d_%d' % (l, i), [128, 16, 512], BF16) for i in range(2)]
        sds = [A('nm_sd%d_%d' % (l, i), [128, 512]) for i in range(2)]
        rss = [A('nm_rs%d_%d' % (l, i), [128, 512]) for i in range(2)]
        for gi, (t0, tn) in enumerate(self.tg):
            b = gi % 2
            sq, sd, rs = sqs[b], sds[b], rss[b]
            xn = 'nm_x%d' % b
            self.dma(xb[b][:, :, 0:tn], self.xT[:, t0:t0 + tn].rearrange('(c p) t -> p c t', p=128), ['xT'], [xn])
            self.E('act', lambda e, b=b, tn=tn, sq=sq: e.activation(sq[:, :, 0:tn], xb[b][:, :, 0:tn], AF.Square), [xn], ['nm_sq%d' % b])
            ps, pn = self.psum()
            for kc in range(16):
                self.E('pe', lambda e, ps=ps, kc=kc, tn=tn, sq=sq: e.matmul(ps[:, 0:tn], self.ones_bf[:], sq[:, kc, 0:tn], start=(kc == 0), stop=(kc == 15)),
                       ['nm_sq%d' % b, 'ones_bf'], [pn])
            self.E('act', lambda e, ps=ps, tn=tn, sd=sd: e.activation(sd[:, 0:tn], ps[:, 0:tn], AF.Sqrt, bias=EPS, scale=1.0 / D), [pn], ['nm_sd%d' % b])
            self.E('dve', lambda e, tn=tn, rs=rs, sd=sd: e.reciprocal(rs[:, 0:tn], sd[:, 0:tn]), ['nm_sd%d' % b], ['nm_rs%d' % b])
            for kc in range(16):
                self.E('dve', lambda e, b=b, kc=kc, t0=t0, tn=tn, rs=rs: e.scalar_tensor_tensor(
                    actT[:, kc, t0:t0 + tn], xb[b][:, kc, 0:tn], self.prm[:, l, poff + kc:poff + kc + 1], rs[:, 0:tn], ALU.mult, ALU.mult),
                    [xn, 'nm_rs%d' % b, 'prm'], ['actT'])

    def TT(self, eng, out, in0, in1, op, r, w):
        self.E(eng, lambda e: e.tensor_tensor(out, in0, in1, op), r, w)

    def TSC(self, eng, out, in0, s1, s2, op0, op1, r, w):
        if op1 is None:
            self.E(eng, lambda e: e.tensor_scalar(out, in0, s1, None, op0), r, w)
        else:
            self.E(eng, lambda e: e.tensor_scalar(out, in0, s1, s2, op0, op1), r, w)

    def STT(self, out, in0, sc, in1, op0, op1, r, w):
        self.E('dve', lambda e: e.scalar_tensor_tensor(out, in0, sc, in1, op0, op1), r, w)

    def ACT(self, out, in_, func, r, w, bias=None, scale=None, accum=None):
        kw = {}
        if bias is not None:
            kw['bias'] = bias
        if scale is not None:
            kw['scale'] = scale
        if accum is not None:
            kw['accum_out'] = accum
        self.E('act', lambda e: e.activation(out, in_, func, **kw), r, w)

    def CP(self, eng, out, in_, r, w):
        if eng == 'act':
            self.ACT(out, in_, AF.Copy, r, w)
        else:
            self.E(eng, lambda e: e.tensor_copy(out, in_), r, w)

    def MM(self, out, lhsT, rhs, start, stop, r, w, sgc=False):
        self.E('pe', lambda e: e.matmul(out, lhsT, rhs, start=start, stop=stop, skip_group_check=sgc), r, w)

    def TR(self, out, in_, r, w, n=128):
        self.E('pe', lambda e: e.transpose(out, in_, self.ident[0:n, 0:n]), list(r) + ['ident'], w)

    def RECIP(self, out, in_, r, w):
        self.E('dve', lambda e: e.reciprocal(out, in_), r, w)

    def MEMSET(self, eng, ap, val, w):
        self.E(eng, lambda e: e.memset(ap, val), [], w)

    def SCAN(self, out, d0, d1, init, r, w):
        self.E('dve', lambda e: e.tensor_tensor_scan(out, d0, d1, init, ALU.mult, ALU.add), r, w)

    def RED(self, out, in_, r, w):
        self.E('dve', lambda e: e.tensor_reduce(out, in_, AX.X, ALU.add), r, w)

    def load_params(self, I):
        L = self.depth
        prm = self.prm

        def ld(idx, n, ap_fn):
            for l in range(L):
                self.dma(prm[:, l, idx:idx + n], ap_fn(l), [], ['prm'], slow=True)

        cp = lambda name: (lambda l: I[name][l].rearrange('(c p) -> p c', p=128))
        ld(P_NM, 16, cp('norm_mix'))
        ld(P_NF, 16, cp('norm_ffn'))
        ld(P_MU, 20, cp('rwkv_mu'))
        ld(P_W0, 6, cp('rwkv_w0'))
        ld(P_A0, 6, cp('rwkv_a0'))
        ld(P_KK, 6, cp('rwkv_k_k'))
        ld(P_KA, 6, cp('rwkv_k_a'))
        ld(P_RK, 6, lambda l: I['rwkv_r_k'][l].rearrange('(c h2) k -> (h2 k) c', h2=2))
        ld(P_LNW, 6, cp('rwkv_ln_w'))
        ld(P_LNB, 6, cp('rwkv_ln_b'))
        ld(P_LB, 6, cp('hgrn_lb_logits'))
        ld(P_HNW, 6, cp('hgrn_norm_w'))
        for j in range(4):
            ld(P_CW + 4 * j, 4, lambda l, j=j: I['rglru_conv_w'][l, j].rearrange('(c p) -> p c', p=128))
        ld(P_CB, 4, cp('rglru_conv_b'))
        ld(P_BA, 4, cp('rglru_ba'))
        ld(P_BX, 4, cp('rglru_bx'))
        ld(P_LAM, 4, cp('rglru_lambda'))
        for l in range(L):
            for j in range(3):
                self.dma(self.fprm[:, l, j * 44:(j + 1) * 44], I['ffn_conv_w'][l, j].rearrange('(c p) -> p c', p=128), [], ['fprm'], slow=True)
            self.dma(self.fprm[:, l, 132:176], I['ffn_conv_b'][l].rearrange('(c p) -> p c', p=128), [], ['fprm'], slow=True)
        self.dma(self.prmf[:, 0:16], I['norm_final'].rearrange('(c p) -> p c', p=128), [], ['prmf'], slow=True)
        for l in range(L):
            self.TSC('dve', prm[:, l, P_OMKA:P_OMKA + 6], prm[:, l, P_KA:P_KA + 6], -1.0, 1.0, ALU.mult, ALU.add, ['prm'], ['prm'])
        ex = self.ptmp
        for l in range(L):
            self.ACT(ex[:, l, 0:6], prm[:, l, P_LB:P_LB + 6], AF.Exp, ['prm'], ['ptmp'])
        self.CP('dve', ex[:, L, 0:6], ex[:, 0, 0:6], ['ptmp'], ['ptmp'])
        for l in range(1, L):
            self.TT('dve', ex[:, L, 0:6], ex[:, L, 0:6], ex[:, l, 0:6], ALU.add, ['ptmp'], ['ptmp'])
        self.RECIP(ex[:, L, 0:6], ex[:, L, 0:6], ['ptmp'], ['ptmp'])
        for l in range(L):
            self.TT('dve', ex[:, l, 0:6], ex[:, l, 0:6], ex[:, L, 0:6], ALU.mult, ['ptmp'], ['ptmp'])
        self.MEMSET('dve', prm[:, 0, P_LB:P_LB + 6], 0.0, ['prm'])
        for l in range(1, L):
            self.TT('dve', prm[:, l, P_LB:P_LB + 6], prm[:, l - 1, P_LB:P_LB + 6], ex[:, l, 0:6], ALU.add, ['prm', 'ptmp'], ['prm'])
        for l in range(L):
            self.TSC('dve', prm[:, l, P_OMLB:P_OMLB + 6], prm[:, l, P_LB:P_LB + 6], -1.0, 1.0, ALU.mult, ALU.add, ['prm'], ['prm'])
            self.ACT(ex[:, l, 8:12], prm[:, l, P_LAM:P_LAM + 4], AF.Exp, ['prm'], ['ptmp'], scale=-1.0)
            self.ACT(ex[:, l, 8:12], ex[:, l, 8:12], AF.Ln, ['ptmp'], ['ptmp'], bias=1.0)
            self.TSC('dve', prm[:, l, P_C8:P_C8 + 4], ex[:, l, 8:12], -8.0, None, ALU.mult, None, ['ptmp'], ['prm'])
            self.TSC('dve', prm[:, l, P_2C8:P_2C8 + 4], ex[:, l, 8:12], -16.0, None, ALU.mult, None, ['ptmp'], ['prm'])
    def phase_rwkv(self, l, I, O_):
        nc = self.nc
        TP, NT = self.TP, self.NT
        prm = self.prm
        C0 = 0.6065306597126334
        with contextlib.ExitStack() as es:
            A = lambda name, shape, dt=F32: es.enter_context(nc.sbuf_tensor('rw%d_%s' % (l, name), list(shape), dt))
            w2a2 = A('w2a2', [128, 768])
            g2 = A('g2', [128, 768])
            self.dma(w2a2[0:64, :], I['rwkv_w2'][l], [], ['w2a2'])
            self.dma(w2a2[64:128, :], I['rwkv_a2'][l], [], ['w2a2'])
            self.dma(g2[:], I['rwkv_g2'][l], [], ['g2'])
            blk = A('blk', [128, 128])
            self.dma(blk[:], self.cst['blkones'], [], ['blk'])
            masks = {}
            for mk in ('1', '16'):
                mbk = A('mbk' + mk, [128, 512])
                ml4 = A('ml4' + mk, [128, 512])
                rst = A('rst' + mk, [128, 128])
                for j in range(2):
                    self.dma(mbk[:, j * 256:(j + 1) * 256], self.cst['up_si' + mk], [], ['mbk' + mk])
                for j in range(4):
                    self.dma(ml4[:, j * 128:(j + 1) * 128], self.cst['low_s' + mk], [], ['ml4' + mk])
                self.dma(rst[:], self.cst['rst' + mk], [], ['rst' + mk])
                masks[mk] = (mbk, ml4, rst)
            cm16 = A('cm16', [128, 16, 128])
            rm16 = A('rm16', [128, 16])
            self.dma(cm16[:], self.cst['cm16'].rearrange('p (s t) -> p s t', s=16), [], ['cm16'])
            self.dma(rm16[:], self.cst['rm16'], [], ['rm16'])

            pbuf = A('pbuf', [128, 20, 144])
            xs = A('xs', [128, 20, 128])
            dd = A('dd', [128, 20, 128])
            tx = A('tx', [128, 128])
            sgx = A('sgx', [128, 128])
            sgw = A('sgw', [128, 6, 128])
            aa = A('aa', [128, 6, 128])
            gT = A('gT', [128, 6, 128])
            kk = A('kk', [128, 6, 128])
            t1 = A('t1', [128, 6, 128])
            t2 = A('t2', [128, 6, 128])
            kf = A('kf', [128, 6, 128])
            nb = A('nb', [128, 6, 128])
            cs = A('cs', [128, 6, 128])
            Gm = A('Gm', [128, 6, 128])
            Gi = A('Gi', [128, 6, 128])
            Gp = A('Gp', [128, 6, 128])
            AR = A('AR', [128, 6, 256])
            KT = A('KT', [128, 6, 128])
            BT = A('BT', [128, 6, 128])
            Vtm = A('Vtm', [128, 768])
            Btm = A('Btm', [128, 768])
            Ktm = A('Ktm', [128, 768])
            M4 = A('M4', [128, 12, 384])
            Pm = [A('Pm%d' % i, [128, 12, 128], NDT) for i in range(2)]
            PTm = [A('PTm%d' % i, [128, 12, 128], NDT) for i in range(2)]
            X = [A('X%d' % i, [128, 12, 128], NDT) for i in range(2)]
            Osb = A('Osb', [128, 12, 64])
            st = A('st', [128, 64])
            bst = A('bst', [128, 12, 6])
            ybf = A('ybf', [128, 6, 128], BF16)
            Hc = A('Hc', [128, 1, 6, 64])
            Hin = A('Hin', [128, 16, 6, 64])
            sin = [A('sin0', [64, 6, 2, 64])] * 2
            sout = sin
            htmp = A('htmp', [128, 64])

            self.MEMSET('dve', Hc[:], 0.0, ['Hc'])
            tiles = [('p', ti) for ti in range(TP // 128)] + [('s', 0)]
            for kind, ti in tiles:
                sample = kind == 's'
                t0 = TP if sample else ti * 128
                mk = '16' if sample else '1'
                nsub = 16 if sample else 1
                mbk, ml4, rst = masks[mk]
                rmk = ['mbk' + mk, 'ml4' + mk, 'rst' + mk]
                if sample:
                    pv = pbuf[:, :, 0:144].rearrange('p c (s t) -> p c s t', t=9)
                    self.dma(dd[:], self.projT[0:RW_COLS, t0:t0 + 128].rearrange('(c p) t -> p c t', p=128), ['projT'], ['dd'])
                    self.CP('dve', pv[:, :, :, 1:9], dd[:].rearrange('p c (s t) -> p c s t', t=8), ['dd'], ['pbuf'])
                    for g in range(5):
                        a0 = 2 * (g % 2)
                        stg = M4[0:16, a0:a0 + 2, :].rearrange('p a b -> p (a b)')[:, 0:512]
                        sres = ['M4_%d' % a0, 'M4_%d' % (a0 + 1)]
                        self.dma(stg, I['state_rwkv_shift'][l, :, g * 512:(g + 1) * 512], [], sres)
                        ps, pn = self.psum()
                        for j in range(4):
                            self.TR(ps[:, j * 16:(j + 1) * 16], stg[:, j * 128:(j + 1) * 128], sres, [pn], n=16)
                        self.CP('dve', pv[:, g * 4:(g + 1) * 4, :, 0], ps[:, 0:64].rearrange('p (c s) -> p c s', s=16), [pn], ['pbuf'])
                    prev = lambda c: pv[:, c, :, 0:8]
                    cur = lambda c: pv[:, c, :, 1:9]
                    v3 = lambda ap: ap.rearrange('p (s t) -> p s t', t=8)
                else:
                    if ti == 0:
                        self.MEMSET('dve', pbuf[:, :, 0:1], 0.0, ['pbuf'])
                        self.dma(pbuf[:, :, 1:129], self.projT[0:RW_COLS, t0:t0 + 128].rearrange('(c p) t -> p c t', p=128), ['projT'], ['pbuf'])
                    prev = lambda c: pbuf[:, c, 0:128]
                    cur = lambda c: pbuf[:, c, 1:129]
                    v3 = lambda ap: ap
                if sample:
                    pall, call = pv[:, :, :, 0:8], pv[:, :, :, 1:9]
                    va = lambda t: t[:].rearrange('p c (s t) -> p c s t', t=8)
                    mub = prm[:, l, P_MU:P_MU + 20].rearrange('p (c a b) -> p c a b', a=1, b=1).to_broadcast([128, 20, 16, 8])
                else:
                    pall, call = pbuf[:, :, 0:128], pbuf[:, :, 1:129]
                    va = lambda t: t[:]
                    mub = prm[:, l, P_MU:P_MU + 20].rearrange('p (c a) -> p c a', a=1).to_broadcast([128, 20, 128])
                self.TT('dve', va(dd), pall, call, ALU.subtract, ['pbuf'], ['dd'])
                self.TT('dve', va(dd), va(dd), mub, ALU.mult, ['dd', 'prm'], ['dd'])
                self.TT('dve', va(xs), va(dd), call, ALU.add, ['dd', 'pbuf'], ['xs'])
                if (not sample) and ti + 1 < TP // 128:
                    self.dma(pbuf[:, :, 0:129], self.projT[0:RW_COLS, t0 + 127:t0 + 256].rearrange('(c p) t -> p c t', p=128), ['projT'], ['pbuf'])
                self.ACT(tx[0:64, :], xs[0:64, 18, :], AF.Tanh, ['xs'], ['tx'])
                self.ACT(sgx[:], xs[:, 19, :], AF.Sigmoid, ['xs'], ['sgx'])
                for (rows, rhs, rres, out, bidx) in ((slice(0, 64), tx[0:64, :], 'tx', sgw, P_W0), (slice(64, 128), xs[64:128, 18, :], 'xs', aa, P_A0)):
                    for half in range(2):
                        ps, pn = self.psum()
                        cs_ = range(0, 4) if half == 0 else range(4, 6)
                        for c in cs_:
                            self.MM(ps[:, (c % 4) * 128:(c % 4 + 1) * 128], w2a2[rows, c * 128:(c + 1) * 128], rhs, True, True, ['w2a2', rres], [pn])
                        for c in cs_:
                            self.ACT(out[:, c, :], ps[:, (c % 4) * 128:(c % 4 + 1) * 128], AF.Sigmoid, [pn, 'prm'], ['sgw' if out is sgw else 'aa'], bias=prm[:, l, bidx + c:bidx + c + 1])
                for half in range(2):
                    ps, pn = self.psum()
                    cs_ = range(0, 4) if half == 0 else range(4, 6)
                    for c in cs_:
                        self.MM(ps[:, (c % 4) * 128:(c % 4 + 1) * 128], g2[:, c * 128:(c + 1) * 128], sgx[:], True, True, ['g2', 'sgx'], [pn])
                    n = len(cs_)
                    self.CP('act', gT[:, cs_[0]:cs_[0] + n, :], ps[:, 0:n * 128].rearrange('p (c t) -> p c t', t=128), [pn], ['gT'])
                SGW, AA = 'sgw', 'aa'
                for c in range(6):
                    self.ACT(kk[:, c, :], xs[:, 6 + c, :], AF.Copy, ['xs', 'prm'], ['kk'], scale=prm[:, l, P_KK + c:P_KK + c + 1])
                self.TT('dve', t1[:], kk[:], kk[:], ALU.mult, ['kk'], ['t1'])
                for half in range(2):
                    ps, pn = self.psum()
                    cs_ = range(0, 4) if half == 0 else range(4, 6)
                    for c in cs_:
                        self.MM(ps[:, (c % 4) * 128:(c % 4 + 1) * 128], blk[:], t1[:, c, :], True, True, ['blk', 't1'], [pn])
                    n = len(cs_)
                    self.TSC('dve', t2[:, cs_[0]:cs_[0] + n, :], ps[:, 0:n * 128].rearrange('p (c t) -> p c t', t=128), 1e-24, None, ALU.max, None, [pn], ['t2'])
                self.ACT(t2[:], t2[:], AF.Sqrt, ['t2'], ['t2'])
                self.RECIP(t2[:], t2[:], ['t2'], ['t2'])
                self.TT('dve', kk[:], kk[:], t2[:], ALU.mult, ['kk', 't2'], ['kk'])
                for c in range(6):
                    self.TSC('dve', t1[:, c, :], aa[:, c, :], prm[:, l, P_KA + c:P_KA + c + 1], prm[:, l, P_OMKA + c:P_OMKA + c + 1], ALU.mult, ALU.add, [AA, 'prm', 't1'], ['t1'])
                self.TT('dve', kf[:], xs[:, 6:12, :], t1[:], ALU.mult, ['xs', 't1'], ['kf'])
                self.TT('dve', nb[:], kk[:], aa[:], ALU.mult, ['kk', AA], ['nb'])
                for c in range(6):
                    self.SCAN(cs[:, c, :], rst[:], sgw[:, c, :], 0.0, [SGW, 'rst' + mk], ['cs'])
                self.ACT(Gm[:], cs[:], AF.Exp, ['cs'], ['Gm'], scale=-C0)
                self.ACT(Gi[:], cs[:], AF.Exp, ['cs'], ['Gi'], scale=C0)
                self.TT('dve', t2[:], cs[:], sgw[:], ALU.subtract, ['cs', SGW], ['t2'])
                self.ACT(Gp[:], t2[:], AF.Exp, ['t2'], ['Gp'], scale=-C0)
                self.STT(AR[:, :, 0:128], kk[:], -1.0, Gp[:], ALU.mult, ALU.mult, ['kk', 'Gp'], ['AR0'])
                self.TT('pool', AR[:, :, 128:256], xs[:, 0:6, :], Gm[:], ALU.mult, ['xs', 'Gm'], ['AR1'])
                self.TT('dve', KT[:], kf[:], Gi[:], ALU.mult, ['kf', 'Gi'], ['KT'])
                self.TT('pool', BT[:], nb[:], Gi[:], ALU.mult, ['nb', 'Gi'], ['BT'])
                if self.rw_stop is not None and self.rw_stop < 10:
                    continue
                for (src_fn, sres, dst, dres) in ((lambda c: xs[:, 12 + c, :], 'xs', Vtm, 'Vtm'), (lambda c: AR[:, c, 0:128], 'AR0', cs[:].rearrange('p c t -> p (c t)'), 'cs'),
                                                  (lambda c: BT[:, c, :], 'BT', Btm, 'Btm'), (lambda c: KT[:, c, :], 'KT', Ktm, 'Ktm')):
                    for half in range(2):
                        ps, pn = self.psum()
                        cs_ = range(0, 4) if half == 0 else range(4, 6)
                        for c in cs_:
                            self.TR(ps[:, (c % 4) * 128:(c % 4 + 1) * 128], src_fn(c), [sres], [pn])
                        n = len(cs_)
                        self.CP('act' if half == 0 else 'dve', dst[:, cs_[0] * 128:(cs_[0] + n) * 128], ps[:, 0:n * 128], [pn], ['cs'] if dres == 'cs' else [dres + str(half)])
                VT_R = ['Vtm0', 'Vtm1']
                if self.rw_stop is not None and self.rw_stop < 11:
                    continue
                for h in range(12):
                    c, pb = h // 2, 64 * (h % 2)
                    ps, pn = self.psum()
                    self.MM(ps[:, 0:256], BT[pb:pb + 64, c, :], AR[pb:pb + 64, c, :], True, True, ['BT', 'AR0', 'AR1'], [pn])
                    self.MM(ps[:, 256:512], KT[pb:pb + 64, c, :], AR[pb:pb + 64, c, :], True, True, ['KT', 'AR0', 'AR1'], [pn])
                    self.TT('dve', PTm[0][:, h, :], ps[:, 0:128], mbk[:, 0:128], ALU.mult, [pn, 'mbk' + mk], ['PTm0_%d' % (h // 4)])
                    self.TT('dve', M4[:, h, :], ps[:, 128:512], mbk[:, 128:512], ALU.mult, [pn, 'mbk' + mk], ['M4_%d' % h])
                Pv = Pm[0][:].rearrange('p (c a) t -> p c a t', a=2)
                for h2 in range(2):
                    pb = 64 * h2
                    for (c0, c1) in ((0, 4), (4, 6)):
                        ps, pn = self.psum()
                        for c in range(c0, c1):
                            self.MM(ps[:, (c - c0) * 128:(c - c0 + 1) * 128], AR[pb:pb + 64, c, 0:128], BT[pb:pb + 64, c, :], True, True, ['AR0', 'BT'], [pn])
                        n = c1 - c0
                        self.TT('dve', Pv[:, c0:c1, h2, :], ps[:, 0:n * 128].rearrange('p (h t) -> p h t', t=128), ml4[:, 0:n * 128].rearrange('p (h t) -> p h t', t=128), ALU.mult,
                                [pn, 'ml4' + mk], sorted(set('Pm0_%d' % ((2 * c + h2) // 4) for c in range(c0, c1))))
                if self.rw_stop is not None and self.rw_stop < 12:
                    continue
                self.CP('act', X[0][:, :, 0:64], cs[:].rearrange('p c (a k) -> p (c a) k', k=64), ['cs'], ['X0_0', 'X0_1', 'X0_2'])
                for half in range(2):
                    ps, pn = self.psum()
                    hs = range(0, 8) if half == 0 else range(8, 12)
                    for h in hs:
                        self.MM(ps[:, (h % 8) * 64:(h % 8 + 1) * 64], M4[:, h, 128:256], Vtm[:, h * 64:(h + 1) * 64], True, True, ['M4_%d' % h] + VT_R, [pn])
                    n = len(hs)
                    self.CP('act', X[0][:, hs[0]:hs[0] + n, 64:128], ps[:, 0:n * 64].rearrange('p (h v) -> p h v', v=64), [pn], ['X0_0', 'X0_1'] if half == 0 else ['X0_2'])
                xres = {0: ['X0_0', 'X0_1', 'X0_2'], 1: ['X1_0', 'X1_1', 'X1_2']}
                if self.rw_stop is not None and self.rw_stop < 13:
                    continue
                NST = 3 if sample else 7
                for s_ in range(NST):
                    a_, b_ = s_ % 2, (s_ + 1) % 2
                    pres = ['Pm%d_%d' % (a_, q) for q in range(3)]
                    ptres = ['PTm%d_%d' % (a_, q) for q in range(3)]
                    for q in range(3):
                        ps, pn = self.psum()
                        for j in range(4):
                            h = q * 4 + j
                            self.MM(ps[:, j * 128:(j + 1) * 128], PTm[a_][:, h, :], X[a_][:, h, :], True, True, ptres + xres[a_], [pn])
                        self.TT('dve', X[b_][:, q * 4:(q + 1) * 4, :], ps[:].rearrange('p (h t) -> p h t', t=128), X[a_][:, q * 4:(q + 1) * 4, :], ALU.add,
                                [pn] + xres[a_], ['X%d_%d' % (b_, q)])
                    if s_ < NST - 1:
                        for q in range(3):
                            ps, pn = self.psum()
                            ps2, pn2 = self.psum()
                            for j in range(4):
                                h = q * 4 + j
                                self.MM(ps[:, j * 128:(j + 1) * 128], PTm[a_][:, h, :], Pm[a_][:, h, :], True, True, pres + ptres, [pn])
                                self.MM(ps2[:, j * 128:(j + 1) * 128], Pm[a_][:, h, :], PTm[a_][:, h, :], True, True, pres + ptres, [pn2])
                            self.CP('act', Pm[b_][:, q * 4:(q + 1) * 4, :], ps[:].rearrange('p (h t) -> p h t', t=128), [pn], ['Pm%d_%d' % (b_, q)])
                            self.CP('act', PTm[b_][:, q * 4:(q + 1) * 4, :], ps2[:].rearrange('p (h t) -> p h t', t=128), [pn2], ['PTm%d_%d' % (b_, q)])
                fin = NST % 2
                Xf = X[fin]
                xfr = xres[fin]
                if self.rw_stop is not None and self.rw_stop < 14:
                    continue
                self.CP('act', Gi[:].rearrange('p c (a k) -> p (c a) k', k=64), Xf[:, :, 0:64], xfr, ['Gi'])
                for half in range(2):
                    ps, pn = self.psum()
                    cs_ = range(0, 4) if half == 0 else range(4, 6)
                    for c in cs_:
                        self.TR(ps[:, (c % 4) * 128:(c % 4 + 1) * 128], Gi[:, c, :], ['Gi'], [pn])
                    n = len(cs_)
                    self.CP('act', Gp[:, cs_[0]:cs_[0] + n, :], ps[:, 0:n * 128].rearrange('p (c t) -> p c t', t=128), [pn], ['Gp'])
                gtr = ['Gp']
                if sample:
                    for s in range(16):
                        b = s % 2
                        self.dma(sin[b][:], I['state_rwkv'][l, s].rearrange('(c h2) v k -> v c h2 k', h2=2), [], ['sin0'])
                        ps, pn = self.psum()
                        for c in range(6):
                            self.TR(ps[:, c * 64:(c + 1) * 64], sin[b][:, c, :, :], ['sin0'], [pn], n=64)
                        self.CP('act' if s % 2 == 0 else 'dve', Hin[:, s, :, :], ps[:, 0:384].rearrange('p (c v) -> p c v', v=64), [pn], ['Hin%d' % s])
                    Hi = lambda sb: Hin[:, sb, :, :]
                    Ho = Hi
                    hir = lambda sb: 'Hin%d' % sb
                    hor = hir
                else:
                    Hi = lambda sb: Hc[:, 0, :, :]
                    Ho = Hi
                    hir = lambda sb: 'Hc'
                    hor = hir
                if self.rw_stop is not None and self.rw_stop < 15:
                    continue
                def inter(which):
                    psl = {0: self.psum(), 1: self.psum()}
                    for c in range(6):
                        if which == 'G':
                            src, sres = Gp[:, c, :], gtr
                        else:
                            src, sres = AR[:, c, 128:256], ['AR1']
                        if sample:
                            for sb in range(16):
                                self.TT('pool' if sb % 4 == 3 else 'dve', dd[:, sb, :], src, cm16[:, sb, :], ALU.mult, sres + ['cm16'], ['dd'])
                        for h2 in range(2):
                            pb = 64 * h2
                            ps, pn = psl[h2]
                            o = ps[:, c * 64:(c + 1) * 64]
                            for sb in range(nsub):
                                if sample:
                                    lh, lr = dd[pb:pb + 64, sb, :], ['dd']
                                else:
                                    lh, lr = src[pb:pb + 64, :], sres
                                self.MM(o, lh, Hi(sb)[pb:pb + 64, c, :], sb == 0, sb == nsub - 1, lr + [hir(sb)], [pn])
                    return psl
                Uv = sgw[:].rearrange('p c (a v) -> p c a v', a=2)
                Xv = Xf[:].rearrange('p (c a) t -> p c a t', a=2)
                psl = inter('G')
                for h2 in range(2):
                    ps, pn = psl[h2]
                    self.TT('dve', Uv[:, :, h2, :], ps[:, 0:384].rearrange('p (c v) -> p c v', v=64), Xv[:, :, h2, 64:128], ALU.add, [pn] + xfr, ['sgw'])
                ur = ['sgw']
                psl = inter('R')
                psO = [self.psum(), self.psum()]
                for h in range(12):
                    ps, pn = psO[h // 8]
                    o = ps[:, (h % 8) * 64:(h % 8 + 1) * 64]
                    self.MM(o, M4[:, h, 0:128], sgw[:].rearrange('p c t -> p (c t)')[:, h * 64:(h + 1) * 64], True, False, ['M4_%d' % h] + ur, [pn])
                    self.MM(o, M4[:, h, 256:384], Vtm[:, h * 64:(h + 1) * 64], False, True, ['M4_%d' % h] + VT_R, [pn])
                for half in range(2):
                    ps, pn = psO[half]
                    hs = range(0, 8) if half == 0 else range(8, 12)
                    n = len(hs)
                    self.CP('act', Osb[:, hs[0]:hs[0] + n, :], ps[:, 0:n * 64].rearrange('p (h v) -> p h v', v=64), [pn], ['Osb%d' % half])
                Ov = Osb[:].rearrange('p (c a) v -> p c a v', a=2)
                for h2 in range(2):
                    ps, pn = psl[h2]
                    self.TT('dve', Ov[:, :, h2, :], Ov[:, :, h2, :], ps[:, 0:384].rearrange('p (c v) -> p c v', v=64), ALU.add, [pn, 'Osb0', 'Osb1'], ['Osb0', 'Osb1'])
                osr = ['Osb0', 'Osb1']
                for h in range(12):
                    self.E('dve', (lambda e, h=h: e.bn_stats(bst[:, h, :], Osb[:, h, :])), osr, ['bst'])
                for h in range(12):
                    self.E('dve', (lambda e, h=h: e.bn_aggr(st[:, 2 * h:2 * h + 2], bst[:, h, :])), ['bst'], ['st'])
                stv = st[:, 0:24].rearrange('p (h a) -> p h a', a=2)
                self.ACT(st[:, 36:48], stv[:, :, 1], AF.Sqrt, ['st'], ['st'], bias=RW_EPS)
                self.RECIP(st[:, 36:48], st[:, 36:48], ['st'], ['st'])
                for h in range(12):
                    self.TSC('dve', t2[:, h // 2, (h % 2) * 64:(h % 2) * 64 + 64], Osb[:, h, :], st[:, 2 * h:2 * h + 1], st[:, 36 + h:37 + h], ALU.subtract, ALU.mult, osr + ['st'], ['t2'])
                for half in range(2):
                    ps, pn = self.psum()
                    cs_ = range(0, 4) if half == 0 else range(4, 6)
                    for c in cs_:
                        self.TR(ps[:, (c % 4) * 128:(c % 4 + 1) * 128], t2[:, c, :], ['t2'], [pn])
                    for c in cs_:
                        self.TSC('dve', cs[:, c, :], ps[:, (c % 4) * 128:(c % 4 + 1) * 128], prm[:, l, P_LNW + c:P_LNW + c + 1], prm[:, l, P_LNB + c:P_LNB + c + 1], ALU.mult, ALU.add, [pn, 'prm'], ['cs'])
                if self.rw_stop is not None and self.rw_stop < 17:
                    continue
                self.TT('pool', t1[:], xs[:, 0:6, :], kf[:], ALU.mult, ['xs', 'kf'], ['t1'])
                for c in range(6):
                    self.TSC('pool', t1[:, c, :], t1[:, c, :], prm[:, l, P_RK + c:P_RK + c + 1], None, ALU.mult, None, ['t1', 'prm'], ['t1'])
                for half in range(2):
                    ps, pn = self.psum()
                    cs_ = range(0, 4) if half == 0 else range(4, 6)
                    for c in cs_:
                        self.MM(ps[:, (c % 4) * 128:(c % 4 + 1) * 128], blk[:], t1[:, c, :], True, True, ['blk', 't1'], [pn])
                    n = len(cs_)
                    self.TT('dve', t2[:, cs_[0]:cs_[0] + n, :], ps[:, 0:n * 128].rearrange('p (c t) -> p c t', t=128), xs[:, 12 + cs_[0]:12 + cs_[0] + n, :], ALU.mult, [pn, 'xs'], ['t2'])
                self.TT('dve', cs[:], cs[:], t2[:], ALU.add, ['cs', 't2'], ['cs'])
                self.TT('dve', ybf[:], cs[:], gT[:], ALU.mult, ['cs', 'gT'], ['ybf'])
                self.dma(self.yT[0:768, t0:t0 + 128].rearrange('(c p) t -> p c t', p=128), ybf[:], ['ybf'], ['yT_rw'])
                if self.rw_stop is not None and self.rw_stop < 18:
                    continue
                for sb in range(nsub):
                    if sample:
                        HG = M4[:, sb % 2, :].rearrange('p (c v) -> p c v', v=64)
                        hgr = 'M4_%d' % (sb % 2)
                        gend = Gm[:, :, sb * 8 + 7:sb * 8 + 8].to_broadcast([128, 6, 64])
                        self.TT('dve', HG, Hin[:, sb, :, :], gend, ALU.mult, ['Hin%d' % sb, 'Gm'], [hgr])
                        self.TSC('dve', kk[:].rearrange('p c t -> p (c t)'), Btm[:], rm16[:, sb:sb + 1], None, ALU.mult, None, ['Btm0', 'Btm1', 'rm16'], ['kk'])
                        self.ACT(nb[:].rearrange('p c t -> p (c t)'), Ktm[:], AF.Copy, ['Ktm0', 'Ktm1', 'rm16'], ['nb'], scale=rm16[:, sb:sb + 1])
                        bl, kl, blr, klr = kk[:].rearrange('p c t -> p (c t)'), nb[:].rearrange('p c t -> p (c t)'), ['kk'], ['nb']
                        tend = sb * 8 + 7
                    else:
                        bl, kl, blr, klr = Btm[:], Ktm[:], ['Btm0', 'Btm1'], ['Ktm0', 'Ktm1']
                        tend = 127
                    for c in range(6):
                        ps, pn = self.psum()
                        self.MM(ps[:, 0:128], bl[:, c * 128:(c + 1) * 128], sgw[:].rearrange('p c t -> p (c t)')[:, c * 128:(c + 1) * 128], True, False, blr + ur, [pn])
                        self.MM(ps[:, 0:128], kl[:, c * 128:(c + 1) * 128], Vtm[:, c * 128:(c + 1) * 128], False, True, klr + VT_R, [pn])
                        for h2 in range(2):
                            pr = slice(64 * h2, 64 * h2 + 64)
                            if sample:
                                self.STT(Ho(sb)[pr, c, :], ps[pr, 64 * h2:64 * h2 + 64], Gm[pr, c, tend:tend + 1], HG[pr, c, :], ALU.mult, ALU.add, [pn, 'Gm', hgr], [hor(sb)])
                            else:
                                self.TT('dve', htmp[pr, :], ps[pr, 64 * h2:64 * h2 + 64], Hi(sb)[pr, c, :], ALU.add, [pn, hir(sb)], ['htmp'])
                                self.TSC('dve', Ho(sb)[pr, c, :], htmp[pr, :], Gm[pr, c, tend:tend + 1], None, ALU.mult, None, ['htmp', 'Gm'], [hor(sb)])
                if self.rw_stop is not None and self.rw_stop < 19:
                    continue
                last_prompt = (not sample) and ti == TP // 128 - 1
                if sample or last_prompt:
                    for sb in range(nsub):
                        b = sb % 2
                        ps, pn = self.psum()
                        ps2, pn2 = self.psum()
                        for c in range(4):
                            self.TR(ps[0:64, c * 128:(c + 1) * 128], Ho(sb)[:, c, :], [hor(sb)], [pn])
                        for c in range(4, 6):
                            self.TR(ps2[0:64, (c - 4) * 128:(c - 3) * 128], Ho(sb)[:, c, :], [hor(sb)], [pn2])
                        self.CP('act', sout[b][:, 0:4, :, :], ps[0:64, :].rearrange('p (c h k) -> p c h k', h=2, k=64), [pn], ['sin0'])
                        self.CP('dve', sout[b][:, 4:6, :, :], ps2[0:64, 0:256].rearrange('p (c h k) -> p c h k', h=2, k=64), [pn2], ['sin0'])
                        dst = (O_['s_rwkv'][l, sb] if sample else O_['p_rwkv'][l]).rearrange('(c h2) v k -> v c h2 k', h2=2)
                        self.dma(dst, sout[b][:], ['sin0'], [])
                    if sample:
                        for g in range(5):
                            a0 = 2 * (g % 2)
                            stg = M4[0:16, a0:a0 + 2, :].rearrange('p a b -> p (a b)')[:, 0:512]
                            sres = ['M4_%d' % a0, 'M4_%d' % (a0 + 1)]
                            ps, pn = self.psum()
                            for j in range(4):
                                self.TR(ps[0:16, j * 128:(j + 1) * 128], pv[:, g * 4 + j, :, 8], ['pbuf'], [pn])
                            self.CP('act', stg, ps[0:16, 0:512], [pn], sres)
                            self.dma(O_['s_rwkv_shift'][l, :, g * 512:(g + 1) * 512], stg, sres, [])
                    else:
                        self.dma(O_['p_rwkv_shift'][l].rearrange('(c p) -> p c', p=128), pbuf[:, :, 128], ['pbuf'], [], slow=True)
    def phase_hgrn(self, l, I, O_, side=None):
        nc = self.nc
        TP = self.TP
        prm = self.prm
        with contextlib.ExitStack() as es:
            A = lambda name, shape, dt=F32: es.enter_context(nc.sbuf_tensor('hg%d_%s' % (l, name), list(shape), dt))
            masks = {}
            for mk, ns in (('4', 4), ('16', 16)):
                up = A('up' + mk, [128, 128])
                rst = A('rst' + mk, [128, 128])
                cm = A('cm' + mk, [128, ns, 128])
                rm = A('rm' + mk, [128, ns])
                self.dma(up[:], self.cst['up_i' + mk], [], ['up' + mk])
                self.dma(rst[:], self.cst['rst' + mk], [], ['rst' + mk])
                self.dma(cm[:], self.cst['cm' + mk].rearrange('p (s t) -> p s t', s=ns), [], ['cm' + mk])
                self.dma(rm[:], self.cst['rm' + mk], [], ['rm' + mk])
                masks[mk] = (up, rst, cm, rm)
            pb = A('pb', [128, 24, 128])
            q = A('q', [128, 6, 128])
            fg = A('fg', [128, 6, 128])
            lf = A('lf', [128, 6, 128])
            kx = A('kx', [128, 6, 128])
            cs = A('cs', [128, 6, 128])
            Gm = A('Gm', [128, 6, 128])
            Gi = A('Gi', [128, 6, 128])
            QT = A('QT', [128, 6, 128])
            KT = A('KT', [128, 6, 128])
            sg = A('sg', [128, 6, 128])
            Vtm = A('Vtm', [128, 6, 128])
            Ktm = A('Ktm', [128, 6, 128])
            KM = A('KM', [128, 16, 128])
            QM = A('QM', [128, 16, 128])
            AT = A('AT', [128, 128])
            Hs = A('Hs', [128, 16, 128])
            Ho = A('Ho', [128, 16, 128])
            Hc = A('Hc', [128, 6, 128])
            htmp = A('htmp', [128, 128])
            Hs6 = A('Hs6', [128, 6, 5, 128])
            AT6 = A('AT6', [128, 6, 128])
            tmp6 = A('tmp6', [128, 6, 128])
            on6 = A('on6', [128, 6, 128])
            ybf6 = A('ybf6', [128, 6, 128], BF16)
            ss6 = A('ss6', [128, 24])
            junk = htmp
            ss = A('ss', [128, 4])
            on = A('on', [128, 128])
            ybf = A('ybf', [128, 128], BF16)
            self.MEMSET('dve', Hc[:], 0.0, ['Hc'])
            tiles = [('p', ti) for ti in range(TP // 128)] + [('s', 0)]
            for kind, ti in tiles:
                sample = kind == 's'
                t0 = TP if sample else ti * 128
                mk = '16' if sample else '4'
                nsub = 16 if sample else 4
                sub = 128 // nsub
                up, rst, cm, rm = masks[mk]
                if sample or ti == 0:
                    self.dma(pb[:], self.projT[RW_COLS:RW_COLS + HG_COLS, t0:t0 + 128].rearrange('(c p) t -> p c t', p=128), ['projT'], ['pb'])
                self.ACT(q[:], pb[:, 0:6, :], AF.Silu, ['pb'], ['q'])
                self.ACT(sg[:], pb[:, 18:24, :], AF.Silu, ['pb'], ['sg'])
                self.ACT(fg[:], pb[:, 6:12, :], AF.Sigmoid, ['pb'], ['fg'])
                for h in range(6):
                    self.TSC('dve', fg[:, h, :], fg[:, h, :], prm[:, l, P_OMLB + h:P_OMLB + h + 1], prm[:, l, P_LB + h:P_LB + h + 1], ALU.mult, ALU.add, ['fg', 'prm'], ['fg'])
                self.ACT(lf[:], fg[:], AF.Ln, ['fg'], ['lf'])
                self.TSC('dve', kx[:], fg[:], -1.0, 1.0, ALU.mult, ALU.add, ['fg'], ['kx'])
                for h in range(6):
                    self.SCAN(cs[:, h, :], rst[:], lf[:, h, :], 0.0, ['lf', 'rst' + mk], ['cs'])
                self.TSC('dve', cs[:], cs[:], -80.0, None, ALU.max, None, ['cs'], ['cs'])
                self.ACT(Gm[:], cs[:], AF.Exp, ['cs'], ['Gm'])
                self.ACT(Gi[:], cs[:], AF.Exp, ['cs'], ['Gi'], scale=-1.0)
                self.TT('dve', QT[:], q[:], Gm[:], ALU.mult, ['q', 'Gm'], ['QT'])
                self.TT('dve', KT[:], kx[:], Gi[:], ALU.mult, ['kx', 'Gi'], ['KT'])
                for (src_fn, sres, dst, dres) in ((lambda h: pb[:, 12 + h, :], 'pb', Vtm, 'Vtm'), (lambda h: KT[:, h, :], 'KT', Ktm, 'Ktm')):
                    for half in range(2):
                        ps, pn = self.psum()
                        hs = range(0, 4) if half == 0 else range(4, 6)
                        for h in hs:
                            self.TR(ps[:, (h % 4) * 128:(h % 4 + 1) * 128], src_fn(h), [sres], [pn])
                        n = len(hs)
                        self.CP('act' if half == 0 else 'dve', dst[:, hs[0]:hs[0] + n, :], ps[:, 0:n * 128].rearrange('p (h t) -> p h t', t=128), [pn], [dres + str(half)])
                vr, kr = ['Vtm0', 'Vtm1'], ['Ktm0', 'Ktm1']
                if (not sample) and ti + 1 < TP // 128:
                    self.dma(pb[:], self.projT[RW_COLS:RW_COLS + HG_COLS, t0 + 128:t0 + 256].rearrange('(c p) t -> p c t', p=128), ['projT'], ['pb'])
                if not sample:
                    for half in range(2):
                        ps, pn = self.psum()
                        hs = range(0, 4) if half == 0 else range(4, 6)
                        for h in hs:
                            self.MM(ps[:, (h % 4) * 128:(h % 4 + 1) * 128], KT[:, h, :], QT[:, h, :], True, True, ['KT', 'QT'], [pn])
                        n = len(hs)
                        self.TT('dve', AT6[:, hs[0]:hs[0] + n, :], ps[:, 0:n * 128].rearrange('p (h t) -> p h t', t=128),
                                up[:, None, :].to_broadcast([128, n, 128]) if False else up[:].rearrange('p (a t) -> p a t', a=1).to_broadcast([128, n, 128]), ALU.mult, [pn, 'up' + mk], ['AT6_%d' % half])
                    atr = ['AT6_0', 'AT6_1']
                    self.TSC('dve', KM[:, 0:6, :], Ktm[:], rm[:, 3:4], None, ALU.mult, None, kr + ['rm' + mk], ['KM3'])
                    self.TT('dve', QM[:, 0:6, :], QT[:], cm[:, 3:4, :].to_broadcast([128, 6, 128]), ALU.mult, ['QT', 'cm' + mk], ['QM3'])
                    self.CP('act', Hs6[:, :, 0, :], Hc[:], ['Hc'], ['Hs6_0'])
                    for sb in range(4):
                        pss = [self.psum(), self.psum()]
                        for h in range(6):
                            ps, pn = pss[h // 4]
                            o = ps[:, (h % 4) * 128:(h % 4 + 1) * 128]
                            if sb == 3:
                                self.MM(o, KM[:, h, :], Vtm[:, h, :], True, True, ['KM3'] + vr, [pn])
                            else:
                                pr = slice(sb * 32, (sb + 1) * 32)
                                self.MM(o, Ktm[pr, h, :], Vtm[pr, h, :], True, True, kr + vr, [pn])
                        for half in range(2):
                            ps, pn = pss[half]
                            hs = range(0, 4) if half == 0 else range(4, 6)
                            n = len(hs)
                            self.TT('dve', tmp6[:, hs[0]:hs[0] + n, :], ps[:, 0:n * 128].rearrange('p (h t) -> p h t', t=128), Hs6[:, hs[0]:hs[0] + n, sb, :], ALU.add,
                                    [pn, 'Hs6_%d' % sb], ['tmp6_%d' % half])
                        tend = sb * 32 + 31
                        self.TT('dve', Hs6[:, :, sb + 1, :], tmp6[:], Gm[:, :, tend:tend + 1].to_broadcast([128, 6, 128]), ALU.mult, ['tmp6_0', 'tmp6_1', 'Gm'], ['Hs6_%d' % (sb + 1)])
                    psO = [self.psum(), self.psum()]
                    for h in range(6):
                        ps, pn = psO[h // 4]
                        o = ps[:, (h % 4) * 128:(h % 4 + 1) * 128]
                        self.MM(o, QM[:, h, :], Hs6[:, h, 3, :], True, False, ['QM3', 'Hs6_3'], [pn], sgc=True)
                        for sb in range(3):
                            pr = slice(sb * 32, (sb + 1) * 32)
                            self.MM(ps[pr, (h % 4) * 128:(h % 4 + 1) * 128], QT[:, h, pr], Hs6[:, h, sb, :], True, False, ['QT', 'Hs6_%d' % sb], [pn], sgc=True)
                        self.MM(o, AT6[:, h, :], Vtm[:, h, :], False, True, atr + vr, [pn], sgc=True)
                    for h in range(6):
                        ps, pn = psO[h // 4]
                        self.ACT(junk[:], ps[:, (h % 4) * 128:(h % 4 + 1) * 128], AF.Square, [pn], ['htmp', 'ss6'], accum=ss6[:, h:h + 1])
                    self.ACT(ss6[:, 6:12], ss6[:, 0:6], AF.Sqrt, ['ss6'], ['ss6'], bias=EPS, scale=1.0 / 128)
                    self.RECIP(ss6[:, 12:18], ss6[:, 6:12], ['ss6'], ['ss6'])
                    for half in range(2):
                        ps, pn = psO[half]
                        hs = range(0, 4) if half == 0 else range(4, 6)
                        n = len(hs)
                        self.TT('dve', on6[:, hs[0]:hs[0] + n, :], ps[:, 0:n * 128].rearrange('p (h t) -> p h t', t=128),
                                ss6[:, 12 + hs[0]:12 + hs[0] + n].rearrange('p (h a) -> p h a', a=1).to_broadcast([128, n, 128]), ALU.mult, [pn, 'ss6'], ['on6_%d' % half])
                    for half in range(2):
                        ps2, pn2 = self.psum()
                        hs = range(0, 4) if half == 0 else range(4, 6)
                        for h in hs:
                            self.TR(ps2[:, (h % 4) * 128:(h % 4 + 1) * 128], on6[:, h, :], ['on6_0', 'on6_1'], [pn2])
                        for h in hs:
                            self.STT(ybf6[:, h, :], ps2[:, (h % 4) * 128:(h % 4 + 1) * 128], prm[:, l, P_HNW + h:P_HNW + h + 1], sg[:, h, :], ALU.mult, ALU.mult, [pn2, 'prm', 'sg'], ['ybf6'])
                    self.dma(self.yT[768:1536, t0:t0 + 128].rearrange('(h p) t -> p h t', p=128), ybf6[:], ['ybf6'], ['yT_hg'])
                    self.CP('act', Hc[:], Hs6[:, :, 4, :], ['Hs6_4'], ['Hc'])
                    if ti == TP // 128 - 1:
                        self.dma(O_['p_hgrn'][l].rearrange('h k v -> k h v'), Hs6[:, :, 4, :], ['Hs6_4'], [])
                    if side is not None and (ti % 2 == 1 or ti == 0):
                        next(side, None)
                    continue
                for h in range(6):
                    ps, pn = self.psum()
                    self.MM(ps[:, 0:128], KT[:, h, :], QT[:, h, :], True, True, ['KT', 'QT'], [pn])
                    self.TT('dve', AT[:], ps[:, 0:128], up[:], ALU.mult, [pn, 'up' + mk], ['AT'])
                    if sample:
                        self.dma(Hs[:, 0:16, :], I['state_hgrn'][l, :, h, :, :].rearrange('s k v -> k s v'), [], ['Hs%d' % sb for sb in range(16)])
                        hin = lambda sb: (Hs[:, sb, :], 'Hs%d' % sb)
                        hout = lambda sb: (Ho[:, sb, :], 'Ho%d' % sb)
                    else:
                        self.CP('pool', Hs[:, 0, :], Hc[:, h, :], ['Hc'], ['Hs0'])
                        hin = lambda sb: (Hs[:, sb, :], 'Hs%d' % sb)
                        hout = lambda sb: (Hs[:, sb + 1, :], 'Hs%d' % (sb + 1))
                    msk = (lambda sb: True) if sample else (lambda sb: sb == 3)
                    for sb in range(nsub):
                        if msk(sb):
                            self.TSC('dve', KM[:, sb, :], Ktm[:, h, :], rm[:, sb:sb + 1], None, ALU.mult, None, kr + ['rm' + mk], ['KM%d' % sb])
                            self.TT('pool' if (sample and sb % 4 == 3) else 'dve', QM[:, sb, :], QT[:, h, :], cm[:, sb, :], ALU.mult, ['QT', 'cm' + mk], ['QM%d' % sb])
                    for sb in range(nsub):
                        ps, pn = self.psum()
                        if msk(sb):
                            self.MM(ps[:, 0:128], KM[:, sb, :], Vtm[:, h, :], True, True, ['KM%d' % sb] + vr, [pn])
                        else:
                            pr = slice(sb * sub, (sb + 1) * sub)
                            self.MM(ps[:, 0:128], Ktm[pr, h, :], Vtm[pr, h, :], True, True, kr + vr, [pn])
                        hi, hir = hin(sb)
                        ho, hor = hout(sb)
                        tend = sb * sub + sub - 1
                        self.TT('dve', htmp[:], ps[:, 0:128], hi, ALU.add, [pn, hir], ['htmp'])
                        self.TSC('dve', ho, htmp[:], Gm[:, h, tend:tend + 1], None, ALU.mult, None, ['htmp', 'Gm'], [hor])
                    ps, pn = self.psum()
                    for sb in ([3, 0, 1, 2] if not sample else range(nsub)):
                        hi, hir = hin(sb)
                        if sample:
                            self.MM(ps[:, 0:128], QM[:, sb, :], hi, sb == 0, False, ['QM%d' % sb, hir], [pn])
                        elif sb == 3:
                            self.MM(ps[:, 0:128], QM[:, sb, :], hi, True, False, ['QM%d' % sb, hir], [pn], sgc=True)
                        else:
                            pr = slice(sb * sub, (sb + 1) * sub)
                            self.MM(ps[pr, 0:128], QT[:, h, pr], hi, True, False, ['QT', hir], [pn], sgc=True)
                    self.MM(ps[:, 0:128], AT[:], Vtm[:, h, :], False, True, ['AT'] + vr, [pn], sgc=not sample)
                    self.ACT(junk[:], ps[:, 0:128], AF.Square, [pn], ['htmp', 'ss'], accum=ss[:, 0:1])
                    self.ACT(ss[:, 1:2], ss[:, 0:1], AF.Sqrt, ['ss'], ['ss'], bias=EPS, scale=1.0 / 128)
                    self.RECIP(ss[:, 2:3], ss[:, 1:2], ['ss'], ['ss'])
                    self.TSC('dve', on[:], ps[:, 0:128], ss[:, 2:3], None, ALU.mult, None, [pn, 'ss'], ['on'])
                    ps2, pn2 = self.psum()
                    self.TR(ps2[:, 0:128], on[:], ['on'], [pn2])
                    self.STT(ybf[:], ps2[:, 0:128], prm[:, l, P_HNW + h:P_HNW + h + 1], sg[:, h, :], ALU.mult, ALU.mult, [pn2, 'prm', 'sg'], ['ybf'])
                    self.dma(self.yT[768 + h * 128:768 + (h + 1) * 128, t0:t0 + 128], ybf[:], ['ybf'], ['yT_hg'])
                    if sample:
                        self.dma(O_['s_hgrn'][l, :, h, :, :].rearrange('s k v -> k s v'), Ho[:], ['Ho%d' % sb for sb in range(16)], [])
                    else:
                        self.CP('pool', Hc[:, h, :], Hs[:, nsub, :], ['Hs%d' % nsub], ['Hc'])
                        if ti == TP // 128 - 1:
                            self.dma(O_['p_hgrn'][l, h, :, :], Hs[:, nsub, :], ['Hs%d' % nsub], [])
                if side is not None and (ti % 2 == 1 or sample or ti == 0):
                    next(side, None)
            if side is not None:
                for _ in side:
                    pass

    def phase_lru(self, l, I, O_):
        for _ in self.lru_gen(l, I, O_):
            pass

    def lru_gen(self, l, I, O_):
        nc = self.nc
        TP = self.TP
        prm = self.prm
        base = RW_COLS + HG_COLS
        with contextlib.ExitStack() as es:
            A = lambda name, shape, dt=F32: es.enter_context(nc.sbuf_tensor('lr%d_%s' % (l, name), list(shape), dt))
            wabd = A('wabd', [128, 4, 128])
            wxbd = A('wxbd', [128, 4, 128])
            self.MEMSET('dve', wabd[:], 0.0, ['wabd'])
            self.MEMSET('dve', wxbd[:], 0.0, ['wxbd'])
            for blk in range(8):
                pr = slice(64 * (blk % 2), 64 * (blk % 2) + 64)
                self.dma(wabd[pr, blk // 2, 64 * (blk % 2):64 * (blk % 2) + 64], I['rglru_wa'][l, blk], [], ['wabd'])
                self.dma(wxbd[pr, blk // 2, 64 * (blk % 2):64 * (blk % 2) + 64], I['rglru_wx'][l, blk], [], ['wxbd'])
            TM = max(TP, 128)
            xpb = [A('xp_p%d' % c, [128, TP + 3]) for c in range(2)]
            gtb = [A('gt_p%d' % c, [128, TP]) for c in range(2)]
            xps = {('p', c): xpb[c % 2] for c in range(4)}
            gts = {('p', c): gtb[c % 2] for c in range(4)}
            for c in range(4):
                xps[('s', c)] = A('xp_s%d' % c, [128, 16 * 11])
                gts[('s', c)] = A('gt_s%d' % c, [128, 128])
            h0s = [A('h0_%d' % c, [128, 16]) for c in range(4)]
            xc = A('xc', [128, TM])
            rr = A('rr', [128, TM])
            ig = A('ig', [128, TM])
            aa = A('aa', [128, TM])
            uu = A('uu', [128, TM])
            hh = xc
            ybf = aa[:].bitcast(BF16)
            def prefetch(kind, c):
                    sample = kind == 's'
                    nseq, T, t0 = (16, 8, TP) if sample else (1, TP, 0)
                    N = nseq * T
                    xp, gt, h0 = xps[(kind, c)], gts[(kind, c)], h0s[c]
                    xn, gn_ = 'xp%s%d' % (kind, c % 2 if kind == 'p' else c), 'gt%s%d' % (kind, c % 2 if kind == 'p' else c)
                    xpv = xp[:, 0:nseq * (T + 3)].rearrange('p (s t) -> p s t', t=T + 3)
                    if sample:
                        s48 = rr[0:48, 0:128]
                        s16 = rr[0:16, 128:256]
                        self.dma(s48, I['cache_rglru_conv'][l, :, :, c * 128:(c + 1) * 128].rearrange('s j p -> (s j) p'), [], ['rr'])
                        self.dma(s16, I['state_rglru'][l, :, c * 128:(c + 1) * 128], [], ['rr'])
                        ps, pn = self.psum()
                        self.TR(ps[:, 0:48], s48, ['rr'], [pn], n=48)
                        self.TR(ps[:, 64:80], s16, ['rr'], [pn], n=16)
                        self.CP('dve', xpv[:, :, 0:3], ps[:, 0:48].rearrange('p (s j) -> p s j', j=3), [pn], [xn])
                        self.CP('dve', h0[:], ps[:, 64:80], [pn], ['h0_%d' % c])
                    else:
                        self.MEMSET('pool', xpv[:, :, 0:3], 0.0, [xn])
                    r0 = base + c * 128
                    self.dma(xpv[:, :, 3:T + 3], self.projT[r0:r0 + 128, t0:t0 + N].rearrange('p (s t) -> p s t', t=T), ['projT'], [xn])
                    self.dma(gt[:, 0:N], self.projT[r0 + 512:r0 + 640, t0:t0 + N], ['projT'], [gn_])
            for c in range(2):
                prefetch('p', c)
            for c in range(4):
                prefetch('s', c)
            yield
            for kind in ('p', 's'):
                sample = kind == 's'
                nseq, T, t0 = (16, 8, TP) if sample else (1, TP, 0)
                N = nseq * T
                for c in range(4):
                    xp, gt, h0 = xps[(kind, c)], gts[(kind, c)], h0s[c]
                    xn, gn_ = 'xp%s%d' % (kind, c % 2 if kind == 'p' else c), 'gt%s%d' % (kind, c % 2 if kind == 'p' else c)
                    xpv = xp[:, 0:nseq * (T + 3)].rearrange('p (s t) -> p s t', t=T + 3)
                    v3 = lambda ap: ap[:, 0:N].rearrange('p (s t) -> p s t', t=T)
                    r0 = base + c * 128
                    cw = lambda j: prm[:, l, P_CW + 4 * j + c:P_CW + 4 * j + c + 1]
                    self.TSC('dve', v3(xc), xpv[:, :, 0:T], cw(0), prm[:, l, P_CB + c:P_CB + c + 1], ALU.mult, ALU.add, [xn, 'prm'], ['xc'])
                    for j in range(1, 4):
                        self.STT(v3(xc), xpv[:, :, j:j + T], cw(j), v3(xc), ALU.mult, ALU.add, [xn, 'prm', 'xc'], ['xc'])
                    g0 = 0
                    while g0 < N:
                        gn = min(512, N - g0)
                        ps, pn = self.psum()
                        self.MM(ps[:, 0:gn], wabd[:, c, :], xc[:, g0:g0 + gn], True, True, ['wabd', 'xc'], [pn])
                        self.ACT(rr[:, g0:g0 + gn], ps[:, 0:gn], AF.Sigmoid, [pn, 'prm'], ['rr'], bias=prm[:, l, P_BA + c:P_BA + c + 1])
                        ps, pn = self.psum()
                        self.MM(ps[:, 0:gn], wxbd[:, c, :], xc[:, g0:g0 + gn], True, True, ['wxbd', 'xc'], [pn])
                        self.ACT(ig[:, g0:g0 + gn], ps[:, 0:gn], AF.Sigmoid, [pn, 'prm'], ['ig'], bias=prm[:, l, P_BX + c:P_BX + c + 1])
                        g0 += gn
                    self.ACT(aa[:, 0:N], rr[:, 0:N], AF.Exp, ['rr', 'prm'], ['aa'], scale=prm[:, l, P_C8 + c:P_C8 + c + 1])
                    self.ACT(uu[:, 0:N], rr[:, 0:N], AF.Exp, ['rr', 'prm'], ['uu'], scale=prm[:, l, P_2C8 + c:P_2C8 + c + 1])
                    self.ACT(uu[:, 0:N], uu[:, 0:N], AF.Sqrt, ['uu'], ['uu'], bias=1.0, scale=-1.0)
                    self.TT('pool', ig[:, 0:N], ig[:, 0:N], xc[:, 0:N], ALU.mult, ['ig', 'xc'], ['ig'])
                    self.TT('dve', uu[:, 0:N], uu[:, 0:N], ig[:, 0:N], ALU.mult, ['uu', 'ig'], ['uu'])
                    if sample:
                        self.TT('dve', h0[:], h0[:], v3(aa)[:, :, 0], ALU.mult, ['h0_%d' % c, 'aa'], ['h0_%d' % c])
                        self.TT('dve', v3(uu)[:, :, 0], v3(uu)[:, :, 0], h0[:], ALU.add, ['uu', 'h0_%d' % c], ['uu'])
                        self.MEMSET('dve', v3(aa)[:, :, 0:1], 0.0, ['aa'])
                    self.SCAN2(hh[:, 0:N], aa[:, 0:N], uu[:, 0:N], ['aa', 'uu'], ['xc'])
                    self.TT('pool', rr[:, 0:N], gt[:, 0:N], gt[:, 0:N], ALU.mult, [gn_, 'rr'], ['rr'])
                    self.TSC('pool', rr[:, 0:N], rr[:, 0:N], 0.044715, 1.0, ALU.mult, ALU.add, ['rr'], ['rr'])
                    self.TT('pool', rr[:, 0:N], rr[:, 0:N], gt[:, 0:N], ALU.mult, ['rr', gn_], ['rr'])
                    self.ACT(rr[:, 0:N], rr[:, 0:N], AF.Sigmoid, ['rr'], ['rr'], scale=1.5957691216057308)
                    self.TT('dve', ig[:, 0:N], hh[:, 0:N], gt[:, 0:N], ALU.mult, ['xc', gn_, 'ig'], ['ig'])
                    self.TT('dve', ybf[:, 0:N], ig[:, 0:N], rr[:, 0:N], ALU.mult, ['ig', 'rr'], ['aa'])
                    self.dma(self.yT[1536 + c * 128:1536 + (c + 1) * 128, t0:t0 + N], ybf[:, 0:N], ['aa'], ['yT_lru'])
                    if sample:
                        self.CP('dve', rr[:, 0:48].rearrange('p (s j) -> p s j', j=3), xpv[:, :, T:T + 3], [xn, 'rr'], ['rr'])
                        self.CP('dve', rr[:, 64:80], v3(hh)[:, :, T - 1], ['xc', 'rr'], ['rr'])
                        ps, pn = self.psum()
                        self.TR(ps[0:48, 0:128], rr[:, 0:48], ['rr'], [pn])
                        ps2, pn2 = self.psum()
                        self.TR(ps2[0:16, 0:128], rr[:, 64:80], ['rr'], [pn2])
                        self.CP('act', ig[0:48, 0:128], ps[0:48, 0:128], [pn, 'ig'], ['ig'])
                        self.CP('act', ig[0:16, 128:256], ps2[0:16, 0:128], [pn2, 'ig'], ['ig'])
                        self.dma(O_['s_rglru_conv'][l, :, :, c * 128:(c + 1) * 128].rearrange('s j p -> (s j) p'), ig[0:48, 0:128], ['ig'], [])
                        self.dma(O_['s_rglru'][l, :, c * 128:(c + 1) * 128], ig[0:16, 128:256], ['ig'], [])
                    else:
                        self.dma(O_['p_rglru_conv'][l, :, c * 128:(c + 1) * 128].rearrange('j p -> p j'), xpv[:, 0, T:T + 3], [xn], [], slow=True)
                        self.dma(O_['p_rglru'][l, c * 128:(c + 1) * 128].rearrange('(p o) -> p o', o=1), hh[:, T - 1:T], ['xc'], [], slow=True)
                    if kind == 'p' and c < 2:
                        prefetch('p', c + 2)
                    yield

    def SCAN2(self, out, d0, d1, r, w):
        self.E('dve', lambda e: e.tensor_tensor_scan(out, d0, d1, 0.0, ALU.mult, ALU.add), r, w)
    def gemm(self, tag, A, KC, blocks, rhs_fn, rhs_res, epilogue, cbw=256, tgs=None, bufs=None, single=False):
        tgs = self.tg if tgs is None else tgs
        if bufs is None:
            wst = [A('%s_wst%d' % (tag, i), [128, KC, cbw]) for i in range(1 if single else 2)]
            if single:
                wst = wst * 2
            wbf = [A('%s_wbf%d' % (tag, i), [128, KC, cbw], BF16) for i in range(2)]
        else:
            wst, wbf, tag = bufs

        sb_ = (lambda b: 0) if single else (lambda b: b)
        ka = (KC * 7) // 16
        kparts = [(0, ka), (ka, 2 * ka), (2 * ka, KC)]
        kp_of = lambda kc: 0 if kc < ka else (1 if kc < 2 * ka else 2)

        def load_dma(bi):
            b = bi % 2
            off = 0
            for ap, w in blocks[bi]:
                self.dma(wst[b][:, :, off:off + w], ap.rearrange('(c p) n -> p c n', p=128), [], ['%s_wst%d' % (tag, sb_(b))])
                off += w
            return off

        def load_cast(bi, off):
            b = bi % 2
            for p, (k0, k1) in enumerate(kparts):
                self.CP(('dve', 'act', 'pool')[p], wbf[b][:, k0:k1, 0:off], wst[b][:, k0:k1, 0:off], ['%s_wst%d' % (tag, sb_(b))], ['%s_wbf%d_%d' % (tag, b, p)])

        widths = {0: load_dma(0)}
        load_cast(0, widths[0])
        for bi in range(len(blocks)):
            if bi + 1 < len(blocks):
                widths[bi + 1] = load_dma(bi + 1)
            b = bi % 2
            nu = widths[bi] // 128
            work = [(u, gi, t0, tn) for u in range(nu) for gi, (t0, tn) in enumerate(tgs)]
            cast_at = (len(work) * 3) // 5
            for wi, (u, gi, t0, tn) in enumerate(work):
                if wi == cast_at and bi + 1 < len(blocks):
                    load_cast(bi + 1, widths[bi + 1])
                ps, pn = self.psum()
                for kc in range(KC):
                    self.MM(ps[:, 0:tn], wbf[b][:, kc, u * 128:(u + 1) * 128], rhs_fn(kc, t0, tn), kc == 0, kc == KC - 1,
                            ['%s_wbf%d_%d' % (tag, b, kp_of(kc))] + rhs_res, [pn])
                epilogue(bi, u, gi, t0, tn, ps, pn)

    def phase_inproj(self, l, actT, I, A):
        st = [A('ip_st%d_%d' % (l, i), [128, 512]) for i in range(4)]
        w_in = I['w_in']
        blocks = [[(w_in[l, :, c0:c0 + 512], 512)] for c0 in range(0, IN_COLS, 512)]
        self._k = 0

        def epi(bi, u, gi, t0, tn, ps, pn):
            k = self._k % 4
            self._k += 1
            c0 = bi * 512 + u * 128
            self.CP('act' if k % 2 == 0 else 'dve', st[k][:, 0:tn], ps[:, 0:tn], [pn], ['ip_st%d' % k])
            self.dma(self.projT[c0:c0 + 128, t0:t0 + tn], st[k][:, 0:tn], ['ip_st%d' % k], ['projT'])

        self.gemm('ip%d' % l, A, 16, blocks, lambda kc, t0, tn: actT[:, kc, t0:t0 + tn], ['actT'], epi, cbw=512)

    def residual_epi(self, A, tag):
        xs = [A('%s_x%d' % (tag, i), [128, 512]) for i in range(4)]
        self._k = 0

        def epi(c0, t0, tn, ps, pn):
            k = self._k % 4
            self._k += 1
            rn = '%s_x%d' % (tag, k)
            self.dma(xs[k][:, 0:tn], self.xT[c0:c0 + 128, t0:t0 + tn], ['xT'], [rn])
            self.TT('dve', xs[k][:, 0:tn], xs[k][:, 0:tn], ps[:, 0:tn], ALU.add, [rn, pn], [rn])
            self.dma(self.xT[c0:c0 + 128, t0:t0 + tn], xs[k][:, 0:tn], [rn], ['xT'])
        return epi

    def phase_outproj(self, l, actT, I, A):
        self.dma(actT[:], self.yT.rearrange('(c p) t -> p c t', p=128), ['yT_rw', 'yT_hg', 'yT_lru'], ['actT'])
        w = I['w_out']
        blocks = [[(w[l, :, c0:c0 + 512], 512)] for c0 in range(0, D, 512)]
        repi = self.residual_epi(A, 'op%d' % l)
        self.gemm('op%d' % l, A, 16, blocks, lambda kc, t0, tn: actT[:, kc, t0:t0 + tn], ['actT'],
                  lambda bi, u, gi, t0, tn, ps, pn: repi(bi * 512 + u * 128, t0, tn, ps, pn), cbw=512)

    def phase_ffn_up(self, l, actT, I, O_, A):
        TP, NT = self.TP, self.NT
        fp = self.fprm
        w = I['ffn_w_up']
        blocks = [[(w[l, :, j * 256:(j + 1) * 256], 256), (w[l, :, DFF + j * 256:DFF + (j + 1) * 256], 256)] for j in range(DFF // 256)]
        gp = [A('fu%d_gp%d' % (l, i), [128, TP + 2]) for i in range(2)]
        gs = [A('fu%d_gs%d' % (l, i), [128, 16, 10]) for i in range(2)]
        vb = [A('fu%d_vb%d' % (l, i), [128, NT]) for i in range(2)]
        cv = A('fu%d_cv' % l, [128, NT])
        hb = [A('fu%d_hb%d' % (l, i), [128, NT], BF16) for i in range(2)]
        for i in range(2):
            self.MEMSET('pool', gp[i][:, 0:2], 0.0, ['gp%d' % i])
        ng = len(self.tg)
        CI = [A('fu%d_ci%d' % (l, i), [32, 128]) for i in range(2)]
        OS = [A('fu%d_os%d' % (l, i), [34, 128]) for i in range(2)]
        tmpo = A('fu%d_tmpo' % l, [128, 34])
        cin = I['cache_ffn_conv'][l].rearrange('s j c -> (s j) c')
        for b0 in range(2):
            self.dma(CI[b0][:], cin[:, b0 * 128:(b0 + 1) * 128], [], ['fu_ci%d' % b0])

        def epi(bi, u, gi, t0, tn, ps, pn):
            b = u % 2
            blk = 2 * bi + b
            if u < 2:
                if gi == 0:
                    ps2, pn2 = self.psum()
                    self.TR(ps2[:, 0:32], CI[b][:], ['fu_ci%d' % b], [pn2], n=32)
                    self.CP('dve', gs[b][:, :, 0:2], ps2[:, 0:32].rearrange('p (s j) -> p s j', j=2), [pn2], ['gs%d' % b])
                if t0 < TP:
                    self.CP('act', gp[b][:, 2 + t0:2 + t0 + tn], ps[:, 0:tn], [pn], ['gp%d' % b])
                else:
                    self.CP('act', gs[b][:, :, 2:10], ps[:, 0:tn].rearrange('p (s t) -> p s t', t=8), [pn], ['gs%d' % b])
            else:
                self.CP('act' if gi % 2 == 0 else 'dve', vb[b][:, t0:t0 + tn], ps[:, 0:tn], [pn], ['vb%d' % b])
                if gi == 0 and bi + 1 < len(blocks):
                    nblk = 2 * (bi + 1) + b
                    self.dma(CI[b][:], cin[:, nblk * 128:(nblk + 1) * 128], [], ['fu_ci%d' % b])
                if gi == ng - 1:
                    self.CP('dve', tmpo[:, 0:32].rearrange('p (s j) -> p s j', j=2), gs[b][:, :, 8:10], ['gs%d' % b], ['fu_tmpo'])
                    self.CP('dve', tmpo[:, 32:34], gp[b][:, TP:TP + 2], ['gp%d' % b], ['fu_tmpo'])
                    ps2, pn2 = self.psum()
                    self.TR(ps2[0:34, 0:128], tmpo[:], ['fu_tmpo'], [pn2])
                    self.CP('act', OS[b][:], ps2[0:34, 0:128], [pn2], ['fu_os%d' % b])
                    self.dma(O_['s_ffn_conv'][l].rearrange('s j c -> (s j) c')[:, blk * 128:(blk + 1) * 128], OS[b][0:32, :], ['fu_os%d' % b], [])
                    self.dma(O_['p_ffn_conv'][l][:, blk * 128:(blk + 1) * 128], OS[b][32:34, :], ['fu_os%d' % b], [])
                    w_ = lambda j: fp[:, l, j * 44 + blk:j * 44 + blk + 1]
                    bb = fp[:, l, 132 + blk:133 + blk]
                    cvs = cv[:, TP:NT].rearrange('p (s t) -> p s t', t=8)
                    self.TSC('dve', cv[:, 0:TP], gp[b][:, 0:TP], w_(0), bb, ALU.mult, ALU.add, ['gp%d' % b, 'fprm'], ['cv'])
                    self.STT(cv[:, 0:TP], gp[b][:, 1:TP + 1], w_(1), cv[:, 0:TP], ALU.mult, ALU.add, ['gp%d' % b, 'fprm', 'cv'], ['cv'])
                    self.STT(cv[:, 0:TP], gp[b][:, 2:TP + 2], w_(2), cv[:, 0:TP], ALU.mult, ALU.add, ['gp%d' % b, 'fprm', 'cv'], ['cv'])
                    self.TSC('dve', cvs, gs[b][:, :, 0:8], w_(0), bb, ALU.mult, ALU.add, ['gs%d' % b, 'fprm', 'cv'], ['cv'])
                    self.STT(cvs, gs[b][:, :, 1:9], w_(1), cvs, ALU.mult, ALU.add, ['gs%d' % b, 'fprm', 'cv'], ['cv'])
                    self.STT(cvs, gs[b][:, :, 2:10], w_(2), cvs, ALU.mult, ALU.add, ['gs%d' % b, 'fprm', 'cv'], ['cv'])
                    self.ACT(cv[:], cv[:], AF.Silu, ['cv'], ['cv'])
                    self.TT('dve', hb[b][:], cv[:], vb[b][:], ALU.mult, ['cv', 'vb%d' % b], ['hb%d' % b])
                    self.dma(self.hT[blk * 128:(blk + 1) * 128, :], hb[b][:], ['hb%d' % b], ['hT'])

        self.gemm('fu%d' % l, A, 16, blocks, lambda kc, t0, tn: actT[:, kc, t0:t0 + tn], ['actT'], epi, cbw=512, single=True)

    def phase_ffn_down(self, l, I, A):
        NT = self.NT
        w = I['ffn_w_down']
        KC = DFF // 128
        parts = [self.tg[0:2], self.tg[2:]] if len(self.tg) > 2 else [self.tg]
        nmax = max(sum(n for _, n in p) for p in parts if p)
        hres = A('fd%d_h' % l, [128, KC, nmax], BF16)
        repi = self.residual_epi(A, 'fd%d' % l)
        blocks = [[(w[l, :, c0:c0 + 256], 256)] for c0 in range(0, D, 256)]
        tag = 'fd%d' % l
        bufs = ([A('%s_wst0' % tag, [128, KC, 256])] * 2, [A('%s_wbf%d' % (tag, i), [128, KC, 256], BF16) for i in range(2)], tag)
        for pi, part in enumerate(parts):
            if not part:
                continue
            s0 = part[0][0]
            n = sum(nn for _, nn in part)
            self.dma(hres[:, :, 0:n], self.hT[:, s0:s0 + n].rearrange('(c p) t -> p c t', p=128), ['hT'], ['hres'])
            self.gemm('fd%d_%d' % (l, pi), A, KC, blocks, lambda kc, t0, tn, s0=s0: hres[:, kc, t0 - s0:t0 - s0 + tn], ['hres'],
                      lambda bi, u, gi, t0, tn, ps, pn: repi(bi * 256 + u * 128, t0, tn, ps, pn), cbw=256, tgs=part, bufs=bufs, single=True)

    def phase_final(self, y_out):
        nc = self.nc
        with contextlib.ExitStack() as es:
            A = lambda name, shape, dt=F32: es.enter_context(nc.sbuf_tensor('fin_' + name, list(shape), dt))
            xg = [A('xg%d' % i, [128, 16, 512]) for i in range(2)]
            sqs = [A('sq%d' % i, [128, 16, 128], BF16) for i in range(2)]
            sds = [A('sd%d' % i, [128, 128]) for i in range(2)]
            rss = [A('rs%d' % i, [128, 128]) for i in range(2)]
            hns = [A('hn%d' % i, [128, 16, 128]) for i in range(2)]
            yo = [A('yo%d' % i, [128, D]) for i in range(2)]
            k = 0
            for gi, (g0, gn) in enumerate(self.tg):
                gb = gi % 2
                xn = 'fxg%d' % gb
                self.dma(xg[gb][:, :, 0:gn], self.xT[:, g0:g0 + gn].rearrange('(c p) t -> p c t', p=128), ['xT'], [xn])
                for j in range(gn // 128):
                    b = k % 2
                    k += 1
                    sq, sd, rs, hn = sqs[b], sds[b], rss[b], hns[b]
                    t0 = g0 + j * 128
                    xv = xg[gb][:, :, j * 128:(j + 1) * 128]
                    self.ACT(sq[:], xv, AF.Square, [xn], ['fsq%d' % b])
                    ps, pn = self.psum()
                    for kc in range(16):
                        self.MM(ps[:, 0:128], self.ones_bf[:], sq[:, kc, :], kc == 0, kc == 15, ['fsq%d' % b, 'ones_bf'], [pn])
                    self.ACT(sd[:], ps[:, 0:128], AF.Sqrt, [pn], ['fsd%d' % b], bias=EPS, scale=1.0 / D)
                    self.RECIP(rs[:], sd[:], ['fsd%d' % b], ['frs%d' % b])
                    for kc in range(16):
                        self.STT(hn[:, kc, :], xg[gb][:, kc, j * 128:(j + 1) * 128], self.prmf[:, kc:kc + 1], rs[:], ALU.mult, ALU.mult, [xn, 'frs%d' % b, 'prmf'], ['fhn%d' % b])
                    for q in range(4):
                        ps, pn = self.psum()
                        for jj in range(4):
                            kc = q * 4 + jj
                            self.TR(ps[:, jj * 128:(jj + 1) * 128], hn[:, kc, :], ['fhn%d' % b], [pn])
                        self.CP('act' if q % 2 == 0 else 'dve', yo[b][:, q * 512:(q + 1) * 512], ps[:], [pn], ['fyo%d_%d' % (b, q)])
                    self.dma(y_out[t0:t0 + 128, :], yo[b][:], ['fyo%d_%d' % (b, q) for q in range(4)], [])


_CACHE = {}
TP_FULL = 2048
W_NAMES = ['norm_mix', 'w_in', 'rwkv_mu', 'rwkv_w0', 'rwkv_w2', 'rwkv_a0', 'rwkv_a2', 'rwkv_g2', 'rwkv_k_k', 'rwkv_k_a',
           'rwkv_r_k', 'rwkv_ln_w', 'rwkv_ln_b', 'hgrn_lb_logits', 'hgrn_norm_w', 'rglru_conv_w', 'rglru_conv_b', 'rglru_wa',
           'rglru_ba', 'rglru_wx', 'rglru_bx', 'rglru_lambda', 'w_out', 'norm_ffn', 'ffn_w_up', 'ffn_conv_w', 'ffn_conv_b',
           'ffn_w_down', 'norm_final']
S_NAMES = ['state_rwkv', 'state_rwkv_shift', 'state_hgrn', 'state_rglru', 'cache_rglru_conv', 'cache_ffn_conv']
O_NAMES = ['rwkv', 'rwkv_shift', 'hgrn', 'rglru', 'rglru_conv', 'ffn_conv']


def kernel(**inputs):
    if 'nc' not in _CACHE:
        b = Builder(TP_FULL, DEPTH)
        _CACHE['nc'] = b.build()
        _CACHE['b'] = b
    nc = _CACHE['nc']
    f32 = lambda a: np.ascontiguousarray(np.asarray(a, dtype=np.float32))
    consts = make_consts()
    shared = {k: f32(inputs[k]) for k in W_NAMES}
    for k, v in consts.items():
        shared['c_' + k] = v
    xp = f32(inputs['x_prompt'])
    xs = f32(inputs['x_sample'])
    states = {k: f32(inputs[k]) for k in S_NAMES}
    ncore = 8
    in_maps = []
    for i in range(ncore):
        m = dict(shared)
        m['x'] = np.ascontiguousarray(np.concatenate([xp[i % 4], xs[NSEQ * i:NSEQ * (i + 1)].reshape(NSEQ * TS, D)], axis=0))
        for k in S_NAMES:
            m[k] = np.ascontiguousarray(states[k][:, NSEQ * i:NSEQ * (i + 1)])
        in_maps.append(m)
    res = run_bass_kernel_spmd(nc, in_maps, core_ids=list(range(ncore)))
    R = res.results
    y_prompt = np.stack([np.asarray(R[b]['y'])[:TP_FULL] for b in range(4)], axis=0).astype(np.float32)
    y_sample = np.concatenate([np.asarray(R[i]['y'])[TP_FULL:].reshape(NSEQ, TS, D) for i in range(ncore)], axis=0).astype(np.float32)
    outs = [y_prompt, y_sample]
    for n in O_NAMES:
        outs.append(np.stack([np.asarray(R[b]['p_' + n]) for b in range(4)], axis=1).astype(np.float32))
    for n in O_NAMES:
        outs.append(np.concatenate([np.asarray(R[i]['s_' + n]) for i in range(ncore)], axis=1).astype(np.float32))
    return tuple(outs)
```

```python
import contextlib
import numpy as np
import concourse.bass as bass
import concourse.mybir as mybir
from concourse.bass_utils import run_bass_kernel_spmd

F32 = mybir.dt.float32
BF16 = mybir.dt.bfloat16
AF = mybir.ActivationFunctionType
ALU = mybir.AluOpType
AX = mybir.AxisListType

D = 2048
DEPTH = 2
NSEQ = 16
TS = 8
RW_COLS = 2560
HG_COLS = 3072
LRU_COLS = 1024
IN_COLS = 6656
DFF = 5632
EPS = 1e-6
RW_EPS = 64e-5
ENG = ('pe', 'act', 'dve', 'pool', 'sp')
NDSEM = 8
NDT = BF16
P_NM, P_NF, P_MU, P_W0, P_A0, P_KK, P_KA, P_OMKA, P_RK, P_LNW, P_LNB = 0, 16, 32, 52, 58, 64, 70, 76, 82, 88, 94
P_LB, P_OMLB, P_HNW, P_CW, P_CB, P_BA, P_BX, P_LAM, P_C8, P_2C8 = 100, 106, 112, 118, 134, 138, 142, 146, 150, 154


class Sched:
    def __init__(self, nc):
        self.nc = nc
        self.ops = {e: [] for e in ENG}
        self.res = {}
        self.bar = {e: {} for e in ENG}
        self.dma_since_bar = []

    def emit(self, eng, fn, reads=(), writes=()):
        deps = {}

        def add(d):
            e, i = d
            if e == 'sp':
                deps.setdefault(('sp', i), i)
            else:
                if deps.get(e, -1) < i:
                    deps[e] = i

        for r in reads:
            st = self.res.get(r)
            if st and st['w'] is not None:
                add(st['w'])
        for w in writes:
            st = self.res.get(w)
            if st:
                if st['w'] is not None:
                    add(st['w'])
                for e, i in st['r'].items():
                    if e == 'sp':
                        for ii in i:
                            add(('sp', ii))
                    else:
                        add((e, i))
        for k, v in self.bar[eng].items():
            if isinstance(k, tuple):
                deps.setdefault(k, v)
            elif deps.get(k, -1) < v:
                deps[k] = v
        self.bar[eng] = {}
        idx = len(self.ops[eng])
        if eng == 'pe':
            deps.pop('pe', None)
        self.ops[eng].append(dict(fn=fn, deps=deps, need=False))
        if eng == 'sp':
            self.dma_since_bar.append(idx)
        for r in reads:
            st = self.res.setdefault(r, {'w': None, 'r': {}})
            if eng == 'sp':
                st['r'].setdefault('sp', []).append(idx)
            else:
                st['r'][eng] = idx
        for w in writes:
            self.res[w] = {'w': (eng, idx), 'r': {}}

    def barrier(self):
        last = {}
        for e in ENG:
            if e == 'sp':
                continue
            if self.ops[e]:
                last[e] = len(self.ops[e]) - 1
        for i in self.dma_since_bar:
            last[('sp', i)] = i
        self.dma_since_bar = []
        for e in ENG:
            d = dict(last)
            d.pop(e, None)
            for k, v in d.items():
                if isinstance(k, tuple):
                    self.bar[e].setdefault(k, v)
                elif self.bar[e].get(k, -1) < v:
                    self.bar[e][k] = v

    def finalize(self):
        nc = self.nc
        ops = self.ops
        for e in ENG:
            for op in ops[e]:
                for k, v in op['deps'].items():
                    if isinstance(k, tuple):
                        continue
                    ops[k][v]['need'] = True
        for e in ENG:
            if e == 'sp':
                continue
            c = 0
            for op in ops[e]:
                if op['need']:
                    c += 1
                op['val'] = c
        cnt = [0] * NDSEM
        for k, op in enumerate(ops['sp']):
            j = k % NDSEM
            op['prev'] = cnt[j]
            cnt[j] += 16
            op['sem'] = j
            op['val'] = cnt[j]
        sems = {e: nc.alloc_semaphore('s_' + e) for e in ENG if e != 'sp'}
        dsems = [nc.alloc_semaphore('s_d%d' % j) for j in range(NDSEM)]

        def run(e, engobj):
            emitted = {}

            def need(key, val):
                if emitted.get(key, 0) < val:
                    sm = dsems[key[1]] if isinstance(key, tuple) else sems[key]
                    engobj.wait_ge(sm, val)
                    emitted[key] = val

            for op in ops[e]:
                for k, v in op['deps'].items():
                    if isinstance(k, tuple):
                        p = ops['sp'][v]
                        need(('d', p['sem']), p['val'])
                    else:
                        need(k, ops[k][v]['val'])
                if e == 'sp' and op['prev'] > 0:
                    need(('d', op['sem']), op['prev'])
                inst = op['fn'](engobj)
                if e == 'sp':
                    inst.then_inc(dsems[op['sem']], 16)
                elif op['need']:
                    inst.then_inc(sems[e], 1)
            if e == 'sp':
                for j in range(NDSEM):
                    if cnt[j] > 0:
                        need(('d', j), cnt[j])

        self._emitted = {}
        with nc.Block() as block:
            @block.tensor
            def _(t):
                run('pe', t)

            @block.scalar
            def _(t):
                run('act', t)

            @block.vector
            def _(t):
                run('dve', t)

            @block.gpsimd
            def _(t):
                run('pool', t)

            @block.sync
            def _(t):
                run('sp', t)


def make_consts():
    c = {}
    c['ident'] = np.eye(128, dtype=np.float32)
    c['ones'] = np.ones((128, 128), np.float32)
    bo = np.zeros((128, 128), np.float32)
    bo[:64, :64] = 1
    bo[64:, 64:] = 1
    c['blkones'] = bo
    i = np.arange(128)
    for name, sub in (('1', 128), ('4', 32), ('16', 8)):
        same = (i[:, None] // sub) == (i[None, :] // sub)
        low_s = ((i[None, :] < i[:, None]) & same).astype(np.float32)
        up_s = ((i[:, None] < i[None, :]) & same).astype(np.float32)
        up_i = ((i[:, None] <= i[None, :]) & same).astype(np.float32)
        c['low_s' + name] = low_s
        c['up_si' + name] = np.concatenate([up_s, up_i], axis=1)
        c['up_i' + name] = up_i
        nsub = 128 // sub
        rst = np.ones((128, 128), np.float32)
        rst[:, ::sub] = 0
        c['rst' + name] = rst
        cm = np.zeros((128, nsub, 128), np.float32)
        rm = np.zeros((128, nsub), np.float32)
        for sb in range(nsub):
            cm[:, sb, sb * sub:(sb + 1) * sub] = 1
            rm[sb * sub:(sb + 1) * sub, sb] = 1
        c['cm' + name] = cm.reshape(128, nsub * 128)
        c['rm' + name] = rm
    return c


class Builder:
    def __init__(self, TP, depth, stop=None, dbg=()):
        self.TP = TP
        self.depth = depth
        self.NT = TP + NSEQ * TS
        self.ntile = self.NT // 128
        self.stop = stop
        self.rw_stop = None
        self.dbg = dbg
        nc = bass.Bass('TRN2', target_bir_lowering=False)
        self.nc = nc
        self.S = Sched(nc)
        self.inputs = {}
        self.outputs = {}
        self.tg = []
        t = 0
        while t < TP:
            n = min(512, TP - t)
            self.tg.append((t, n))
            t += n
        self.tg.append((TP, NSEQ * TS))

    def din(self, name, shape, dt=F32):
        t = self.nc.dram_tensor(name, list(shape), dt, kind='ExternalInput')
        self.inputs[name] = t
        return t.ap()

    def dout(self, name, shape, dt=F32):
        t = self.nc.dram_tensor(name, list(shape), dt, kind='ExternalOutput')
        self.outputs[name] = t
        return t.ap()

    def dscr(self, name, shape, dt=F32):
        return self.nc.dram_tensor(name, list(shape), dt, kind='Internal').ap()

    def dma(self, out, in_, reads, writes, slow=False):
        if slow:
            self.S.emit('sp', lambda e: e.dma_start(out=out, in_=in_, allow_slow_non_contiguous=True), reads, writes)
        else:
            self.S.emit('sp', lambda e: e.dma_start(out=out, in_=in_), reads, writes)

    def E(self, eng, fn, reads, writes):
        self.S.emit(eng, fn, reads, writes)

    def psum(self):
        k = self._pk
        self._pk = (k + 1) % 8
        return self.ps[k], 'ps%d' % k

    def build(self):
        nc = self.nc
        TP, NT, L = self.TP, self.NT, self.depth
        I = {}
        I['x'] = self.din('x', [NT, D])
        shapes = dict(
            state_rwkv=[L, NSEQ, 12, 64, 64], state_rwkv_shift=[L, NSEQ, RW_COLS], state_hgrn=[L, NSEQ, 6, 128, 128],
            state_rglru=[L, NSEQ, 512], cache_rglru_conv=[L, NSEQ, 3, 512], cache_ffn_conv=[L, NSEQ, 2, DFF],
            norm_mix=[L, D], w_in=[L, D, IN_COLS], rwkv_mu=[L, RW_COLS], rwkv_w0=[L, 768], rwkv_w2=[L, 64, 768],
            rwkv_a0=[L, 768], rwkv_a2=[L, 64, 768], rwkv_g2=[L, 128, 768], rwkv_k_k=[L, 768], rwkv_k_a=[L, 768],
            rwkv_r_k=[L, 12, 64], rwkv_ln_w=[L, 768], rwkv_ln_b=[L, 768], hgrn_lb_logits=[L, 768], hgrn_norm_w=[L, 768],
            rglru_conv_w=[L, 4, 512], rglru_conv_b=[L, 512], rglru_wa=[L, 8, 64, 64], rglru_ba=[L, 512],
            rglru_wx=[L, 8, 64, 64], rglru_bx=[L, 512], rglru_lambda=[L, 512], w_out=[L, D, D], norm_ffn=[L, D],
            ffn_w_up=[L, D, 2 * DFF], ffn_conv_w=[L, 3, DFF], ffn_conv_b=[L, DFF], ffn_w_down=[L, DFF, D], norm_final=[D])
        for k, shp in shapes.items():
            I[k] = self.din(k, shp)
        self.cst = {k: self.din('c_' + k, v.shape) for k, v in make_consts().items()}
        O_ = {}
        O_['y'] = self.dout('y', [NT, D])
        oshapes = dict(p_rwkv=[L, 12, 64, 64], p_rwkv_shift=[L, RW_COLS], p_hgrn=[L, 6, 128, 128], p_rglru=[L, 512],
                       p_rglru_conv=[L, 3, 512], p_ffn_conv=[L, 2, DFF],
                       s_rwkv=[L, NSEQ, 12, 64, 64], s_rwkv_shift=[L, NSEQ, RW_COLS], s_hgrn=[L, NSEQ, 6, 128, 128],
                       s_rglru=[L, NSEQ, 512], s_rglru_conv=[L, NSEQ, 3, 512], s_ffn_conv=[L, NSEQ, 2, DFF])
        for k, shp in oshapes.items():
            O_[k] = self.dout(k, shp)
        self.xT = self.dscr('xT', [D, NT])
        self.projT = self.dscr('projT', [IN_COLS, NT])
        self.yT = self.dscr('yT', [D, NT], BF16)
        self.hT = self.dscr('hT', [DFF, NT], BF16)
        dbg = {k: self.dout('d_' + k, shp, dt) for k, (shp, dt) in dict(
            projT=([IN_COLS, NT], F32), xT=([D, NT], F32), yT=([D, NT], BF16), hT=([DFF, NT], BF16)).items() if k in self.dbg}

        with contextlib.ExitStack() as es:
            A = lambda name, shape, dt=F32: es.enter_context(nc.sbuf_tensor(name, list(shape), dt))
            self.ps = [es.enter_context(nc.psum_tensor('ps%d' % k, [128, 512], F32)) for k in range(8)]
            self._pk = 0
            self.ident = A('ident', [128, 128])
            self.ones_bf = A('ones_bf', [128, 128], BF16)
            self.dma(self.ident[:], self.cst['ident'], [], ['ident'])
            self.MEMSET('dve', self.ones_bf[:], 1.0, ['ones_bf'])
            self.prm = A('prm', [128, L, 160])
            self.fprm = A('fprm', [128, L, 176])
            self.prmf = A('prmf', [128, 16])
            self.ptmp = A('ptmp', [128, L + 1, 16])
            self.load_params(I)
            self.phase_transpose_in(I['x'])
            self.S.barrier()
            stop = self.stop
            for l in range(L):
                def sub(fn):
                    with contextlib.ExitStack() as es2:
                        A2 = lambda name, shape, dt=F32: es2.enter_context(nc.sbuf_tensor(name, list(shape), dt))
                        fn(A2)
                    self.S.barrier()

                def p_in(A2):
                    actT = A2('actT_a%d' % l, [128, 16, NT], BF16)
                    self.phase_norm(l, actT, P_NM, A2)
                    self.S.barrier()
                    self.phase_inproj(l, actT, I, A2)
                sub(p_in)
                if stop == 'inproj':
                    break
                self.phase_rwkv(l, I, O_)
                self.S.barrier()
                if stop == 'rwkv':
                    break
                self.phase_hgrn(l, I, O_, side=self.lru_gen(l, I, O_))
                self.S.barrier()
                if stop == 'mix':
                    break

                def p_out(A2):
                    actT = A2('actT_b%d' % l, [128, 16, NT], BF16)
                    self.phase_outproj(l, actT, I, A2)
                sub(p_out)

                def p_up(A2):
                    actT = A2('actT_c%d' % l, [128, 16, NT], BF16)
                    self.phase_norm(l, actT, P_NF, A2)
                    self.S.barrier()
                    self.phase_ffn_up(l, actT, I, O_, A2)
                sub(p_up)
                if stop == 'ffnup':
                    break
                sub(lambda A2: self.phase_ffn_down(l, I, A2))
            if stop is None:
                self.phase_final(O_['y'])
            self.S.barrier()
            for k, ap in dbg.items():
                if k == 'yT' and stop == 'rwkv':
                    self.dma(ap[0:768], self.yT[0:768], [], [])
                else:
                    self.dma(ap, getattr(self, k), [], [])
            self.S.finalize()
        return nc

    def phase_transpose_in(self, x_in):
        nc = self.nc
        with contextlib.ExitStack() as es:
            A = lambda name, shape, dt=F32: es.enter_context(nc.sbuf_tensor(name, list(shape), dt))
            xt = [A('ti_x%d' % i, [128, D]) for i in range(3)]
            xo = [A('ti_o%d' % i, [128, 16, 512]) for i in range(2)]
            k = 0
            for gi, (g0, gn) in enumerate(self.tg):
                ob = gi % 2
                for j in range(gn // 128):
                    b = k % 3
                    k += 1
                    t0 = g0 + j * 128
                    self.dma(xt[b][:], x_in[t0:t0 + 128, :], [], ['ti_x%d' % b])
                    for q in range(4):
                        ps, pn = self.psum()
                        for jj in range(4):
                            kc = q * 4 + jj
                            self.TR(ps[:, jj * 128:(jj + 1) * 128], xt[b][:, kc * 128:(kc + 1) * 128], ['ti_x%d' % b], [pn])
                        self.CP('act' if q % 2 == 0 else 'dve', xo[ob][:, q * 4:(q + 1) * 4, j * 128:(j + 1) * 128], ps[:].rearrange('p (a t) -> p a t', a=4), [pn], ['ti_o%d' % ob])
                self.dma(self.xT[:, g0:g0 + gn].rearrange('(c p) t -> p c t', p=128), xo[ob][:, :, 0:gn], ['ti_o%d' % ob], ['xT'])

    def phase_norm(self, l, actT, poff, A_unused):
        with contextlib.ExitStack() as es:
            self._phase_norm(l, actT, poff, lambda name, shape, dt=F32: es.enter_context(self.nc.sbuf_tensor('%s_%d' % (name, poff), list(shape), dt)))

    def _phase_norm(self, l, actT, poff, A):
        xb = [A('nm_x%d_%d' % (l, i), [128, 16, 512]) for i in range(2)]
        sqs = [A('nm_sq%d_%d' % (l, i), [128, 16, 512], BF16) for i in range(2)]
        sds = [A('nm_sd%d_%d' % (l, i), [128, 512]) for i in range(2)]
        rss = [A('nm_rs%d_%d' % (l, i), [128, 512]) for i in range(2)]
        for gi, (t0, tn) in enumerate(self.tg):
            b = gi % 2
            sq, sd, rs = sqs[b], sds[b], rss[b]
            xn = 'nm_x%d' % b
            self.dma(xb[b][:, :, 0:tn], self.xT[:, t0:t0 + tn].rearrange('(c p) t -> p c t', p=128), ['xT'], [xn])
            self.E('act', lambda e, b=b, tn=tn, sq=sq: e.activation(sq[:, :, 0:tn], xb[b][:, :, 0:tn], AF.Square), [xn], ['nm_sq%d' % b])
            ps, pn = self.psum()
            for kc in range(16):
                self.E('pe', lambda e, ps=ps, kc=kc, tn=tn, sq=sq: e.matmul(ps[:, 0:tn], self.ones_bf[:], sq[:, kc, 0:tn], start=(kc == 0), stop=(kc == 15)),
                       ['nm_sq%d' % b, 'ones_bf'], [pn])
            self.E('act', lambda e, ps=ps, tn=tn, sd=sd: e.activation(sd[:, 0:tn], ps[:, 0:tn], AF.Sqrt, bias=EPS, scale=1.0 / D), [pn], ['nm_sd%d' % b])
            self.E('dve', lambda e, tn=tn, rs=rs, sd=sd: e.reciprocal(rs[:, 0:tn], sd[:, 0:tn]), ['nm_sd%d' % b], ['nm_rs%d' % b])
            for kc in range(16):
                self.E('dve', lambda e, b=b, kc=kc, t0=t0, tn=tn, rs=rs: e.scalar_tensor_tensor(
                    actT[:, kc, t0:t0 + tn], xb[b][:, kc, 0:tn], self.prm[:, l, poff + kc:poff + kc + 1], rs[:, 0:tn], ALU.mult, ALU.mult),
                    [xn, 'nm_rs%d' % b, 'prm'], ['actT'])

    def TT(self, eng, out, in0, in1, op, r, w):
        self.E(eng, lambda e: e.tensor_tensor(out, in0, in1, op), r, w)

    def TSC(self, eng, out, in0, s1, s2, op0, op1, r, w):
        if op1 is None:
            self.E(eng, lambda e: e.tensor_scalar(out, in0, s1, None, op0), r, w)
        else:
            self.E(eng, lambda e: e.tensor_scalar(out, in0, s1, s2, op0, op1), r, w)

    def STT(self, out, in0, sc, in1, op0, op1, r, w):
        self.E('dve', lambda e: e.scalar_tensor_tensor(out, in0, sc, in1, op0, op1), r, w)

    def ACT(self, out, in_, func, r, w, bias=None, scale=None, accum=None):
        kw = {}
        if bias is not None:
            kw['bias'] = bias
        if scale is not None:
            kw['scale'] = scale
        if accum is not None:
            kw['accum_out'] = accum
        self.E('act', lambda e: e.activation(out, in_, func, **kw), r, w)

    def CP(self, eng, out, in_, r, w):
        if eng == 'act':
            self.ACT(out, in_, AF.Copy, r, w)
        else:
            self.E(eng, lambda e: e.tensor_copy(out, in_), r, w)

    def MM(self, out, lhsT, rhs, start, stop, r, w, sgc=False):
        self.E('pe', lambda e: e.matmul(out, lhsT, rhs, start=start, stop=stop, skip_group_check=sgc), r, w)

    def TR(self, out, in_, r, w, n=128):
        self.E('pe', lambda e: e.transpose(out, in_, self.ident[0:n, 0:n]), list(r) + ['ident'], w)

    def RECIP(self, out, in_, r, w):
        self.E('dve', lambda e: e.reciprocal(out, in_), r, w)

    def MEMSET(self, eng, ap, val, w):
        self.E(eng, lambda e: e.memset(ap, val), [], w)

    def SCAN(self, out, d0, d1, init, r, w):
        self.E('dve', lambda e: e.tensor_tensor_scan(out, d0, d1, init, ALU.mult, ALU.add), r, w)

    def RED(self, out, in_, r, w):
        self.E('dve', lambda e: e.tensor_reduce(out, in_, AX.X, ALU.add), r, w)

    def load_params(self, I):
        L = self.depth
        prm = self.prm

        def ld(idx, n, ap_fn):
            for l in range(L):
                self.dma(prm[:, l, idx:idx + n], ap_fn(l), [], ['prm'], slow=True)

        cp = lambda name: (lambda l: I[name][l].rearrange('(c p) -> p c', p=128))
        ld(P_NM, 16, cp('norm_mix'))
        ld(P_NF, 16, cp('norm_ffn'))
        ld(P_MU, 20, cp('rwkv_mu'))
        ld(P_W0, 6, cp('rwkv_w0'))
        ld(P_A0, 6, cp('rwkv_a0'))
        ld(P_KK, 6, cp('rwkv_k_k'))
        ld(P_KA, 6, cp('rwkv_k_a'))
        ld(P_RK, 6, lambda l: I['rwkv_r_k'][l].rearrange('(c h2) k -> (h2 k) c', h2=2))
        ld(P_LNW, 6, cp('rwkv_ln_w'))
        ld(P_LNB, 6, cp('rwkv_ln_b'))
        ld(P_LB, 6, cp('hgrn_lb_logits'))
        ld(P_HNW, 6, cp('hgrn_norm_w'))
        for j in range(4):
            ld(P_CW + 4 * j, 4, lambda l, j=j: I['rglru_conv_w'][l, j].rearrange('(c p) -> p c', p=128))
        ld(P_CB, 4, cp('rglru_conv_b'))
        ld(P_BA, 4, cp('rglru_ba'))
        ld(P_BX, 4, cp('rglru_bx'))
        ld(P_LAM, 4, cp('rglru_lambda'))
        for l in range(L):
            for j in range(3):
                self.dma(self.fprm[:, l, j * 44:(j + 1) * 44], I['ffn_conv_w'][l, j].rearrange('(c p) -> p c', p=128), [], ['fprm'], slow=True)
            self.dma(self.fprm[:, l, 132:176], I['ffn_conv_b'][l].rearrange('(c p) -> p c', p=128), [], ['fprm'], slow=True)
        self.dma(self.prmf[:, 0:16], I['norm_final'].rearrange('(c p) -> p c', p=128), [], ['prmf'], slow=True)
        for l in range(L):
            self.TSC('dve', prm[:, l, P_OMKA:P_OMKA + 6], prm[:, l, P_KA:P_KA + 6], -1.0, 1.0, ALU.mult, ALU.add, ['prm'], ['prm'])
        ex = self.ptmp
        for l in range(L):
            self.ACT(ex[:, l, 0:6], prm[:, l, P_LB:P_LB + 6], AF.Exp, ['prm'], ['ptmp'])
        self.CP('dve', ex[:, L, 0:6], ex[:, 0, 0:6], ['ptmp'], ['ptmp'])
        for l in range(1, L):
            self.TT('dve', ex[:, L, 0:6], ex[:, L, 0:6], ex[:, l, 0:6], ALU.add, ['ptmp'], ['ptmp'])
        self.RECIP(ex[:, L, 0:6], ex[:, L, 0:6], ['ptmp'], ['ptmp'])
        for l in range(L):
            self.TT('dve', ex[:, l, 0:6], ex[:, l, 0:6], ex[:, L, 0:6], ALU.mult, ['ptmp'], ['ptmp'])
        self.MEMSET('dve', prm[:, 0, P_LB:P_LB + 6], 0.0, ['prm'])
        for l in range(1, L):
            self.TT('dve', prm[:, l, P_LB:P_LB + 6], prm[:, l - 1, P_LB:P_LB + 6], ex[:, l, 0:6], ALU.add, ['prm', 'ptmp'], ['prm'])
        for l in range(L):
            self.TSC('dve', prm[:, l, P_OMLB:P_OMLB + 6], prm[:, l, P_LB:P_LB + 6], -1.0, 1.0, ALU.mult, ALU.add, ['prm'], ['prm'])
            self.ACT(ex[:, l, 8:12], prm[:, l, P_LAM:P_LAM + 4], AF.Exp, ['prm'], ['ptmp'], scale=-1.0)
            self.ACT(ex[:, l, 8:12], ex[:, l, 8:12], AF.Ln, ['ptmp'], ['ptmp'], bias=1.0)
            self.TSC('dve', prm[:, l, P_C8:P_C8 + 4], ex[:, l, 8:12], -8.0, None, ALU.mult, None, ['ptmp'], ['prm'])
            self.TSC('dve', prm[:, l, P_2C8:P_2C8 + 4], ex[:, l, 8:12], -16.0, None, ALU.mult, None, ['ptmp'], ['prm'])
    def phase_rwkv(self, l, I, O_):
        nc = self.nc
        TP, NT = self.TP, self.NT
        prm = self.prm
        C0 = 0.6065306597126334
        with contextlib.ExitStack() as es:
            A = lambda name, shape, dt=F32: es.enter_context(nc.sbuf_tensor('rw%d_%s' % (l, name), list(shape), dt))
            w2a2 = A('w2a2', [128, 768])
            g2 = A('g2', [128, 768])
            self.dma(w2a2[0:64, :], I['rwkv_w2'][l], [], ['w2a2'])
            self.dma(w2a2[64:128, :], I['rwkv_a2'][l], [], ['w2a2'])
            self.dma(g2[:], I['rwkv_g2'][l], [], ['g2'])
            blk = A('blk', [128, 128])
            self.dma(blk[:], self.cst['blkones'], [], ['blk'])
            masks = {}
            for mk in ('1', '16'):
                mbk = A('mbk' + mk, [128, 512])
                ml4 = A('ml4' + mk, [128, 512])
                rst = A('rst' + mk, [128, 128])
                for j in range(2):
                    self.dma(mbk[:, j * 256:(j + 1) * 256], self.cst['up_si' + mk], [], ['mbk' + mk])
                for j in range(4):
                    self.dma(ml4[:, j * 128:(j + 1) * 128], self.cst['low_s' + mk], [], ['ml4' + mk])
                self.dma(rst[:], self.cst['rst' + mk], [], ['rst' + mk])
                masks[mk] = (mbk, ml4, rst)
            cm16 = A('cm16', [128, 16, 128])
            rm16 = A('rm16', [128, 16])
            self.dma(cm16[:], self.cst['cm16'].rearrange('p (s t) -> p s t', s=16), [], ['cm16'])
            self.dma(rm16[:], self.cst['rm16'], [], ['rm16'])

            pbuf = A('pbuf', [128, 20, 144])
            xs = A('xs', [128, 20, 128])
            dd = A('dd', [128, 20, 128])
            tx = A('tx', [128, 128])
            sgx = A('sgx', [128, 128])
            sgw = A('sgw', [128, 6, 128])
            aa = A('aa', [128, 6, 128])
            gT = A('gT', [128, 6, 128])
            kk = A('kk', [128, 6, 128])
            t1 = A('t1', [128, 6, 128])
            t2 = A('t2', [128, 6, 128])
            kf = A('kf', [128, 6, 128])
            nb = A('nb', [128, 6, 128])
            cs = A('cs', [128, 6, 128])
            Gm = A('Gm', [128, 6, 128])
            Gi = A('Gi', [128, 6, 128])
            Gp = A('Gp', [128, 6, 128])
            AR = A('AR', [128, 6, 256])
            KT = A('KT', [128, 6, 128])
            BT = A('BT', [128, 6, 128])
            Vtm = A('Vtm', [128, 768])
            Btm = A('Btm', [128, 768])
            Ktm = A('Ktm', [128, 768])
            M4 = A('M4', [128, 12, 384])
            Pm = [A('Pm%d' % i, [128, 12, 128], NDT) for i in range(2)]
            PTm = [A('PTm%d' % i, [128, 12, 128], NDT) for i in range(2)]
            X = [A('X%d' % i, [128, 12, 128], NDT) for i in range(2)]
            Osb = A('Osb', [128, 12, 64])
            st = A('st', [128, 64])
            bst = A('bst', [128, 12, 6])
            ybf = A('ybf', [128, 6, 128], BF16)
            Hc = A('Hc', [128, 1, 6, 64])
            Hin = A('Hin', [128, 16, 6, 64])
            sin = [A('sin0', [64, 6, 2, 64])] * 2
            sout = sin
            htmp = A('htmp', [128, 64])

            self.MEMSET('dve', Hc[:], 0.0, ['Hc'])
            tiles = [('p', ti) for ti in range(TP // 128)] + [('s', 0)]
            for kind, ti in tiles:
                sample = kind == 's'
                t0 = TP if sample else ti * 128
                mk = '16' if sample else '1'
                nsub = 16 if sample else 1
                mbk, ml4, rst = masks[mk]
                rmk = ['mbk' + mk, 'ml4' + mk, 'rst' + mk]
                if sample:
                    pv = pbuf[:, :, 0:144].rearrange('p c (s t) -> p c s t', t=9)
                    self.dma(dd[:], self.projT[0:RW_COLS, t0:t0 + 128].rearrange('(c p) t -> p c t', p=128), ['projT'], ['dd'])
                    self.CP('dve', pv[:, :, :, 1:9], dd[:].rearrange('p c (s t) -> p c s t', t=8), ['dd'], ['pbuf'])
                    for g in range(5):
                        a0 = 2 * (g % 2)
                        stg = M4[0:16, a0:a0 + 2, :].rearrange('p a b -> p (a b)')[:, 0:512]
                        sres = ['M4_%d' % a0, 'M4_%d' % (a0 + 1)]
                        self.dma(stg, I['state_rwkv_shift'][l, :, g * 512:(g + 1) * 512], [], sres)
                        ps, pn = self.psum()
                        for j in range(4):
                            self.TR(ps[:, j * 16:(j + 1) * 16], stg[:, j * 128:(j + 1) * 128], sres, [pn], n=16)
                        self.CP('dve', pv[:, g * 4:(g + 1) * 4, :, 0], ps[:, 0:64].rearrange('p (c s) -> p c s', s=16), [pn], ['pbuf'])
                    prev = lambda c: pv[:, c, :, 0:8]
                    cur = lambda c: pv[:, c, :, 1:9]
                    v3 = lambda ap: ap.rearrange('p (s t) -> p s t', t=8)
                else:
                    if ti == 0:
                        self.MEMSET('dve', pbuf[:, :, 0:1], 0.0, ['pbuf'])
                        self.dma(pbuf[:, :, 1:129], self.projT[0:RW_COLS, t0:t0 + 128].rearrange('(c p) t -> p c t', p=128), ['projT'], ['pbuf'])
                    prev = lambda c: pbuf[:, c, 0:128]
                    cur = lambda c: pbuf[:, c, 1:129]
                    v3 = lambda ap: ap
                if sample:
                    pall, call = pv[:, :, :, 0:8], pv[:, :, :, 1:9]
                    va = lambda t: t[:].rearrange('p c (s t) -> p c s t', t=8)
                    mub = prm[:, l, P_MU:P_MU + 20].rearrange('p (c a b) -> p c a b', a=1, b=1).to_broadcast([128, 20, 16, 8])
                else:
                    pall, call = pbuf[:, :, 0:128], pbuf[:, :, 1:129]
                    va = lambda t: t[:]
                    mub = prm[:, l, P_MU:P_MU + 20].rearrange('p (c a) -> p c a', a=1).to_broadcast([128, 20, 128])
                self.TT('dve', va(dd), pall, call, ALU.subtract, ['pbuf'], ['dd'])
                self.TT('dve', va(dd), va(dd), mub, ALU.mult, ['dd', 'prm'], ['dd'])
                self.TT('dve', va(xs), va(dd), call, ALU.add, ['dd', 'pbuf'], ['xs'])
                if (not sample) and ti + 1 < TP // 128:
                    self.dma(pbuf[:, :, 0:129], self.projT[0:RW_COLS, t0 + 127:t0 + 256].rearrange('(c p) t -> p c t', p=128), ['projT'], ['pbuf'])
                self.ACT(tx[0:64, :], xs[0:64, 18, :], AF.Tanh, ['xs'], ['tx'])
                self.ACT(sgx[:], xs[:, 19, :], AF.Sigmoid, ['xs'], ['sgx'])
                for (rows, rhs, rres, out, bidx) in ((slice(0, 64), tx[0:64, :], 'tx', sgw, P_W0), (slice(64, 128), xs[64:128, 18, :], 'xs', aa, P_A0)):
                    for half in range(2):
                        ps, pn = self.psum()
                        cs_ = range(0, 4) if half == 0 else range(4, 6)
                        for c in cs_:
                            self.MM(ps[:, (c % 4) * 128:(c % 4 + 1) * 128], w2a2[rows, c * 128:(c + 1) * 128], rhs, True, True, ['w2a2', rres], [pn])
                        for c in cs_:
                            self.ACT(out[:, c, :], ps[:, (c % 4) * 128:(c % 4 + 1) * 128], AF.Sigmoid, [pn, 'prm'], ['sgw' if out is sgw else 'aa'], bias=prm[:, l, bidx + c:bidx + c + 1])
                for half in range(2):
                    ps, pn = self.psum()
                    cs_ = range(0, 4) if half == 0 else range(4, 6)
                    for c in cs_:
                        self.MM(ps[:, (c % 4) * 128:(c % 4 + 1) * 128], g2[:, c * 128:(c + 1) * 128], sgx[:], True, True, ['g2', 'sgx'], [pn])
                    n = len(cs_)
                    self.CP('act', gT[:, cs_[0]:cs_[0] + n, :], ps[:, 0:n * 128].rearrange('p (c t) -> p c t', t=128), [pn], ['gT'])
                SGW, AA = 'sgw', 'aa'
                for c in range(6):
                    self.ACT(kk[:, c, :], xs[:, 6 + c, :], AF.Copy, ['xs', 'prm'], ['kk'], scale=prm[:, l, P_KK + c:P_KK + c + 1])
                self.TT('dve', t1[:], kk[:], kk[:], ALU.mult, ['kk'], ['t1'])
                for half in range(2):
                    ps, pn = self.psum()
                    cs_ = range(0, 4) if half == 0 else range(4, 6)
                    for c in cs_:
                        self.MM(ps[:, (c % 4) * 128:(c % 4 + 1) * 128], blk[:], t1[:, c, :], True, True, ['blk', 't1'], [pn])
                    n = len(cs_)
                    self.TSC('dve', t2[:, cs_[0]:cs_[0] + n, :], ps[:, 0:n * 128].rearrange('p (c t) -> p c t', t=128), 1e-24, None, ALU.max, None, [pn], ['t2'])
                self.ACT(t2[:], t2[:], AF.Sqrt, ['t2'], ['t2'])
                self.RECIP(t2[:], t2[:], ['t2'], ['t2'])
                self.TT('dve', kk[:], kk[:], t2[:], ALU.mult, ['kk', 't2'], ['kk'])
                for c in range(6):
                    self.TSC('dve', t1[:, c, :], aa[:, c, :], prm[:, l, P_KA + c:P_KA + c + 1], prm[:, l, P_OMKA + c:P_OMKA + c + 1], ALU.mult, ALU.add, [AA, 'prm', 't1'], ['t1'])
                self.TT('dve', kf[:], xs[:, 6:12, :], t1[:], ALU.mult, ['xs', 't1'], ['kf'])
                self.TT('dve', nb[:], kk[:], aa[:], ALU.mult, ['kk', AA], ['nb'])
                for c in range(6):
                    self.SCAN(cs[:, c, :], rst[:], sgw[:, c, :], 0.0, [SGW, 'rst' + mk], ['cs'])
                self.ACT(Gm[:], cs[:], AF.Exp, ['cs'], ['Gm'], scale=-C0)
                self.ACT(Gi[:], cs[:], AF.Exp, ['cs'], ['Gi'], scale=C0)
                self.TT('dve', t2[:], cs[:], sgw[:], ALU.subtract, ['cs', SGW], ['t2'])
                self.ACT(Gp[:], t2[:], AF.Exp, ['t2'], ['Gp'], scale=-C0)
                self.STT(AR[:, :, 0:128], kk[:], -1.0, Gp[:], ALU.mult, ALU.mult, ['kk', 'Gp'], ['AR0'])
                self.TT('pool', AR[:, :, 128:256], xs[:, 0:6, :], Gm[:], ALU.mult, ['xs', 'Gm'], ['AR1'])
                self.TT('dve', KT[:], kf[:], Gi[:], ALU.mult, ['kf', 'Gi'], ['KT'])
                self.TT('pool', BT[:], nb[:], Gi[:], ALU.mult, ['nb', 'Gi'], ['BT'])
                if self.rw_stop is not None and self.rw_stop < 10:
                    continue
                for (src_fn, sres, dst, dres) in ((lambda c: xs[:, 12 + c, :], 'xs', Vtm, 'Vtm'), (lambda c: AR[:, c, 0:128], 'AR0', cs[:].rearrange('p c t -> p (c t)'), 'cs'),
                                                  (lambda c: BT[:, c, :], 'BT', Btm, 'Btm'), (lambda c: KT[:, c, :], 'KT', Ktm, 'Ktm')):
                    for half in range(2):
                        ps, pn = self.psum()
                        cs_ = range(0, 4) if half == 0 else range(4, 6)
                        for c in cs_:
                            self.TR(ps[:, (c % 4) * 128:(c % 4 + 1) * 128], src_fn(c), [sres], [pn])
                        n = len(cs_)
                        self.CP('act' if half == 0 else 'dve', dst[:, cs_[0] * 128:(cs_[0] + n) * 128], ps[:, 0:n * 128], [pn], ['cs'] if dres == 'cs' else [dres + str(half)])
                VT_R = ['Vtm0', 'Vtm1']
                if self.rw_stop is not None and self.rw_stop < 11:
                    continue
                for h in range(12):
                    c, pb = h // 2, 64 * (h % 2)
                    ps, pn = self.psum()
                    self.MM(ps[:, 0:256], BT[pb:pb + 64, c, :], AR[pb:pb + 64, c, :], True, True, ['BT', 'AR0', 'AR1'], [pn])
                    self.MM(ps[:, 256:512], KT[pb:pb + 64, c, :], AR[pb:pb + 64, c, :], True, True, ['KT', 'AR0', 'AR1'], [pn])
                    self.TT('dve', PTm[0][:, h, :], ps[:, 0:128], mbk[:, 0:128], ALU.mult, [pn, 'mbk' + mk], ['PTm0_%d' % (h // 4)])
                    self.TT('dve', M4[:, h, :], ps[:, 128:512], mbk[:, 128:512], ALU.mult, [pn, 'mbk' + mk], ['M4_%d' % h])
                Pv = Pm[0][:].rearrange('p (c a) t -> p c a t', a=2)
                for h2 in range(2):
                    pb = 64 * h2
                    for (c0, c1) in ((0, 4), (4, 6)):
                        ps, pn = self.psum()
                        for c in range(c0, c1):
                            self.MM(ps[:, (c - c0) * 128:(c - c0 + 1) * 128], AR[pb:pb + 64, c, 0:128], BT[pb:pb + 64, c, :], True, True, ['AR0', 'BT'], [pn])
                        n = c1 - c0
                        self.TT('dve', Pv[:, c0:c1, h2, :], ps[:, 0:n * 128].rearrange('p (h t) -> p h t', t=128), ml4[:, 0:n * 128].rearrange('p (h t) -> p h t', t=128), ALU.mult,
                                [pn, 'ml4' + mk], sorted(set('Pm0_%d' % ((2 * c + h2) // 4) for c in range(c0, c1))))
                if self.rw_stop is not None and self.rw_stop < 12:
                    continue
                self.CP('act', X[0][:, :, 0:64], cs[:].rearrange('p c (a k) -> p (c a) k', k=64), ['cs'], ['X0_0', 'X0_1', 'X0_2'])
                for half in range(2):
                    ps, pn = self.psum()
                    hs = range(0, 8) if half == 0 else range(8, 12)
                    for h in hs:
                        self.MM(ps[:, (h % 8) * 64:(h % 8 + 1) * 64], M4[:, h, 128:256], Vtm[:, h * 64:(h + 1) * 64], True, True, ['M4_%d' % h] + VT_R, [pn])
                    n = len(hs)
                    self.CP('act', X[0][:, hs[0]:hs[0] + n, 64:128], ps[:, 0:n * 64].rearrange('p (h v) -> p h v', v=64), [pn], ['X0_0', 'X0_1'] if half == 0 else ['X0_2'])
                xres = {0: ['X0_0', 'X0_1', 'X0_2'], 1: ['X1_0', 'X1_1', 'X1_2']}
                if self.rw_stop is not None and self.rw_stop < 13:
                    continue
                NST = 3 if sample else 7
                for s_ in range(NST):
                    a_, b_ = s_ % 2, (s_ + 1) % 2
                    pres = ['Pm%d_%d' % (a_, q) for q in range(3)]
                    ptres = ['PTm%d_%d' % (a_, q) for q in range(3)]
                    for q in range(3):
                        ps, pn = self.psum()
                        for j in range(4):
                            h = q * 4 + j
                            self.MM(ps[:, j * 128:(j + 1) * 128], PTm[a_][:, h, :], X[a_][:, h, :], True, True, ptres + xres[a_], [pn])
                        self.TT('dve', X[b_][:, q * 4:(q + 1) * 4, :], ps[:].rearrange('p (h t) -> p h t', t=128), X[a_][:, q * 4:(q + 1) * 4, :], ALU.add,
                                [pn] + xres[a_], ['X%d_%d' % (b_, q)])
                    if s_ < NST - 1:
                        for q in range(3):
                            ps, pn = self.psum()
                            ps2, pn2 = self.psum()
                            for j in range(4):
                                h = q * 4 + j
                                self.MM(ps[:, j * 128:(j + 1) * 128], PTm[a_][:, h, :], Pm[a_][:, h, :], True, True, pres + ptres, [pn])
                                self.MM(ps2[:, j * 128:(j + 1) * 128], Pm[a_][:, h, :], PTm[a_][:, h, :], True, True, pres + ptres, [pn2])
                            self.CP('act', Pm[b_][:, q * 4:(q + 1) * 4, :], ps[:].rearrange('p (h t) -> p h t', t=128), [pn], ['Pm%d_%d' % (b_, q)])
                            self.CP('act', PTm[b_][:, q * 4:(q + 1) * 4, :], ps2[:].rearrange('p (h t) -> p h t', t=128), [pn2], ['PTm%d_%d' % (b_, q)])
                fin = NST % 2
                Xf = X[fin]
                xfr = xres[fin]
                if self.rw_stop is not None and self.rw_stop < 14:
                    continue
                self.CP('act', Gi[:].rearrange('p c (a k) -> p (c a) k', k=64), Xf[:, :, 0:64], xfr, ['Gi'])
                for half in range(2):
                    ps, pn = self.psum()
                    cs_ = range(0, 4) if half == 0 else range(4, 6)
                    for c in cs_:
                        self.TR(ps[:, (c % 4) * 128:(c % 4 + 1) * 128], Gi[:, c, :], ['Gi'], [pn])
                    n = len(cs_)
                    self.CP('act', Gp[:, cs_[0]:cs_[0] + n, :], ps[:, 0:n * 128].rearrange('p (c t) -> p c t', t=128), [pn], ['Gp'])
                gtr = ['Gp']
                if sample:
                    for s in range(16):
                        b = s % 2
                        self.dma(sin[b][:], I['state_rwkv'][l, s].rearrange('(c h2) v k -> v c h2 k', h2=2), [], ['sin0'])
                        ps, pn = self.psum()
                        for c in range(6):
                            self.TR(ps[:, c * 64:(c + 1) * 64], sin[b][:, c, :, :], ['sin0'], [pn], n=64)
                        self.CP('act' if s % 2 == 0 else 'dve', Hin[:, s, :, :], ps[:, 0:384].rearrange('p (c v) -> p c v', v=64), [pn], ['Hin%d' % s])
                    Hi = lambda sb: Hin[:, sb, :, :]
                    Ho = Hi
                    hir = lambda sb: 'Hin%d' % sb
                    hor = hir
                else:
                    Hi = lambda sb: Hc[:, 0, :, :]
                    Ho = Hi
                    hir = lambda sb: 'Hc'
                    hor = hir
                if self.rw_stop is not None and self.rw_stop < 15:
                    continue
                def inter(which):
                    psl = {0: self.psum(), 1: self.psum()}
                    for c in range(6):
                        if which == 'G':
                            src, sres = Gp[:, c, :], gtr
                        else:
                            src, sres = AR[:, c, 128:256], ['AR1']
                        if sample:
                            for sb in range(16):
                                self.TT('pool' if sb % 4 == 3 else 'dve', dd[:, sb, :], src, cm16[:, sb, :], ALU.mult, sres + ['cm16'], ['dd'])
                        for h2 in range(2):
                            pb = 64 * h2
                            ps, pn = psl[h2]
                            o = ps[:, c * 64:(c + 1) * 64]
                            for sb in range(nsub):
                                if sample:
                                    lh, lr = dd[pb:pb + 64, sb, :], ['dd']
                                else:
                                    lh, lr = src[pb:pb + 64, :], sres
                                self.MM(o, lh, Hi(sb)[pb:pb + 64, c, :], sb == 0, sb == nsub - 1, lr + [hir(sb)], [pn])
                    return psl
                Uv = sgw[:].rearrange('p c (a v) -> p c a v', a=2)
                Xv = Xf[:].rearrange('p (c a) t -> p c a t', a=2)
                psl = inter('G')
                for h2 in range(2):
                    ps, pn = psl[h2]
                    self.TT('dve', Uv[:, :, h2, :], ps[:, 0:384].rearrange('p (c v) -> p c v', v=64), Xv[:, :, h2, 64:128], ALU.add, [pn] + xfr, ['sgw'])
                ur = ['sgw']
                psl = inter('R')
                psO = [self.psum(), self.psum()]
                for h in range(12):
                    ps, pn = psO[h // 8]
                    o = ps[:, (h % 8) * 64:(h % 8 + 1) * 64]
                    self.MM(o, M4[:, h, 0:128], sgw[:].rearrange('p c t -> p (c t)')[:, h * 64:(h + 1) * 64], True, False, ['M4_%d' % h] + ur, [pn])
                    self.MM(o, M4[:, h, 256:384], Vtm[:, h * 64:(h + 1) * 64], False, True, ['M4_%d' % h] + VT_R, [pn])
                for half in range(2):
                    ps, pn = psO[half]
                    hs = range(0, 8) if half == 0 else range(8, 12)
                    n = len(hs)
                    self.CP('act', Osb[:, hs[0]:hs[0] + n, :], ps[:, 0:n * 64].rearrange('p (h v) -> p h v', v=64), [pn], ['Osb%d' % half])
                Ov = Osb[:].rearrange('p (c a) v -> p c a v', a=2)
                for h2 in range(2):
                    ps, pn = psl[h2]
                    self.TT('dve', Ov[:, :, h2, :], Ov[:, :, h2, :], ps[:, 0:384].rearrange('p (c v) -> p c v', v=64), ALU.add, [pn, 'Osb0', 'Osb1'], ['Osb0', 'Osb1'])
                osr = ['Osb0', 'Osb1']
                for h in range(12):
                    self.E('dve', (lambda e, h=h: e.bn_stats(bst[:, h, :], Osb[:, h, :])), osr, ['bst'])
                for h in range(12):
                    self.E('dve', (lambda e, h=h: e.bn_aggr(st[:, 2 * h:2 * h + 2], bst[:, h, :])), ['bst'], ['st'])
                stv = st[:, 0:24].rearrange('p (h a) -> p h a', a=2)
                self.ACT(st[:, 36:48], stv[:, :, 1], AF.Sqrt, ['st'], ['st'], bias=RW_EPS)
                self.RECIP(st[:, 36:48], st[:, 36:48], ['st'], ['st'])
                for h in range(12):
                    self.TSC('dve', t2[:, h // 2, (h % 2) * 64:(h % 2) * 64 + 64], Osb[:, h, :], st[:, 2 * h:2 * h + 1], st[:, 36 + h:37 + h], ALU.subtract, ALU.mult, osr + ['st'], ['t2'])
                for half in range(2):
                    ps, pn = self.psum()
                    cs_ = range(0, 4) if half == 0 else range(4, 6)
                    for c in cs_:
                        self.TR(ps[:, (c % 4) * 128:(c % 4 + 1) * 128], t2[:, c, :], ['t2'], [pn])
                    for c in cs_:
                        self.TSC('dve', cs[:, c, :], ps[:, (c % 4) * 128:(c % 4 + 1) * 128], prm[:, l, P_LNW + c:P_LNW + c + 1], prm[:, l, P_LNB + c:P_LNB + c + 1], ALU.mult, ALU.add, [pn, 'prm'], ['cs'])
                if self.rw_stop is not None and self.rw_stop < 17:
                    continue
                self.TT('pool', t1[:], xs[:, 0:6, :], kf[:], ALU.mult, ['xs', 'kf'], ['t1'])
                for c in range(6):
                    self.TSC('pool', t1[:, c, :], t1[:, c, :], prm[:, l, P_RK + c:P_RK + c + 1], None, ALU.mult, None, ['t1', 'prm'], ['t1'])
                for half in range(2):
                    ps, pn = self.psum()
                    cs_ = range(0, 4) if half == 0 else range(4, 6)
                    for c in cs_:
                        self.MM(ps[:, (c % 4) * 128:(c % 4 + 1) * 128], blk[:], t1[:, c, :], True, True, ['blk', 't1'], [pn])
                    n = len(cs_)
                    self.TT('dve', t2[:, cs_[0]:cs_[0] + n, :], ps[:, 0:n * 128].rearrange('p (c t) -> p c t', t=128), xs[:, 12 + cs_[0]:12 + cs_[0] + n, :], ALU.mult, [pn, 'xs'], ['t2'])
                self.TT('dve', cs[:], cs[:], t2[:], ALU.add, ['cs', 't2'], ['cs'])
                self.TT('dve', ybf[:], cs[:], gT[:], ALU.mult, ['cs', 'gT'], ['ybf'])
                self.dma(self.yT[0:768, t0:t0 + 128].rearrange('(c p) t -> p c t', p=128), ybf[:], ['ybf'], ['yT_rw'])
                if self.rw_stop is not None and self.rw_stop < 18:
                    continue
                for sb in range(nsub):
                    if sample:
                        HG = M4[:, sb % 2, :].rearrange('p (c v) -> p c v', v=64)
                        hgr = 'M4_%d' % (sb % 2)
                        gend = Gm[:, :, sb * 8 + 7:sb * 8 + 8].to_broadcast([128, 6, 64])
                        self.TT('dve', HG, Hin[:, sb, :, :], gend, ALU.mult, ['Hin%d' % sb, 'Gm'], [hgr])
                        self.TSC('dve', kk[:].rearrange('p c t -> p (c t)'), Btm[:], rm16[:, sb:sb + 1], None, ALU.mult, None, ['Btm0', 'Btm1', 'rm16'], ['kk'])
                        self.ACT(nb[:].rearrange('p c t -> p (c t)'), Ktm[:], AF.Copy, ['Ktm0', 'Ktm1', 'rm16'], ['nb'], scale=rm16[:, sb:sb + 1])
                        bl, kl, blr, klr = kk[:].rearrange('p c t -> p (c t)'), nb[:].rearrange('p c t -> p (c t)'), ['kk'], ['nb']
                        tend = sb * 8 + 7
                    else:
                        bl, kl, blr, klr = Btm[:], Ktm[:], ['Btm0', 'Btm1'], ['Ktm0', 'Ktm1']
                        tend = 127
                    for c in range(6):
                        ps, pn = self.psum()
                        self.MM(ps[:, 0:128], bl[:, c * 128:(c + 1) * 128], sgw[:].rearrange('p c t -> p (c t)')[:, c * 128:(c + 1) * 128], True, False, blr + ur, [pn])
                        self.MM(ps[:, 0:128], kl[:, c * 128:(c + 1) * 128], Vtm[:, c * 128:(c + 1) * 128], False, True, klr + VT_R, [pn])
                        for h2 in range(2):
                            pr = slice(64 * h2, 64 * h2 + 64)
                            if sample:
                                self.STT(Ho(sb)[pr, c, :], ps[pr, 64 * h2:64 * h2 + 64], Gm[pr, c, tend:tend + 1], HG[pr, c, :], ALU.mult, ALU.add, [pn, 'Gm', hgr], [hor(sb)])
                            else:
                                self.TT('dve', htmp[pr, :], ps[pr, 64 * h2:64 * h2 + 64], Hi(sb)[pr, c, :], ALU.add, [pn, hir(sb)], ['htmp'])
                                self.TSC('dve', Ho(sb)[pr, c, :], htmp[pr, :], Gm[pr, c, tend:tend + 1], None, ALU.mult, None, ['htmp', 'Gm'], [hor(sb)])
                if self.rw_stop is not None and self.rw_stop < 19:
                    continue
                last_prompt = (not sample) and ti == TP // 128 - 1
                if sample or last_prompt:
                    for sb in range(nsub):
                        b = sb % 2
                        ps, pn = self.psum()
                        ps2, pn2 = self.psum()
                        for c in range(4):
                            self.TR(ps[0:64, c * 128:(c + 1) * 128], Ho(sb)[:, c, :], [hor(sb)], [pn])
                        for c in range(4, 6):
                            self.TR(ps2[0:64, (c - 4) * 128:(c - 3) * 128], Ho(sb)[:, c, :], [hor(sb)], [pn2])
                        self.CP('act', sout[b][:, 0:4, :, :], ps[0:64, :].rearrange('p (c h k) -> p c h k', h=2, k=64), [pn], ['sin0'])
                        self.CP('dve', sout[b][:, 4:6, :, :], ps2[0:64, 0:256].rearrange('p (c h k) -> p c h k', h=2, k=64), [pn2], ['sin0'])
                        dst = (O_['s_rwkv'][l, sb] if sample else O_['p_rwkv'][l]).rearrange('(c h2) v k -> v c h2 k', h2=2)
                        self.dma(dst, sout[b][:], ['sin0'], [])
                    if sample:
                        for g in range(5):
                            a0 = 2 * (g % 2)
                            stg = M4[0:16, a0:a0 + 2, :].rearrange('p a b -> p (a b)')[:, 0:512]
                            sres = ['M4_%d' % a0, 'M4_%d' % (a0 + 1)]
                            ps, pn = self.psum()
                            for j in range(4):
                                self.TR(ps[0:16, j * 128:(j + 1) * 128], pv[:, g * 4 + j, :, 8], ['pbuf'], [pn])
                            self.CP('act', stg, ps[0:16, 0:512], [pn], sres)
                            self.dma(O_['s_rwkv_shift'][l, :, g * 512:(g + 1) * 512], stg, sres, [])
                    else:
                        self.dma(O_['p_rwkv_shift'][l].rearrange('(c p) -> p c', p=128), pbuf[:, :, 128], ['pbuf'], [], slow=True)
    def phase_hgrn(self, l, I, O_, side=None):
        nc = self.nc
        TP = self.TP
        prm = self.prm
        with contextlib.ExitStack() as es:
            A = lambda name, shape, dt=F32: es.enter_context(nc.sbuf_tensor('hg%d_%s' % (l, name), list(shape), dt))
            masks = {}
            for mk, ns in (('4', 4), ('16', 16)):
                up = A('up' + mk, [128, 128])
                rst = A('rst' + mk, [128, 128])
                cm = A('cm' + mk, [128, ns, 128])
                rm = A('rm' + mk, [128, ns])
                self.dma(up[:], self.cst['up_i' + mk], [], ['up' + mk])
                self.dma(rst[:], self.cst['rst' + mk], [], ['rst' + mk])
                self.dma(cm[:], self.cst['cm' + mk].rearrange('p (s t) -> p s t', s=ns), [], ['cm' + mk])
                self.dma(rm[:], self.cst['rm' + mk], [], ['rm' + mk])
                masks[mk] = (up, rst, cm, rm)
            pb = A('pb', [128, 24, 128])
            q = A('q', [128, 6, 128])
            fg = A('fg', [128, 6, 128])
            lf = A('lf', [128, 6, 128])
            kx = A('kx', [128, 6, 128])
            cs = A('cs', [128, 6, 128])
            Gm = A('Gm', [128, 6, 128])
            Gi = A('Gi', [128, 6, 128])
            QT = A('QT', [128, 6, 128])
            KT = A('KT', [128, 6, 128])
            sg = A('sg', [128, 6, 128])
            Vtm = A('Vtm', [128, 6, 128])
            Ktm = A('Ktm', [128, 6, 128])
            KM = A('KM', [128, 16, 128])
            QM = A('QM', [128, 16, 128])
            AT = A('AT', [128, 128])
            Hs = A('Hs', [128, 16, 128])
            Ho = A('Ho', [128, 16, 128])
            Hc = A('Hc', [128, 6, 128])
            htmp = A('htmp', [128, 128])
            Hs6 = A('Hs6', [128, 6, 5, 128])
            AT6 = A('AT6', [128, 6, 128])
            tmp6 = A('tmp6', [128, 6, 128])
            on6 = A('on6', [128, 6, 128])
            ybf6 = A('ybf6', [128, 6, 128], BF16)
            ss6 = A('ss6', [128, 24])
            junk = htmp
            ss = A('ss', [128, 4])
            on = A('on', [128, 128])
            ybf = A('ybf', [128, 128], BF16)
            self.MEMSET('dve', Hc[:], 0.0, ['Hc'])
            tiles = [('p', ti) for ti in range(TP // 128)] + [('s', 0)]
            for kind, ti in tiles:
                sample = kind == 's'
                t0 = TP if sample else ti * 128
                mk = '16' if sample else '4'
                nsub = 16 if sample else 4
                sub = 128 // nsub
                up, rst, cm, rm = masks[mk]
                if sample or ti == 0:
                    self.dma(pb[:], self.projT[RW_COLS:RW_COLS + HG_COLS, t0:t0 + 128].rearrange('(c p) t -> p c t', p=128), ['projT'], ['pb'])
                self.ACT(q[:], pb[:, 0:6, :], AF.Silu, ['pb'], ['q'])
                self.ACT(sg[:], pb[:, 18:24, :], AF.Silu, ['pb'], ['sg'])
                self.ACT(fg[:], pb[:, 6:12, :], AF.Sigmoid, ['pb'], ['fg'])
                for h in range(6):
                    self.TSC('dve', fg[:, h, :], fg[:, h, :], prm[:, l, P_OMLB + h:P_OMLB + h + 1], prm[:, l, P_LB + h:P_LB + h + 1], ALU.mult, ALU.add, ['fg', 'prm'], ['fg'])
                self.ACT(lf[:], fg[:], AF.Ln, ['fg'], ['lf'])
                self.TSC('dve', kx[:], fg[:], -1.0, 1.0, ALU.mult, ALU.add, ['fg'], ['kx'])
                for h in range(6):
                    self.SCAN(cs[:, h, :], rst[:], lf[:, h, :], 0.0, ['lf', 'rst' + mk], ['cs'])
                self.TSC('dve', cs[:], cs[:], -80.0, None, ALU.max, None, ['cs'], ['cs'])
                self.ACT(Gm[:], cs[:], AF.Exp, ['cs'], ['Gm'])
                self.ACT(Gi[:], cs[:], AF.Exp, ['cs'], ['Gi'], scale=-1.0)
                self.TT('dve', QT[:], q[:], Gm[:], ALU.mult, ['q', 'Gm'], ['QT'])
                self.TT('dve', KT[:], kx[:], Gi[:], ALU.mult, ['kx', 'Gi'], ['KT'])
                for (src_fn, sres, dst, dres) in ((lambda h: pb[:, 12 + h, :], 'pb', Vtm, 'Vtm'), (lambda h: KT[:, h, :], 'KT', Ktm, 'Ktm')):
                    for half in range(2):
                        ps, pn = self.psum()
                        hs = range(0, 4) if half == 0 else range(4, 6)
                        for h in hs:
                            self.TR(ps[:, (h % 4) * 128:(h % 4 + 1) * 128], src_fn(h), [sres], [pn])
                        n = len(hs)
                        self.CP('act' if half == 0 else 'dve', dst[:, hs[0]:hs[0] + n, :], ps[:, 0:n * 128].rearrange('p (h t) -> p h t', t=128), [pn], [dres + str(half)])
                vr, kr = ['Vtm0', 'Vtm1'], ['Ktm0', 'Ktm1']
                if (not sample) and ti + 1 < TP // 128:
                    self.dma(pb[:], self.projT[RW_COLS:RW_COLS + HG_COLS, t0 + 128:t0 + 256].rearrange('(c p) t -> p c t', p=128), ['projT'], ['pb'])
                if not sample:
                    for half in range(2):
                        ps, pn = self.psum()
                        hs = range(0, 4) if half == 0 else range(4, 6)
                        for h in hs:
                            self.MM(ps[:, (h % 4) * 128:(h % 4 + 1) * 128], KT[:, h, :], QT[:, h, :], True, True, ['KT', 'QT'], [pn])
                        n = len(hs)
                        self.TT('dve', AT6[:, hs[0]:hs[0] + n, :], ps[:, 0:n * 128].rearrange('p (h t) -> p h t', t=128),
                                up[:, None, :].to_broadcast([128, n, 128]) if False else up[:].rearrange('p (a t) -> p a t', a=1).to_broadcast([128, n, 128]), ALU.mult, [pn, 'up' + mk], ['AT6_%d' % half])
                    atr = ['AT6_0', 'AT6_1']
                    self.TSC('dve', KM[:, 0:6, :], Ktm[:], rm[:, 3:4], None, ALU.mult, None, kr + ['rm' + mk], ['KM3'])
                    self.TT('dve', QM[:, 0:6, :], QT[:], cm[:, 3:4, :].to_broadcast([128, 6, 128]), ALU.mult, ['QT', 'cm' + mk], ['QM3'])
                    self.CP('act', Hs6[:, :, 0, :], Hc[:], ['Hc'], ['Hs6_0'])
                    for sb in range(4):
                        pss = [self.psum(), self.psum()]
                        for h in range(6):
                            ps, pn = pss[h // 4]
                            o = ps[:, (h % 4) * 128:(h % 4 + 1) * 128]
                            if sb == 3:
                                self.MM(o, KM[:, h, :], Vtm[:, h, :], True, True, ['KM3'] + vr, [pn])
                            else:
                                pr = slice(sb * 32, (sb + 1) * 32)
                                self.MM(o, Ktm[pr, h, :], Vtm[pr, h, :], True, True, kr + vr, [pn])
                        for half in range(2):
                            ps, pn = pss[half]
                            hs = range(0, 4) if half == 0 else range(4, 6)
                            n = len(hs)
                            self.TT('dve', tmp6[:, hs[0]:hs[0] + n, :], ps[:, 0:n * 128].rearrange('p (h t) -> p h t', t=128), Hs6[:, hs[0]:hs[0] + n, sb, :], ALU.add,
                                    [pn, 'Hs6_%d' % sb], ['tmp6_%d' % half])
                        tend = sb * 32 + 31
                        self.TT('dve', Hs6[:, :, sb + 1, :], tmp6[:], Gm[:, :, tend:tend + 1].to_broadcast([128, 6, 128]), ALU.mult, ['tmp6_0', 'tmp6_1', 'Gm'], ['Hs6_%d' % (sb + 1)])
                    psO = [self.psum(), self.psum()]
                    for h in range(6):
                        ps, pn = psO[h // 4]
                        o = ps[:, (h % 4) * 128:(h % 4 + 1) * 128]
                        self.MM(o, QM[:, h, :], Hs6[:, h, 3, :], True, False, ['QM3', 'Hs6_3'], [pn], sgc=True)
                        for sb in range(3):
                            pr = slice(sb * 32, (sb + 1) * 32)
                            self.MM(ps[pr, (h % 4) * 128:(h % 4 + 1) * 128], QT[:, h, pr], Hs6[:, h, sb, :], True, False, ['QT', 'Hs6_%d' % sb], [pn], sgc=True)
                        self.MM(o, AT6[:, h, :], Vtm[:, h, :], False, True, atr + vr, [pn], sgc=True)
                    for h in range(6):
                        ps, pn = psO[h // 4]
                        self.ACT(junk[:], ps[:, (h % 4) * 128:(h % 4 + 1) * 128], AF.Square, [pn], ['htmp', 'ss6'], accum=ss6[:, h:h + 1])
                    self.ACT(ss6[:, 6:12], ss6[:, 0:6], AF.Sqrt, ['ss6'], ['ss6'], bias=EPS, scale=1.0 / 128)
                    self.RECIP(ss6[:, 12:18], ss6[:, 6:12], ['ss6'], ['ss6'])
                    for half in range(2):
                        ps, pn = psO[half]
                        hs = range(0, 4) if half == 0 else range(4, 6)
                        n = len(hs)
                        self.TT('dve', on6[:, hs[0]:hs[0] + n, :], ps[:, 0:n * 128].rearrange('p (h t) -> p h t', t=128),
                                ss6[:, 12 + hs[0]:12 + hs[0] + n].rearrange('p (h a) -> p h a', a=1).to_broadcast([128, n, 128]), ALU.mult, [pn, 'ss6'], ['on6_%d' % half])
                    for half in range(2):
                        ps2, pn2 = self.psum()
                        hs = range(0, 4) if half == 0 else range(4, 6)
                        for h in hs:
                            self.TR(ps2[:, (h % 4) * 128:(h % 4 + 1) * 128], on6[:, h, :], ['on6_0', 'on6_1'], [pn2])
                        for h in hs:
                            self.STT(ybf6[:, h, :], ps2[:, (h % 4) * 128:(h % 4 + 1) * 128], prm[:, l, P_HNW + h:P_HNW + h + 1], sg[:, h, :], ALU.mult, ALU.mult, [pn2, 'prm', 'sg'], ['ybf6'])
                    self.dma(self.yT[768:1536, t0:t0 + 128].rearrange('(h p) t -> p h t', p=128), ybf6[:], ['ybf6'], ['yT_hg'])
                    self.CP('act', Hc[:], Hs6[:, :, 4, :], ['Hs6_4'], ['Hc'])
                    if ti == TP // 128 - 1:
                        self.dma(O_['p_hgrn'][l].rearrange('h k v -> k h v'), Hs6[:, :, 4, :], ['Hs6_4'], [])
                    if side is not None and (ti % 2 == 1 or ti == 0):
                        next(side, None)
                    continue
                for h in range(6):
                    ps, pn = self.psum()
                    self.MM(ps[:, 0:128], KT[:, h, :], QT[:, h, :], True, True, ['KT', 'QT'], [pn])
                    self.TT('dve', AT[:], ps[:, 0:128], up[:], ALU.mult, [pn, 'up' + mk], ['AT'])
                    if sample:
                        Hs6v = Hs6[:].rearrange('p h s v -> p (h s) v')[:, 0:16, :]
                        all6 = ['Hs6_%d' % k for k in range(5)]
                        ibufs = [(Hs, lambda sb: ['Hs%d' % sb], ['Hs%d' % sb for sb in range(16)]), (Hs6v, lambda sb: all6, all6)]
                        if h == 0:
                            self.dma(ibufs[0][0][:, 0:16, :], I['state_hgrn'][l, :, 0, :, :].rearrange('s k v -> k s v'), [], ibufs[0][2])
                        if h + 1 < 6:
                            nb_ = ibufs[(h + 1) % 2]
                            self.dma(nb_[0][:, 0:16, :], I['state_hgrn'][l, :, h + 1, :, :].rearrange('s k v -> k s v'), [], nb_[2])
                        cb_ = ibufs[h % 2]
                        hin = lambda sb, cb_=cb_: (cb_[0][:, sb, :], cb_[1](sb))
                        hout = lambda sb: (Ho[:, sb, :], ['Ho%d' % sb])
                    else:
                        self.CP('pool', Hs[:, 0, :], Hc[:, h, :], ['Hc'], ['Hs0'])
                        hin = lambda sb: (Hs[:, sb, :], ['Hs%d' % sb])
                        hout = lambda sb: (Hs[:, sb + 1, :], ['Hs%d' % (sb + 1)])
                    msk = (lambda sb: True) if sample else (lambda sb: sb == 3)
                    for sb in range(nsub):
                        if msk(sb):
                            self.TSC('dve', KM[:, sb, :], Ktm[:, h, :], rm[:, sb:sb + 1], None, ALU.mult, None, kr + ['rm' + mk], ['KM%d' % sb])
                            self.TT('pool' if (sample and sb % 4 == 3) else 'dve', QM[:, sb, :], QT[:, h, :], cm[:, sb, :], ALU.mult, ['QT', 'cm' + mk], ['QM%d' % sb])
                    for sb in range(nsub):
                        ps, pn = self.psum()
                        if msk(sb):
                            self.MM(ps[:, 0:128], KM[:, sb, :], Vtm[:, h, :], True, True, ['KM%d' % sb] + vr, [pn])
                        else:
                            pr = slice(sb * sub, (sb + 1) * sub)
                            self.MM(ps[:, 0:128], Ktm[pr, h, :], Vtm[pr, h, :], True, True, kr + vr, [pn])
                        hi, hir = hin(sb)
                        ho, hor = hout(sb)
                        tend = sb * sub + sub - 1
                        self.TT('dve', htmp[:], ps[:, 0:128], hi, ALU.add, [pn] + hir, ['htmp'])
                        self.TSC('dve', ho, htmp[:], Gm[:, h, tend:tend + 1], None, ALU.mult, None, ['htmp', 'Gm'], hor)
                    ps, pn = self.psum()
                    for sb in ([3, 0, 1, 2] if not sample else range(nsub)):
                        hi, hir = hin(sb)
                        if sample:
                            self.MM(ps[:, 0:128], QM[:, sb, :], hi, sb == 0, False, ['QM%d' % sb] + hir, [pn])
                        elif sb == 3:
                            self.MM(ps[:, 0:128], QM[:, sb, :], hi, True, False, ['QM%d' % sb] + hir, [pn], sgc=True)
                        else:
                            pr = slice(sb * sub, (sb + 1) * sub)
                            self.MM(ps[pr, 0:128], QT[:, h, pr], hi, True, False, ['QT'] + hir, [pn], sgc=True)
                    self.MM(ps[:, 0:128], AT[:], Vtm[:, h, :], False, True, ['AT'] + vr, [pn], sgc=not sample)
                    self.ACT(junk[:], ps[:, 0:128], AF.Square, [pn], ['htmp', 'ss'], accum=ss[:, 0:1])
                    self.ACT(ss[:, 1:2], ss[:, 0:1], AF.Sqrt, ['ss'], ['ss'], bias=EPS, scale=1.0 / 128)
                    self.RECIP(ss[:, 2:3], ss[:, 1:2], ['ss'], ['ss'])
                    self.TSC('dve', on[:], ps[:, 0:128], ss[:, 2:3], None, ALU.mult, None, [pn, 'ss'], ['on'])
                    ps2, pn2 = self.psum()
                    self.TR(ps2[:, 0:128], on[:], ['on'], [pn2])
                    self.STT(ybf[:], ps2[:, 0:128], prm[:, l, P_HNW + h:P_HNW + h + 1], sg[:, h, :], ALU.mult, ALU.mult, [pn2, 'prm', 'sg'], ['ybf'])
                    self.dma(self.yT[768 + h * 128:768 + (h + 1) * 128, t0:t0 + 128], ybf[:], ['ybf'], ['yT_hg'])
                    if sample:
                        self.dma(O_['s_hgrn'][l, :, h, :, :].rearrange('s k v -> k s v'), Ho[:], ['Ho%d' % sb for sb in range(16)], [])
                    else:
                        self.CP('pool', Hc[:, h, :], Hs[:, nsub, :], ['Hs%d' % nsub], ['Hc'])
                        if ti == TP // 128 - 1:
                            self.dma(O_['p_hgrn'][l, h, :, :], Hs[:, nsub, :], ['Hs%d' % nsub], [])
                if side is not None and (ti % 2 == 1 or sample or ti == 0):
                    next(side, None)
            if side is not None:
                for _ in side:
                    pass

    def phase_lru(self, l, I, O_):
        for _ in self.lru_gen(l, I, O_):
            pass

    def lru_gen(self, l, I, O_):
        nc = self.nc
        TP = self.TP
        prm = self.prm
        base = RW_COLS + HG_COLS
        with contextlib.ExitStack() as es:
            A = lambda name, shape, dt=F32: es.enter_context(nc.sbuf_tensor('lr%d_%s' % (l, name), list(shape), dt))
            wabd = A('wabd', [128, 4, 128])
            wxbd = A('wxbd', [128, 4, 128])
            self.MEMSET('dve', wabd[:], 0.0, ['wabd'])
            self.MEMSET('dve', wxbd[:], 0.0, ['wxbd'])
            for blk in range(8):
                pr = slice(64 * (blk % 2), 64 * (blk % 2) + 64)
                self.dma(wabd[pr, blk // 2, 64 * (blk % 2):64 * (blk % 2) + 64], I['rglru_wa'][l, blk], [], ['wabd'])
                self.dma(wxbd[pr, blk // 2, 64 * (blk % 2):64 * (blk % 2) + 64], I['rglru_wx'][l, blk], [], ['wxbd'])
            TM = max(TP, 128)
            xpb = [A('xp_p%d' % c, [128, TP + 3]) for c in range(2)]
            gtb = [A('gt_p%d' % c, [128, TP]) for c in range(2)]
            xps = {('p', c): xpb[c % 2] for c in range(4)}
            gts = {('p', c): gtb[c % 2] for c in range(4)}
            for c in range(4):
                xps[('s', c)] = A('xp_s%d' % c, [128, 16 * 11])
                gts[('s', c)] = A('gt_s%d' % c, [128, 128])
            h0s = [A('h0_%d' % c, [128, 16]) for c in range(4)]
            xc = A('xc', [128, TM])
            rr = A('rr', [128, TM])
            ig = A('ig', [128, TM])
            aa = A('aa', [128, TM])
            uu = A('uu', [128, TM])
            hh = xc
            ybf = aa[:].bitcast(BF16)
            def prefetch(kind, c):
                    sample = kind == 's'
                    nseq, T, t0 = (16, 8, TP) if sample else (1, TP, 0)
                    N = nseq * T
                    xp, gt, h0 = xps[(kind, c)], gts[(kind, c)], h0s[c]
                    xn, gn_ = 'xp%s%d' % (kind, c % 2 if kind == 'p' else c), 'gt%s%d' % (kind, c % 2 if kind == 'p' else c)
                    xpv = xp[:, 0:nseq * (T + 3)].rearrange('p (s t) -> p s t', t=T + 3)
                    if sample:
                        s48 = rr[0:48, 0:128]
                        s16 = rr[0:16, 128:256]
                        self.dma(s48, I['cache_rglru_conv'][l, :, :, c * 128:(c + 1) * 128].rearrange('s j p -> (s j) p'), [], ['rr'])
                        self.dma(s16, I['state_rglru'][l, :, c * 128:(c + 1) * 128], [], ['rr'])
                        ps, pn = self.psum()
                        self.TR(ps[:, 0:48], s48, ['rr'], [pn], n=48)
                        self.TR(ps[:, 64:80], s16, ['rr'], [pn], n=16)
                        self.CP('dve', xpv[:, :, 0:3], ps[:, 0:48].rearrange('p (s j) -> p s j', j=3), [pn], [xn])
                        self.CP('dve', h0[:], ps[:, 64:80], [pn], ['h0_%d' % c])
                    else:
                        self.MEMSET('pool', xpv[:, :, 0:3], 0.0, [xn])
                    r0 = base + c * 128
                    self.dma(xpv[:, :, 3:T + 3], self.projT[r0:r0 + 128, t0:t0 + N].rearrange('p (s t) -> p s t', t=T), ['projT'], [xn])
                    self.dma(gt[:, 0:N], self.projT[r0 + 512:r0 + 640, t0:t0 + N], ['projT'], [gn_])
            for c in range(2):
                prefetch('p', c)
            for c in range(4):
                prefetch('s', c)
            yield
            for kind in ('p', 's'):
                sample = kind == 's'
                nseq, T, t0 = (16, 8, TP) if sample else (1, TP, 0)
                N = nseq * T
                for c in range(4):
                    xp, gt, h0 = xps[(kind, c)], gts[(kind, c)], h0s[c]
                    xn, gn_ = 'xp%s%d' % (kind, c % 2 if kind == 'p' else c), 'gt%s%d' % (kind, c % 2 if kind == 'p' else c)
                    xpv = xp[:, 0:nseq * (T + 3)].rearrange('p (s t) -> p s t', t=T + 3)
                    v3 = lambda ap: ap[:, 0:N].rearrange('p (s t) -> p s t', t=T)
                    r0 = base + c * 128
                    cw = lambda j: prm[:, l, P_CW + 4 * j + c:P_CW + 4 * j + c + 1]
                    self.TSC('dve', v3(xc), xpv[:, :, 0:T], cw(0), prm[:, l, P_CB + c:P_CB + c + 1], ALU.mult, ALU.add, [xn, 'prm'], ['xc'])
                    for j in range(1, 4):
                        self.STT(v3(xc), xpv[:, :, j:j + T], cw(j), v3(xc), ALU.mult, ALU.add, [xn, 'prm', 'xc'], ['xc'])
                    g0 = 0
                    while g0 < N:
                        gn = min(512, N - g0)
                        ps, pn = self.psum()
                        self.MM(ps[:, 0:gn], wabd[:, c, :], xc[:, g0:g0 + gn], True, True, ['wabd', 'xc'], [pn])
                        self.ACT(rr[:, g0:g0 + gn], ps[:, 0:gn], AF.Sigmoid, [pn, 'prm'], ['rr'], bias=prm[:, l, P_BA + c:P_BA + c + 1])
                        ps, pn = self.psum()
                        self.MM(ps[:, 0:gn], wxbd[:, c, :], xc[:, g0:g0 + gn], True, True, ['wxbd', 'xc'], [pn])
                        self.ACT(ig[:, g0:g0 + gn], ps[:, 0:gn], AF.Sigmoid, [pn, 'prm'], ['ig'], bias=prm[:, l, P_BX + c:P_BX + c + 1])
                        g0 += gn
                    self.ACT(aa[:, 0:N], rr[:, 0:N], AF.Exp, ['rr', 'prm'], ['aa'], scale=prm[:, l, P_C8 + c:P_C8 + c + 1])
                    self.ACT(uu[:, 0:N], rr[:, 0:N], AF.Exp, ['rr', 'prm'], ['uu'], scale=prm[:, l, P_2C8 + c:P_2C8 + c + 1])
                    self.ACT(uu[:, 0:N], uu[:, 0:N], AF.Sqrt, ['uu'], ['uu'], bias=1.0, scale=-1.0)
                    self.TT('pool', ig[:, 0:N], ig[:, 0:N], xc[:, 0:N], ALU.mult, ['ig', 'xc'], ['ig'])
                    self.TT('dve', uu[:, 0:N], uu[:, 0:N], ig[:, 0:N], ALU.mult, ['uu', 'ig'], ['uu'])
                    if sample:
                        self.TT('dve', h0[:], h0[:], v3(aa)[:, :, 0], ALU.mult, ['h0_%d' % c, 'aa'], ['h0_%d' % c])
                        self.TT('dve', v3(uu)[:, :, 0], v3(uu)[:, :, 0], h0[:], ALU.add, ['uu', 'h0_%d' % c], ['uu'])
                        self.MEMSET('dve', v3(aa)[:, :, 0:1], 0.0, ['aa'])
                    self.SCAN2(hh[:, 0:N], aa[:, 0:N], uu[:, 0:N], ['aa', 'uu'], ['xc'])
                    self.TT('pool', rr[:, 0:N], gt[:, 0:N], gt[:, 0:N], ALU.mult, [gn_, 'rr'], ['rr'])
                    self.TSC('pool', rr[:, 0:N], rr[:, 0:N], 0.044715, 1.0, ALU.mult, ALU.add, ['rr'], ['rr'])
                    self.TT('pool', rr[:, 0:N], rr[:, 0:N], gt[:, 0:N], ALU.mult, ['rr', gn_], ['rr'])
                    self.ACT(rr[:, 0:N], rr[:, 0:N], AF.Sigmoid, ['rr'], ['rr'], scale=1.5957691216057308)
                    self.TT('dve', ig[:, 0:N], hh[:, 0:N], gt[:, 0:N], ALU.mult, ['xc', gn_, 'ig'], ['ig'])
                    self.TT('dve', ybf[:, 0:N], ig[:, 0:N], rr[:, 0:N], ALU.mult, ['ig', 'rr'], ['aa'])
                    self.dma(self.yT[1536 + c * 128:1536 + (c + 1) * 128, t0:t0 + N], ybf[:, 0:N], ['aa'], ['yT_lru'])
                    if sample:
                        self.CP('dve', rr[:, 0:48].rearrange('p (s j) -> p s j', j=3), xpv[:, :, T:T + 3], [xn, 'rr'], ['rr'])
                        self.CP('dve', rr[:, 64:80], v3(hh)[:, :, T - 1], ['xc', 'rr'], ['rr'])
                        ps, pn = self.psum()
                        self.TR(ps[0:48, 0:128], rr[:, 0:48], ['rr'], [pn])
                        ps2, pn2 = self.psum()
                        self.TR(ps2[0:16, 0:128], rr[:, 64:80], ['rr'], [pn2])
                        self.CP('act', ig[0:48, 0:128], ps[0:48, 0:128], [pn, 'ig'], ['ig'])
                        self.CP('act', ig[0:16, 128:256], ps2[0:16, 0:128], [pn2, 'ig'], ['ig'])
                        self.dma(O_['s_rglru_conv'][l, :, :, c * 128:(c + 1) * 128].rearrange('s j p -> (s j) p'), ig[0:48, 0:128], ['ig'], [])
                        self.dma(O_['s_rglru'][l, :, c * 128:(c + 1) * 128], ig[0:16, 128:256], ['ig'], [])
                    else:
                        self.dma(O_['p_rglru_conv'][l, :, c * 128:(c + 1) * 128].rearrange('j p -> p j'), xpv[:, 0, T:T + 3], [xn], [], slow=True)
                        self.dma(O_['p_rglru'][l, c * 128:(c + 1) * 128].rearrange('(p o) -> p o', o=1), hh[:, T - 1:T], ['xc'], [], slow=True)
                    if kind == 'p' and c < 2:
                        prefetch('p', c + 2)
                    yield

    def SCAN2(self, out, d0, d1, r, w):
        self.E('dve', lambda e: e.tensor_tensor_scan(out, d0, d1, 0.0, ALU.mult, ALU.add), r, w)
    def gemm(self, tag, A, KC, blocks, rhs_fn, rhs_res, epilogue, cbw=256, tgs=None, bufs=None, single=False):
        tgs = self.tg if tgs is None else tgs
        if bufs is None:
            wst = [A('%s_wst%d' % (tag, i), [128, KC, cbw]) for i in range(1 if single else 2)]
            if single:
                wst = wst * 2
            wbf = [A('%s_wbf%d' % (tag, i), [128, KC, cbw], BF16) for i in range(2)]
        else:
            wst, wbf, tag = bufs

        sb_ = (lambda b: 0) if single else (lambda b: b)
        ka = (KC * 7) // 16
        kparts = [(0, ka), (ka, 2 * ka), (2 * ka, KC)]
        kp_of = lambda kc: 0 if kc < ka else (1 if kc < 2 * ka else 2)

        def load_dma(bi):
            b = bi % 2
            off = 0
            for ap, w in blocks[bi]:
                self.dma(wst[b][:, :, off:off + w], ap.rearrange('(c p) n -> p c n', p=128), [], ['%s_wst%d' % (tag, sb_(b))])
                off += w
            return off

        def load_cast(bi, off):
            b = bi % 2
            for p, (k0, k1) in enumerate(kparts):
                self.CP(('dve', 'act', 'pool')[p], wbf[b][:, k0:k1, 0:off], wst[b][:, k0:k1, 0:off], ['%s_wst%d' % (tag, sb_(b))], ['%s_wbf%d_%d' % (tag, b, p)])

        widths = {0: load_dma(0)}
        load_cast(0, widths[0])
        for bi in range(len(blocks)):
            if bi + 1 < len(blocks):
                widths[bi + 1] = load_dma(bi + 1)
            b = bi % 2
            nu = widths[bi] // 128
            work = [(u, gi, t0, tn) for u in range(nu) for gi, (t0, tn) in enumerate(tgs)]
            cast_at = (len(work) * 3) // 5
            for wi, (u, gi, t0, tn) in enumerate(work):
                if wi == cast_at and bi + 1 < len(blocks):
                    load_cast(bi + 1, widths[bi + 1])
                ps, pn = self.psum()
                for kc in range(KC):
                    self.MM(ps[:, 0:tn], wbf[b][:, kc, u * 128:(u + 1) * 128], rhs_fn(kc, t0, tn), kc == 0, kc == KC - 1,
                            ['%s_wbf%d_%d' % (tag, b, kp_of(kc))] + rhs_res, [pn])
                epilogue(bi, u, gi, t0, tn, ps, pn)

    def phase_inproj(self, l, actT, I, A):
        st = [A('ip_st%d_%d' % (l, i), [128, 512]) for i in range(4)]
        w_in = I['w_in']
        blocks = [[(w_in[l, :, c0:c0 + 512], 512)] for c0 in range(0, IN_COLS, 512)]
        self._k = 0

        def epi(bi, u, gi, t0, tn, ps, pn):
            k = self._k % 4
            self._k += 1
            c0 = bi * 512 + u * 128
            self.CP('act' if k % 2 == 0 else 'dve', st[k][:, 0:tn], ps[:, 0:tn], [pn], ['ip_st%d' % k])
            self.dma(self.projT[c0:c0 + 128, t0:t0 + tn], st[k][:, 0:tn], ['ip_st%d' % k], ['projT'])

        self.gemm('ip%d' % l, A, 16, blocks, lambda kc, t0, tn: actT[:, kc, t0:t0 + tn], ['actT'], epi, cbw=512)

    def residual_epi(self, A, tag):
        xs = [A('%s_x%d' % (tag, i), [128, 512]) for i in range(4)]
        self._k = 0

        def epi(c0, t0, tn, ps, pn):
            k = self._k % 4
            self._k += 1
            rn = '%s_x%d' % (tag, k)
            self.dma(xs[k][:, 0:tn], self.xT[c0:c0 + 128, t0:t0 + tn], ['xT'], [rn])
            self.TT('dve', xs[k][:, 0:tn], xs[k][:, 0:tn], ps[:, 0:tn], ALU.add, [rn, pn], [rn])
            self.dma(self.xT[c0:c0 + 128, t0:t0 + tn], xs[k][:, 0:tn], [rn], ['xT'])
        return epi

    def phase_outproj(self, l, actT, I, A):
        self.dma(actT[:], self.yT.rearrange('(c p) t -> p c t', p=128), ['yT_rw', 'yT_hg', 'yT_lru'], ['actT'])
        w = I['w_out']
        blocks = [[(w[l, :, c0:c0 + 512], 512)] for c0 in range(0, D, 512)]
        repi = self.residual_epi(A, 'op%d' % l)
        self.gemm('op%d' % l, A, 16, blocks, lambda kc, t0, tn: actT[:, kc, t0:t0 + tn], ['actT'],
                  lambda bi, u, gi, t0, tn, ps, pn: repi(bi * 512 + u * 128, t0, tn, ps, pn), cbw=512)

    def phase_ffn_up(self, l, actT, I, O_, A):
        TP, NT = self.TP, self.NT
        fp = self.fprm
        w = I['ffn_w_up']
        blocks = [[(w[l, :, j * 256:(j + 1) * 256], 256), (w[l, :, DFF + j * 256:DFF + (j + 1) * 256], 256)] for j in range(DFF // 256)]
        gp = [A('fu%d_gp%d' % (l, i), [128, TP + 2]) for i in range(2)]
        gs = [A('fu%d_gs%d' % (l, i), [128, 16, 10]) for i in range(2)]
        vb = [A('fu%d_vb%d' % (l, i), [128, NT]) for i in range(2)]
        cv = A('fu%d_cv' % l, [128, NT])
        hb = [A('fu%d_hb%d' % (l, i), [128, NT], BF16) for i in range(2)]
        for i in range(2):
            self.MEMSET('pool', gp[i][:, 0:2], 0.0, ['gp%d' % i])
        ng = len(self.tg)
        CI = [A('fu%d_ci%d' % (l, i), [32, 128]) for i in range(2)]
        OS = [A('fu%d_os%d' % (l, i), [34, 128]) for i in range(2)]
        tmpo = A('fu%d_tmpo' % l, [128, 34])
        cin = I['cache_ffn_conv'][l].rearrange('s j c -> (s j) c')
        for b0 in range(2):
            self.dma(CI[b0][:], cin[:, b0 * 128:(b0 + 1) * 128], [], ['fu_ci%d' % b0])

        def epi(bi, u, gi, t0, tn, ps, pn):
            b = u % 2
            blk = 2 * bi + b
            if u < 2:
                if gi == 0:
                    ps2, pn2 = self.psum()
                    self.TR(ps2[:, 0:32], CI[b][:], ['fu_ci%d' % b], [pn2], n=32)
                    self.CP('dve', gs[b][:, :, 0:2], ps2[:, 0:32].rearrange('p (s j) -> p s j', j=2), [pn2], ['gs%d' % b])
                if t0 < TP:
                    self.CP('act', gp[b][:, 2 + t0:2 + t0 + tn], ps[:, 0:tn], [pn], ['gp%d' % b])
                else:
                    self.CP('act', gs[b][:, :, 2:10], ps[:, 0:tn].rearrange('p (s t) -> p s t', t=8), [pn], ['gs%d' % b])
            else:
                self.CP('act' if gi % 2 == 0 else 'dve', vb[b][:, t0:t0 + tn], ps[:, 0:tn], [pn], ['vb%d' % b])
                if gi == 0 and bi + 1 < len(blocks):
                    nblk = 2 * (bi + 1) + b
                    self.dma(CI[b][:], cin[:, nblk * 128:(nblk + 1) * 128], [], ['fu_ci%d' % b])
                if gi == ng - 1:
                    self.CP('dve', tmpo[:, 0:32].rearrange('p (s j) -> p s j', j=2), gs[b][:, :, 8:10], ['gs%d' % b], ['fu_tmpo'])
                    self.CP('dve', tmpo[:, 32:34], gp[b][:, TP:TP + 2], ['gp%d' % b], ['fu_tmpo'])
                    ps2, pn2 = self.psum()
                    self.TR(ps2[0:34, 0:128], tmpo[:], ['fu_tmpo'], [pn2])
                    self.CP('act', OS[b][:], ps2[0:34, 0:128], [pn2], ['fu_os%d' % b])
                    self.dma(O_['s_ffn_conv'][l].rearrange('s j c -> (s j) c')[:, blk * 128:(blk + 1) * 128], OS[b][0:32, :], ['fu_os%d' % b], [])
                    self.dma(O_['p_ffn_conv'][l][:, blk * 128:(blk + 1) * 128], OS[b][32:34, :], ['fu_os%d' % b], [])
                    w_ = lambda j: fp[:, l, j * 44 + blk:j * 44 + blk + 1]
                    bb = fp[:, l, 132 + blk:133 + blk]
                    cvs = cv[:, TP:NT].rearrange('p (s t) -> p s t', t=8)
                    self.TSC('dve', cv[:, 0:TP], gp[b][:, 0:TP], w_(0), bb, ALU.mult, ALU.add, ['gp%d' % b, 'fprm'], ['cv'])
                    self.STT(cv[:, 0:TP], gp[b][:, 1:TP + 1], w_(1), cv[:, 0:TP], ALU.mult, ALU.add, ['gp%d' % b, 'fprm', 'cv'], ['cv'])
                    self.STT(cv[:, 0:TP], gp[b][:, 2:TP + 2], w_(2), cv[:, 0:TP], ALU.mult, ALU.add, ['gp%d' % b, 'fprm', 'cv'], ['cv'])
                    self.TSC('dve', cvs, gs[b][:, :, 0:8], w_(0), bb, ALU.mult, ALU.add, ['gs%d' % b, 'fprm', 'cv'], ['cv'])
                    self.STT(cvs, gs[b][:, :, 1:9], w_(1), cvs, ALU.mult, ALU.add, ['gs%d' % b, 'fprm', 'cv'], ['cv'])
                    self.STT(cvs, gs[b][:, :, 2:10], w_(2), cvs, ALU.mult, ALU.add, ['gs%d' % b, 'fprm', 'cv'], ['cv'])
                    self.ACT(cv[:], cv[:], AF.Silu, ['cv'], ['cv'])
                    self.TT('dve', hb[b][:], cv[:], vb[b][:], ALU.mult, ['cv', 'vb%d' % b], ['hb%d' % b])
                    self.dma(self.hT[blk * 128:(blk + 1) * 128, :], hb[b][:], ['hb%d' % b], ['hT'])

        self.gemm('fu%d' % l, A, 16, blocks, lambda kc, t0, tn: actT[:, kc, t0:t0 + tn], ['actT'], epi, cbw=512, single=True)

    def phase_ffn_down(self, l, I, A):
        NT = self.NT
        w = I['ffn_w_down']
        KC = DFF // 128
        parts = [self.tg[0:2], self.tg[2:]] if len(self.tg) > 2 else [self.tg]
        nmax = max(sum(n for _, n in p) for p in parts if p)
        hres = A('fd%d_h' % l, [128, KC, nmax], BF16)
        repi = self.residual_epi(A, 'fd%d' % l)
        blocks = [[(w[l, :, c0:c0 + 256], 256)] for c0 in range(0, D, 256)]
        tag = 'fd%d' % l
        bufs = ([A('%s_wst0' % tag, [128, KC, 256])] * 2, [A('%s_wbf%d' % (tag, i), [128, KC, 256], BF16) for i in range(2)], tag)
        for pi, part in enumerate(parts):
            if not part:
                continue
            s0 = part[0][0]
            n = sum(nn for _, nn in part)
            self.dma(hres[:, :, 0:n], self.hT[:, s0:s0 + n].rearrange('(c p) t -> p c t', p=128), ['hT'], ['hres'])
            self.gemm('fd%d_%d' % (l, pi), A, KC, blocks, lambda kc, t0, tn, s0=s0: hres[:, kc, t0 - s0:t0 - s0 + tn], ['hres'],
                      lambda bi, u, gi, t0, tn, ps, pn: repi(bi * 256 + u * 128, t0, tn, ps, pn), cbw=256, tgs=part, bufs=bufs, single=True)

    def phase_final(self, y_out):
        nc = self.nc
        with contextlib.ExitStack() as es:
            A = lambda name, shape, dt=F32: es.enter_context(nc.sbuf_tensor('fin_' + name, list(shape), dt))
            xg = [A('xg%d' % i, [128, 16, 512]) for i in range(2)]
            sqs = [A('sq%d' % i, [128, 16, 128], BF16) for i in range(2)]
            sds = [A('sd%d' % i, [128, 128]) for i in range(2)]
            rss = [A('rs%d' % i, [128, 128]) for i in range(2)]
            hns = [A('hn%d' % i, [128, 16, 128]) for i in range(2)]
            yo = [A('yo%d' % i, [128, D]) for i in range(2)]
            k = 0
            for gi, (g0, gn) in enumerate(self.tg):
                gb = gi % 2
                xn = 'fxg%d' % gb
                self.dma(xg[gb][:, :, 0:gn], self.xT[:, g0:g0 + gn].rearrange('(c p) t -> p c t', p=128), ['xT'], [xn])
                for j in range(gn // 128):
                    b = k % 2
                    k += 1
                    sq, sd, rs, hn = sqs[b], sds[b], rss[b], hns[b]
                    t0 = g0 + j * 128
                    xv = xg[gb][:, :, j * 128:(j + 1) * 128]
                    self.ACT(sq[:], xv, AF.Square, [xn], ['fsq%d' % b])
                    ps, pn = self.psum()
                    for kc in range(16):
                        self.MM(ps[:, 0:128], self.ones_bf[:], sq[:, kc, :], kc == 0, kc == 15, ['fsq%d' % b, 'ones_bf'], [pn])
                    self.ACT(sd[:], ps[:, 0:128], AF.Sqrt, [pn], ['fsd%d' % b], bias=EPS, scale=1.0 / D)
                    self.RECIP(rs[:], sd[:], ['fsd%d' % b], ['frs%d' % b])
                    for kc in range(16):
                        self.STT(hn[:, kc, :], xg[gb][:, kc, j * 128:(j + 1) * 128], self.prmf[:, kc:kc + 1], rs[:], ALU.mult, ALU.mult, [xn, 'frs%d' % b, 'prmf'], ['fhn%d' % b])
                    for q in range(4):
                        ps, pn = self.psum()
                        for jj in range(4):
                            kc = q * 4 + jj
                            self.TR(ps[:, jj * 128:(jj + 1) * 128], hn[:, kc, :], ['fhn%d' % b], [pn])
                        self.CP('act' if q % 2 == 0 else 'dve', yo[b][:, q * 512:(q + 1) * 512], ps[:], [pn], ['fyo%d_%d' % (b, q)])
                    self.dma(y_out[t0:t0 + 128, :], yo[b][:], ['fyo%d_%d' % (b, q) for q in range(4)], [])


_CACHE = {}
TP_FULL = 2048
W_NAMES = ['norm_mix', 'w_in', 'rwkv_mu', 'rwkv_w0', 'rwkv_w2', 'rwkv_a0', 'rwkv_a2', 'rwkv_g2', 'rwkv_k_k', 'rwkv_k_a',
           'rwkv_r_k', 'rwkv_ln_w', 'rwkv_ln_b', 'hgrn_lb_logits', 'hgrn_norm_w', 'rglru_conv_w', 'rglru_conv_b', 'rglru_wa',
           'rglru_ba', 'rglru_wx', 'rglru_bx', 'rglru_lambda', 'w_out', 'norm_ffn', 'ffn_w_up', 'ffn_conv_w', 'ffn_conv_b',
           'ffn_w_down', 'norm_final']
S_NAMES = ['state_rwkv', 'state_rwkv_shift', 'state_hgrn', 'state_rglru', 'cache_rglru_conv', 'cache_ffn_conv']
O_NAMES = ['rwkv', 'rwkv_shift', 'hgrn', 'rglru', 'rglru_conv', 'ffn_conv']


def kernel(**inputs):
    if 'nc' not in _CACHE:
        b = Builder(TP_FULL, DEPTH)
        _CACHE['nc'] = b.build()
        _CACHE['b'] = b
    nc = _CACHE['nc']
    f32 = lambda a: np.ascontiguousarray(np.asarray(a, dtype=np.float32))
    consts = make_consts()
    shared = {k: f32(inputs[k]) for k in W_NAMES}
    for k, v in consts.items():
        shared['c_' + k] = v
    xp = f32(inputs['x_prompt'])
    xs = f32(inputs['x_sample'])
    states = {k: f32(inputs[k]) for k in S_NAMES}
    ncore = 8
    in_maps = []
    for i in range(ncore):
        m = dict(shared)
        m['x'] = np.ascontiguousarray(np.concatenate([xp[i % 4], xs[NSEQ * i:NSEQ * (i + 1)].reshape(NSEQ * TS, D)], axis=0))
        for k in S_NAMES:
            m[k] = np.ascontiguousarray(states[k][:, NSEQ * i:NSEQ * (i + 1)])
        in_maps.append(m)
    res = run_bass_kernel_spmd(nc, in_maps, core_ids=list(range(ncore)))
    R = res.results
    y_prompt = np.stack([np.asarray(R[b]['y'])[:TP_FULL] for b in range(4)], axis=0).astype(np.float32)
    y_sample = np.concatenate([np.asarray(R[i]['y'])[TP_FULL:].reshape(NSEQ, TS, D) for i in range(ncore)], axis=0).astype(np.float32)
    outs = [y_prompt, y_sample]
    for n in O_NAMES:
        outs.append(np.stack([np.asarray(R[b]['p_' + n]) for b in range(4)], axis=1).astype(np.float32))
    for n in O_NAMES:
        outs.append(np.concatenate([np.asarray(R[i]['s_' + n]) for i in range(ncore)], axis=1).astype(np.float32))
    return tuple(outs)
```
